# Optimizing a Trainium2 kernel written in Bass

```python
import jax, jax.numpy as jnp
from jax import lax
import numpy as np

D_MODEL = 1024
BATCH = 16
SEQ = 4096
DEPTH = 2

MLA_HEADS = 8
MLA_NOPE = 64
MLA_ROPE = 32
MLA_V = 64
MLA_Q_RANK = 256
MLA_KV_RANK = 128
Q_BLOCK = 128
CONV_CH = 512
CONV_WIDTH = 31
RET_HEADS = 4
RET_DK = 128
RET_DV = 256
RET_CHUNK = 128
FFN_HIDDEN = -(-8 * D_MODEL // (3 * 256)) * 256
N_BRANCH = 3
ROPE_BASE = 10000.0
EPS = 1e-6

IN_SPLITS = (MLA_Q_RANK, MLA_KV_RANK, MLA_ROPE, 2 * CONV_CH,
             RET_HEADS * RET_DK, RET_HEADS * RET_DK, RET_HEADS * RET_DV, RET_HEADS * RET_DV,
             N_BRANCH * D_MODEL)
IN_COLS = (MLA_Q_RANK + MLA_KV_RANK + MLA_ROPE + 2 * CONV_CH + 2 * RET_HEADS * RET_DK
           + 2 * RET_HEADS * RET_DV + N_BRANCH * D_MODEL)

kernel_name = "hybrid_mla_conformer_retention_encoder"


def rms_norm(x, g):
    xf = x.astype(jnp.float32)
    y = xf * lax.rsqrt(jnp.mean(xf * xf, -1, keepdims=True) + EPS)
    return (y * g.astype(jnp.float32)).astype(x.dtype)


def layer_norm(x, g, b):
    xf = x.astype(jnp.float32)
    mu = jnp.mean(xf, -1, keepdims=True)
    var = jnp.mean(jnp.square(xf - mu), -1, keepdims=True)
    y = (xf - mu) * lax.rsqrt(var + EPS)
    return (y * g.astype(jnp.float32) + b.astype(jnp.float32)).astype(x.dtype)


def rope_tables(positions, dim):
    inv = ROPE_BASE ** (-jnp.arange(0, dim, 2, dtype=jnp.float32) / dim)
    ang = positions.astype(jnp.float32)[..., None] * inv
    return jnp.cos(ang), jnp.sin(ang)


def apply_rope(x, cos, sin):
    if x.ndim == 4:
        cos, sin = cos[:, :, None, :], sin[:, :, None, :]
    cos, sin = cos.astype(x.dtype), sin.astype(x.dtype)
    x1, x2 = jnp.split(x, 2, axis=-1)
    return jnp.concatenate([x1 * cos - x2 * sin, x2 * cos + x1 * sin], axis=-1)


def mla_branch(c_q, c_kv, k_pe, cos, sin, q_norm, w_uq, kv_norm, w_ukv, w_o):
    B, S, _ = c_q.shape
    q = (rms_norm(c_q, q_norm) @ w_uq).reshape(B, S, MLA_HEADS, MLA_NOPE + MLA_ROPE)
    q_nope = q[..., :MLA_NOPE]
    q_pe = apply_rope(q[..., MLA_NOPE:], cos, sin)
    kv = (rms_norm(c_kv, kv_norm) @ w_ukv).reshape(B, S, MLA_HEADS, MLA_NOPE + MLA_V)
    k_nope, v = kv[..., :MLA_NOPE], kv[..., MLA_NOPE:]
    k_pe = apply_rope(k_pe, cos, sin)
    scale = (MLA_NOPE + MLA_ROPE) ** -0.5
    nq = S // Q_BLOCK
    qn_b = q_nope.reshape(B, nq, Q_BLOCK, MLA_HEADS, MLA_NOPE).swapaxes(0, 1)
    qp_b = q_pe.reshape(B, nq, Q_BLOCK, MLA_HEADS, MLA_ROPE).swapaxes(0, 1)

    def block(args):
        qn, qp = args
        s = (jnp.einsum('bqhd,bkhd->bhqk', qn, k_nope)
             + jnp.einsum('bqhr,bkr->bhqk', qp, k_pe))
        p = jax.nn.softmax(s.astype(jnp.float32) * scale, axis=-1).astype(v.dtype)
        return jnp.einsum('bhqk,bkhd->bqhd', p, v)

    o = lax.map(block, (qn_b, qp_b))
    o = o.swapaxes(0, 1).reshape(B, S, MLA_HEADS * MLA_V)
    return o @ w_o


def conv_branch(u, w_dw, b_dw, ln_g, ln_b, w_pw):
    a, gate = jnp.split(u, 2, axis=-1)
    a = a * jax.nn.sigmoid(gate)
    y = lax.conv_general_dilated(
        a, w_dw[:, None, :].astype(a.dtype), window_strides=(1,),
        padding=[(CONV_WIDTH // 2, CONV_WIDTH // 2)],
        dimension_numbers=('NWC', 'WIO', 'NWC'), feature_group_count=CONV_CH) + b_dw
    y = jax.nn.silu(layer_norm(y, ln_g, ln_b))
    return y @ w_pw


def retention_direction(q, k, v, log_gamma, strict):
    C = q.shape[3]
    idx = jnp.arange(C, dtype=jnp.float32)
    diff = idx[:, None] - idx[None, :]
    mask = (diff > 0) if strict else (diff >= 0)
    lg = log_gamma[:, None, None]
    decay = jnp.where(mask, jnp.exp(jnp.where(mask, diff, 0.0) * lg), 0.0).astype(q.dtype)
    scores = jnp.einsum('bhnqd,bhnkd->bhnqk', q, k) * decay[None, :, None]
    inner = jnp.einsum('bhnqk,bhnke->bhnqe', scores, v)
    k_dec = k * jnp.exp((C - 1 - idx)[None, :] * log_gamma[:, None]).astype(k.dtype)[None, :, None, :, None]
    u = jnp.einsum('bhnkd,bhnke->nbhde', k_dec, v)
    chunk_decay = jnp.exp(C * log_gamma).astype(u.dtype)[None, :, None, None]

    def step(state, xs):
        q_c, u_c = xs
        out = jnp.einsum('bhqd,bhde->bhqe', q_c, state)
        return chunk_decay * state + u_c, out

    state0 = jnp.zeros(u.shape[1:], u.dtype)
    _, cross = lax.scan(step, state0, (jnp.moveaxis(q, 2, 0), u))
    cross = jnp.moveaxis(cross, 0, 2) * jnp.exp((idx + 1)[None, :] * log_gamma[:, None]).astype(q.dtype)[None, :, None, :, None]
    return inner + cross


def retention_branch(q, k, v, g, cos, sin, decay_logits, gn_g, w_o):
    B, S, _ = q.shape
    nc = S // RET_CHUNK
    q = apply_rope(q.reshape(B, S, RET_HEADS, RET_DK), cos, sin)
    k = apply_rope(k.reshape(B, S, RET_HEADS, RET_DK), cos, sin) * (RET_DK ** -0.5)
    v = v.reshape(B, S, RET_HEADS, RET_DV)

    def chunked(t):
        return t.reshape(B, nc, RET_CHUNK, RET_HEADS, -1).transpose(0, 3, 1, 2, 4)

    def unchunk(t):
        return t.transpose(0, 2, 3, 1, 4).reshape(B, S, RET_HEADS, -1)

    def flip(t):
        return jnp.flip(t, axis=1)

    log_gamma = jax.nn.log_sigmoid(decay_logits.astype(jnp.float32))
    fwd = retention_direction(chunked(q), chunked(k), chunked(v), log_gamma[0], False)
    bwd = retention_direction(chunked(flip(q)), chunked(flip(k)), chunked(flip(v)), log_gamma[1], True)
    o = unchunk(fwd) + flip(unchunk(bwd))
    of = o.astype(jnp.float32)
    mu = jnp.mean(of, -1, keepdims=True)
    var = jnp.mean(jnp.square(of - mu), -1, keepdims=True)
    on = ((of - mu) * lax.rsqrt(var + EPS)).reshape(B, S, RET_HEADS * RET_DV)
    on = (on * gn_g.astype(jnp.float32)).astype(g.dtype)
    return (jax.nn.silu(g) * on) @ w_o


def setup_inputs(seed: int = 0) -> dict:
    key = jax.random.key(seed)
    ks = jax.random.split(key, 32)
    f32 = jnp.float32

    def w(k, shape, fan_in):
        return jax.random.normal(k, shape, f32) * (fan_in ** -0.5)

    def gain(k, shape):
        return 1.0 + 0.02 * jax.random.normal(k, shape, f32)

    def bias(k, shape):
        return 0.02 * jax.random.normal(k, shape, f32)

    L = DEPTH
    x = jax.random.normal(ks[0], (BATCH, SEQ, D_MODEL), f32)
    offset = jax.random.randint(ks[1], (BATCH, 1), 0, SEQ, dtype=jnp.int32)
    positions = offset + jnp.arange(SEQ, dtype=jnp.int32)[None, :]
    k_exp = 5.0 + jnp.arange(RET_HEADS, dtype=f32)
    base_logit = jnp.log(jnp.exp2(k_exp) - 1.0)
    ret_decay_logits = base_logit[None, None, :] + 0.1 * jax.random.normal(ks[2], (L, 2, RET_HEADS), f32)
    return {
        "x": x,
        "positions": positions,
        "ln_mix_pre": gain(ks[3], (L, D_MODEL)),
        "ln_mix_post": gain(ks[4], (L, D_MODEL)),
        "ln_ffn_pre": gain(ks[5], (L, D_MODEL)),
        "ln_ffn_post": gain(ks[6], (L, D_MODEL)),
        "w_in": w(ks[7], (L, D_MODEL, IN_COLS), D_MODEL),
        "mla_q_norm": gain(ks[8], (L, MLA_Q_RANK)),
        "mla_w_uq": w(ks[9], (L, MLA_Q_RANK, MLA_HEADS * (MLA_NOPE + MLA_ROPE)), MLA_Q_RANK),
        "mla_kv_norm": gain(ks[10], (L, MLA_KV_RANK)),
        "mla_w_ukv": w(ks[11], (L, MLA_KV_RANK, MLA_HEADS * (MLA_NOPE + MLA_V)), MLA_KV_RANK),
        "mla_w_o": w(ks[12], (L, MLA_HEADS * MLA_V, D_MODEL), MLA_HEADS * MLA_V),
        "conv_w_dw": w(ks[13], (L, CONV_WIDTH, CONV_CH), CONV_WIDTH),
        "conv_b_dw": bias(ks[14], (L, CONV_CH)),
        "conv_ln_g": gain(ks[15], (L, CONV_CH)),
        "conv_ln_b": bias(ks[16], (L, CONV_CH)),
        "conv_w_pw": w(ks[17], (L, CONV_CH, D_MODEL), CONV_CH),
        "ret_decay_logits": ret_decay_logits,
        "ret_gn_g": gain(ks[18], (L, RET_HEADS * RET_DV)),
        "ret_w_o": w(ks[19], (L, RET_HEADS * RET_DV, D_MODEL), RET_HEADS * RET_DV),
        "w_out": w(ks[20], (L, D_MODEL, D_MODEL), D_MODEL),
        "ffn_w_gate": w(ks[21], (L, D_MODEL, FFN_HIDDEN), D_MODEL),
        "ffn_w_up": w(ks[22], (L, D_MODEL, FFN_HIDDEN), D_MODEL),
        "ffn_w_down": w(ks[23], (L, FFN_HIDDEN, D_MODEL), FFN_HIDDEN),
    }


def reference(x, positions, ln_mix_pre, ln_mix_post, ln_ffn_pre, ln_ffn_post, w_in,
              mla_q_norm, mla_w_uq, mla_kv_norm, mla_w_ukv, mla_w_o,
              conv_w_dw, conv_b_dw, conv_ln_g, conv_ln_b, conv_w_pw,
              ret_decay_logits, ret_gn_g, ret_w_o, w_out,
              ffn_w_gate, ffn_w_up, ffn_w_down):
    B, S, _ = x.shape
    points = np.cumsum(IN_SPLITS)[:-1].tolist()
    cos_m, sin_m = rope_tables(positions, MLA_ROPE)
    cos_r, sin_r = rope_tables(positions, RET_DK)
    for l in range(DEPTH):
        h = rms_norm(x, ln_mix_pre[l])
        proj = h @ w_in[l]
        c_q, c_kv, k_pe, conv_u, r_q, r_k, r_v, r_g, gate_logits = jnp.split(proj, points, axis=-1)
        y_mla = mla_branch(c_q, c_kv, k_pe, cos_m, sin_m, mla_q_norm[l], mla_w_uq[l],
                           mla_kv_norm[l], mla_w_ukv[l], mla_w_o[l])
        y_conv = conv_branch(conv_u, conv_w_dw[l], conv_b_dw[l], conv_ln_g[l], conv_ln_b[l], conv_w_pw[l])
        y_ret = retention_branch(r_q, r_k, r_v, r_g, cos_r, sin_r, ret_decay_logits[l],
                                 ret_gn_g[l], ret_w_o[l])
        gates = jax.nn.sigmoid(gate_logits).reshape(B, S, N_BRANCH, D_MODEL)
        merged = gates[:, :, 0] * y_mla + gates[:, :, 1] * y_conv + gates[:, :, 2] * y_ret
        x = x + rms_norm(merged @ w_out[l], ln_mix_post[l])
        h = rms_norm(x, ln_ffn_pre[l])
        f = (jax.nn.silu(h @ ffn_w_gate[l]) * (h @ ffn_w_up[l])) @ ffn_w_down[l]
        x = x + rms_norm(f, ln_ffn_post[l])
    return x
```

```python
import numpy as np
import concourse.bass as bass
import concourse.mybir as mybir
from concourse.bass_utils import run_bass_kernel_spmd
from contextlib import ExitStack

F32 = mybir.dt.float32
BF16 = mybir.dt.bfloat16
I32 = mybir.dt.int32
AF = mybir.ActivationFunctionType
ALU = mybir.AluOpType

D = 1024
L = 2
NH = 8
RH = 4
FH = 2816
INC = 7584
EPS = 1e-6
SAME_ENGINE_SYNC = False
import os
SUB = os.environ.get("SUB", "lcrg")
CUT = float(os.environ.get("CUT", "9"))
TWO_PI = float(2 * np.pi)
PI = float(np.pi)

O_CQ, O_CKV, O_KPE, O_CONV, O_RQ, O_RK, O_RV, O_RG, O_GATE = 0, 256, 384, 416, 1440, 1952, 2464, 3488, 4512


class Tr:
    __slots__ = ("w", "r", "sem", "cnt", "name", "excl")

    def __init__(self, name="", excl=False):
        self.excl = excl
        self.w = {}
        self.r = {}
        self.sem = None
        self.cnt = 0
        self.name = name


class Eng:
    def __init__(self, e, sem, key):
        self.e = e
        self.sem = sem
        self.key = key
        self.cnt = 0
        self.waited = {}


class Prog:
    def __init__(self, nc, es):
        self.nc = nc
        self.es = es
        self.sems = {}
        self.nsem = 0
        self.E = {}
        self.pool = []
        self.live = []
        self.uid = 0
        for name, e in (("pe", nc.tensor), ("act", nc.scalar), ("dve", nc.vector), ("pool", nc.gpsimd), ("sp", nc.sync)):
            s, k = self.newsem("e_" + name)
            self.E[name] = Eng(e, s, k)

    def newsem(self, name):
        s = self.es.enter_context(self.nc.semaphore(name + "_%d" % self.nsem))
        k = self.nsem
        self.nsem += 1
        self.sems[k] = s
        return s, k

    def _waits(self, E, r, w, pw):
        need = {}
        for t in r:
            for k, v in t.w.items():
                if need.get(k, 0) < v:
                    need[k] = v
            if t.excl:
                for k, v in t.r.items():
                    if need.get(k, 0) < v:
                        need[k] = v
        for t in w:
            for d in (t.w, t.r):
                for k, v in d.items():
                    if need.get(k, 0) < v:
                        need[k] = v
        for t in pw:
            for d in (t.w, t.r):
                for k, v in d.items():
                    if need.get(k, 0) < v:
                        need[k] = v
        for k, v in need.items():
            if k == E.key and not SAME_ENGINE_SYNC:
                continue
            if E.waited.get(k, 0) < v:
                E.e.wait_ge(self.sems[k], v)
                E.waited[k] = v

    def op(self, en, fn, r=(), w=(), pw=()):
        E = self.E[en]
        self._waits(E, r, w, pw)
        ins = fn(E.e)
        E.cnt += 1
        ins.then_inc(E.sem, 1)
        for t in r:
            t.r[E.key] = E.cnt
        for t in w:
            t.w = {E.key: E.cnt}
            t.r = {}
        for t in pw:
            t.w[E.key] = E.cnt

    def dma(self, q, out, in_, sb, r=(), w=(), pw=()):
        Q = self.E[q]
        self._waits(Q, r, w, pw)
        if sb.sem is None:
            if self.pool:
                sb.sem, sb.cnt = self.pool.pop()
            else:
                sb.sem = self.newsem("d")
            self.live.append(sb)
        sem, key = sb.sem
        sb.cnt += 16
        Q.e.dma_start(out=out, in_=in_).then_inc(sem, 16)
        for t in r:
            t.r[key] = sb.cnt
        for t in w:
            t.w = {key: sb.cnt}
            t.r = {}
        for t in pw:
            t.w[key] = sb.cnt

    def barrier(self):
        sp = self.E["sp"]
        for en in ("pe", "act", "dve", "pool"):
            E = self.E[en]
            if sp.waited.get(E.key, 0) < E.cnt:
                sp.e.wait_ge(E.sem, E.cnt)
                sp.waited[E.key] = E.cnt
        for sb in self.live:
            sem, key = sb.sem
            if sp.waited.get(key, 0) < sb.cnt:
                sp.e.wait_ge(sem, sb.cnt)
                sp.waited[key] = sb.cnt
        if not hasattr(self, "bar"):
            self.bar = self.newsem("bar")
            self.barcnt = 0
        self.barcnt += 1
        sp.e.sem_inc(self.bar[0], 1)
        for en in ("pe", "act", "dve", "pool"):
            E = self.E[en]
            E.e.wait_ge(self.bar[0], self.barcnt)
            for en2 in ("pe", "act", "dve", "pool"):
                E.waited[self.E[en2].key] = max(E.waited.get(self.E[en2].key, 0), self.E[en2].cnt)
            for sb in self.live:
                E.waited[sb.sem[1]] = max(E.waited.get(sb.sem[1], 0), sb.cnt)
        for sb in self.live:
            self.pool.append((sb.sem, sb.cnt))
            sb.sem = None
        self.live = []

    def wait_all(self, en, trs):
        E = self.E[en]
        self._waits(E, trs, (), ())

    def mm(self, out, lhsT, rhs, start, stop, r, tr, first=None):
        if first is None:
            first = start
        self.op("pe", lambda e: e.matmul(out, lhsT, rhs, start=bool(start), stop=bool(stop)), r=r,
                w=[tr] if first else (), pw=() if first else [tr])

    def tp(self, out, in_, ident, r, tr, first):
        self.op("pe", lambda e: e.transpose(out, in_, ident), r=r, w=[tr] if first else (), pw=() if first else [tr])


_UID = [0]


def sbt(nc, name, shape, dt):
    _UID[0] += 1
    return nc.sbuf_tensor("%s_u%d" % (name, _UID[0]), shape, dt)


class Ring:
    cnt = [0]

    def __init__(self, nc, es, name, shape, dt, n):
        Ring.cnt[0] += 1
        self.t = [es.enter_context(sbt(nc, "%s_%d_%d" % (name, Ring.cnt[0], i), shape, dt)) for i in range(n)]
        self.tr = [Tr("%s%d" % (name, i)) for i in range(n)]
        self.i = 0
        self.n = n

    def next(self):
        i = self.i
        self.i = (i + 1) % self.n
        return self.t[i], self.tr[i]


def build(S, NS, debug=False, nlayers=L, stages="WTNACRMF"):
    T = S // 128
    NB = S // 512
    nc = bass.Bass("TRN2", target_bir_lowering=False)

    def din(name, shape, dt=F32):
        return nc.dram_tensor(name, shape, dt, kind="ExternalInput").ap()

    x_in = din("x", [NS, S, D])
    pos_in = din("positions", [NS, S], I32)
    ln_mix_pre = din("ln_mix_pre", [L, D]); ln_mix_post = din("ln_mix_post", [L, D])
    ln_ffn_pre = din("ln_ffn_pre", [L, D]); ln_ffn_post = din("ln_ffn_post", [L, D])
    w_in = din("w_in", [L, D, INC])
    mla_q_norm = din("mla_q_norm", [L, 256]); mla_w_uq = din("mla_w_uq", [L, 256, 768])
    mla_kv_norm = din("mla_kv_norm", [L, 128]); mla_w_ukv = din("mla_w_ukv", [L, 128, 1024])
    mla_w_o = din("mla_w_o", [L, 512, D])
    conv_w_dw = din("conv_w_dw", [L, 31, 512]); conv_b_dw = din("conv_b_dw", [L, 512])
    conv_ln_g = din("conv_ln_g", [L, 512]); conv_ln_b = din("conv_ln_b", [L, 512])
    conv_w_pw = din("conv_w_pw", [L, 512, D])
    ret_decay_logits = din("ret_decay_logits", [L, 2, RH])
    ret_gn_g = din("ret_gn_g", [L, D]); ret_w_o = din("ret_w_o", [L, D, D])
    w_out = din("w_out", [L, D, D])
    ffn_w_gate = din("ffn_w_gate", [L, D, FH]); ffn_w_up = din("ffn_w_up", [L, D, FH]); ffn_w_down = din("ffn_w_down", [L, FH, D])
    cst_in = din("rope_consts", [2, 160])
    y_out = nc.dram_tensor("y", [NS, S, D], F32, kind="ExternalOutput").ap()

    skind = "ExternalOutput" if debug else "Internal"

    def scr(name, shape, dt):
        return nc.dram_tensor(name, shape, dt, kind=skind).ap()

    wb = {
        "in": scr("wb_in", [L, D, INC], BF16), "uq": scr("wb_uq", [L, 256, 768], BF16), "ukv": scr("wb_ukv", [L, 128, 1024], BF16),
        "mo": scr("wb_mo", [L, 512, D], BF16), "pw": scr("wb_pw", [L, 512, D], BF16), "ro": scr("wb_ro", [L, D, D], BF16),
        "wo": scr("wb_wo", [L, D, D], BF16), "fg": scr("wb_fg", [L, D, FH], BF16), "fu": scr("wb_fu", [L, D, FH], BF16),
        "fd": scr("wb_fd", [L, FH, D], BF16),
    }
    wsrc = {"in": w_in, "uq": mla_w_uq, "ukv": mla_w_ukv, "mo": mla_w_o, "pw": conv_w_pw, "ro": ret_w_o, "wo": w_out,
            "fg": ffn_w_gate, "fu": ffn_w_up, "fd": ffn_w_down}
    wb_tr = {(k, l): Tr("wb_%s%d" % (k, l)) for k in wb for l in range(L)}

    tabd = scr("tabd", [NS, 128, T * 160], F32)
    QTd = scr("QTd", [NS, NH, 96, S], BF16); KTd = scr("KTd", [NS, NH, 96, S], BF16)
    Vd = scr("Vd", [NS, S, NH * 65], BF16)
    oTd = scr("oTd", [NS, 4, 128, S], BF16)
    aTd = scr("aTd", [NS, 4, 128, S], F32); cvTd = scr("cvTd", [NS, 4, 128, S], BF16)
    rqd = scr("rqd", [NS, S, 512], BF16); rkd = scr("rkd", [NS, S, 512], BF16)
    rvd = scr("rvd", [NS, S, 1024], BF16); sgd = scr("sgd", [NS, S, 1024], F32)
    rtTd = scr("rtTd", [NS, 8, 128, S], BF16)
    gtd = scr("gtd", [NS, 24, 128, S], F32)
    x1d = scr("x1d", [NS, S, D], F32)
    xLd = scr("xLd", [NS, S, D], F32)
    if debug:
        dbg_qT = scr("dbg_qT", [128, 3 * RH * 128], BF16); dbg_kT = scr("dbg_kT", [128, RH * 128], BF16)
        dbg_sm = scr("dbg_sm", [128, RH * 128], BF16); dbg_DT = scr("dbg_DT", [128, RH * 128], F32)
        dbg_CF = scr("dbg_CF", [128, RH * 128], F32); dbg_CB = scr("dbg_CB", [128, RH * 128], F32)
        dbg_O = scr("dbg_O", [128, 1024], F32); dbg_Sf = scr("dbg_Sf", [128, 1024], F32); dbg_R = scr("dbg_R", [128, 1024], BF16)
        dbg_pc = scr("dbg_pc", [128, 16], F32); dbg_kf = scr("dbg_kf", [128, 512], BF16)
    dtr = {}
    for nm in ("tabd", "QTd", "KTd", "Vd", "oTd", "aTd", "cvTd", "rqd", "rkd", "rvd", "sgd", "rtTd", "gtd", "x1d", "xLd", "y"):
        for s in range(NS):
            dtr[(nm, s)] = Tr("%s_%d" % (nm, s))

    with ExitStack() as es:
        P = Prog(nc, es)
        ps = [es.enter_context(nc.psum_tensor("psb%d" % i, [128, 512], F32)) for i in range(8)]
        pst = [Tr("ps%d" % i, excl=True) for i in range(8)]
        identf = es.enter_context(sbt(nc, "identf", [128, 128], F32))
        ident = es.enter_context(sbt(nc, "ident", [128, 128], BF16))
        onesf = es.enter_context(sbt(nc, "onesf", [128, 128], F32))
        epsc = es.enter_context(sbt(nc, "epsc", [128, 1], F32))
        ctr = Tr("consts")
        P.op("pool", lambda e: e.memset(identf[:], 0.0), w=[ctr])
        P.op("pool", lambda e: e.affine_select(out=identf[:], in_=identf[:], pattern=[[-1, 128]], compare_op=ALU.not_equal,
                                               fill=1.0, base=0, channel_multiplier=1), w=[ctr])
        P.op("pool", lambda e: e.tensor_copy(out=ident[:], in_=identf[:]), r=[ctr], pw=[ctr])
        P.op("pool", lambda e: e.memset(onesf[:], 1.0), pw=[ctr])
        P.op("pool", lambda e: e.memset(epsc[:], EPS), pw=[ctr])

        def rstd_from_ssq(ssq, rstd, n, trs):
            P.op("act", lambda e: e.activation(out=rstd, in_=ssq, func=AF.Sqrt, bias=epsc[:, 0:1], scale=1.0 / n), r=trs + [ctr], w=trs)
            P.op("dve", lambda e: e.reciprocal(out=rstd, in_=rstd), r=trs, w=trs)

        def ssq4(xt, sqt, sm, xtr_, sqtr, smtr):
            P.op("act", lambda e: e.activation(out=sqt[:], in_=xt[:], func=AF.Square), r=[xtr_], w=[sqtr])
            P.op("dve", lambda e: e.reduce_sum(out=sm[:, 0:1], in_=sqt[:], axis=mybir.AxisListType.X), r=[sqtr], w=[smtr])

        def stage_W(l):
            with ExitStack() as st:
                stf = Ring(nc, st, "wstf", [128, 2048], F32, 3)
                stb = Ring(nc, st, "wstb", [128, 2048], BF16, 3)
                i = 0
                for k in ("in", "uq", "ukv", "mo", "pw", "ro", "wo", "fg", "fu", "fd"):
                    src = wsrc[k][l]
                    dst = wb[k][l]
                    K, N = src.shape
                    for kc in range(K // 128):
                        for c0 in range(0, N, 2048):
                            w = min(2048, N - c0)
                            f, ftr = stf.next()
                            b, btr = stb.next()
                            P.dma("sp", f[:, 0:w], src[kc * 128:(kc + 1) * 128, c0:c0 + w], ftr, w=[ftr])
                            en = ("dve", "pool", "act")[i % 3]
                            if en == "act":
                                P.op(en, lambda e, f=f, b=b, w=w: e.copy(out=b[:, 0:w], in_=f[:, 0:w]), r=[ftr], w=[btr])
                            else:
                                P.op(en, lambda e, f=f, b=b, w=w: e.tensor_copy(out=b[:, 0:w], in_=f[:, 0:w]), r=[ftr], w=[btr])
                            P.dma("pool", dst[kc * 128:(kc + 1) * 128, c0:c0 + w], b[:, 0:w], btr, r=[btr], pw=[wb_tr[(k, l)]])
                            i += 1

                P.barrier()
        def stage_T(s):
            with ExitStack() as st:
                posrow = st.enter_context(sbt(nc, "posrow", [2, S], F32))
                posi = st.enter_context(sbt(nc, "posi", [1, S], I32))
                cst = st.enter_context(sbt(nc, "cst", [2, 160], F32))
                tab = st.enter_context(sbt(nc, "tab", [128, T * 160], F32))
                tmpf = st.enter_context(sbt(nc, "tmpf", [128, T * 160], F32))
                tmpi = st.enter_context(sbt(nc, "tmpi", [128, T * 160], I32))
                t_pr, t_pi, t_c, t_tab, t_f, t_i = Tr(), Tr(), Tr(), Tr(), Tr(), Tr()
                P.op("dve", lambda e: e.memset(posrow[:], 1.0), w=[t_pr])
                P.dma("sp", posi[:], pos_in[s:s + 1, :], t_pi, w=[t_pi])
                P.dma("sp", cst[:], cst_in[:, :], t_c, w=[t_c])
                P.op("dve", lambda e: e.tensor_copy(out=posrow[0:1, :], in_=posi[:]), r=[t_pi], pw=[t_pr])
                for t0 in range(0, T, 3):
                    n = min(3, T - t0)
                    bk = (t0 // 3) % 2
                    for j in range(n):
                        t = t0 + j
                        P.mm(ps[bk][:, j * 160:(j + 1) * 160], posrow[0:2, t * 128:(t + 1) * 128], cst[0:2, :], True, True,
                             [t_pr, t_c], pst[bk], first=(j == 0))
                    P.op("dve", lambda e, bk=bk, n=n, t0=t0: e.tensor_copy(out=tab[:, t0 * 160:(t0 + n) * 160], in_=ps[bk][:, 0:n * 160]),
                         r=[pst[bk]], pw=[t_tab])
                W = T * 160
                for c0 in range(0, W, 2560):
                    c1 = min(W, c0 + 2560)
                    a = tab[:, c0:c1]; f = tmpf[:, c0:c1]; ii = tmpi[:, c0:c1]
                    P.op("dve", lambda e, a=a, f=f: e.tensor_scalar(out=f, in0=a, scalar1=1.0 / TWO_PI, scalar2=None, op0=ALU.mult), r=[t_tab], w=[t_f])
                    P.op("dve", lambda e, f=f, ii=ii: e.tensor_copy(out=ii, in_=f), r=[t_f], w=[t_i])
                    P.op("dve", lambda e, f=f, ii=ii: e.tensor_copy(out=f, in_=ii), r=[t_i], w=[t_f])
                    P.op("dve", lambda e, a=a, f=f: e.scalar_tensor_tensor(out=a, in0=f, scalar=-6.28125, in1=a, op0=ALU.mult, op1=ALU.add), r=[t_f], w=[t_tab])
                    P.op("dve", lambda e, a=a, f=f: e.scalar_tensor_tensor(out=a, in0=f, scalar=-(TWO_PI - 6.28125), in1=a, op0=ALU.mult, op1=ALU.add), r=[t_f], w=[t_tab])
                    P.op("dve", lambda e, a=a, f=f: e.tensor_scalar(out=f, in0=a, scalar1=PI, scalar2=-TWO_PI, op0=ALU.is_gt, op1=ALU.mult), r=[t_tab], w=[t_f])
                    P.op("dve", lambda e, a=a, f=f: e.tensor_tensor(out=a, in0=a, in1=f, op=ALU.add), r=[t_f], w=[t_tab])
                    P.op("dve", lambda e, a=a, f=f: e.tensor_scalar(out=f, in0=a, scalar1=-PI, scalar2=TWO_PI, op0=ALU.is_lt, op1=ALU.mult), r=[t_tab], w=[t_f])
                    P.op("dve", lambda e, a=a, f=f: e.tensor_tensor(out=a, in0=a, in1=f, op=ALU.add), r=[t_f], w=[t_tab])
                    P.op("dve", lambda e, a=a: e.tensor_scalar(out=a, in0=a, scalar1=-3.1415925, scalar2=3.1415925, op0=ALU.max, op1=ALU.min), r=[t_tab], w=[t_tab])
                    P.op("act", lambda e, a=a: e.activation(out=a, in_=a, func=AF.Sin), r=[t_tab], w=[t_tab])
                P.dma("pool", tabd[s], tab[:], t_tab, r=[t_tab], w=[dtr[("tabd", s)]])

                P.barrier()
        def stage_NP(l, s, xsrc, xtr):
            with ExitStack() as st:
                hT = st.enter_context(sbt(nc, "hT", [128, 8, S], BF16)); t_hT = Tr("hT")
                gbc = st.enter_context(sbt(nc, "np_gbc", [128, D], F32)); t_g = Tr()
                gq = st.enter_context(sbt(nc, "np_gq", [128, 384], F32))
                tab = st.enter_context(sbt(nc, "np_tab", [128, T, 160], F32)); t_tab = Tr()
                P.dma("sp", gbc[:], ln_mix_pre[l].partition_broadcast(128), t_g, w=[t_g])
                P.dma("sp", gq[:, 0:256], mla_q_norm[l].partition_broadcast(128), t_g, pw=[t_g])
                P.dma("sp", gq[:, 256:384], mla_kv_norm[l].partition_broadcast(128), t_g, pw=[t_g])
                P.dma("sp", tab[:].rearrange("p t c -> p (t c)"), tabd[s], t_tab, r=[dtr[("tabd", s)]], w=[t_tab])
                xr = Ring(nc, st, "np_x", [128, D], F32, 3)
                hb = Ring(nc, st, "np_hb", [128, D], BF16, 2)
                sq = Ring(nc, st, "np_sq", [128, D], F32, 2)
                stt = Ring(nc, st, "np_st", [128, 8], F32, 4)
                for t in range(T):
                    xt, xtr_ = xr.next()
                    P.dma("sp", xt[:], xsrc[s, t * 128:(t + 1) * 128, :], xtr_, r=[xtr[s]], w=[xtr_])
                    sqt, sqtr = sq.next()
                    sm, smtr = stt.next()
                    ssq4(xt, sqt, sm, xtr_, sqtr, smtr)
                    rstd_from_ssq(sm[:, 0:1], sm[:, 1:2], D, [smtr])
                    h, htr = hb.next()
                    P.op("dve", lambda e, xt=xt, h=h, sm=sm: e.scalar_tensor_tensor(out=h[:], in0=xt[:], scalar=sm[:, 1:2], in1=gbc[:], op0=ALU.mult, op1=ALU.mult),
                         r=[xtr_, smtr, t_g], w=[htr])
                    bk = t % 2
                    pv = ps[bk][:].bitcast(BF16)
                    for kc in range(8):
                        P.tp(pv[:, kc * 128:(kc + 1) * 128], h[:, kc * 128:(kc + 1) * 128], ident[:], [htr, ctr], pst[bk], kc == 0)
                    en = "act" if t % 2 == 0 else "dve"
                    if en == "act":
                        P.op("act", lambda e, pv=pv, t=t: e.copy(out=hT[:, :, t * 128:(t + 1) * 128], in_=pv.rearrange("p (k c) -> p k c", k=8)), r=[pst[bk]], pw=[t_hT])
                    else:
                        P.op("dve", lambda e, pv=pv, t=t: e.tensor_copy(out=hT[:, :, t * 128:(t + 1) * 128], in_=pv.rearrange("p (k c) -> p k c", k=8)), r=[pst[bk]], pw=[t_hT])

                win = wb["in"][l].rearrange("(kc p) n -> p kc n", p=128)
                wtr_in = wb_tr[("in", l)]

                for g in ([ExitStack()] if "l" in SUB else []):
                    wl = g.enter_context(sbt(nc, "wl", [128, 8, 416], BF16)); t_wl = Tr()
                    wuq = g.enter_context(sbt(nc, "wuq", [128, 2, 768], BF16)); t_wuq = Tr()
                    wk = g.enter_context(sbt(nc, "wk", [128, 8, 64], BF16))
                    wv = g.enter_context(sbt(nc, "wv", [128, 8, 64], BF16)); t_wkv = Tr()
                    P.dma("sp", wl[:], win[:, :, 0:416], t_wl, r=[wtr_in], w=[t_wl])
                    P.dma("sp", wuq[:], wb["uq"][l].rearrange("(kc p) n -> p kc n", p=128), t_wuq, r=[wb_tr[("uq", l)]], w=[t_wuq])
                    ukv_v = wb["ukv"][l].rearrange("p (h c) -> p h c", h=8)
                    P.dma("sp", wk[:], ukv_v[:, :, 0:64], t_wkv, r=[wb_tr[("ukv", l)]], w=[t_wkv])
                    P.dma("sp", wv[:], ukv_v[:, :, 64:128], t_wkv, pw=[t_wkv])
                    lat = Ring(nc, g, "lat", [128, 416], BF16, 2)
                    sqj = Ring(nc, g, "lsq", [128, 256], F32, 2)
                    stl = Ring(nc, g, "lst", [128, 8], F32, 3)
                    tmpr = Ring(nc, g, "ltmp", [128, 4, 8, 16], F32, 2)
                    latT = Ring(nc, g, "latT", [128, 4, 128], BF16, 2)
                    qsb = Ring(nc, g, "qsb", [128, 8, 128], BF16, 2)
                    for qt_, qttr_ in zip(qsb.t, qsb.tr):
                        P.op("pool", lambda e, qt_=qt_: e.memset(qt_[:], 0.0), w=[qttr_])
                    qTb = Ring(nc, g, "qTb", [96, 8, 512], BF16, 2)
                    kTb = Ring(nc, g, "kTb", [96, 8, 512], BF16, 2)
                    ckb = Ring(nc, g, "ckb", [128, 512], BF16, 2)
                    kpb = Ring(nc, g, "kpb", [32, 512], BF16, 2)
                    vsb = Ring(nc, g, "vsb", [128, 8, 65], BF16, 2)
                    for vt, vtr in zip(vsb.t, vsb.tr):
                        P.op("pool", lambda e, vt=vt: e.memset(vt[:], 1.0), w=[vtr])
                    for tb in range(NB):
                        qT, qTtr = qTb.next()
                        kT, kTtr = kTb.next()
                        ck, cktr = ckb.next()
                        kp, kptr = kpb.next()
                        for tt in range(4):
                            t = tb * 4 + tt
                            tok = slice(t * 128, (t + 1) * 128)
                            for kc in range(8):
                                P.mm(ps[2][:, 0:416], hT[:, kc, tok], wl[:, kc, :], kc == 0, kc == 7, [t_hT, t_wl], pst[2])
                            la, latr = lat.next()
                            sj, sjtr = sqj.next()
                            sm, smtr = stl.next()
                            P.op("pool", lambda e, sm=sm: e.memset(sm[:], 0.0), w=[smtr])
                            P.op("act", lambda e, sj=sj, sm=sm: e.activation(out=sj[:, 0:256], in_=ps[2][:, 0:256], func=AF.Square, accum_out=sm[:, 0:1]), r=[pst[2], smtr], w=[sjtr], pw=[smtr])
                            P.op("act", lambda e, sj=sj, sm=sm: e.activation(out=sj[:, 0:128], in_=ps[2][:, 256:384], func=AF.Square, accum_out=sm[:, 1:2]), r=[pst[2], smtr], w=[sjtr], pw=[smtr])
                            P.op("act", lambda e, sm=sm: e.activation(out=sm[:, 2:3], in_=sm[:, 0:1], func=AF.Sqrt, bias=epsc[:, 0:1], scale=1.0 / 256), r=[smtr, ctr], pw=[smtr])
                            P.op("act", lambda e, sm=sm: e.activation(out=sm[:, 3:4], in_=sm[:, 1:2], func=AF.Sqrt, bias=epsc[:, 0:1], scale=1.0 / 128), r=[smtr], pw=[smtr])
                            P.op("dve", lambda e, sm=sm: e.reciprocal(out=sm[:, 4:6], in_=sm[:, 2:4]), r=[smtr], pw=[smtr])
                            P.op("dve", lambda e, la=la, sm=sm: e.scalar_tensor_tensor(out=la[:, 0:256], in0=ps[2][:, 0:256], scalar=sm[:, 4:5], in1=gq[:, 0:256], op0=ALU.mult, op1=ALU.mult),
                                 r=[pst[2], smtr, t_g], w=[latr])
                            P.op("dve", lambda e, la=la, sm=sm: e.scalar_tensor_tensor(out=la[:, 256:384], in0=ps[2][:, 256:384], scalar=sm[:, 5:6], in1=gq[:, 256:384], op0=ALU.mult, op1=ALU.mult),
                                 r=[pst[2], smtr, t_g], pw=[latr])
                            tm, tmtr = tmpr.next()
                            sn = tab[:, t, 0:16]; cs = tab[:, t, 16:32]
                            x1 = ps[2][:, 384:400]; x2 = ps[2][:, 400:416]
                            P.op("dve", lambda e, tm=tm, x1=x1, cs=cs: e.tensor_tensor(out=tm[:, 0, 0, :], in0=x1, in1=cs, op=ALU.mult), r=[pst[2], t_tab], w=[tmtr])
                            P.op("dve", lambda e, tm=tm, x2=x2, sn=sn: e.tensor_tensor(out=tm[:, 1, 0, :], in0=x2, in1=sn, op=ALU.mult), r=[pst[2]], pw=[tmtr])
                            P.op("dve", lambda e, tm=tm, x2=x2, cs=cs: e.tensor_tensor(out=tm[:, 2, 0, :], in0=x2, in1=cs, op=ALU.mult), r=[pst[2]], pw=[tmtr])
                            P.op("dve", lambda e, tm=tm, x1=x1, sn=sn: e.tensor_tensor(out=tm[:, 3, 0, :], in0=x1, in1=sn, op=ALU.mult), r=[pst[2]], pw=[tmtr])
                            P.op("dve", lambda e, tm=tm, la=la: e.tensor_tensor(out=la[:, 384:400], in0=tm[:, 0, 0, :], in1=tm[:, 1, 0, :], op=ALU.subtract), r=[tmtr], pw=[latr])
                            P.op("dve", lambda e, tm=tm, la=la: e.tensor_tensor(out=la[:, 400:416], in0=tm[:, 2, 0, :], in1=tm[:, 3, 0, :], op=ALU.add), r=[tmtr], pw=[latr])
                            if CUT < 2:
                                continue
                            pv = ps[3][:].bitcast(BF16)
                            for j in range(3):
                                P.tp(pv[:, j * 128:(j + 1) * 128], la[:, j * 128:(j + 1) * 128], ident[:], [latr, ctr], pst[3], j == 0)
                            P.tp(pv[0:32, 384:512], la[:, 384:416], ident[:], [latr], pst[3], False)
                            lT, lTtr = latT.next()
                            P.op("dve", lambda e, lT=lT, pv=pv: e.tensor_copy(out=lT[:, 0:3, :], in_=pv[:, 0:384].rearrange("p (j c) -> p j c", j=3)), r=[pst[3]], w=[lTtr])
                            P.op("dve", lambda e, ck=ck, pv=pv, tt=tt: e.tensor_copy(out=ck[:, tt * 128:(tt + 1) * 128], in_=pv[:, 256:384]), r=[pst[3]], pw=[cktr] if tt else (), w=[cktr] if tt == 0 else ())
                            P.op("dve", lambda e, kp=kp, pv=pv, tt=tt: e.tensor_copy(out=kp[:, tt * 128:(tt + 1) * 128], in_=pv[0:32, 384:512]), r=[pst[3]], pw=[kptr] if tt else (), w=[kptr] if tt == 0 else ())
                            if CUT < 2.1:
                                continue
                            for kc in range(2):
                                P.mm(ps[4][:, 0:480], lT[:, kc, :], wuq[:, kc, 0:480], kc == 0, kc == 1, [lTtr, t_wuq], pst[4])
                            for kc in range(2):
                                P.mm(ps[5][:, 0:288], lT[:, kc, :], wuq[:, kc, 480:768], kc == 0, kc == 1, [lTtr, t_wuq], pst[5])
                            if CUT < 2.3:
                                continue
                            q, qtr = qsb.next()
                            tm2, tm2tr = tmpr.next()
                            first = True
                            for (bk, h0, nh) in ((4, 0, 5), (5, 5, 3)):
                                pq = ps[bk][:, 0:nh * 96].rearrange("p (h d) -> p h d", h=nh)
                                qo = q[:, h0:h0 + nh, :]
                                P.op("act", lambda e, pq=pq, qo=qo: e.copy(out=qo[:, :, 0:64], in_=pq[:, :, 0:64]), r=[pst[bk]], pw=[qtr])
                                if CUT < 2.5:
                                    continue
                                csb = tab[:, t:t + 1, 16:32].to_broadcast([128, nh, 16])
                                snb = tab[:, t:t + 1, 0:16].to_broadcast([128, nh, 16])
                                x1 = pq[:, :, 64:80]; x2 = pq[:, :, 80:96]
                                tv = tm2[:, :, h0:h0 + nh, :]
                                P.op("dve", lambda e, tv=tv, x1=x1, csb=csb: e.tensor_tensor(out=tv[:, 0], in0=x1, in1=csb, op=ALU.mult), r=[pst[bk], t_tab], w=[tm2tr] if first else (), pw=() if first else [tm2tr])
                                P.op("dve", lambda e, tv=tv, x2=x2, snb=snb: e.tensor_tensor(out=tv[:, 1], in0=x2, in1=snb, op=ALU.mult), r=[pst[bk]], pw=[tm2tr])
                                P.op("dve", lambda e, tv=tv, x2=x2, csb=csb: e.tensor_tensor(out=tv[:, 2], in0=x2, in1=csb, op=ALU.mult), r=[pst[bk]], pw=[tm2tr])
                                P.op("dve", lambda e, tv=tv, x1=x1, snb=snb: e.tensor_tensor(out=tv[:, 3], in0=x1, in1=snb, op=ALU.mult), r=[pst[bk]], pw=[tm2tr])
                                P.op("dve", lambda e, tv=tv, qo=qo: e.tensor_tensor(out=qo[:, :, 64:80], in0=tv[:, 0], in1=tv[:, 1], op=ALU.subtract), r=[tm2tr], pw=[qtr])
                                P.op("dve", lambda e, tv=tv, qo=qo: e.tensor_tensor(out=qo[:, :, 80:96], in0=tv[:, 2], in1=tv[:, 3], op=ALU.add), r=[tm2tr], pw=[qtr])
                                first = False
                            if CUT < 2.7:
                                continue
                            pv6 = ps[6][:].bitcast(BF16)
                            for h in range(8):
                                P.tp(pv6[:, h * 128:(h + 1) * 128], q[:, h, :], ident[:], [qtr, ctr], pst[6], h == 0)
                            if CUT < 2.9:
                                continue
                            P.op("act", lambda e, qT=qT, pv6=pv6, tt=tt: e.copy(out=qT[:, :, tt * 128:(tt + 1) * 128], in_=pv6[0:96, :].rearrange("p (h c) -> p h c", h=8)),
                                 r=[pst[6]], w=[qTtr] if tt == 0 else (), pw=[qTtr] if tt else ())
                            if CUT < 4:
                                continue
                            P.mm(ps[7][:, 0:512], lT[:, 2, :], wv[:].rearrange("p h c -> p (h c)"), True, True, [lTtr, t_wkv], pst[7])
                            v, vtr = vsb.next()
                            P.op("dve", lambda e, v=v: e.tensor_copy(out=v[:, :, 0:64], in_=ps[7][:, 0:512].rearrange("p (h c) -> p h c", h=8)), r=[pst[7]], w=[vtr])
                            P.dma("pool", Vd[s, tok, :], v[:].rearrange("p h c -> p (h c)"), vtr, r=[vtr], pw=[dtr[("Vd", s)]])
                        if CUT < 5:
                            continue
                        blk = slice(tb * 512, (tb + 1) * 512)
                        for h in range(8):
                            bk = 2 + (h % 2) * 5
                            P.mm(ps[bk][0:64, :], wk[:, h, :], ck[:], True, True, [cktr, t_wkv], pst[bk])
                            if h % 2 == 0:
                                P.op("act", lambda e, kT=kT, h=h, bk=bk: e.copy(out=kT[0:64, h, :], in_=ps[bk][0:64, :]), r=[pst[bk]], w=[kTtr] if h == 0 else (), pw=[kTtr] if h else ())
                            else:
                                P.op("dve", lambda e, kT=kT, h=h, bk=bk: e.tensor_copy(out=kT[0:64, h, :], in_=ps[bk][0:64, :]), r=[pst[bk]], pw=[kTtr])
                        for h in range(8):
                            if h % 2 == 0:
                                P.op("act", lambda e, kT=kT, kp=kp, h=h: e.copy(out=kT[64:96, h, :], in_=kp[:, :]), r=[kptr], pw=[kTtr])
                            else:
                                P.op("dve", lambda e, kT=kT, kp=kp, h=h: e.tensor_copy(out=kT[64:96, h, :], in_=kp[:, :]), r=[kptr], pw=[kTtr])
                        P.dma("pool", QTd[s, :, :, blk].rearrange("h p c -> p h c"), qT[:], qTtr, r=[qTtr], pw=[dtr[("QTd", s)]])
                        P.dma("pool", KTd[s, :, :, blk].rearrange("h p c -> p h c"), kT[:], kTtr, r=[kTtr], pw=[dtr[("KTd", s)]])

                    P.barrier()
                    g.close()
                for g in ([ExitStack()] if "c" in SUB else []):
                    wc = g.enter_context(sbt(nc, "wc", [128, 8, 1024], BF16)); t_wc = Tr()
                    P.dma("sp", wc[:], win[:, :, O_CONV:O_CONV + 1024], t_wc, r=[wtr_in], w=[t_wc])
                    sgr = Ring(nc, g, "cv_sg", [128, 512], F32, 2)
                    aor = Ring(nc, g, "cv_a", [128, 512], F32, 3)
                    for tb in range(NB):
                        blk = slice(tb * 512, (tb + 1) * 512)
                        for j in range(4):
                            ba, bg = (2, 3) if j % 2 == 0 else (4, 5)
                            for kc in range(8):
                                P.mm(ps[ba][:], wc[:, kc, j * 128:(j + 1) * 128], hT[:, kc, blk], kc == 0, kc == 7, [t_hT, t_wc], pst[ba])
                            for kc in range(8):
                                P.mm(ps[bg][:], wc[:, kc, 512 + j * 128:512 + (j + 1) * 128], hT[:, kc, blk], kc == 0, kc == 7, [t_hT, t_wc], pst[bg])
                            sg, sgtr = sgr.next()
                            ao, aotr = aor.next()
                            P.op("act", lambda e, sg=sg, bg=bg: e.activation(out=sg[:], in_=ps[bg][:], func=AF.Sigmoid), r=[pst[bg]], w=[sgtr])
                            P.op("dve", lambda e, ao=ao, sg=sg, ba=ba: e.tensor_tensor(out=ao[:], in0=ps[ba][:], in1=sg[:], op=ALU.mult), r=[pst[ba], sgtr], w=[aotr])
                            P.dma("pool", aTd[s, j, :, blk], ao[:], aotr, r=[aotr], pw=[dtr[("aTd", s)]])

                    P.barrier()
                    g.close()
                for g in ([ExitStack()] if "r" in SUB else []):
                    wr = Ring(nc, g, "wr", [128, 8, 512], BF16, 2)
                    tmq = Ring(nc, g, "r_tm", [128, 4, 4, 64], F32, 2)
                    o16 = Ring(nc, g, "r_o16", [128, 512], BF16, 3)
                    o32 = Ring(nc, g, "r_o32", [128, 512], F32, 3)
                    for (kind, c0) in (("q", O_RQ), ("k", O_RK), ("v0", O_RV), ("v1", O_RV + 512), ("g0", O_RG), ("g1", O_RG + 512)):
                        w, wtr = wr.next()
                        P.dma("sp", w[:], win[:, :, c0:c0 + 512], wtr, r=[wtr_in], w=[wtr])
                        for t in range(T):
                            tok = slice(t * 128, (t + 1) * 128)
                            bk = 2 + (t % 4)
                            for kc in range(8):
                                P.mm(ps[bk][:], hT[:, kc, tok], w[:, kc, :], kc == 0, kc == 7, [t_hT, wtr], pst[bk])
                            if kind in ("q", "k"):
                                o, otr = o16.next()
                                tm, tmtr = tmq.next()
                                pq = ps[bk][:].rearrange("p (h d) -> p h d", h=4)
                                ov = o[:].rearrange("p (h d) -> p h d", h=4)
                                csb = tab[:, t:t + 1, 96:160].to_broadcast([128, 4, 64])
                                snb = tab[:, t:t + 1, 32:96].to_broadcast([128, 4, 64])
                                x1 = pq[:, :, 0:64]; x2 = pq[:, :, 64:128]
                                P.op("dve", lambda e, tm=tm, x1=x1, csb=csb: e.tensor_tensor(out=tm[:, 0], in0=x1, in1=csb, op=ALU.mult), r=[pst[bk], t_tab], w=[tmtr])
                                P.op("dve", lambda e, tm=tm, x2=x2, snb=snb: e.tensor_tensor(out=tm[:, 1], in0=x2, in1=snb, op=ALU.mult), r=[pst[bk]], pw=[tmtr])
                                P.op("dve", lambda e, tm=tm, x2=x2, csb=csb: e.tensor_tensor(out=tm[:, 2], in0=x2, in1=csb, op=ALU.mult), r=[pst[bk]], pw=[tmtr])
                                P.op("dve", lambda e, tm=tm, x1=x1, snb=snb: e.tensor_tensor(out=tm[:, 3], in0=x1, in1=snb, op=ALU.mult), r=[pst[bk]], pw=[tmtr])
                                if kind == "q":
                                    P.op("dve", lambda e, tm=tm, ov=ov: e.tensor_tensor(out=ov[:, :, 0:64], in0=tm[:, 0], in1=tm[:, 1], op=ALU.subtract), r=[tmtr], w=[otr])
                                    P.op("dve", lambda e, tm=tm, ov=ov: e.tensor_tensor(out=ov[:, :, 64:128], in0=tm[:, 2], in1=tm[:, 3], op=ALU.add), r=[tmtr], pw=[otr])
                                else:
                                    sc = float(128 ** -0.5)
                                    P.op("dve", lambda e, tm=tm: e.tensor_tensor(out=tm[:, 0], in0=tm[:, 0], in1=tm[:, 1], op=ALU.subtract), r=[tmtr], w=[tmtr])
                                    P.op("dve", lambda e, tm=tm: e.tensor_tensor(out=tm[:, 2], in0=tm[:, 2], in1=tm[:, 3], op=ALU.add), r=[tmtr], w=[tmtr])
                                    P.op("act", lambda e, tm=tm, ov=ov: e.mul(out=ov[:, :, 0:64], in_=tm[:, 0], mul=sc), r=[tmtr], w=[otr])
                                    P.op("act", lambda e, tm=tm, ov=ov: e.mul(out=ov[:, :, 64:128], in_=tm[:, 2], mul=sc), r=[tmtr], pw=[otr])
                                dst = (rqd if kind == "q" else rkd)[s, tok, :]
                                P.dma("pool", dst, o[:], otr, r=[otr], pw=[dtr[("rqd" if kind == "q" else "rkd", s)]])
                            elif kind in ("v0", "v1"):
                                o, otr = o16.next()
                                if t % 2 == 0:
                                    P.op("act", lambda e, o=o, bk=bk: e.copy(out=o[:], in_=ps[bk][:]), r=[pst[bk]], w=[otr])
                                else:
                                    P.op("dve", lambda e, o=o, bk=bk: e.tensor_copy(out=o[:], in_=ps[bk][:]), r=[pst[bk]], w=[otr])
                                half = 0 if kind == "v0" else 512
                                P.dma("pool", rvd[s, tok, half:half + 512], o[:], otr, r=[otr], pw=[dtr[("rvd", s)]])
                            else:
                                o, otr = o32.next()
                                P.op("act", lambda e, o=o, bk=bk: e.activation(out=o[:], in_=ps[bk][:], func=AF.Silu), r=[pst[bk]], w=[otr])
                                half = 0 if kind == "g0" else 512
                                P.dma("pool", sgd[s, tok, half:half + 512], o[:], otr, r=[otr], pw=[dtr[("sgd", s)]])

                    P.barrier()
                    g.close()
                for g in ([ExitStack()] if "g" in SUB else []):
                    wr = Ring(nc, g, "wg_", [128, 8, 512], BF16, 2)
                    gor = Ring(nc, g, "g_o", [128, 512], F32, 3)
                    for gg in range(6):
                        w, wtr = wr.next()
                        P.dma("sp", w[:], win[:, :, O_GATE + gg * 512:O_GATE + (gg + 1) * 512], wtr, r=[wtr_in], w=[wtr])
                        for tb in range(NB):
                            blk = slice(tb * 512, (tb + 1) * 512)
                            for j in range(4):
                                bk = 2 + (j % 4)
                                for kc in range(8):
                                    P.mm(ps[bk][:], w[:, kc, j * 128:(j + 1) * 128], hT[:, kc, blk], kc == 0, kc == 7, [t_hT, wtr], pst[bk])
                                o, otr = gor.next()
                                P.op("act", lambda e, o=o, bk=bk: e.activation(out=o[:], in_=ps[bk][:], func=AF.Sigmoid), r=[pst[bk]], w=[otr])
                                P.dma("pool", gtd[s, gg * 4 + j, :, blk], o[:], otr, r=[otr], pw=[dtr[("gtd", s)]])

                    P.barrier()
                    g.close()
                P.barrier()
        def stage_A(l, s):
            scale = float(96 ** -0.5)
            with ExitStack() as st:
                V = st.enter_context(sbt(nc, "at_V", [128, T, NH * 65], BF16)); t_V = Tr()
                sel = st.enter_context(sbt(nc, "at_sel", [65, 64], F32)); t_sel = Tr()
                P.op("pool", lambda e: e.memset(sel[:], 0.0), w=[t_sel])
                P.op("pool", lambda e: e.memset(sel[64:65, :], 1.0), pw=[t_sel])
                Vv = Vd[s].rearrange("(t p) c -> p t c", p=128)
                for t0 in range(0, T, 8):
                    P.dma("sp", V[:, t0:t0 + 8, :], Vv[:, t0:t0 + 8, :], t_V, r=[dtr[("Vd", s)]], pw=[t_V] if t0 else (), w=[t_V] if t0 == 0 else ())
                qr = Ring(nc, st, "at_q", [96, S], BF16, 2)
                kr = Ring(nc, st, "at_k", [96, S], BF16, 2)
                pr = Ring(nc, st, "at_p", [128, 512], BF16, 3)
                osb = Ring(nc, st, "at_o", [65, 512], F32, 2)
                rbc = Ring(nc, st, "at_r", [64, 512], F32, 2)
                oT = Ring(nc, st, "at_oT", [64, 512], BF16, 2)
                sb_i = 0
                for h in range(NH):
                    q, qtr = qr.next()
                    k, ktr = kr.next()
                    P.dma("sp", q[:], QTd[s, h], qtr, r=[dtr[("QTd", s)]], w=[qtr])
                    P.dma("sp", k[:], KTd[s, h], ktr, r=[dtr[("KTd", s)]], w=[ktr])
                    for qb in range(NB):
                        qs = slice(qb * 512, (qb + 1) * 512)
                        ob = 4 + (qb % 2)
                        for kt in range(T):
                            sbk = sb_i % 3
                            sb_i += 1
                            P.mm(ps[sbk][:], k[:, kt * 128:(kt + 1) * 128], q[:, qs], True, True, [qtr, ktr], pst[sbk])
                            p, ptr = pr.next()
                            P.op("act", lambda e, p=p, sbk=sbk: e.activation(out=p[:], in_=ps[sbk][:], func=AF.Exp, scale=scale), r=[pst[sbk]], w=[ptr])
                            P.mm(ps[ob][0:65, :], V[:, kt, h * 65:(h + 1) * 65], p[:], kt == 0, kt == T - 1, [t_V, ptr], pst[ob])
                        o, otr = osb.next()
                        P.op("dve", lambda e, o=o, ob=ob: e.tensor_copy(out=o[:], in_=ps[ob][0:65, :]), r=[pst[ob]], w=[otr])
                        P.mm(ps[6][0:64, :], sel[:], o[:], True, True, [t_sel, otr], pst[6])
                        rb, rbtr = rbc.next()
                        P.op("dve", lambda e, rb=rb: e.reciprocal(out=rb[:], in_=ps[6][0:64, :]), r=[pst[6]], w=[rbtr])
                        ot, ottr = oT.next()
                        P.op("dve", lambda e, ot=ot, o=o, rb=rb: e.tensor_tensor(out=ot[:], in0=o[0:64, :], in1=rb[:], op=ALU.mult), r=[otr, rbtr], w=[ottr])
                        P.dma("pool", oTd[s, h // 2, (h % 2) * 64:(h % 2) * 64 + 64, qs], ot[:], ottr, r=[ottr], pw=[dtr[("oTd", s)]])

                P.barrier()
        def stage_C(l, s):
            with ExitStack() as st:
                cw = st.enter_context(sbt(nc, "cv_cw", [34, 512], F32)); t_cw = Tr()
                cwT = st.enter_context(sbt(nc, "cv_cwT", [128, 4, 34], F32)); t_cwT = Tr()
                P.dma("sp", cw[0:31, :], conv_w_dw[l], t_cw, w=[t_cw])
                P.dma("sp", cw[31:32, :], conv_b_dw[l:l + 1, :], t_cw, pw=[t_cw])
                P.dma("sp", cw[32:33, :], conv_ln_g[l:l + 1, :], t_cw, pw=[t_cw])
                P.dma("sp", cw[33:34, :], conv_ln_b[l:l + 1, :], t_cw, pw=[t_cw])
                for cc in range(4):
                    P.tp(ps[0][:, cc * 34:(cc + 1) * 34], cw[0:34, cc * 128:(cc + 1) * 128], identf[0:34, 0:34], [t_cw, ctr], pst[0], cc == 0)
                P.op("dve", lambda e: e.tensor_copy(out=cwT[:], in_=ps[0][:, 0:136].rearrange("p (c j) -> p c j", c=4)), r=[pst[0]], w=[t_cwT])
                apad = Ring(nc, st, "cv_ap", [128, S + 30], F32, 2)
                for a_, atr in zip(apad.t, apad.tr):
                    P.op("pool", lambda e, a_=a_: e.memset(a_[:, 0:15], 0.0), w=[atr])
                    P.op("pool", lambda e, a_=a_: e.memset(a_[:, S + 15:S + 30], 0.0), pw=[atr])
                yv = [st.enter_context(sbt(nc, "cv_y%d" % cc, [128, S], F32)) for cc in range(4)]
                ytr = [Tr() for _ in range(4)]
                for cc in range(4):
                    a_, atr = apad.next()
                    P.dma("sp", a_[:, 15:S + 15], aTd[s, cc], atr, r=[dtr[("aTd", s)]], pw=[atr])
                    y_ = yv[cc]
                    P.op("dve", lambda e, a_=a_, y_=y_, cc=cc: e.tensor_scalar(out=y_[:], in0=a_[:, 0:S], scalar1=cwT[:, cc, 0:1], scalar2=cwT[:, cc, 31:32], op0=ALU.mult, op1=ALU.add),
                         r=[atr, t_cwT], w=[ytr[cc]])
                    for j in range(1, 31):
                        P.op("dve", lambda e, a_=a_, y_=y_, cc=cc, j=j: e.scalar_tensor_tensor(out=y_[:], in0=a_[:, j:j + S], scalar=cwT[:, cc, j:j + 1], in1=y_[:], op0=ALU.mult, op1=ALU.add),
                             r=[atr], w=[ytr[cc]])
                sqr = Ring(nc, st, "cv_sq", [128, 512], F32, 2)
                mr = Ring(nc, st, "cv_m", [128, 3, 512], F32, 2)
                tr_ = Ring(nc, st, "cv_t", [128, 512], F32, 2)
                outr = Ring(nc, st, "cv_o", [128, 512], BF16, 3)
                for tb in range(NB):
                    blk = slice(tb * 512, (tb + 1) * 512)
                    for cc in range(4):
                        P.mm(ps[1][:], onesf[:], yv[cc][:, blk], cc == 0, cc == 3, [ytr[cc], ctr], pst[1])
                    for cc in range(4):
                        sq_, sqtr = sqr.next()
                        P.op("act", lambda e, sq_=sq_, cc=cc: e.activation(out=sq_[:], in_=yv[cc][:, blk], func=AF.Square), r=[ytr[cc]], w=[sqtr])
                        P.mm(ps[2][:], onesf[:], sq_[:], cc == 0, cc == 3, [sqtr, ctr], pst[2])
                    m, mtr = mr.next()
                    P.op("dve", lambda e, m=m: e.tensor_scalar(out=m[:, 0, :], in0=ps[1][:], scalar1=1.0 / 512, scalar2=None, op0=ALU.mult), r=[pst[1]], w=[mtr])
                    P.op("dve", lambda e, m=m: e.tensor_tensor(out=m[:, 1, :], in0=m[:, 0, :], in1=m[:, 0, :], op=ALU.mult), r=[mtr], pw=[mtr])
                    P.op("dve", lambda e, m=m: e.scalar_tensor_tensor(out=m[:, 1, :], in0=ps[2][:], scalar=1.0 / 512, in1=m[:, 1, :], op0=ALU.mult, op1=ALU.subtract), r=[pst[2], mtr], pw=[mtr])
                    P.op("act", lambda e, m=m: e.activation(out=m[:, 2, :], in_=m[:, 1, :], func=AF.Sqrt, bias=epsc[:, 0:1], scale=1.0), r=[mtr, ctr], pw=[mtr])
                    P.op("dve", lambda e, m=m: e.reciprocal(out=m[:, 2, :], in_=m[:, 2, :]), r=[mtr], pw=[mtr])
                    for cc in range(4):
                        t_, ttr = tr_.next()
                        P.op("dve", lambda e, t_=t_, m=m, cc=cc: e.tensor_tensor(out=t_[:], in0=yv[cc][:, blk], in1=m[:, 0, :], op=ALU.subtract), r=[ytr[cc], mtr], w=[ttr])
                        P.op("dve", lambda e, t_=t_, m=m: e.tensor_tensor(out=t_[:], in0=t_[:], in1=m[:, 2, :], op=ALU.mult), r=[mtr], w=[ttr])
                        o, otr = outr.next()
                        P.op("act", lambda e, o=o, t_=t_, cc=cc: e.activation(out=o[:], in_=t_[:], func=AF.Silu, bias=cwT[:, cc, 33:34], scale=cwT[:, cc, 32:33]), r=[ttr, t_cwT], w=[otr])
                        P.dma("pool", cvTd[s, cc, :, blk], o[:], otr, r=[otr], pw=[dtr[("cvTd", s)]])

                P.barrier()
        def stage_R(l, s):
            with ExitStack() as st:
                lgr = st.enter_context(sbt(nc, "rt_lg", [128, 8], F32)); t_lg = Tr()
                iof = st.enter_context(sbt(nc, "rt_iof", [128, 128], F32))
                ioq = st.enter_context(sbt(nc, "rt_ioq", [128, 128], F32))
                iop = st.enter_context(sbt(nc, "rt_iop", [128, 1], F32))
                t_io = Tr()
                tA = st.enter_context(sbt(nc, "rt_tA", [128, 128], F32))
                tB = st.enter_context(sbt(nc, "rt_tB", [128, 128], F32))
                tC = st.enter_context(sbt(nc, "rt_tC", [128, 128], F32)); t_tmp = Tr()
                DT = st.enter_context(sbt(nc, "rt_DT", [128, RH, 128], F32))
                CF = st.enter_context(sbt(nc, "rt_CF", [128, RH, 128], F32))
                CB = st.enter_context(sbt(nc, "rt_CB", [128, RH, 128], F32))
                pc = st.enter_context(sbt(nc, "rt_pc", [128, 4, RH], F32))
                t_tb = Tr()
                P.dma("sp", lgr[:], ret_decay_logits[l].rearrange("a h -> (a h)").partition_broadcast(128), t_lg, w=[t_lg])
                P.op("act", lambda e: e.activation(out=lgr[:], in_=lgr[:], func=AF.Exp, scale=-1.0), r=[t_lg], w=[t_lg])
                P.op("dve", lambda e: e.tensor_scalar(out=lgr[:], in0=lgr[:], scalar1=1.0, scalar2=None, op0=ALU.add), r=[t_lg], w=[t_lg])
                P.op("act", lambda e: e.activation(out=lgr[:], in_=lgr[:], func=AF.Ln), r=[t_lg], w=[t_lg])
                P.op("dve", lambda e: e.tensor_scalar(out=lgr[:], in0=lgr[:], scalar1=-1.0, scalar2=None, op0=ALU.mult), r=[t_lg], w=[t_lg])
                P.op("pool", lambda e: e.iota(iof[:], [[1, 128]], base=0, channel_multiplier=-1, allow_small_or_imprecise_dtypes=True), w=[t_io])
                P.op("pool", lambda e: e.iota(ioq[:], [[1, 128]], base=0, channel_multiplier=0, allow_small_or_imprecise_dtypes=True), pw=[t_io])
                P.op("dve", lambda e: e.tensor_scalar(out=iop[:], in0=iof[:, 0:1], scalar1=-1.0, scalar2=None, op0=ALU.mult), r=[t_io], pw=[t_io])
                for h in range(RH):
                    lf = lgr[:, h:h + 1]; lb = lgr[:, RH + h:RH + h + 1]
                    P.op("dve", lambda e: e.tensor_scalar(out=tA[:], in0=iof[:], scalar1=0.0, scalar2=None, op0=ALU.max), r=[t_io], w=[t_tmp])
                    P.op("act", lambda e, lf=lf: e.activation(out=tA[:], in_=tA[:], func=AF.Exp, scale=lf), r=[t_tmp, t_lg], w=[t_tmp])
                    P.op("dve", lambda e: e.tensor_scalar(out=tB[:], in0=iof[:], scalar1=0.0, scalar2=None, op0=ALU.is_ge), r=[t_io], pw=[t_tmp])
                    P.op("dve", lambda e: e.tensor_tensor(out=tA[:], in0=tA[:], in1=tB[:], op=ALU.mult), r=[t_tmp], w=[t_tmp])
                    P.op("dve", lambda e: e.tensor_scalar(out=tC[:], in0=iof[:], scalar1=-1.0, scalar2=0.0, op0=ALU.mult, op1=ALU.max), r=[t_io], pw=[t_tmp])
                    P.op("act", lambda e, lb=lb: e.activation(out=tC[:], in_=tC[:], func=AF.Exp, scale=lb), r=[t_tmp], w=[t_tmp])
                    P.op("dve", lambda e: e.tensor_scalar(out=tB[:], in0=iof[:], scalar1=0.0, scalar2=None, op0=ALU.is_lt), r=[t_io], w=[t_tmp])
                    P.op("dve", lambda e: e.tensor_tensor(out=tC[:], in0=tC[:], in1=tB[:], op=ALU.mult), r=[t_tmp], w=[t_tmp])
                    P.op("dve", lambda e, h=h: e.tensor_tensor(out=DT[:, h, :], in0=tA[:], in1=tC[:], op=ALU.add), r=[t_tmp], pw=[t_tb])
                    P.op("dve", lambda e: e.tensor_scalar(out=tA[:], in0=ioq[:], scalar1=1.0, scalar2=None, op0=ALU.add), r=[t_io], w=[t_tmp])
                    P.op("act", lambda e, h=h, lf=lf: e.activation(out=CF[:, h, :], in_=tA[:], func=AF.Exp, scale=lf), r=[t_tmp], pw=[t_tb])
                    P.op("dve", lambda e: e.tensor_scalar(out=tA[:], in0=ioq[:], scalar1=-1.0, scalar2=128.0, op0=ALU.mult, op1=ALU.add), r=[t_io], w=[t_tmp])
                    P.op("act", lambda e, h=h, lb=lb: e.activation(out=CB[:, h, :], in_=tA[:], func=AF.Exp, scale=lb), r=[t_tmp], pw=[t_tb])
                    P.op("dve", lambda e: e.tensor_scalar(out=tA[:, 0:1], in0=iop[:], scalar1=-1.0, scalar2=127.0, op0=ALU.mult, op1=ALU.add), r=[t_io], w=[t_tmp])
                    P.op("act", lambda e, h=h, lf=lf: e.activation(out=pc[:, 0, h:h + 1], in_=tA[:, 0:1], func=AF.Exp, scale=lf), r=[t_tmp], pw=[t_tb])
                    P.op("act", lambda e, h=h, lb=lb: e.activation(out=pc[:, 1, h:h + 1], in_=iop[:], func=AF.Exp, scale=lb), r=[t_io], pw=[t_tb])
                    P.op("act", lambda e, h=h, lf=lf: e.activation(out=pc[:, 2, h:h + 1], in_=lf, func=AF.Exp, scale=128.0), r=[t_lg], pw=[t_tb])
                    P.op("act", lambda e, h=h, lb=lb: e.activation(out=pc[:, 3, h:h + 1], in_=lb, func=AF.Exp, scale=128.0), r=[t_lg], pw=[t_tb])
                gnb = st.enter_context(sbt(nc, "rt_gn", [128, D], F32)); t_gn = Tr()
                P.dma("sp", gnb[:], ret_gn_g[l].partition_broadcast(128), t_gn, w=[t_gn])

                Rall = st.enter_context(sbt(nc, "rt_Rall", [128, T, 1024], BF16)); t_Rall = Tr()
                Rb = st.enter_context(sbt(nc, "rt_Rb", [128, RH, 256], F32)); t_Rb = Tr()
                Sf = st.enter_context(sbt(nc, "rt_Sf", [128, RH, 256], F32))
                Sfb = st.enter_context(sbt(nc, "rt_Sfb", [128, RH, 256], BF16)); t_Sf = Tr()
                P.op("pool", lambda e: e.memset(Rb[:], 0.0), w=[t_Rb])
                P.op("pool", lambda e: e.memset(Rall[:, T - 1, :], 0.0), w=[t_Rall])
                P.op("pool", lambda e: e.memset(Sf[:], 0.0), w=[t_Sf])
                P.op("pool", lambda e: e.memset(Sfb[:], 0.0), pw=[t_Sf])
                kin = Ring(nc, st, "rt_k", [128, 512], BF16, 3)
                vin = Ring(nc, st, "rt_v", [128, 1024], BF16, 3)
                qin = Ring(nc, st, "rt_q", [128, 512], BF16, 2)
                gin = Ring(nc, st, "rt_g", [128, 1024], F32, 2)
                kc_ = Ring(nc, st, "rt_kc", [128, RH, 128], BF16, 2)
                for c in range(T - 1, 0, -1):
                    tok = slice(c * 128, (c + 1) * 128)
                    k, ktr = kin.next(); v, vtr = vin.next()
                    P.dma("sp", k[:], rkd[s, tok, :], ktr, r=[dtr[("rkd", s)]], w=[ktr])
                    P.dma("sp", v[:], rvd[s, tok, :], vtr, r=[dtr[("rvd", s)]], w=[vtr])
                    kb, kbtr = kc_.next()
                    for h in range(RH):
                        if h % 2:
                            P.op("dve", lambda e, kb=kb, k=k, h=h: e.tensor_scalar(out=kb[:, h, :], in0=k[:, h * 128:(h + 1) * 128], scalar1=pc[:, 1, h:h + 1], scalar2=None, op0=ALU.mult),
                                 r=[ktr, t_tb], pw=[kbtr])
                        else:
                            P.op("act", lambda e, kb=kb, k=k, h=h: e.mul(out=kb[:, h, :], in_=k[:, h * 128:(h + 1) * 128], mul=pc[:, 1, h:h + 1]),
                                 r=[ktr, t_tb], w=[kbtr] if h == 0 else (), pw=[kbtr] if h else ())
                    for h in range(RH):
                        bk = h // 2
                        P.mm(ps[bk][:, (h % 2) * 256:(h % 2) * 256 + 256], kb[:, h, :], v[:, h * 256:(h + 1) * 256], True, True, [kbtr, vtr], pst[bk], first=(h % 2 == 0))
                    for h in range(RH):
                        bk = h // 2
                        P.op("dve", lambda e, h=h, bk=bk: e.scalar_tensor_tensor(out=Rb[:, h, :], in0=Rb[:, h, :], scalar=pc[:, 3, h:h + 1], in1=ps[bk][:, (h % 2) * 256:(h % 2) * 256 + 256], op0=ALU.mult, op1=ALU.add),
                             r=[pst[bk], t_tb], w=[t_Rb])
                    P.op("act", lambda e, c=c: e.copy(out=Rall[:, c - 1, :], in_=Rb[:].rearrange("p h e -> p (h e)")), r=[t_Rb], pw=[t_Rall])
                qT3 = Ring(nc, st, "rt_qT", [128, 3, RH, 128], BF16, 2)
                kTr = Ring(nc, st, "rt_kT", [128, RH, 128], BF16, 2)
                stm = Ring(nc, st, "rt_stm", [128, RH, 128], BF16, 2)
                bnr = Ring(nc, st, "rt_bn", [128, RH, 8], F32, 2)
                onr = Ring(nc, st, "rt_on", [128, D], F32, 2)
                gtd_ = Ring(nc, st, "rt_gt", [128, D], BF16, 2)
                gTr = Ring(nc, st, "rt_gT", [128, 8, 128], BF16, 2)
                for c in range(T):
                    tok = slice(c * 128, (c + 1) * 128)
                    k, ktr = kin.next(); v, vtr = vin.next(); q, qtr = qin.next(); g_, gtr = gin.next()
                    P.dma("sp", q[:], rqd[s, tok, :], qtr, r=[dtr[("rqd", s)]], w=[qtr])
                    P.dma("sp", k[:], rkd[s, tok, :], ktr, r=[dtr[("rkd", s)]], w=[ktr])
                    P.dma("sp", v[:], rvd[s, tok, :], vtr, r=[dtr[("rvd", s)]], w=[vtr])
                    P.dma("sp", g_[:], sgd[s, tok, :], gtr, r=[dtr[("sgd", s)]], w=[gtr])
                    pv = ps[0][:].bitcast(BF16)
                    for h in range(RH):
                        P.tp(pv[:, h * 128:(h + 1) * 128], q[:, h * 128:(h + 1) * 128], ident[:], [qtr, ctr], pst[0], h == 0)
                    for h in range(RH):
                        P.tp(pv[:, 512 + h * 128:512 + (h + 1) * 128], k[:, h * 128:(h + 1) * 128], ident[:], [ktr], pst[0], False)
                    qT, qTtr = qT3.next(); kT, kTtr = kTr.next()
                    pq = pv[:, 0:512].rearrange("p (h c) -> p h c", h=RH)
                    P.op("act", lambda e, qT=qT, pq=pq: e.copy(out=qT[:, 0], in_=pq), r=[pst[0]], w=[qTtr])
                    P.op("dve", lambda e, qT=qT, pq=pq: e.tensor_tensor(out=qT[:, 1], in0=pq, in1=CF[:], op=ALU.mult), r=[pst[0], t_tb], pw=[qTtr])
                    P.op("dve", lambda e, qT=qT, pq=pq: e.tensor_tensor(out=qT[:, 2], in0=pq, in1=CB[:], op=ALU.mult), r=[pst[0], t_tb], pw=[qTtr])
                    P.op("act", lambda e, kT=kT, pv=pv: e.copy(out=kT[:], in_=pv[:, 512:1024].rearrange("p (h c) -> p h c", h=RH)), r=[pst[0]], w=[kTtr])
                    kf, kftr = kc_.next()
                    for h in range(RH):
                        P.op("act", lambda e, kf=kf, k=k, h=h: e.mul(out=kf[:, h, :], in_=k[:, h * 128:(h + 1) * 128], mul=pc[:, 0, h:h + 1]),
                             r=[ktr, t_tb], w=[kftr] if h == 0 else (), pw=[kftr] if h else ())
                    for h in range(RH):
                        P.mm(ps[1][:, h * 128:(h + 1) * 128], kT[:, h, :], qT[:, 0, h, :], True, True, [kTtr, qTtr], pst[1], first=(h == 0))
                    sm_, smtr = stm.next()
                    P.op("dve", lambda e, sm_=sm_: e.tensor_tensor(out=sm_[:], in0=ps[1][:].rearrange("p (h c) -> p h c", h=RH), in1=DT[:], op=ALU.mult), r=[pst[1], t_tb], w=[smtr])
                    for h in range(RH):
                        bk = 2 + h // 2
                        oc = slice((h % 2) * 256, (h % 2) * 256 + 256)
                        P.mm(ps[bk][:, oc], sm_[:, h, :], v[:, h * 256:(h + 1) * 256], True, False, [smtr, vtr], pst[bk], first=(h % 2 == 0))
                        P.mm(ps[bk][:, oc], qT[:, 1, h, :], Sfb[:, h, :], False, False, [qTtr, t_Sf], pst[bk])
                        P.mm(ps[bk][:, oc], qT[:, 2, h, :], Rall[:, c, h * 256:(h + 1) * 256], False, True, [qTtr, t_Rall], pst[bk])
                    for h in range(RH):
                        bk = 4 + h // 2
                        oc = slice((h % 2) * 256, (h % 2) * 256 + 256)
                        P.mm(ps[bk][:, oc], kf[:, h, :], v[:, h * 256:(h + 1) * 256], True, True, [kftr, vtr], pst[bk], first=(h % 2 == 0))
                    for h in range(RH):
                        bk = 4 + h // 2
                        oc = slice((h % 2) * 256, (h % 2) * 256 + 256)
                        P.op("dve", lambda e, h=h, bk=bk, oc=oc: e.scalar_tensor_tensor(out=Sf[:, h, :], in0=Sf[:, h, :], scalar=pc[:, 2, h:h + 1], in1=ps[bk][:, oc], op0=ALU.mult, op1=ALU.add),
                             r=[pst[bk], t_tb], w=[t_Sf])
                    P.op("act", lambda e: e.copy(out=Sfb[:], in_=Sf[:]), r=[t_Sf], w=[t_Sf])
                    if debug and c == 1:
                        dO = st.enter_context(sbt(nc, "dbgO", [128, 1024], F32)); t_dO = Tr()
                        P.op("dve", lambda e: e.tensor_copy(out=dO[:, 0:512], in_=ps[2][:]), r=[pst[2]], w=[t_dO])
                        P.op("dve", lambda e: e.tensor_copy(out=dO[:, 512:1024], in_=ps[3][:]), r=[pst[3]], pw=[t_dO])
                        P.dma("pool", dbg_O[:, :], dO[:], t_dO, r=[t_dO])
                        P.dma("pool", dbg_qT[:, :], qT[:].rearrange("p a h c -> p (a h c)"), qTtr, r=[qTtr])
                        P.dma("pool", dbg_kT[:, :], kT[:].rearrange("p h c -> p (h c)"), kTtr, r=[kTtr])
                        P.dma("pool", dbg_sm[:, :], sm_[:].rearrange("p h c -> p (h c)"), smtr, r=[smtr])
                        P.dma("pool", dbg_kf[:, :], kf[:].rearrange("p h c -> p (h c)"), kftr, r=[kftr])
                        P.dma("pool", dbg_DT[:, :], DT[:].rearrange("p h c -> p (h c)"), t_dO, r=[t_tb])
                        P.dma("pool", dbg_CF[:, :], CF[:].rearrange("p h c -> p (h c)"), t_dO, r=[t_tb])
                        P.dma("pool", dbg_CB[:, :], CB[:].rearrange("p h c -> p (h c)"), t_dO, r=[t_tb])
                        P.dma("pool", dbg_pc[:, :], pc[:].rearrange("p a h -> p (a h)"), t_dO, r=[t_tb])
                        P.dma("pool", dbg_Sf[:, :], Sf[:].rearrange("p h e -> p (h e)"), t_dO, r=[t_Sf])
                        P.dma("pool", dbg_R[:, :], Rall[:, c, :], t_dO, r=[t_Rall])
                    bn, bntr = bnr.next()
                    on, ontr = onr.next()
                    for h in range(RH):
                        bk = 2 + h // 2
                        oc = slice((h % 2) * 256, (h % 2) * 256 + 256)
                        P.op("dve", lambda e, bn=bn, h=h, bk=bk, oc=oc: e.bn_stats(out=bn[:, h, 0:6], in_=ps[bk][:, oc]), r=[pst[bk]], w=[bntr] if h == 0 else (), pw=[bntr] if h else ())
                    for h in range(RH):
                        P.op("dve", lambda e, bn=bn, h=h: e.bn_aggr(out=bn[:, h, 6:8], in_=bn[:, h, 0:6]), r=[bntr], pw=[bntr])
                    P.op("act", lambda e, bn=bn: e.activation(out=bn[:, :, 0], in_=bn[:, :, 7], func=AF.Sqrt, bias=epsc[:, 0:1], scale=1.0), r=[bntr, ctr], pw=[bntr])
                    P.op("dve", lambda e, bn=bn: e.reciprocal(out=bn[:, :, 1], in_=bn[:, :, 0]), r=[bntr], pw=[bntr])
                    for h in range(RH):
                        bk = 2 + h // 2
                        oc = slice((h % 2) * 256, (h % 2) * 256 + 256)
                        P.op("dve", lambda e, on=on, bn=bn, h=h, bk=bk, oc=oc: e.tensor_scalar(out=on[:, h * 256:(h + 1) * 256], in0=ps[bk][:, oc], scalar1=bn[:, h, 6:7], scalar2=bn[:, h, 1:2], op0=ALU.subtract, op1=ALU.mult),
                             r=[pst[bk], bntr], w=[ontr] if h == 0 else (), pw=[ontr] if h else ())
                    P.op("pool", lambda e, on=on: e.tensor_tensor(out=on[:], in0=on[:], in1=gnb[:], op=ALU.mult), r=[ontr, t_gn], w=[ontr])
                    gt, gttr = gtd_.next()
                    P.op("dve", lambda e, gt=gt, on=on, g_=g_: e.tensor_tensor(out=gt[:], in0=on[:], in1=g_[:], op=ALU.mult), r=[ontr, gtr], w=[gttr])
                    pv6 = ps[6][:].bitcast(BF16)
                    for kc in range(8):
                        P.tp(pv6[:, kc * 128:(kc + 1) * 128], gt[:, kc * 128:(kc + 1) * 128], ident[:], [gttr, ctr], pst[6], kc == 0)
                    gT, gTtr = gTr.next()
                    P.op("act", lambda e, gT=gT, pv6=pv6: e.copy(out=gT[:], in_=pv6.rearrange("p (k c) -> p k c", k=8)), r=[pst[6]], w=[gTtr])
                    P.dma("pool", rtTd[s, :, :, tok].rearrange("k p c -> p k c"), gT[:], gTtr, r=[gTtr], pw=[dtr[("rtTd", s)]])

                P.barrier()
        def postnorm_residual(bA, bB, xt, xtr_, gb, t_g, o, otr, sqr, stt):
            sq_, sqtr = sqr.next()
            sm, smtr = stt.next()
            P.op("act", lambda e: e.activation(out=sq_[:, 0:512], in_=ps[bA][:], func=AF.Square), r=[pst[bA]], w=[sqtr])
            P.op("act", lambda e: e.activation(out=sq_[:, 512:1024], in_=ps[bB][:], func=AF.Square), r=[pst[bB]], pw=[sqtr])
            P.op("dve", lambda e: e.reduce_sum(out=sm[:, 2:3], in_=sq_[:], axis=mybir.AxisListType.X), r=[sqtr], w=[smtr])
            P.op("act", lambda e: e.activation(out=sm[:, 3:4], in_=sm[:, 2:3], func=AF.Sqrt, bias=epsc[:, 0:1], scale=1.0 / D), r=[smtr, ctr], pw=[smtr])
            P.op("dve", lambda e: e.reciprocal(out=sm[:, 3:4], in_=sm[:, 3:4]), r=[smtr], pw=[smtr])
            P.op("dve", lambda e: e.scalar_tensor_tensor(out=o[:, 0:512], in0=ps[bA][:], scalar=sm[:, 3:4], in1=gb[:, 0:512], op0=ALU.mult, op1=ALU.mult), r=[pst[bA], smtr, t_g], w=[otr])
            P.op("dve", lambda e: e.scalar_tensor_tensor(out=o[:, 512:1024], in0=ps[bB][:], scalar=sm[:, 3:4], in1=gb[:, 512:1024], op0=ALU.mult, op1=ALU.mult), r=[pst[bB], smtr], pw=[otr])
            P.op("pool", lambda e: e.tensor_tensor(out=o[:], in0=o[:], in1=xt[:], op=ALU.add), r=[xtr_], w=[otr])

        def stage_M(l, s, xsrc, xtr):
            with ExitStack() as st:
                wmo = st.enter_context(sbt(nc, "m_wmo", [128, 4, D], BF16))
                wpw = st.enter_context(sbt(nc, "m_wpw", [128, 4, D], BF16))
                wro = st.enter_context(sbt(nc, "m_wro", [128, 8, D], BF16))
                wou = st.enter_context(sbt(nc, "m_wou", [128, 8, D], BF16))
                gb = st.enter_context(sbt(nc, "m_gb", [128, D], F32))
                t_w = Tr(); t_g = Tr()
                P.dma("sp", wmo[:], wb["mo"][l].rearrange("(kc p) n -> p kc n", p=128), t_w, r=[wb_tr[("mo", l)]], w=[t_w])
                P.dma("sp", wpw[:], wb["pw"][l].rearrange("(kc p) n -> p kc n", p=128), t_w, r=[wb_tr[("pw", l)]], pw=[t_w])
                P.dma("sp", wro[:], wb["ro"][l].rearrange("(kc p) n -> p kc n", p=128), t_w, r=[wb_tr[("ro", l)]], pw=[t_w])
                P.dma("sp", wou[:], wb["wo"][l].rearrange("(kc p) n -> p kc n", p=128), t_w, r=[wb_tr[("wo", l)]], pw=[t_w])
                P.dma("sp", gb[:], ln_mix_post[l].partition_broadcast(128), t_g, w=[t_g])
                oTr = Ring(nc, st, "m_oT", [128, 4, 512], BF16, 2)
                cTr = Ring(nc, st, "m_cT", [128, 4, 512], BF16, 2)
                rTr = Ring(nc, st, "m_rT", [128, 8, 512], BF16, 2)
                gtr_ = Ring(nc, st, "m_gt", [128, 3, 512], F32, 3)
                mt = Ring(nc, st, "m_t", [128, 2, 512], F32, 2)
                mgr = Ring(nc, st, "m_mg", [128, 8, 512], BF16, 2)
                xr = Ring(nc, st, "m_x", [128, D], F32, 2)
                outr = Ring(nc, st, "m_o", [128, D], F32, 2)
                sqr = Ring(nc, st, "m_sq", [128, D], F32, 2)
                stt = Ring(nc, st, "m_st", [128, 8], F32, 3)
                for tb in range(NB):
                    blk = slice(tb * 512, (tb + 1) * 512)
                    o_, otr_ = oTr.next(); c_, ctr_ = cTr.next(); r_, rtr_ = rTr.next()
                    P.dma("sp", o_[:], oTd[s, :, :, blk].rearrange("k p c -> p k c"), otr_, r=[dtr[("oTd", s)]], w=[otr_])
                    P.dma("sp", c_[:], cvTd[s, :, :, blk].rearrange("k p c -> p k c"), ctr_, r=[dtr[("cvTd", s)]], w=[ctr_])
                    P.dma("sp", r_[:], rtTd[s, :, :, blk].rearrange("k p c -> p k c"), rtr_, r=[dtr[("rtTd", s)]], w=[rtr_])
                    mg, mgtr = mgr.next()
                    for rc in range(8):
                        cs = slice(rc * 128, (rc + 1) * 128)
                        gt, gttr = gtr_.next()
                        for b in range(3):
                            P.dma("sp", gt[:, b, :], gtd[s, b * 8 + rc, :, blk], gttr, r=[dtr[("gtd", s)]], w=[gttr] if b == 0 else (), pw=[gttr] if b else ())
                        b0 = (rc % 2) * 3
                        for kc in range(4):
                            P.mm(ps[b0][:], wmo[:, kc, cs], o_[:, kc, :], kc == 0, kc == 3, [t_w, otr_], pst[b0])
                        for kc in range(4):
                            P.mm(ps[b0 + 1][:], wpw[:, kc, cs], c_[:, kc, :], kc == 0, kc == 3, [t_w, ctr_], pst[b0 + 1])
                        for kc in range(8):
                            P.mm(ps[b0 + 2][:], wro[:, kc, cs], r_[:, kc, :], kc == 0, kc == 7, [t_w, rtr_], pst[b0 + 2])
                        t_, ttr = mt.next()
                        P.op("dve", lambda e, t_=t_, gt=gt, b0=b0: e.tensor_tensor(out=t_[:, 0, :], in0=ps[b0][:], in1=gt[:, 0, :], op=ALU.mult), r=[pst[b0], gttr], w=[ttr])
                        P.op("dve", lambda e, t_=t_, gt=gt, b0=b0: e.tensor_tensor(out=t_[:, 1, :], in0=ps[b0 + 1][:], in1=gt[:, 1, :], op=ALU.mult), r=[pst[b0 + 1], gttr], pw=[ttr])
                        P.op("pool", lambda e, t_=t_: e.tensor_tensor(out=t_[:, 0, :], in0=t_[:, 0, :], in1=t_[:, 1, :], op=ALU.add), r=[ttr], w=[ttr])
                        P.op("dve", lambda e, t_=t_, gt=gt, b0=b0: e.tensor_tensor(out=t_[:, 1, :], in0=ps[b0 + 2][:], in1=gt[:, 2, :], op=ALU.mult), r=[pst[b0 + 2], gttr], w=[ttr])
                        P.op("pool", lambda e, t_=t_, mg=mg, rc=rc: e.tensor_tensor(out=mg[:, rc, :], in0=t_[:, 0, :], in1=t_[:, 1, :], op=ALU.add), r=[ttr], w=[mgtr] if rc == 0 else (), pw=[mgtr] if rc else ())
                    for tt in range(4):
                        t = tb * 4 + tt
                        tok = slice(t * 128, (t + 1) * 128)
                        xt, xtr_ = xr.next()
                        P.dma("sp", xt[:], xsrc[s, tok, :], xtr_, r=[xtr[s]], w=[xtr_])
                        for nb in range(2):
                            for kc in range(8):
                                P.mm(ps[6 + nb][:], mg[:, kc, tt * 128:(tt + 1) * 128], wou[:, kc, nb * 512:(nb + 1) * 512], kc == 0, kc == 7, [mgtr, t_w], pst[6 + nb])
                        o, otr = outr.next()
                        postnorm_residual(6, 7, xt, xtr_, gb, t_g, o, otr, sqr, stt)
                        P.dma("pool", x1d[s, tok, :], o[:], otr, r=[otr], pw=[dtr[("x1d", s)]])

                P.barrier()
        def stage_F(l, s, ydst, ykey):
            with ExitStack() as st:
                wg = st.enter_context(sbt(nc, "f_wg", [128, 8, FH], BF16))
                wu = st.enter_context(sbt(nc, "f_wu", [128, 8, FH], BF16))
                t_w = Tr(); t_g = Tr()
                gpre = st.enter_context(sbt(nc, "f_gpre", [128, D], F32))
                gpost = st.enter_context(sbt(nc, "f_gpost", [128, D], F32))
                P.dma("sp", wg[:], wb["fg"][l].rearrange("(kc p) n -> p kc n", p=128), t_w, r=[wb_tr[("fg", l)]], w=[t_w])
                P.dma("sp", wu[:], wb["fu"][l].rearrange("(kc p) n -> p kc n", p=128), t_w, r=[wb_tr[("fu", l)]], pw=[t_w])
                P.dma("sp", gpre[:], ln_ffn_pre[l].partition_broadcast(128), t_g, w=[t_g])
                P.dma("sp", gpost[:], ln_ffn_post[l].partition_broadcast(128), t_g, pw=[t_g])
                wdr = Ring(nc, st, "f_wd", [128, D], BF16, 8)
                xr = Ring(nc, st, "f_x", [128, D], F32, 5)
                hb = Ring(nc, st, "f_hb", [128, D], BF16, 2)
                sqr = Ring(nc, st, "f_sq", [128, D], F32, 2)
                stt = Ring(nc, st, "f_st", [128, 8], F32, 4)
                h2T = Ring(nc, st, "f_h2T", [128, 8, 512], BF16, 2)
                hid = Ring(nc, st, "f_hid", [128, 22, 512], BF16, 1)
                sgr = Ring(nc, st, "f_sg", [128, 512], F32, 2)
                outr = Ring(nc, st, "f_o", [128, D], F32, 2)
                wdv = wb["fd"][l]
                for tb in range(NB):
                    hT_, hTtr = h2T.next()
                    xts = []
                    for tt in range(4):
                        t = tb * 4 + tt
                        tok = slice(t * 128, (t + 1) * 128)
                        xt, xtr_ = xr.next()
                        xts.append((xt, xtr_))
                        P.dma("sp", xt[:], x1d[s, tok, :], xtr_, r=[dtr[("x1d", s)]], w=[xtr_])
                        sq_, sqtr = sqr.next()
                        sm, smtr = stt.next()
                        ssq4(xt, sq_, sm, xtr_, sqtr, smtr)
                        rstd_from_ssq(sm[:, 0:1], sm[:, 1:2], D, [smtr])
                        h, htr = hb.next()
                        P.op("dve", lambda e, xt=xt, h=h, sm=sm: e.scalar_tensor_tensor(out=h[:], in0=xt[:], scalar=sm[:, 1:2], in1=gpre[:], op0=ALU.mult, op1=ALU.mult), r=[xtr_, smtr, t_g], w=[htr])
                        pv = ps[4 + (tt % 2)][:].bitcast(BF16)
                        bk = 4 + (tt % 2)
                        for kc in range(8):
                            P.tp(pv[:, kc * 128:(kc + 1) * 128], h[:, kc * 128:(kc + 1) * 128], ident[:], [htr, ctr], pst[bk], kc == 0)
                        P.op("act", lambda e, hT_=hT_, pv=pv, tt=tt: e.copy(out=hT_[:, :, tt * 128:(tt + 1) * 128], in_=pv.rearrange("p (k c) -> p k c", k=8)), r=[pst[bk]],
                             w=[hTtr] if tt == 0 else (), pw=[hTtr] if tt else ())
                    hd, hdtr = hid.next()
                    for hc in range(22):
                        bg, bu = (4, 5) if hc % 2 == 0 else (6, 7)
                        cs = slice(hc * 128, (hc + 1) * 128)
                        for kc in range(8):
                            P.mm(ps[bg][:], wg[:, kc, cs], hT_[:, kc, :], kc == 0, kc == 7, [t_w, hTtr], pst[bg])
                        for kc in range(8):
                            P.mm(ps[bu][:], wu[:, kc, cs], hT_[:, kc, :], kc == 0, kc == 7, [t_w, hTtr], pst[bu])
                        sg, sgtr = sgr.next()
                        P.op("act", lambda e, sg=sg, bg=bg: e.activation(out=sg[:], in_=ps[bg][:], func=AF.Silu), r=[pst[bg]], w=[sgtr])
                        P.op("dve", lambda e, hd=hd, sg=sg, bu=bu, hc=hc: e.tensor_tensor(out=hd[:, hc, :], in0=ps[bu][:], in1=sg[:], op=ALU.mult), r=[pst[bu], sgtr],
                             w=[hdtr] if hc == 0 else (), pw=[hdtr] if hc else ())
                    for half in range(2):
                        for hc in range(22):
                            wd, wdtr = wdr.next()
                            P.dma("sp", wd[:], wdv[hc * 128:(hc + 1) * 128, :], wdtr, r=[wb_tr[("fd", l)]], w=[wdtr])
                            for t2 in range(2):
                                tt = half * 2 + t2
                                for nb in range(2):
                                    bk = t2 * 2 + nb
                                    P.mm(ps[bk][:], hd[:, hc, tt * 128:(tt + 1) * 128], wd[:, nb * 512:(nb + 1) * 512], hc == 0, hc == 21, [hdtr, wdtr], pst[bk])
                        for t2 in range(2):
                            tt = half * 2 + t2
                            t = tb * 4 + tt
                            tok = slice(t * 128, (t + 1) * 128)
                            xt, xtr_ = xts[tt]
                            o, otr = outr.next()
                            postnorm_residual(t2 * 2, t2 * 2 + 1, xt, xtr_, gpost, t_g, o, otr, sqr, stt)
                            P.dma("pool", ydst[s, tok, :], o[:], otr, r=[otr], pw=[dtr[(ykey, s)]])

                P.barrier()
        for l in range(nlayers):
            if "W" in stages:
                stage_W(l)
        for s in range(NS):
            if "T" in stages:
                stage_T(s)
        xin_tr = [Tr() for _ in range(NS)]
        for l in range(nlayers):
            last = (l == nlayers - 1)
            for s in range(NS):
                if l == 0:
                    xsrc, xtr = x_in, xin_tr
                else:
                    xsrc, xtr = xLd, [dtr[("xLd", s_)] for s_ in range(NS)]
                if "N" in stages:
                    stage_NP(l, s, xsrc, xtr)
                if "A" in stages:
                    stage_A(l, s)
                if "C" in stages:
                    stage_C(l, s)
                if "R" in stages:
                    stage_R(l, s)
                if "M" in stages:
                    stage_M(l, s, xsrc, xtr)
                if "F" in stages:
                    stage_F(l, s, y_out if last else xLd, "y" if last else "xLd")
        P.wait_all("sp", [dtr[("y", s)] for s in range(NS)])
        for en in ("pe", "act", "dve", "pool"):
            E = P.E[en]
            if E.cnt:
                if P.E["sp"].waited.get(E.key, 0) < E.cnt:
                    P.E["sp"].e.wait_ge(E.sem, E.cnt)
    return nc


def rope_consts():
    def inv(dim):
        return (np.float32(10000.0) ** (-(np.arange(0, dim, 2, dtype=np.float32)) / np.float32(dim))).astype(np.float32)
    im, ir = inv(32), inv(128)
    c = np.zeros((2, 160), np.float32)
    c[0] = np.concatenate([im, im, ir, ir])
    c[1] = np.concatenate([np.zeros(16), np.full(16, np.pi / 2), np.zeros(64), np.full(64, np.pi / 2)]).astype(np.float32)
    return c


WEIGHT_NAMES = ["ln_mix_pre", "ln_mix_post", "ln_ffn_pre", "ln_ffn_post", "w_in", "mla_q_norm", "mla_w_uq", "mla_kv_norm",
                "mla_w_ukv", "mla_w_o", "conv_w_dw", "conv_b_dw", "conv_ln_g", "conv_ln_b", "conv_w_pw", "ret_decay_logits",
                "ret_gn_g", "ret_w_o", "w_out", "ffn_w_gate", "ffn_w_up", "ffn_w_down"]


def kernel(**inputs):
    x = np.ascontiguousarray(np.asarray(inputs["x"], dtype=np.float32))
    pos = np.ascontiguousarray(np.asarray(inputs["positions"], dtype=np.int32))
    B, S, _ = x.shape
    ncores = 8
    NS = B // ncores
    nc = build(S, NS)
    shared = {k: np.ascontiguousarray(np.asarray(inputs[k], dtype=np.float32)) for k in WEIGHT_NAMES}
    shared["rope_consts"] = rope_consts()
    in_maps = []
    for c in range(ncores):
        m = dict(shared)
        m["x"] = x[c * NS:(c + 1) * NS]
        m["positions"] = pos[c * NS:(c + 1) * NS]
        in_maps.append(m)
    res = run_bass_kernel_spmd(nc, in_maps, core_ids=list(range(ncores)))
    return np.concatenate([r["y"] for r in res.results], axis=0).astype(np.float32)
```

```python
import numpy as np
import concourse.bass as bass
import concourse.mybir as mybir
from concourse.bass_utils import run_bass_kernel_spmd
from contextlib import ExitStack

F32 = mybir.dt.float32
BF16 = mybir.dt.bfloat16
I32 = mybir.dt.int32
AF = mybir.ActivationFunctionType
ALU = mybir.AluOpType

D = 1024
L = 2
NH = 8
RH = 4
FH = 2816
INC = 7584
EPS = 1e-6
SAME_ENGINE_SYNC = False
import os
SUB = os.environ.get("SUB", "lcrg")
CUT = float(os.environ.get("CUT", "9"))
TWO_PI = float(2 * np.pi)
PI = float(np.pi)

O_CQ, O_CKV, O_KPE, O_CONV, O_RQ, O_RK, O_RV, O_RG, O_GATE = 0, 256, 384, 416, 1440, 1952, 2464, 3488, 4512


class Tr:
    __slots__ = ("w", "r", "sem", "cnt", "name", "excl")

    def __init__(self, name="", excl=False):
        self.excl = excl
        self.w = {}
        self.r = {}
        self.sem = None
        self.cnt = 0
        self.name = name


class Eng:
    def __init__(self, e, sem, key):
        self.e = e
        self.sem = sem
        self.key = key
        self.cnt = 0
        self.waited = {}


class Prog:
    def __init__(self, nc, es):
        self.nc = nc
        self.es = es
        self.sems = {}
        self.nsem = 0
        self.E = {}
        self.pool = []
        self.live = []
        self.uid = 0
        for name, e in (("pe", nc.tensor), ("act", nc.scalar), ("dve", nc.vector), ("pool", nc.gpsimd), ("sp", nc.sync)):
            s, k = self.newsem("e_" + name)
            self.E[name] = Eng(e, s, k)

    def newsem(self, name):
        s = self.es.enter_context(self.nc.semaphore(name + "_%d" % self.nsem))
        k = self.nsem
        self.nsem += 1
        self.sems[k] = s
        return s, k

    def _waits(self, E, r, w, pw):
        need = {}
        for t in r:
            for k, v in t.w.items():
                if need.get(k, 0) < v:
                    need[k] = v
            if t.excl:
                for k, v in t.r.items():
                    if need.get(k, 0) < v:
                        need[k] = v
        for t in w:
            for d in (t.w, t.r):
                for k, v in d.items():
                    if need.get(k, 0) < v:
                        need[k] = v
        for t in pw:
            for d in (t.w, t.r):
                for k, v in d.items():
                    if need.get(k, 0) < v:
                        need[k] = v
        for k, v in need.items():
            if k == E.key and not SAME_ENGINE_SYNC:
                continue
            if E.waited.get(k, 0) < v:
                E.e.wait_ge(self.sems[k], v)
                E.waited[k] = v

    def op(self, en, fn, r=(), w=(), pw=()):
        E = self.E[en]
        self._waits(E, r, w, pw)
        ins = fn(E.e)
        E.cnt += 1
        ins.then_inc(E.sem, 1)
        for t in r:
            t.r[E.key] = E.cnt
        for t in w:
            t.w = {E.key: E.cnt}
            t.r = {}
        for t in pw:
            t.w[E.key] = E.cnt

    def dma(self, q, out, in_, sb, r=(), w=(), pw=()):
        Q = self.E[q]
        self._waits(Q, r, w, pw)
        if sb.sem is None:
            if self.pool:
                sb.sem, sb.cnt = self.pool.pop()
            else:
                sb.sem = self.newsem("d")
            self.live.append(sb)
        sem, key = sb.sem
        sb.cnt += 16
        Q.e.dma_start(out=out, in_=in_).then_inc(sem, 16)
        for t in r:
            t.r[key] = sb.cnt
        for t in w:
            t.w = {key: sb.cnt}
            t.r = {}
        for t in pw:
            t.w[key] = sb.cnt

    def barrier(self):
        sp = self.E["sp"]
        for en in ("pe", "act", "dve", "pool"):
            E = self.E[en]
            if sp.waited.get(E.key, 0) < E.cnt:
                sp.e.wait_ge(E.sem, E.cnt)
                sp.waited[E.key] = E.cnt
        for sb in self.live:
            sem, key = sb.sem
            if sp.waited.get(key, 0) < sb.cnt:
                sp.e.wait_ge(sem, sb.cnt)
                sp.waited[key] = sb.cnt
        if not hasattr(self, "bar"):
            self.bar = self.newsem("bar")
            self.barcnt = 0
        self.barcnt += 1
        sp.e.sem_inc(self.bar[0], 1)
        for en in ("pe", "act", "dve", "pool"):
            E = self.E[en]
            E.e.wait_ge(self.bar[0], self.barcnt)
            for en2 in ("pe", "act", "dve", "pool"):
                E.waited[self.E[en2].key] = max(E.waited.get(self.E[en2].key, 0), self.E[en2].cnt)
            for sb in self.live:
                E.waited[sb.sem[1]] = max(E.waited.get(sb.sem[1], 0), sb.cnt)
        for sb in self.live:
            self.pool.append((sb.sem, sb.cnt))
            sb.sem = None
        self.live = []

    def wait_all(self, en, trs):
        E = self.E[en]
        self._waits(E, trs, (), ())

    def mm(self, out, lhsT, rhs, start, stop, r, tr, first=None):
        if first is None:
            first = start
        self.op("pe", lambda e: e.matmul(out, lhsT, rhs, start=bool(start), stop=bool(stop)), r=r,
                w=[tr] if first else (), pw=() if first else [tr])

    def tp(self, out, in_, ident, r, tr, first):
        self.op("pe", lambda e: e.transpose(out, in_, ident), r=r, w=[tr] if first else (), pw=() if first else [tr])


_UID = [0]


def sbt(nc, name, shape, dt):
    _UID[0] += 1
    return nc.sbuf_tensor("%s_u%d" % (name, _UID[0]), shape, dt)


class Ring:
    cnt = [0]

    def __init__(self, nc, es, name, shape, dt, n):
        Ring.cnt[0] += 1
        self.t = [es.enter_context(sbt(nc, "%s_%d_%d" % (name, Ring.cnt[0], i), shape, dt)) for i in range(n)]
        self.tr = [Tr("%s%d" % (name, i)) for i in range(n)]
        self.i = 0
        self.n = n

    def next(self):
        i = self.i
        self.i = (i + 1) % self.n
        return self.t[i], self.tr[i]


def build(S, NS, debug=False, nlayers=L, stages="WTNACRMF"):
    T = S // 128
    NB = S // 512
    nc = bass.Bass("TRN2", target_bir_lowering=False)

    def din(name, shape, dt=F32):
        return nc.dram_tensor(name, shape, dt, kind="ExternalInput").ap()

    x_in = din("x", [NS, S, D])
    pos_in = din("positions", [NS, S], I32)
    ln_mix_pre = din("ln_mix_pre", [L, D]); ln_mix_post = din("ln_mix_post", [L, D])
    ln_ffn_pre = din("ln_ffn_pre", [L, D]); ln_ffn_post = din("ln_ffn_post", [L, D])
    w_in = din("w_in", [L, D, INC])
    mla_q_norm = din("mla_q_norm", [L, 256]); mla_w_uq = din("mla_w_uq", [L, 256, 768])
    mla_kv_norm = din("mla_kv_norm", [L, 128]); mla_w_ukv = din("mla_w_ukv", [L, 128, 1024])
    mla_w_o = din("mla_w_o", [L, 512, D])
    conv_w_dw = din("conv_w_dw", [L, 31, 512]); conv_b_dw = din("conv_b_dw", [L, 512])
    conv_ln_g = din("conv_ln_g", [L, 512]); conv_ln_b = din("conv_ln_b", [L, 512])
    conv_w_pw = din("conv_w_pw", [L, 512, D])
    ret_decay_logits = din("ret_decay_logits", [L, 2, RH])
    ret_gn_g = din("ret_gn_g", [L, D]); ret_w_o = din("ret_w_o", [L, D, D])
    w_out = din("w_out", [L, D, D])
    ffn_w_gate = din("ffn_w_gate", [L, D, FH]); ffn_w_up = din("ffn_w_up", [L, D, FH]); ffn_w_down = din("ffn_w_down", [L, FH, D])
    cst_in = din("rope_consts", [2, 160])
    y_out = nc.dram_tensor("y", [NS, S, D], F32, kind="ExternalOutput").ap()

    skind = "ExternalOutput" if debug else "Internal"

    def scr(name, shape, dt):
        return nc.dram_tensor(name, shape, dt, kind=skind).ap()

    wb = {
        "in": scr("wb_in", [L, D, INC], BF16), "uq": scr("wb_uq", [L, 256, 768], BF16), "ukv": scr("wb_ukv", [L, 128, 1024], BF16),
        "mo": scr("wb_mo", [L, 512, D], BF16), "pw": scr("wb_pw", [L, 512, D], BF16), "ro": scr("wb_ro", [L, D, D], BF16),
        "wo": scr("wb_wo", [L, D, D], BF16), "fg": scr("wb_fg", [L, D, FH], BF16), "fu": scr("wb_fu", [L, D, FH], BF16),
        "fd": scr("wb_fd", [L, FH, D], BF16),
    }
    wsrc = {"in": w_in, "uq": mla_w_uq, "ukv": mla_w_ukv, "mo": mla_w_o, "pw": conv_w_pw, "ro": ret_w_o, "wo": w_out,
            "fg": ffn_w_gate, "fu": ffn_w_up, "fd": ffn_w_down}
    wb_tr = {(k, l): Tr("wb_%s%d" % (k, l)) for k in wb for l in range(L)}

    tabd = scr("tabd", [NS, 128, T * 160], F32)
    QTd = scr("QTd", [NS, NH, 96, S], BF16); KTd = scr("KTd", [NS, NH, 96, S], BF16)
    Vd = scr("Vd", [NS, S, NH * 65], BF16)
    oTd = scr("oTd", [NS, 4, 128, S], BF16)
    aTd = scr("aTd", [NS, 4, 128, S], F32); cvTd = scr("cvTd", [NS, 4, 128, S], BF16)
    rqd = scr("rqd", [NS, S, 512], BF16); rkd = scr("rkd", [NS, S, 512], BF16)
    rvd = scr("rvd", [NS, S, 1024], BF16); sgd = scr("sgd", [NS, S, 1024], F32)
    rtTd = scr("rtTd", [NS, 8, 128, S], BF16)
    gtd = scr("gtd", [NS, 24, 128, S], F32)
    x1d = scr("x1d", [NS, S, D], F32)
    xLd = scr("xLd", [NS, S, D], F32)
    if debug:
        dbg_qT = scr("dbg_qT", [128, 3 * RH * 128], BF16); dbg_kT = scr("dbg_kT", [128, RH * 128], BF16)
        dbg_sm = scr("dbg_sm", [128, RH * 128], BF16); dbg_DT = scr("dbg_DT", [128, RH * 128], F32)
        dbg_CF = scr("dbg_CF", [128, RH * 128], F32); dbg_CB = scr("dbg_CB", [128, RH * 128], F32)
        dbg_O = scr("dbg_O", [128, 1024], F32); dbg_Sf = scr("dbg_Sf", [128, 1024], F32); dbg_R = scr("dbg_R", [128, 1024], BF16)
        dbg_pc = scr("dbg_pc", [128, 16], F32); dbg_kf = scr("dbg_kf", [128, 512], BF16)
    dtr = {}
    for nm in ("tabd", "QTd", "KTd", "Vd", "oTd", "aTd", "cvTd", "rqd", "rkd", "rvd", "sgd", "rtTd", "gtd", "x1d", "xLd", "y"):
        for s in range(NS):
            dtr[(nm, s)] = Tr("%s_%d" % (nm, s))

    with ExitStack() as es:
        P = Prog(nc, es)
        ps = [es.enter_context(nc.psum_tensor("psb%d" % i, [128, 512], F32)) for i in range(8)]
        pst = [Tr("ps%d" % i, excl=True) for i in range(8)]
        identf = es.enter_context(sbt(nc, "identf", [128, 128], F32))
        ident = es.enter_context(sbt(nc, "ident", [128, 128], BF16))
        onesf = es.enter_context(sbt(nc, "onesf", [128, 128], F32))
        epsc = es.enter_context(sbt(nc, "epsc", [128, 1], F32))
        ctr = Tr("consts")
        P.op("pool", lambda e: e.memset(identf[:], 0.0), w=[ctr])
        P.op("pool", lambda e: e.affine_select(out=identf[:], in_=identf[:], pattern=[[-1, 128]], compare_op=ALU.not_equal,
                                               fill=1.0, base=0, channel_multiplier=1), w=[ctr])
        P.op("pool", lambda e: e.tensor_copy(out=ident[:], in_=identf[:]), r=[ctr], pw=[ctr])
        P.op("pool", lambda e: e.memset(onesf[:], 1.0), pw=[ctr])
        P.op("pool", lambda e: e.memset(epsc[:], EPS), pw=[ctr])

        def rstd_from_ssq(ssq, rstd, n, trs):
            P.op("act", lambda e: e.activation(out=rstd, in_=ssq, func=AF.Sqrt, bias=epsc[:, 0:1], scale=1.0 / n), r=trs + [ctr], w=trs)
            P.op("dve", lambda e: e.reciprocal(out=rstd, in_=rstd), r=trs, w=trs)

        def ssq4(xt, sqt, sm, xtr_, sqtr, smtr):
            P.op("act", lambda e: e.activation(out=sqt[:], in_=xt[:], func=AF.Square), r=[xtr_], w=[sqtr])
            P.op("dve", lambda e: e.reduce_sum(out=sm[:, 0:1], in_=sqt[:], axis=mybir.AxisListType.X), r=[sqtr], w=[smtr])

        def stage_W(l):
            with ExitStack() as st:
                stf = Ring(nc, st, "wstf", [128, 2048], F32, 3)
                stb = Ring(nc, st, "wstb", [128, 2048], BF16, 3)
                i = 0
                for k in ("in", "uq", "ukv", "mo", "pw", "ro", "wo", "fg", "fu", "fd"):
                    src = wsrc[k][l]
                    dst = wb[k][l]
                    K, N = src.shape
                    for kc in range(K // 128):
                        for c0 in range(0, N, 2048):
                            w = min(2048, N - c0)
                            f, ftr = stf.next()
                            b, btr = stb.next()
                            P.dma("sp", f[:, 0:w], src[kc * 128:(kc + 1) * 128, c0:c0 + w], ftr, w=[ftr])
                            en = ("dve", "pool", "act")[i % 3]
                            if en == "act":
                                P.op(en, lambda e, f=f, b=b, w=w: e.copy(out=b[:, 0:w], in_=f[:, 0:w]), r=[ftr], w=[btr])
                            else:
                                P.op(en, lambda e, f=f, b=b, w=w: e.tensor_copy(out=b[:, 0:w], in_=f[:, 0:w]), r=[ftr], w=[btr])
                            P.dma("pool", dst[kc * 128:(kc + 1) * 128, c0:c0 + w], b[:, 0:w], btr, r=[btr], pw=[wb_tr[(k, l)]])
                            i += 1

                P.barrier()
        def stage_T(s):
            with ExitStack() as st:
                posrow = st.enter_context(sbt(nc, "posrow", [2, S], F32))
                posi = st.enter_context(sbt(nc, "posi", [1, S], I32))
                cst = st.enter_context(sbt(nc, "cst", [2, 160], F32))
                tab = st.enter_context(sbt(nc, "tab", [128, T * 160], F32))
                tmpf = st.enter_context(sbt(nc, "tmpf", [128, T * 160], F32))
                tmpi = st.enter_context(sbt(nc, "tmpi", [128, T * 160], I32))
                t_pr, t_pi, t_c, t_tab, t_f, t_i = Tr(), Tr(), Tr(), Tr(), Tr(), Tr()
                P.op("dve", lambda e: e.memset(posrow[:], 1.0), w=[t_pr])
                P.dma("sp", posi[:], pos_in[s:s + 1, :], t_pi, w=[t_pi])
                P.dma("sp", cst[:], cst_in[:, :], t_c, w=[t_c])
                P.op("dve", lambda e: e.tensor_copy(out=posrow[0:1, :], in_=posi[:]), r=[t_pi], pw=[t_pr])
                for t0 in range(0, T, 3):
                    n = min(3, T - t0)
                    bk = (t0 // 3) % 2
                    for j in range(n):
                        t = t0 + j
                        P.mm(ps[bk][:, j * 160:(j + 1) * 160], posrow[0:2, t * 128:(t + 1) * 128], cst[0:2, :], True, True,
                             [t_pr, t_c], pst[bk], first=(j == 0))
                    P.op("dve", lambda e, bk=bk, n=n, t0=t0: e.tensor_copy(out=tab[:, t0 * 160:(t0 + n) * 160], in_=ps[bk][:, 0:n * 160]),
                         r=[pst[bk]], pw=[t_tab])
                W = T * 160
                for c0 in range(0, W, 2560):
                    c1 = min(W, c0 + 2560)
                    a = tab[:, c0:c1]; f = tmpf[:, c0:c1]; ii = tmpi[:, c0:c1]
                    P.op("dve", lambda e, a=a, f=f: e.tensor_scalar(out=f, in0=a, scalar1=1.0 / TWO_PI, scalar2=None, op0=ALU.mult), r=[t_tab], w=[t_f])
                    P.op("dve", lambda e, f=f, ii=ii: e.tensor_copy(out=ii, in_=f), r=[t_f], w=[t_i])
                    P.op("dve", lambda e, f=f, ii=ii: e.tensor_copy(out=f, in_=ii), r=[t_i], w=[t_f])
                    P.op("dve", lambda e, a=a, f=f: e.scalar_tensor_tensor(out=a, in0=f, scalar=-6.28125, in1=a, op0=ALU.mult, op1=ALU.add), r=[t_f], w=[t_tab])
                    P.op("dve", lambda e, a=a, f=f: e.scalar_tensor_tensor(out=a, in0=f, scalar=-(TWO_PI - 6.28125), in1=a, op0=ALU.mult, op1=ALU.add), r=[t_f], w=[t_tab])
                    P.op("dve", lambda e, a=a, f=f: e.tensor_scalar(out=f, in0=a, scalar1=PI, scalar2=-TWO_PI, op0=ALU.is_gt, op1=ALU.mult), r=[t_tab], w=[t_f])
                    P.op("dve", lambda e, a=a, f=f: e.tensor_tensor(out=a, in0=a, in1=f, op=ALU.add), r=[t_f], w=[t_tab])
                    P.op("dve", lambda e, a=a, f=f: e.tensor_scalar(out=f, in0=a, scalar1=-PI, scalar2=TWO_PI, op0=ALU.is_lt, op1=ALU.mult), r=[t_tab], w=[t_f])
                    P.op("dve", lambda e, a=a, f=f: e.tensor_tensor(out=a, in0=a, in1=f, op=ALU.add), r=[t_f], w=[t_tab])
                    P.op("dve", lambda e, a=a: e.tensor_scalar(out=a, in0=a, scalar1=-3.1415925, scalar2=3.1415925, op0=ALU.max, op1=ALU.min), r=[t_tab], w=[t_tab])
                    P.op("act", lambda e, a=a: e.activation(out=a, in_=a, func=AF.Sin), r=[t_tab], w=[t_tab])
                P.dma("pool", tabd[s], tab[:], t_tab, r=[t_tab], w=[dtr[("tabd", s)]])

                P.barrier()
        def stage_NP(l, s, xsrc, xtr):
            with ExitStack() as st:
                hT = st.enter_context(sbt(nc, "hT", [128, 8, S], BF16)); t_hT = Tr("hT")
                gbc = st.enter_context(sbt(nc, "np_gbc", [128, D], F32)); t_g = Tr()
                gq = st.enter_context(sbt(nc, "np_gq", [128, 384], F32))
                tab = st.enter_context(sbt(nc, "np_tab", [128, T, 160], F32)); t_tab = Tr()
                P.dma("sp", gbc[:], ln_mix_pre[l].partition_broadcast(128), t_g, w=[t_g])
                P.dma("sp", gq[:, 0:256], mla_q_norm[l].partition_broadcast(128), t_g, pw=[t_g])
                P.dma("sp", gq[:, 256:384], mla_kv_norm[l].partition_broadcast(128), t_g, pw=[t_g])
                P.dma("sp", tab[:].rearrange("p t c -> p (t c)"), tabd[s], t_tab, r=[dtr[("tabd", s)]], w=[t_tab])
                xr = Ring(nc, st, "np_x", [128, D], F32, 3)
                hb = Ring(nc, st, "np_hb", [128, D], BF16, 2)
                sq = Ring(nc, st, "np_sq", [128, D], F32, 2)
                stt = Ring(nc, st, "np_st", [128, 8], F32, 4)
                for t in range(T):
                    xt, xtr_ = xr.next()
                    P.dma("sp", xt[:], xsrc[s, t * 128:(t + 1) * 128, :], xtr_, r=[xtr[s]], w=[xtr_])
                    sqt, sqtr = sq.next()
                    sm, smtr = stt.next()
                    ssq4(xt, sqt, sm, xtr_, sqtr, smtr)
                    rstd_from_ssq(sm[:, 0:1], sm[:, 1:2], D, [smtr])
                    h, htr = hb.next()
                    P.op("dve", lambda e, xt=xt, h=h, sm=sm: e.scalar_tensor_tensor(out=h[:], in0=xt[:], scalar=sm[:, 1:2], in1=gbc[:], op0=ALU.mult, op1=ALU.mult),
                         r=[xtr_, smtr, t_g], w=[htr])
                    bk = t % 2
                    pv = ps[bk][:].bitcast(BF16)
                    for kc in range(8):
                        P.tp(pv[:, kc * 128:(kc + 1) * 128], h[:, kc * 128:(kc + 1) * 128], ident[:], [htr, ctr], pst[bk], kc == 0)
                    en = "act" if t % 2 == 0 else "dve"
                    if en == "act":
                        P.op("act", lambda e, pv=pv, t=t: e.copy(out=hT[:, :, t * 128:(t + 1) * 128], in_=pv.rearrange("p (k c) -> p k c", k=8)), r=[pst[bk]], pw=[t_hT])
                    else:
                        P.op("dve", lambda e, pv=pv, t=t: e.tensor_copy(out=hT[:, :, t * 128:(t + 1) * 128], in_=pv.rearrange("p (k c) -> p k c", k=8)), r=[pst[bk]], pw=[t_hT])

                win = wb["in"][l].rearrange("(kc p) n -> p kc n", p=128)
                wtr_in = wb_tr[("in", l)]

                for g in ([ExitStack()] if "l" in SUB else []):
                    wl = g.enter_context(sbt(nc, "wl", [128, 8, 416], BF16)); t_wl = Tr()
                    wuq = g.enter_context(sbt(nc, "wuq", [128, 2, 768], BF16)); t_wuq = Tr()
                    wk = g.enter_context(sbt(nc, "wk", [128, 8, 64], BF16))
                    wv = g.enter_context(sbt(nc, "wv", [128, 8, 64], BF16)); t_wkv = Tr()
                    P.dma("sp", wl[:], win[:, :, 0:416], t_wl, r=[wtr_in], w=[t_wl])
                    P.dma("sp", wuq[:], wb["uq"][l].rearrange("(kc p) n -> p kc n", p=128), t_wuq, r=[wb_tr[("uq", l)]], w=[t_wuq])
                    ukv_v = wb["ukv"][l].rearrange("p (h c) -> p h c", h=8)
                    P.dma("sp", wk[:], ukv_v[:, :, 0:64], t_wkv, r=[wb_tr[("ukv", l)]], w=[t_wkv])
                    P.dma("sp", wv[:], ukv_v[:, :, 64:128], t_wkv, pw=[t_wkv])
                    lat = Ring(nc, g, "lat", [128, 416], BF16, 2)
                    sqj = Ring(nc, g, "lsq", [128, 256], F32, 2)
                    stl = Ring(nc, g, "lst", [128, 8], F32, 3)
                    tmpr = Ring(nc, g, "ltmp", [128, 4, 8, 16], F32, 2)
                    latT = Ring(nc, g, "latT", [128, 4, 128], BF16, 2)
                    qsb = Ring(nc, g, "qsb", [128, 8, 128], BF16, 2)
                    for qt_, qttr_ in zip(qsb.t, qsb.tr):
                        P.op("pool", lambda e, qt_=qt_: e.memset(qt_[:], 0.0), w=[qttr_])
                    qTb = Ring(nc, g, "qTb", [96, 8, 512], BF16, 2)
                    kTb = Ring(nc, g, "kTb", [96, 8, 512], BF16, 2)
                    ckb = Ring(nc, g, "ckb", [128, 512], BF16, 2)
                    kpb = Ring(nc, g, "kpb", [32, 512], BF16, 2)
                    vsb = Ring(nc, g, "vsb", [128, 8, 65], BF16, 2)
                    for vt, vtr in zip(vsb.t, vsb.tr):
                        P.op("pool", lambda e, vt=vt: e.memset(vt[:], 1.0), w=[vtr])
                    for tb in range(NB):
                        qT, qTtr = qTb.next()
                        kT, kTtr = kTb.next()
                        ck, cktr = ckb.next()
                        kp, kptr = kpb.next()
                        for tt in range(4):
                            t = tb * 4 + tt
                            tok = slice(t * 128, (t + 1) * 128)
                            for kc in range(8):
                                P.mm(ps[2][:, 0:416], hT[:, kc, tok], wl[:, kc, :], kc == 0, kc == 7, [t_hT, t_wl], pst[2])
                            la, latr = lat.next()
                            sj, sjtr = sqj.next()
                            sm, smtr = stl.next()
                            P.op("pool", lambda e, sm=sm: e.memset(sm[:], 0.0), w=[smtr])
                            P.op("act", lambda e, sj=sj, sm=sm: e.activation(out=sj[:, 0:256], in_=ps[2][:, 0:256], func=AF.Square, accum_out=sm[:, 0:1]), r=[pst[2], smtr], w=[sjtr], pw=[smtr])
                            P.op("act", lambda e, sj=sj, sm=sm: e.activation(out=sj[:, 0:128], in_=ps[2][:, 256:384], func=AF.Square, accum_out=sm[:, 1:2]), r=[pst[2], smtr], w=[sjtr], pw=[smtr])
                            P.op("act", lambda e, sm=sm: e.activation(out=sm[:, 2:3], in_=sm[:, 0:1], func=AF.Sqrt, bias=epsc[:, 0:1], scale=1.0 / 256), r=[smtr, ctr], pw=[smtr])
                            P.op("act", lambda e, sm=sm: e.activation(out=sm[:, 3:4], in_=sm[:, 1:2], func=AF.Sqrt, bias=epsc[:, 0:1], scale=1.0 / 128), r=[smtr], pw=[smtr])
                            P.op("dve", lambda e, sm=sm: e.reciprocal(out=sm[:, 4:6], in_=sm[:, 2:4]), r=[smtr], pw=[smtr])
                            P.op("dve", lambda e, la=la, sm=sm: e.scalar_tensor_tensor(out=la[:, 0:256], in0=ps[2][:, 0:256], scalar=sm[:, 4:5], in1=gq[:, 0:256], op0=ALU.mult, op1=ALU.mult),
                                 r=[pst[2], smtr, t_g], w=[latr])
                            P.op("dve", lambda e, la=la, sm=sm: e.scalar_tensor_tensor(out=la[:, 256:384], in0=ps[2][:, 256:384], scalar=sm[:, 5:6], in1=gq[:, 256:384], op0=ALU.mult, op1=ALU.mult),
                                 r=[pst[2], smtr, t_g], pw=[latr])
                            tm, tmtr = tmpr.next()
                            sn = tab[:, t, 0:16]; cs = tab[:, t, 16:32]
                            x1 = ps[2][:, 384:400]; x2 = ps[2][:, 400:416]
                            P.op("dve", lambda e, tm=tm, x1=x1, cs=cs: e.tensor_tensor(out=tm[:, 0, 0, :], in0=x1, in1=cs, op=ALU.mult), r=[pst[2], t_tab], w=[tmtr])
                            P.op("dve", lambda e, tm=tm, x2=x2, sn=sn: e.tensor_tensor(out=tm[:, 1, 0, :], in0=x2, in1=sn, op=ALU.mult), r=[pst[2]], pw=[tmtr])
                            P.op("dve", lambda e, tm=tm, x2=x2, cs=cs: e.tensor_tensor(out=tm[:, 2, 0, :], in0=x2, in1=cs, op=ALU.mult), r=[pst[2]], pw=[tmtr])
                            P.op("dve", lambda e, tm=tm, x1=x1, sn=sn: e.tensor_tensor(out=tm[:, 3, 0, :], in0=x1, in1=sn, op=ALU.mult), r=[pst[2]], pw=[tmtr])
                            P.op("dve", lambda e, tm=tm, la=la: e.tensor_tensor(out=la[:, 384:400], in0=tm[:, 0, 0, :], in1=tm[:, 1, 0, :], op=ALU.subtract), r=[tmtr], pw=[latr])
                            P.op("dve", lambda e, tm=tm, la=la: e.tensor_tensor(out=la[:, 400:416], in0=tm[:, 2, 0, :], in1=tm[:, 3, 0, :], op=ALU.add), r=[tmtr], pw=[latr])
                            if CUT < 2:
                                continue
                            pv = ps[3][:].bitcast(BF16)
                            for j in range(3):
                                P.tp(pv[:, j * 128:(j + 1) * 128], la[:, j * 128:(j + 1) * 128], ident[:], [latr, ctr], pst[3], j == 0)
                            P.tp(pv[0:32, 384:512], la[:, 384:416], ident[:], [latr], pst[3], False)
                            lT, lTtr = latT.next()
                            P.op("dve", lambda e, lT=lT, pv=pv: e.tensor_copy(out=lT[:, 0:3, :], in_=pv[:, 0:384].rearrange("p (j c) -> p j c", j=3)), r=[pst[3]], w=[lTtr])
                            P.op("dve", lambda e, ck=ck, pv=pv, tt=tt: e.tensor_copy(out=ck[:, tt * 128:(tt + 1) * 128], in_=pv[:, 256:384]), r=[pst[3]], pw=[cktr] if tt else (), w=[cktr] if tt == 0 else ())
                            P.op("dve", lambda e, kp=kp, pv=pv, tt=tt: e.tensor_copy(out=kp[:, tt * 128:(tt + 1) * 128], in_=pv[0:32, 384:512]), r=[pst[3]], pw=[kptr] if tt else (), w=[kptr] if tt == 0 else ())
                            if CUT < 2.1:
                                continue
                            for kc in range(2):
                                P.mm(ps[4][:, 0:480], lT[:, kc, :], wuq[:, kc, 0:480], kc == 0, kc == 1, [lTtr, t_wuq], pst[4])
                            for kc in range(2):
                                P.mm(ps[5][:, 0:288], lT[:, kc, :], wuq[:, kc, 480:768], kc == 0, kc == 1, [lTtr, t_wuq], pst[5])
                            if CUT < 2.3:
                                continue
                            q, qtr = qsb.next()
                            tm2, tm2tr = tmpr.next()
                            first = True
                            for (bk, h0, nh) in ((4, 0, 5), (5, 5, 3)):
                                pq = ps[bk][:, 0:nh * 96].rearrange("p (h d) -> p h d", h=nh)
                                qo = q[:, h0:h0 + nh, :]
                                P.op("act", lambda e, pq=pq, qo=qo: e.copy(out=qo[:, :, 0:64], in_=pq[:, :, 0:64]), r=[pst[bk]], pw=[qtr])
                                if CUT < 2.5:
                                    continue
                                csb = tab[:, t:t + 1, 16:32].to_broadcast([128, nh, 16])
                                snb = tab[:, t:t + 1, 0:16].to_broadcast([128, nh, 16])
                                x1 = pq[:, :, 64:80]; x2 = pq[:, :, 80:96]
                                tv = tm2[:, :, h0:h0 + nh, :]
                                P.op("dve", lambda e, tv=tv, x1=x1, csb=csb: e.tensor_tensor(out=tv[:, 0], in0=x1, in1=csb, op=ALU.mult), r=[pst[bk], t_tab], w=[tm2tr] if first else (), pw=() if first else [tm2tr])
                                P.op("dve", lambda e, tv=tv, x2=x2, snb=snb: e.tensor_tensor(out=tv[:, 1], in0=x2, in1=snb, op=ALU.mult), r=[pst[bk]], pw=[tm2tr])
                                P.op("dve", lambda e, tv=tv, x2=x2, csb=csb: e.tensor_tensor(out=tv[:, 2], in0=x2, in1=csb, op=ALU.mult), r=[pst[bk]], pw=[tm2tr])
                                P.op("dve", lambda e, tv=tv, x1=x1, snb=snb: e.tensor_tensor(out=tv[:, 3], in0=x1, in1=snb, op=ALU.mult), r=[pst[bk]], pw=[tm2tr])
                                P.op("dve", lambda e, tv=tv, qo=qo: e.tensor_tensor(out=qo[:, :, 64:80], in0=tv[:, 0], in1=tv[:, 1], op=ALU.subtract), r=[tm2tr], pw=[qtr])
                                P.op("dve", lambda e, tv=tv, qo=qo: e.tensor_tensor(out=qo[:, :, 80:96], in0=tv[:, 2], in1=tv[:, 3], op=ALU.add), r=[tm2tr], pw=[qtr])
                                first = False
                            if CUT < 2.7:
                                continue
                            pv6 = ps[6][:].bitcast(BF16)
                            for h in range(8):
                                P.tp(pv6[:, h * 128:(h + 1) * 128], q[:, h, :], ident[:], [qtr, ctr], pst[6], h == 0)
                            if CUT < 2.9:
                                continue
                            P.op("act", lambda e, qT=qT, pv6=pv6, tt=tt: e.copy(out=qT[:, :, tt * 128:(tt + 1) * 128], in_=pv6[0:96, :].rearrange("p (h c) -> p h c", h=8)),
                                 r=[pst[6]], w=[qTtr] if tt == 0 else (), pw=[qTtr] if tt else ())
                            if CUT < 4:
                                continue
                            P.mm(ps[7][:, 0:512], lT[:, 2, :], wv[:].rearrange("p h c -> p (h c)"), True, True, [lTtr, t_wkv], pst[7])
                            v, vtr = vsb.next()
                            P.op("dve", lambda e, v=v: e.tensor_copy(out=v[:, :, 0:64], in_=ps[7][:, 0:512].rearrange("p (h c) -> p h c", h=8)), r=[pst[7]], w=[vtr])
                            P.dma("pool", Vd[s, tok, :], v[:].rearrange("p h c -> p (h c)"), vtr, r=[vtr], pw=[dtr[("Vd", s)]])
                        if CUT < 5:
                            continue
                        blk = slice(tb * 512, (tb + 1) * 512)
                        for h in range(8):
                            bk = 2 + (h % 2) * 5
                            P.mm(ps[bk][0:64, :], wk[:, h, :], ck[:], True, True, [cktr, t_wkv], pst[bk])
                            if h % 2 == 0:
                                P.op("act", lambda e, kT=kT, h=h, bk=bk: e.copy(out=kT[0:64, h, :], in_=ps[bk][0:64, :]), r=[pst[bk]], w=[kTtr] if h == 0 else (), pw=[kTtr] if h else ())
                            else:
                                P.op("dve", lambda e, kT=kT, h=h, bk=bk: e.tensor_copy(out=kT[0:64, h, :], in_=ps[bk][0:64, :]), r=[pst[bk]], pw=[kTtr])
                        for h in range(8):
                            if h % 2 == 0:
                                P.op("act", lambda e, kT=kT, kp=kp, h=h: e.copy(out=kT[64:96, h, :], in_=kp[:, :]), r=[kptr], pw=[kTtr])
                            else:
                                P.op("dve", lambda e, kT=kT, kp=kp, h=h: e.tensor_copy(out=kT[64:96, h, :], in_=kp[:, :]), r=[kptr], pw=[kTtr])
                        P.dma("pool", QTd[s, :, :, blk].rearrange("h p c -> p h c"), qT[:], qTtr, r=[qTtr], pw=[dtr[("QTd", s)]])
                        P.dma("pool", KTd[s, :, :, blk].rearrange("h p c -> p h c"), kT[:], kTtr, r=[kTtr], pw=[dtr[("KTd", s)]])

                    P.barrier()
                    g.close()
                for g in ([ExitStack()] if "c" in SUB else []):
                    wc = g.enter_context(sbt(nc, "wc", [128, 8, 1024], BF16)); t_wc = Tr()
                    P.dma("sp", wc[:], win[:, :, O_CONV:O_CONV + 1024], t_wc, r=[wtr_in], w=[t_wc])
                    sgr = Ring(nc, g, "cv_sg", [128, 512], F32, 2)
                    aor = Ring(nc, g, "cv_a", [128, 512], F32, 3)
                    for tb in range(NB):
                        blk = slice(tb * 512, (tb + 1) * 512)
                        for j in range(4):
                            ba, bg = (2, 3) if j % 2 == 0 else (4, 5)
                            for kc in range(8):
                                P.mm(ps[ba][:], wc[:, kc, j * 128:(j + 1) * 128], hT[:, kc, blk], kc == 0, kc == 7, [t_hT, t_wc], pst[ba])
                            for kc in range(8):
                                P.mm(ps[bg][:], wc[:, kc, 512 + j * 128:512 + (j + 1) * 128], hT[:, kc, blk], kc == 0, kc == 7, [t_hT, t_wc], pst[bg])
                            sg, sgtr = sgr.next()
                            ao, aotr = aor.next()
                            P.op("act", lambda e, sg=sg, bg=bg: e.activation(out=sg[:], in_=ps[bg][:], func=AF.Sigmoid), r=[pst[bg]], w=[sgtr])
                            P.op("dve", lambda e, ao=ao, sg=sg, ba=ba: e.tensor_tensor(out=ao[:], in0=ps[ba][:], in1=sg[:], op=ALU.mult), r=[pst[ba], sgtr], w=[aotr])
                            P.dma("pool", aTd[s, j, :, blk], ao[:], aotr, r=[aotr], pw=[dtr[("aTd", s)]])

                    P.barrier()
                    g.close()
                for g in ([ExitStack()] if "r" in SUB else []):
                    wr = Ring(nc, g, "wr", [128, 8, 512], BF16, 2)
                    tmq = Ring(nc, g, "r_tm", [128, 4, 4, 64], F32, 2)
                    o16 = Ring(nc, g, "r_o16", [128, 512], BF16, 3)
                    o32 = Ring(nc, g, "r_o32", [128, 512], F32, 3)
                    for (kind, c0) in (("q", O_RQ), ("k", O_RK), ("v0", O_RV), ("v1", O_RV + 512), ("g0", O_RG), ("g1", O_RG + 512)):
                        w, wtr = wr.next()
                        P.dma("sp", w[:], win[:, :, c0:c0 + 512], wtr, r=[wtr_in], w=[wtr])
                        for t in range(T):
                            tok = slice(t * 128, (t + 1) * 128)
                            bk = 2 + (t % 4)
                            for kc in range(8):
                                P.mm(ps[bk][:], hT[:, kc, tok], w[:, kc, :], kc == 0, kc == 7, [t_hT, wtr], pst[bk])
                            if kind in ("q", "k"):
                                o, otr = o16.next()
                                tm, tmtr = tmq.next()
                                pq = ps[bk][:].rearrange("p (h d) -> p h d", h=4)
                                ov = o[:].rearrange("p (h d) -> p h d", h=4)
                                csb = tab[:, t:t + 1, 96:160].to_broadcast([128, 4, 64])
                                snb = tab[:, t:t + 1, 32:96].to_broadcast([128, 4, 64])
                                x1 = pq[:, :, 0:64]; x2 = pq[:, :, 64:128]
                                P.op("dve", lambda e, tm=tm, x1=x1, csb=csb: e.tensor_tensor(out=tm[:, 0], in0=x1, in1=csb, op=ALU.mult), r=[pst[bk], t_tab], w=[tmtr])
                                P.op("dve", lambda e, tm=tm, x2=x2, snb=snb: e.tensor_tensor(out=tm[:, 1], in0=x2, in1=snb, op=ALU.mult), r=[pst[bk]], pw=[tmtr])
                                P.op("dve", lambda e, tm=tm, x2=x2, csb=csb: e.tensor_tensor(out=tm[:, 2], in0=x2, in1=csb, op=ALU.mult), r=[pst[bk]], pw=[tmtr])
                                P.op("dve", lambda e, tm=tm, x1=x1, snb=snb: e.tensor_tensor(out=tm[:, 3], in0=x1, in1=snb, op=ALU.mult), r=[pst[bk]], pw=[tmtr])
                                if kind == "q":
                                    P.op("dve", lambda e, tm=tm, ov=ov: e.tensor_tensor(out=ov[:, :, 0:64], in0=tm[:, 0], in1=tm[:, 1], op=ALU.subtract), r=[tmtr], w=[otr])
                                    P.op("dve", lambda e, tm=tm, ov=ov: e.tensor_tensor(out=ov[:, :, 64:128], in0=tm[:, 2], in1=tm[:, 3], op=ALU.add), r=[tmtr], pw=[otr])
                                else:
                                    sc = float(128 ** -0.5)
                                    P.op("dve", lambda e, tm=tm: e.tensor_tensor(out=tm[:, 0], in0=tm[:, 0], in1=tm[:, 1], op=ALU.subtract), r=[tmtr], w=[tmtr])
                                    P.op("dve", lambda e, tm=tm: e.tensor_tensor(out=tm[:, 2], in0=tm[:, 2], in1=tm[:, 3], op=ALU.add), r=[tmtr], w=[tmtr])
                                    P.op("act", lambda e, tm=tm, ov=ov: e.mul(out=ov[:, :, 0:64], in_=tm[:, 0], mul=sc), r=[tmtr], w=[otr])
                                    P.op("act", lambda e, tm=tm, ov=ov: e.mul(out=ov[:, :, 64:128], in_=tm[:, 2], mul=sc), r=[tmtr], pw=[otr])
                                dst = (rqd if kind == "q" else rkd)[s, tok, :]
                                P.dma("pool", dst, o[:], otr, r=[otr], pw=[dtr[("rqd" if kind == "q" else "rkd", s)]])
                            elif kind in ("v0", "v1"):
                                o, otr = o16.next()
                                if t % 2 == 0:
                                    P.op("act", lambda e, o=o, bk=bk: e.copy(out=o[:], in_=ps[bk][:]), r=[pst[bk]], w=[otr])
                                else:
                                    P.op("dve", lambda e, o=o, bk=bk: e.tensor_copy(out=o[:], in_=ps[bk][:]), r=[pst[bk]], w=[otr])
                                half = 0 if kind == "v0" else 512
                                P.dma("pool", rvd[s, tok, half:half + 512], o[:], otr, r=[otr], pw=[dtr[("rvd", s)]])
                            else:
                                o, otr = o32.next()
                                P.op("act", lambda e, o=o, bk=bk: e.activation(out=o[:], in_=ps[bk][:], func=AF.Silu), r=[pst[bk]], w=[otr])
                                half = 0 if kind == "g0" else 512
                                P.dma("pool", sgd[s, tok, half:half + 512], o[:], otr, r=[otr], pw=[dtr[("sgd", s)]])

                    P.barrier()
                    g.close()
                for g in ([ExitStack()] if "g" in SUB else []):
                    wr = Ring(nc, g, "wg_", [128, 8, 512], BF16, 2)
                    gor = Ring(nc, g, "g_o", [128, 512], F32, 3)
                    for gg in range(6):
                        w, wtr = wr.next()
                        P.dma("sp", w[:], win[:, :, O_GATE + gg * 512:O_GATE + (gg + 1) * 512], wtr, r=[wtr_in], w=[wtr])
                        for tb in range(NB):
                            blk = slice(tb * 512, (tb + 1) * 512)
                            for j in range(4):
                                bk = 2 + (j % 4)
                                for kc in range(8):
                                    P.mm(ps[bk][:], w[:, kc, j * 128:(j + 1) * 128], hT[:, kc, blk], kc == 0, kc == 7, [t_hT, wtr], pst[bk])
                                o, otr = gor.next()
                                P.op("act", lambda e, o=o, bk=bk: e.activation(out=o[:], in_=ps[bk][:], func=AF.Sigmoid), r=[pst[bk]], w=[otr])
                                P.dma("pool", gtd[s, gg * 4 + j, :, blk], o[:], otr, r=[otr], pw=[dtr[("gtd", s)]])

                    P.barrier()
                    g.close()
                P.barrier()
        def stage_A(l, s):
            scale = float(96 ** -0.5)
            with ExitStack() as st:
                V = st.enter_context(sbt(nc, "at_V", [128, T, NH * 65], BF16)); t_V = Tr()
                sel = st.enter_context(sbt(nc, "at_sel", [65, 64], F32)); t_sel = Tr()
                P.op("pool", lambda e: e.memset(sel[:], 0.0), w=[t_sel])
                P.op("pool", lambda e: e.memset(sel[64:65, :], 1.0), pw=[t_sel])
                Vv = Vd[s].rearrange("(t p) c -> p t c", p=128)
                for t0 in range(0, T, 8):
                    P.dma("sp", V[:, t0:t0 + 8, :], Vv[:, t0:t0 + 8, :], t_V, r=[dtr[("Vd", s)]], pw=[t_V] if t0 else (), w=[t_V] if t0 == 0 else ())
                qr = Ring(nc, st, "at_q", [96, S], BF16, 2)
                kr = Ring(nc, st, "at_k", [96, S], BF16, 2)
                pr = Ring(nc, st, "at_p", [128, 512], BF16, 4)
                osb = Ring(nc, st, "at_o", [65, 512], F32, 2)
                rbc = Ring(nc, st, "at_r", [64, 512], F32, 2)
                oT = Ring(nc, st, "at_oT", [64, 512], BF16, 2)
                its = [(h, qb, kt) for h in range(NH) for qb in range(NB) for kt in range(T)]
                PF = 3
                qk = {}

                def get_qk(h):
                    if h not in qk:
                        q, qtr = qr.next()
                        k, ktr = kr.next()
                        P.dma("sp", q[:], QTd[s, h], qtr, r=[dtr[("QTd", s)]], w=[qtr])
                        P.dma("sp", k[:], KTd[s, h], ktr, r=[dtr[("KTd", s)]], w=[ktr])
                        qk[h] = (q, qtr, k, ktr)
                    return qk[h]

                def emit_score(i):
                    h, qb, kt = its[i]
                    q, qtr, k, ktr = get_qk(h)
                    sbk = i % 4
                    P.mm(ps[sbk][:], k[:, kt * 128:(kt + 1) * 128], q[:, qb * 512:(qb + 1) * 512], True, True, [qtr, ktr], pst[sbk])

                for i in range(min(PF, len(its))):
                    emit_score(i)
                for i, (h, qb, kt) in enumerate(its):
                    qs = slice(qb * 512, (qb + 1) * 512)
                    ob = 4 + (qb % 2)
                    sbk = i % 4
                    p, ptr = pr.next()
                    P.op("act", lambda e, p=p, sbk=sbk: e.activation(out=p[:], in_=ps[sbk][:], func=AF.Exp, scale=scale), r=[pst[sbk]], w=[ptr])
                    if i + PF < len(its):
                        emit_score(i + PF)
                    P.mm(ps[ob][0:65, :], V[:, kt, h * 65:(h + 1) * 65], p[:], kt == 0, kt == T - 1, [t_V, ptr], pst[ob])
                    if kt == T - 1:
                        o, otr = osb.next()
                        P.op("dve", lambda e, o=o, ob=ob: e.tensor_copy(out=o[:], in_=ps[ob][0:65, :]), r=[pst[ob]], w=[otr])
                        P.mm(ps[6][0:64, :], sel[:], o[:], True, True, [t_sel, otr], pst[6])
                        rb, rbtr = rbc.next()
                        P.op("dve", lambda e, rb=rb: e.reciprocal(out=rb[:], in_=ps[6][0:64, :]), r=[pst[6]], w=[rbtr])
                        ot, ottr = oT.next()
                        P.op("dve", lambda e, ot=ot, o=o, rb=rb: e.tensor_tensor(out=ot[:], in0=o[0:64, :], in1=rb[:], op=ALU.mult), r=[otr, rbtr], w=[ottr])
                        P.dma("pool", oTd[s, h // 2, (h % 2) * 64:(h % 2) * 64 + 64, qs], ot[:], ottr, r=[ottr], pw=[dtr[("oTd", s)]])
                P.barrier()

        def stage_C(l, s):
            with ExitStack() as st:
                cw = st.enter_context(sbt(nc, "cv_cw", [34, 512], F32)); t_cw = Tr()
                cwT = st.enter_context(sbt(nc, "cv_cwT", [128, 4, 34], F32)); t_cwT = Tr()
                P.dma("sp", cw[0:31, :], conv_w_dw[l], t_cw, w=[t_cw])
                P.dma("sp", cw[31:32, :], conv_b_dw[l:l + 1, :], t_cw, pw=[t_cw])
                P.dma("sp", cw[32:33, :], conv_ln_g[l:l + 1, :], t_cw, pw=[t_cw])
                P.dma("sp", cw[33:34, :], conv_ln_b[l:l + 1, :], t_cw, pw=[t_cw])
                for cc in range(4):
                    P.tp(ps[0][:, cc * 34:(cc + 1) * 34], cw[0:34, cc * 128:(cc + 1) * 128], identf[0:34, 0:34], [t_cw, ctr], pst[0], cc == 0)
                P.op("dve", lambda e: e.tensor_copy(out=cwT[:], in_=ps[0][:, 0:136].rearrange("p (c j) -> p c j", c=4)), r=[pst[0]], w=[t_cwT])
                apad = Ring(nc, st, "cv_ap", [128, S + 30], F32, 2)
                for a_, atr in zip(apad.t, apad.tr):
                    P.op("pool", lambda e, a_=a_: e.memset(a_[:, 0:15], 0.0), w=[atr])
                    P.op("pool", lambda e, a_=a_: e.memset(a_[:, S + 15:S + 30], 0.0), pw=[atr])
                yv = [st.enter_context(sbt(nc, "cv_y%d" % cc, [128, S], F32)) for cc in range(4)]
                ytr = [Tr() for _ in range(4)]
                for cc in range(4):
                    a_, atr = apad.next()
                    P.dma("sp", a_[:, 15:S + 15], aTd[s, cc], atr, r=[dtr[("aTd", s)]], pw=[atr])
                    y_ = yv[cc]
                    P.op("dve", lambda e, a_=a_, y_=y_, cc=cc: e.tensor_scalar(out=y_[:], in0=a_[:, 0:S], scalar1=cwT[:, cc, 0:1], scalar2=cwT[:, cc, 31:32], op0=ALU.mult, op1=ALU.add),
                         r=[atr, t_cwT], w=[ytr[cc]])
                    for j in range(1, 31):
                        P.op("dve", lambda e, a_=a_, y_=y_, cc=cc, j=j: e.scalar_tensor_tensor(out=y_[:], in0=a_[:, j:j + S], scalar=cwT[:, cc, j:j + 1], in1=y_[:], op0=ALU.mult, op1=ALU.add),
                             r=[atr], w=[ytr[cc]])
                sqr = Ring(nc, st, "cv_sq", [128, 512], F32, 2)
                mr = Ring(nc, st, "cv_m", [128, 3, 512], F32, 2)
                tr_ = Ring(nc, st, "cv_t", [128, 512], F32, 2)
                outr = Ring(nc, st, "cv_o", [128, 512], BF16, 3)
                for tb in range(NB):
                    blk = slice(tb * 512, (tb + 1) * 512)
                    for cc in range(4):
                        P.mm(ps[1][:], onesf[:], yv[cc][:, blk], cc == 0, cc == 3, [ytr[cc], ctr], pst[1])
                    for cc in range(4):
                        sq_, sqtr = sqr.next()
                        P.op("act", lambda e, sq_=sq_, cc=cc: e.activation(out=sq_[:], in_=yv[cc][:, blk], func=AF.Square), r=[ytr[cc]], w=[sqtr])
                        P.mm(ps[2][:], onesf[:], sq_[:], cc == 0, cc == 3, [sqtr, ctr], pst[2])
                    m, mtr = mr.next()
                    P.op("dve", lambda e, m=m: e.tensor_scalar(out=m[:, 0, :], in0=ps[1][:], scalar1=1.0 / 512, scalar2=None, op0=ALU.mult), r=[pst[1]], w=[mtr])
                    P.op("dve", lambda e, m=m: e.tensor_tensor(out=m[:, 1, :], in0=m[:, 0, :], in1=m[:, 0, :], op=ALU.mult), r=[mtr], pw=[mtr])
                    P.op("dve", lambda e, m=m: e.scalar_tensor_tensor(out=m[:, 1, :], in0=ps[2][:], scalar=1.0 / 512, in1=m[:, 1, :], op0=ALU.mult, op1=ALU.subtract), r=[pst[2], mtr], pw=[mtr])
                    P.op("act", lambda e, m=m: e.activation(out=m[:, 2, :], in_=m[:, 1, :], func=AF.Sqrt, bias=epsc[:, 0:1], scale=1.0), r=[mtr, ctr], pw=[mtr])
                    P.op("dve", lambda e, m=m: e.reciprocal(out=m[:, 2, :], in_=m[:, 2, :]), r=[mtr], pw=[mtr])
                    for cc in range(4):
                        t_, ttr = tr_.next()
                        P.op("dve", lambda e, t_=t_, m=m, cc=cc: e.tensor_tensor(out=t_[:], in0=yv[cc][:, blk], in1=m[:, 0, :], op=ALU.subtract), r=[ytr[cc], mtr], w=[ttr])
                        P.op("dve", lambda e, t_=t_, m=m: e.tensor_tensor(out=t_[:], in0=t_[:], in1=m[:, 2, :], op=ALU.mult), r=[mtr], w=[ttr])
                        o, otr = outr.next()
                        P.op("act", lambda e, o=o, t_=t_, cc=cc: e.activation(out=o[:], in_=t_[:], func=AF.Silu, bias=cwT[:, cc, 33:34], scale=cwT[:, cc, 32:33]), r=[ttr, t_cwT], w=[otr])
                        P.dma("pool", cvTd[s, cc, :, blk], o[:], otr, r=[otr], pw=[dtr[("cvTd", s)]])

                P.barrier()
        def stage_R(l, s):
            with ExitStack() as st:
                lgr = st.enter_context(sbt(nc, "rt_lg", [128, 8], F32)); t_lg = Tr()
                iof = st.enter_context(sbt(nc, "rt_iof", [128, 128], F32))
                ioq = st.enter_context(sbt(nc, "rt_ioq", [128, 128], F32))
                iop = st.enter_context(sbt(nc, "rt_iop", [128, 1], F32))
                t_io = Tr()
                tA = st.enter_context(sbt(nc, "rt_tA", [128, 128], F32))
                tB = st.enter_context(sbt(nc, "rt_tB", [128, 128], F32))
                tC = st.enter_context(sbt(nc, "rt_tC", [128, 128], F32)); t_tmp = Tr()
                DT = st.enter_context(sbt(nc, "rt_DT", [128, RH, 128], F32))
                CF = st.enter_context(sbt(nc, "rt_CF", [128, RH, 128], F32))
                CB = st.enter_context(sbt(nc, "rt_CB", [128, RH, 128], F32))
                pc = st.enter_context(sbt(nc, "rt_pc", [128, 4, RH], F32))
                t_tb = Tr()
                P.dma("sp", lgr[:], ret_decay_logits[l].rearrange("a h -> (a h)").partition_broadcast(128), t_lg, w=[t_lg])
                P.op("act", lambda e: e.activation(out=lgr[:], in_=lgr[:], func=AF.Exp, scale=-1.0), r=[t_lg], w=[t_lg])
                P.op("dve", lambda e: e.tensor_scalar(out=lgr[:], in0=lgr[:], scalar1=1.0, scalar2=None, op0=ALU.add), r=[t_lg], w=[t_lg])
                P.op("act", lambda e: e.activation(out=lgr[:], in_=lgr[:], func=AF.Ln), r=[t_lg], w=[t_lg])
                P.op("dve", lambda e: e.tensor_scalar(out=lgr[:], in0=lgr[:], scalar1=-1.0, scalar2=None, op0=ALU.mult), r=[t_lg], w=[t_lg])
                P.op("pool", lambda e: e.iota(iof[:], [[1, 128]], base=0, channel_multiplier=-1, allow_small_or_imprecise_dtypes=True), w=[t_io])
                P.op("pool", lambda e: e.iota(ioq[:], [[1, 128]], base=0, channel_multiplier=0, allow_small_or_imprecise_dtypes=True), pw=[t_io])
                P.op("dve", lambda e: e.tensor_scalar(out=iop[:], in0=iof[:, 0:1], scalar1=-1.0, scalar2=None, op0=ALU.mult), r=[t_io], pw=[t_io])
                for h in range(RH):
                    lf = lgr[:, h:h + 1]; lb = lgr[:, RH + h:RH + h + 1]
                    P.op("dve", lambda e: e.tensor_scalar(out=tA[:], in0=iof[:], scalar1=0.0, scalar2=None, op0=ALU.max), r=[t_io], w=[t_tmp])
                    P.op("act", lambda e, lf=lf: e.activation(out=tA[:], in_=tA[:], func=AF.Exp, scale=lf), r=[t_tmp, t_lg], w=[t_tmp])
                    P.op("dve", lambda e: e.tensor_scalar(out=tB[:], in0=iof[:], scalar1=0.0, scalar2=None, op0=ALU.is_ge), r=[t_io], pw=[t_tmp])
                    P.op("dve", lambda e: e.tensor_tensor(out=tA[:], in0=tA[:], in1=tB[:], op=ALU.mult), r=[t_tmp], w=[t_tmp])
                    P.op("dve", lambda e: e.tensor_scalar(out=tC[:], in0=iof[:], scalar1=-1.0, scalar2=0.0, op0=ALU.mult, op1=ALU.max), r=[t_io], pw=[t_tmp])
                    P.op("act", lambda e, lb=lb: e.activation(out=tC[:], in_=tC[:], func=AF.Exp, scale=lb), r=[t_tmp], w=[t_tmp])
                    P.op("dve", lambda e: e.tensor_scalar(out=tB[:], in0=iof[:], scalar1=0.0, scalar2=None, op0=ALU.is_lt), r=[t_io], w=[t_tmp])
                    P.op("dve", lambda e: e.tensor_tensor(out=tC[:], in0=tC[:], in1=tB[:], op=ALU.mult), r=[t_tmp], w=[t_tmp])
                    P.op("dve", lambda e, h=h: e.tensor_tensor(out=DT[:, h, :], in0=tA[:], in1=tC[:], op=ALU.add), r=[t_tmp], pw=[t_tb])
                    P.op("dve", lambda e: e.tensor_scalar(out=tA[:], in0=ioq[:], scalar1=1.0, scalar2=None, op0=ALU.add), r=[t_io], w=[t_tmp])
                    P.op("act", lambda e, h=h, lf=lf: e.activation(out=CF[:, h, :], in_=tA[:], func=AF.Exp, scale=lf), r=[t_tmp], pw=[t_tb])
                    P.op("dve", lambda e: e.tensor_scalar(out=tA[:], in0=ioq[:], scalar1=-1.0, scalar2=128.0, op0=ALU.mult, op1=ALU.add), r=[t_io], w=[t_tmp])
                    P.op("act", lambda e, h=h, lb=lb: e.activation(out=CB[:, h, :], in_=tA[:], func=AF.Exp, scale=lb), r=[t_tmp], pw=[t_tb])
                    P.op("dve", lambda e: e.tensor_scalar(out=tA[:, 0:1], in0=iop[:], scalar1=-1.0, scalar2=127.0, op0=ALU.mult, op1=ALU.add), r=[t_io], w=[t_tmp])
                    P.op("act", lambda e, h=h, lf=lf: e.activation(out=pc[:, 0, h:h + 1], in_=tA[:, 0:1], func=AF.Exp, scale=lf), r=[t_tmp], pw=[t_tb])
                    P.op("act", lambda e, h=h, lb=lb: e.activation(out=pc[:, 1, h:h + 1], in_=iop[:], func=AF.Exp, scale=lb), r=[t_io], pw=[t_tb])
                    P.op("act", lambda e, h=h, lf=lf: e.activation(out=pc[:, 2, h:h + 1], in_=lf, func=AF.Exp, scale=128.0), r=[t_lg], pw=[t_tb])
                    P.op("act", lambda e, h=h, lb=lb: e.activation(out=pc[:, 3, h:h + 1], in_=lb, func=AF.Exp, scale=128.0), r=[t_lg], pw=[t_tb])
                gnb = st.enter_context(sbt(nc, "rt_gn", [128, D], F32)); t_gn = Tr()
                P.dma("sp", gnb[:], ret_gn_g[l].partition_broadcast(128), t_gn, w=[t_gn])

                Rall = st.enter_context(sbt(nc, "rt_Rall", [128, T, 1024], BF16)); t_Rall = Tr()
                Rb = st.enter_context(sbt(nc, "rt_Rb", [128, RH, 256], F32)); t_Rb = Tr()
                Sf = st.enter_context(sbt(nc, "rt_Sf", [128, RH, 256], F32))
                Sfb = st.enter_context(sbt(nc, "rt_Sfb", [128, RH, 256], BF16)); t_Sf = Tr()
                P.op("pool", lambda e: e.memset(Rb[:], 0.0), w=[t_Rb])
                P.op("pool", lambda e: e.memset(Rall[:, T - 1, :], 0.0), w=[t_Rall])
                P.op("pool", lambda e: e.memset(Sf[:], 0.0), w=[t_Sf])
                P.op("pool", lambda e: e.memset(Sfb[:], 0.0), pw=[t_Sf])
                kin = Ring(nc, st, "rt_k", [128, 512], BF16, 3)
                vin = Ring(nc, st, "rt_v", [128, 1024], BF16, 3)
                qin = Ring(nc, st, "rt_q", [128, 512], BF16, 2)
                gin = Ring(nc, st, "rt_g", [128, 1024], F32, 2)
                kc_ = Ring(nc, st, "rt_kc", [128, RH, 128], BF16, 2)
                for c in range(T - 1, 0, -1):
                    tok = slice(c * 128, (c + 1) * 128)
                    k, ktr = kin.next(); v, vtr = vin.next()
                    P.dma("sp", k[:], rkd[s, tok, :], ktr, r=[dtr[("rkd", s)]], w=[ktr])
                    P.dma("sp", v[:], rvd[s, tok, :], vtr, r=[dtr[("rvd", s)]], w=[vtr])
                    kb, kbtr = kc_.next()
                    for h in range(RH):
                        if h % 2:
                            P.op("dve", lambda e, kb=kb, k=k, h=h: e.tensor_scalar(out=kb[:, h, :], in0=k[:, h * 128:(h + 1) * 128], scalar1=pc[:, 1, h:h + 1], scalar2=None, op0=ALU.mult),
                                 r=[ktr, t_tb], pw=[kbtr])
                        else:
                            P.op("act", lambda e, kb=kb, k=k, h=h: e.mul(out=kb[:, h, :], in_=k[:, h * 128:(h + 1) * 128], mul=pc[:, 1, h:h + 1]),
                                 r=[ktr, t_tb], w=[kbtr] if h == 0 else (), pw=[kbtr] if h else ())
                    for h in range(RH):
                        bk = h // 2
                        P.mm(ps[bk][:, (h % 2) * 256:(h % 2) * 256 + 256], kb[:, h, :], v[:, h * 256:(h + 1) * 256], True, True, [kbtr, vtr], pst[bk], first=(h % 2 == 0))
                    for h in range(RH):
                        bk = h // 2
                        P.op("dve", lambda e, h=h, bk=bk: e.scalar_tensor_tensor(out=Rb[:, h, :], in0=Rb[:, h, :], scalar=pc[:, 3, h:h + 1], in1=ps[bk][:, (h % 2) * 256:(h % 2) * 256 + 256], op0=ALU.mult, op1=ALU.add),
                             r=[pst[bk], t_tb], w=[t_Rb])
                    P.op("act", lambda e, c=c: e.copy(out=Rall[:, c - 1, :], in_=Rb[:].rearrange("p h e -> p (h e)")), r=[t_Rb], pw=[t_Rall])
                qT3 = Ring(nc, st, "rt_qT", [128, 3, RH, 128], BF16, 2)
                kTr = Ring(nc, st, "rt_kT", [128, RH, 128], BF16, 2)
                stm = Ring(nc, st, "rt_stm", [128, RH, 128], BF16, 2)
                bnr = Ring(nc, st, "rt_bn", [128, RH, 8], F32, 2)
                onr = Ring(nc, st, "rt_on", [128, D], F32, 2)
                gtd_ = Ring(nc, st, "rt_gt", [128, D], BF16, 2)
                gTr = Ring(nc, st, "rt_gT", [128, 8, 128], BF16, 2)
                for c in range(T):
                    tok = slice(c * 128, (c + 1) * 128)
                    k, ktr = kin.next(); v, vtr = vin.next(); q, qtr = qin.next(); g_, gtr = gin.next()
                    P.dma("sp", q[:], rqd[s, tok, :], qtr, r=[dtr[("rqd", s)]], w=[qtr])
                    P.dma("sp", k[:], rkd[s, tok, :], ktr, r=[dtr[("rkd", s)]], w=[ktr])
                    P.dma("sp", v[:], rvd[s, tok, :], vtr, r=[dtr[("rvd", s)]], w=[vtr])
                    P.dma("sp", g_[:], sgd[s, tok, :], gtr, r=[dtr[("sgd", s)]], w=[gtr])
                    pv = ps[0][:].bitcast(BF16)
                    for h in range(RH):
                        P.tp(pv[:, h * 128:(h + 1) * 128], q[:, h * 128:(h + 1) * 128], ident[:], [qtr, ctr], pst[0], h == 0)
                    for h in range(RH):
                        P.tp(pv[:, 512 + h * 128:512 + (h + 1) * 128], k[:, h * 128:(h + 1) * 128], ident[:], [ktr], pst[0], False)
                    qT, qTtr = qT3.next(); kT, kTtr = kTr.next()
                    pq = pv[:, 0:512].rearrange("p (h c) -> p h c", h=RH)
                    P.op("act", lambda e, qT=qT, pq=pq: e.copy(out=qT[:, 0], in_=pq), r=[pst[0]], w=[qTtr])
                    P.op("dve", lambda e, qT=qT, pq=pq: e.tensor_tensor(out=qT[:, 1], in0=pq, in1=CF[:], op=ALU.mult), r=[pst[0], t_tb], pw=[qTtr])
                    P.op("dve", lambda e, qT=qT, pq=pq: e.tensor_tensor(out=qT[:, 2], in0=pq, in1=CB[:], op=ALU.mult), r=[pst[0], t_tb], pw=[qTtr])
                    P.op("act", lambda e, kT=kT, pv=pv: e.copy(out=kT[:], in_=pv[:, 512:1024].rearrange("p (h c) -> p h c", h=RH)), r=[pst[0]], w=[kTtr])
                    kf, kftr = kc_.next()
                    for h in range(RH):
                        P.op("act", lambda e, kf=kf, k=k, h=h: e.mul(out=kf[:, h, :], in_=k[:, h * 128:(h + 1) * 128], mul=pc[:, 0, h:h + 1]),
                             r=[ktr, t_tb], w=[kftr] if h == 0 else (), pw=[kftr] if h else ())
                    for h in range(RH):
                        P.mm(ps[1][:, h * 128:(h + 1) * 128], kT[:, h, :], qT[:, 0, h, :], True, True, [kTtr, qTtr], pst[1], first=(h == 0))
                    sm_, smtr = stm.next()
                    P.op("dve", lambda e, sm_=sm_: e.tensor_tensor(out=sm_[:], in0=ps[1][:].rearrange("p (h c) -> p h c", h=RH), in1=DT[:], op=ALU.mult), r=[pst[1], t_tb], w=[smtr])
                    for h in range(RH):
                        bk = 2 + h // 2
                        oc = slice((h % 2) * 256, (h % 2) * 256 + 256)
                        P.mm(ps[bk][:, oc], sm_[:, h, :], v[:, h * 256:(h + 1) * 256], True, False, [smtr, vtr], pst[bk], first=(h % 2 == 0))
                        P.mm(ps[bk][:, oc], qT[:, 1, h, :], Sfb[:, h, :], False, False, [qTtr, t_Sf], pst[bk])
                        P.mm(ps[bk][:, oc], qT[:, 2, h, :], Rall[:, c, h * 256:(h + 1) * 256], False, True, [qTtr, t_Rall], pst[bk])
                    for h in range(RH):
                        bk = 4 + h // 2
                        oc = slice((h % 2) * 256, (h % 2) * 256 + 256)
                        P.mm(ps[bk][:, oc], kf[:, h, :], v[:, h * 256:(h + 1) * 256], True, True, [kftr, vtr], pst[bk], first=(h % 2 == 0))
                    for h in range(RH):
                        bk = 4 + h // 2
                        oc = slice((h % 2) * 256, (h % 2) * 256 + 256)
                        P.op("dve", lambda e, h=h, bk=bk, oc=oc: e.scalar_tensor_tensor(out=Sf[:, h, :], in0=Sf[:, h, :], scalar=pc[:, 2, h:h + 1], in1=ps[bk][:, oc], op0=ALU.mult, op1=ALU.add),
                             r=[pst[bk], t_tb], w=[t_Sf])
                    P.op("act", lambda e: e.copy(out=Sfb[:], in_=Sf[:]), r=[t_Sf], w=[t_Sf])
                    if debug and c == 1:
                        dO = st.enter_context(sbt(nc, "dbgO", [128, 1024], F32)); t_dO = Tr()
                        P.op("dve", lambda e: e.tensor_copy(out=dO[:, 0:512], in_=ps[2][:]), r=[pst[2]], w=[t_dO])
                        P.op("dve", lambda e: e.tensor_copy(out=dO[:, 512:1024], in_=ps[3][:]), r=[pst[3]], pw=[t_dO])
                        P.dma("pool", dbg_O[:, :], dO[:], t_dO, r=[t_dO])
                        P.dma("pool", dbg_qT[:, :], qT[:].rearrange("p a h c -> p (a h c)"), qTtr, r=[qTtr])
                        P.dma("pool", dbg_kT[:, :], kT[:].rearrange("p h c -> p (h c)"), kTtr, r=[kTtr])
                        P.dma("pool", dbg_sm[:, :], sm_[:].rearrange("p h c -> p (h c)"), smtr, r=[smtr])
                        P.dma("pool", dbg_kf[:, :], kf[:].rearrange("p h c -> p (h c)"), kftr, r=[kftr])
                        P.dma("pool", dbg_DT[:, :], DT[:].rearrange("p h c -> p (h c)"), t_dO, r=[t_tb])
                        P.dma("pool", dbg_CF[:, :], CF[:].rearrange("p h c -> p (h c)"), t_dO, r=[t_tb])
                        P.dma("pool", dbg_CB[:, :], CB[:].rearrange("p h c -> p (h c)"), t_dO, r=[t_tb])
                        P.dma("pool", dbg_pc[:, :], pc[:].rearrange("p a h -> p (a h)"), t_dO, r=[t_tb])
                        P.dma("pool", dbg_Sf[:, :], Sf[:].rearrange("p h e -> p (h e)"), t_dO, r=[t_Sf])
                        P.dma("pool", dbg_R[:, :], Rall[:, c, :], t_dO, r=[t_Rall])
                    bn, bntr = bnr.next()
                    on, ontr = onr.next()
                    for h in range(RH):
                        bk = 2 + h // 2
                        oc = slice((h % 2) * 256, (h % 2) * 256 + 256)
                        P.op("dve", lambda e, bn=bn, h=h, bk=bk, oc=oc: e.bn_stats(out=bn[:, h, 0:6], in_=ps[bk][:, oc]), r=[pst[bk]], w=[bntr] if h == 0 else (), pw=[bntr] if h else ())
                    for h in range(RH):
                        P.op("dve", lambda e, bn=bn, h=h: e.bn_aggr(out=bn[:, h, 6:8], in_=bn[:, h, 0:6]), r=[bntr], pw=[bntr])
                    P.op("act", lambda e, bn=bn: e.activation(out=bn[:, :, 0], in_=bn[:, :, 7], func=AF.Sqrt, bias=epsc[:, 0:1], scale=1.0), r=[bntr, ctr], pw=[bntr])
                    P.op("dve", lambda e, bn=bn: e.reciprocal(out=bn[:, :, 1], in_=bn[:, :, 0]), r=[bntr], pw=[bntr])
                    for h in range(RH):
                        bk = 2 + h // 2
                        oc = slice((h % 2) * 256, (h % 2) * 256 + 256)
                        P.op("dve", lambda e, on=on, bn=bn, h=h, bk=bk, oc=oc: e.tensor_scalar(out=on[:, h * 256:(h + 1) * 256], in0=ps[bk][:, oc], scalar1=bn[:, h, 6:7], scalar2=bn[:, h, 1:2], op0=ALU.subtract, op1=ALU.mult),
                             r=[pst[bk], bntr], w=[ontr] if h == 0 else (), pw=[ontr] if h else ())
                    P.op("pool", lambda e, on=on: e.tensor_tensor(out=on[:], in0=on[:], in1=gnb[:], op=ALU.mult), r=[ontr, t_gn], w=[ontr])
                    gt, gttr = gtd_.next()
                    P.op("dve", lambda e, gt=gt, on=on, g_=g_: e.tensor_tensor(out=gt[:], in0=on[:], in1=g_[:], op=ALU.mult), r=[ontr, gtr], w=[gttr])
                    pv6 = ps[6][:].bitcast(BF16)
                    for kc in range(8):
                        P.tp(pv6[:, kc * 128:(kc + 1) * 128], gt[:, kc * 128:(kc + 1) * 128], ident[:], [gttr, ctr], pst[6], kc == 0)
                    gT, gTtr = gTr.next()
                    P.op("act", lambda e, gT=gT, pv6=pv6: e.copy(out=gT[:], in_=pv6.rearrange("p (k c) -> p k c", k=8)), r=[pst[6]], w=[gTtr])
                    P.dma("pool", rtTd[s, :, :, tok].rearrange("k p c -> p k c"), gT[:], gTtr, r=[gTtr], pw=[dtr[("rtTd", s)]])

                P.barrier()
        def postnorm_residual(bA, bB, xt, xtr_, gb, t_g, o, otr, sqr, stt):
            sq_, sqtr = sqr.next()
            sm, smtr = stt.next()
            P.op("act", lambda e: e.activation(out=sq_[:, 0:512], in_=ps[bA][:], func=AF.Square), r=[pst[bA]], w=[sqtr])
            P.op("act", lambda e: e.activation(out=sq_[:, 512:1024], in_=ps[bB][:], func=AF.Square), r=[pst[bB]], pw=[sqtr])
            P.op("dve", lambda e: e.reduce_sum(out=sm[:, 2:3], in_=sq_[:], axis=mybir.AxisListType.X), r=[sqtr], w=[smtr])
            P.op("act", lambda e: e.activation(out=sm[:, 3:4], in_=sm[:, 2:3], func=AF.Sqrt, bias=epsc[:, 0:1], scale=1.0 / D), r=[smtr, ctr], pw=[smtr])
            P.op("dve", lambda e: e.reciprocal(out=sm[:, 3:4], in_=sm[:, 3:4]), r=[smtr], pw=[smtr])
            P.op("dve", lambda e: e.scalar_tensor_tensor(out=o[:, 0:512], in0=ps[bA][:], scalar=sm[:, 3:4], in1=gb[:, 0:512], op0=ALU.mult, op1=ALU.mult), r=[pst[bA], smtr, t_g], w=[otr])
            P.op("dve", lambda e: e.scalar_tensor_tensor(out=o[:, 512:1024], in0=ps[bB][:], scalar=sm[:, 3:4], in1=gb[:, 512:1024], op0=ALU.mult, op1=ALU.mult), r=[pst[bB], smtr], pw=[otr])
            P.op("pool", lambda e: e.tensor_tensor(out=o[:], in0=o[:], in1=xt[:], op=ALU.add), r=[xtr_], w=[otr])

        def stage_M(l, s, xsrc, xtr):
            with ExitStack() as st:
                wmo = st.enter_context(sbt(nc, "m_wmo", [128, 4, D], BF16))
                wpw = st.enter_context(sbt(nc, "m_wpw", [128, 4, D], BF16))
                wro = st.enter_context(sbt(nc, "m_wro", [128, 8, D], BF16))
                wou = st.enter_context(sbt(nc, "m_wou", [128, 8, D], BF16))
                gb = st.enter_context(sbt(nc, "m_gb", [128, D], F32))
                t_w = Tr(); t_g = Tr()
                P.dma("sp", wmo[:], wb["mo"][l].rearrange("(kc p) n -> p kc n", p=128), t_w, r=[wb_tr[("mo", l)]], w=[t_w])
                P.dma("sp", wpw[:], wb["pw"][l].rearrange("(kc p) n -> p kc n", p=128), t_w, r=[wb_tr[("pw", l)]], pw=[t_w])
                P.dma("sp", wro[:], wb["ro"][l].rearrange("(kc p) n -> p kc n", p=128), t_w, r=[wb_tr[("ro", l)]], pw=[t_w])
                P.dma("sp", wou[:], wb["wo"][l].rearrange("(kc p) n -> p kc n", p=128), t_w, r=[wb_tr[("wo", l)]], pw=[t_w])
                P.dma("sp", gb[:], ln_mix_post[l].partition_broadcast(128), t_g, w=[t_g])
                oTr = Ring(nc, st, "m_oT", [128, 4, 512], BF16, 2)
                cTr = Ring(nc, st, "m_cT", [128, 4, 512], BF16, 2)
                rTr = Ring(nc, st, "m_rT", [128, 8, 512], BF16, 2)
                gtr_ = Ring(nc, st, "m_gt", [128, 3, 512], F32, 3)
                mt = Ring(nc, st, "m_t", [128, 2, 512], F32, 2)
                mgr = Ring(nc, st, "m_mg", [128, 8, 512], BF16, 2)
                xr = Ring(nc, st, "m_x", [128, D], F32, 2)
                outr = Ring(nc, st, "m_o", [128, D], F32, 2)
                sqr = Ring(nc, st, "m_sq", [128, D], F32, 2)
                stt = Ring(nc, st, "m_st", [128, 8], F32, 3)
                for tb in range(NB):
                    blk = slice(tb * 512, (tb + 1) * 512)
                    o_, otr_ = oTr.next(); c_, ctr_ = cTr.next(); r_, rtr_ = rTr.next()
                    P.dma("sp", o_[:], oTd[s, :, :, blk].rearrange("k p c -> p k c"), otr_, r=[dtr[("oTd", s)]], w=[otr_])
                    P.dma("sp", c_[:], cvTd[s, :, :, blk].rearrange("k p c -> p k c"), ctr_, r=[dtr[("cvTd", s)]], w=[ctr_])
                    P.dma("sp", r_[:], rtTd[s, :, :, blk].rearrange("k p c -> p k c"), rtr_, r=[dtr[("rtTd", s)]], w=[rtr_])
                    mg, mgtr = mgr.next()
                    for rc in range(8):
                        cs = slice(rc * 128, (rc + 1) * 128)
                        gt, gttr = gtr_.next()
                        for b in range(3):
                            P.dma("sp", gt[:, b, :], gtd[s, b * 8 + rc, :, blk], gttr, r=[dtr[("gtd", s)]], w=[gttr] if b == 0 else (), pw=[gttr] if b else ())
                        b0 = (rc % 2) * 3
                        for kc in range(4):
                            P.mm(ps[b0][:], wmo[:, kc, cs], o_[:, kc, :], kc == 0, kc == 3, [t_w, otr_], pst[b0])
                        for kc in range(4):
                            P.mm(ps[b0 + 1][:], wpw[:, kc, cs], c_[:, kc, :], kc == 0, kc == 3, [t_w, ctr_], pst[b0 + 1])
                        for kc in range(8):
                            P.mm(ps[b0 + 2][:], wro[:, kc, cs], r_[:, kc, :], kc == 0, kc == 7, [t_w, rtr_], pst[b0 + 2])
                        t_, ttr = mt.next()
                        P.op("dve", lambda e, t_=t_, gt=gt, b0=b0: e.tensor_tensor(out=t_[:, 0, :], in0=ps[b0][:], in1=gt[:, 0, :], op=ALU.mult), r=[pst[b0], gttr], w=[ttr])
                        P.op("dve", lambda e, t_=t_, gt=gt, b0=b0: e.tensor_tensor(out=t_[:, 1, :], in0=ps[b0 + 1][:], in1=gt[:, 1, :], op=ALU.mult), r=[pst[b0 + 1], gttr], pw=[ttr])
                        P.op("pool", lambda e, t_=t_: e.tensor_tensor(out=t_[:, 0, :], in0=t_[:, 0, :], in1=t_[:, 1, :], op=ALU.add), r=[ttr], w=[ttr])
                        P.op("dve", lambda e, t_=t_, gt=gt, b0=b0: e.tensor_tensor(out=t_[:, 1, :], in0=ps[b0 + 2][:], in1=gt[:, 2, :], op=ALU.mult), r=[pst[b0 + 2], gttr], w=[ttr])
                        P.op("pool", lambda e, t_=t_, mg=mg, rc=rc: e.tensor_tensor(out=mg[:, rc, :], in0=t_[:, 0, :], in1=t_[:, 1, :], op=ALU.add), r=[ttr], w=[mgtr] if rc == 0 else (), pw=[mgtr] if rc else ())
                    for tt in range(4):
                        t = tb * 4 + tt
                        tok = slice(t * 128, (t + 1) * 128)
                        xt, xtr_ = xr.next()
                        P.dma("sp", xt[:], xsrc[s, tok, :], xtr_, r=[xtr[s]], w=[xtr_])
                        for nb in range(2):
                            for kc in range(8):
                                P.mm(ps[6 + nb][:], mg[:, kc, tt * 128:(tt + 1) * 128], wou[:, kc, nb * 512:(nb + 1) * 512], kc == 0, kc == 7, [mgtr, t_w], pst[6 + nb])
                        o, otr = outr.next()
                        postnorm_residual(6, 7, xt, xtr_, gb, t_g, o, otr, sqr, stt)
                        P.dma("pool", x1d[s, tok, :], o[:], otr, r=[otr], pw=[dtr[("x1d", s)]])

                P.barrier()
        def stage_F(l, s, ydst, ykey):
            with ExitStack() as st:
                wg = st.enter_context(sbt(nc, "f_wg", [128, 8, FH], BF16))
                wu = st.enter_context(sbt(nc, "f_wu", [128, 8, FH], BF16))
                t_w = Tr(); t_g = Tr()
                gpre = st.enter_context(sbt(nc, "f_gpre", [128, D], F32))
                gpost = st.enter_context(sbt(nc, "f_gpost", [128, D], F32))
                P.dma("sp", wg[:], wb["fg"][l].rearrange("(kc p) n -> p kc n", p=128), t_w, r=[wb_tr[("fg", l)]], w=[t_w])
                P.dma("sp", wu[:], wb["fu"][l].rearrange("(kc p) n -> p kc n", p=128), t_w, r=[wb_tr[("fu", l)]], pw=[t_w])
                P.dma("sp", gpre[:], ln_ffn_pre[l].partition_broadcast(128), t_g, w=[t_g])
                P.dma("sp", gpost[:], ln_ffn_post[l].partition_broadcast(128), t_g, pw=[t_g])
                wdr = Ring(nc, st, "f_wd", [128, D], BF16, 8)
                xr = Ring(nc, st, "f_x", [128, D], F32, 5)
                hb = Ring(nc, st, "f_hb", [128, D], BF16, 2)
                sqr = Ring(nc, st, "f_sq", [128, D], F32, 2)
                stt = Ring(nc, st, "f_st", [128, 8], F32, 4)
                h2T = Ring(nc, st, "f_h2T", [128, 8, 512], BF16, 2)
                hid = Ring(nc, st, "f_hid", [128, 22, 512], BF16, 1)
                sgr = Ring(nc, st, "f_sg", [128, 512], F32, 2)
                outr = Ring(nc, st, "f_o", [128, D], F32, 2)
                wdv = wb["fd"][l]
                for tb in range(NB):
                    hT_, hTtr = h2T.next()
                    xts = []
                    for tt in range(4):
                        t = tb * 4 + tt
                        tok = slice(t * 128, (t + 1) * 128)
                        xt, xtr_ = xr.next()
                        xts.append((xt, xtr_))
                        P.dma("sp", xt[:], x1d[s, tok, :], xtr_, r=[dtr[("x1d", s)]], w=[xtr_])
                        sq_, sqtr = sqr.next()
                        sm, smtr = stt.next()
                        ssq4(xt, sq_, sm, xtr_, sqtr, smtr)
                        rstd_from_ssq(sm[:, 0:1], sm[:, 1:2], D, [smtr])
                        h, htr = hb.next()
                        P.op("dve", lambda e, xt=xt, h=h, sm=sm: e.scalar_tensor_tensor(out=h[:], in0=xt[:], scalar=sm[:, 1:2], in1=gpre[:], op0=ALU.mult, op1=ALU.mult), r=[xtr_, smtr, t_g], w=[htr])
                        pv = ps[4 + (tt % 2)][:].bitcast(BF16)
                        bk = 4 + (tt % 2)
                        for kc in range(8):
                            P.tp(pv[:, kc * 128:(kc + 1) * 128], h[:, kc * 128:(kc + 1) * 128], ident[:], [htr, ctr], pst[bk], kc == 0)
                        P.op("act", lambda e, hT_=hT_, pv=pv, tt=tt: e.copy(out=hT_[:, :, tt * 128:(tt + 1) * 128], in_=pv.rearrange("p (k c) -> p k c", k=8)), r=[pst[bk]],
                             w=[hTtr] if tt == 0 else (), pw=[hTtr] if tt else ())
                    hd, hdtr = hid.next()
                    for hc in range(22):
                        bg, bu = (4, 5) if hc % 2 == 0 else (6, 7)
                        cs = slice(hc * 128, (hc + 1) * 128)
                        for kc in range(8):
                            P.mm(ps[bg][:], wg[:, kc, cs], hT_[:, kc, :], kc == 0, kc == 7, [t_w, hTtr], pst[bg])
                        for kc in range(8):
                            P.mm(ps[bu][:], wu[:, kc, cs], hT_[:, kc, :], kc == 0, kc == 7, [t_w, hTtr], pst[bu])
                        sg, sgtr = sgr.next()
                        P.op("act", lambda e, sg=sg, bg=bg: e.activation(out=sg[:], in_=ps[bg][:], func=AF.Silu), r=[pst[bg]], w=[sgtr])
                        P.op("dve", lambda e, hd=hd, sg=sg, bu=bu, hc=hc: e.tensor_tensor(out=hd[:, hc, :], in0=ps[bu][:], in1=sg[:], op=ALU.mult), r=[pst[bu], sgtr],
                             w=[hdtr] if hc == 0 else (), pw=[hdtr] if hc else ())
                    for half in range(2):
                        for hc in range(22):
                            wd, wdtr = wdr.next()
                            P.dma("sp", wd[:], wdv[hc * 128:(hc + 1) * 128, :], wdtr, r=[wb_tr[("fd", l)]], w=[wdtr])
                            for t2 in range(2):
                                tt = half * 2 + t2
                                for nb in range(2):
                                    bk = t2 * 2 + nb
                                    P.mm(ps[bk][:], hd[:, hc, tt * 128:(tt + 1) * 128], wd[:, nb * 512:(nb + 1) * 512], hc == 0, hc == 21, [hdtr, wdtr], pst[bk])
                        for t2 in range(2):
                            tt = half * 2 + t2
                            t = tb * 4 + tt
                            tok = slice(t * 128, (t + 1) * 128)
                            xt, xtr_ = xts[tt]
                            o, otr = outr.next()
                            postnorm_residual(t2 * 2, t2 * 2 + 1, xt, xtr_, gpost, t_g, o, otr, sqr, stt)
                            P.dma("pool", ydst[s, tok, :], o[:], otr, r=[otr], pw=[dtr[(ykey, s)]])

                P.barrier()
        for l in range(nlayers):
            if "W" in stages:
                stage_W(l)
        for s in range(NS):
            if "T" in stages:
                stage_T(s)
        xin_tr = [Tr() for _ in range(NS)]
        for l in range(nlayers):
            last = (l == nlayers - 1)
            for s in range(NS):
                if l == 0:
                    xsrc, xtr = x_in, xin_tr
                else:
                    xsrc, xtr = xLd, [dtr[("xLd", s_)] for s_ in range(NS)]
                if "N" in stages:
                    stage_NP(l, s, xsrc, xtr)
                if "A" in stages:
                    stage_A(l, s)
                if "C" in stages:
                    stage_C(l, s)
                if "R" in stages:
                    stage_R(l, s)
                if "M" in stages:
                    stage_M(l, s, xsrc, xtr)
                if "F" in stages:
                    stage_F(l, s, y_out if last else xLd, "y" if last else "xLd")
        P.wait_all("sp", [dtr[("y", s)] for s in range(NS)])
        for en in ("pe", "act", "dve", "pool"):
            E = P.E[en]
            if E.cnt:
                if P.E["sp"].waited.get(E.key, 0) < E.cnt:
                    P.E["sp"].e.wait_ge(E.sem, E.cnt)
    return nc


def rope_consts():
    def inv(dim):
        return (np.float32(10000.0) ** (-(np.arange(0, dim, 2, dtype=np.float32)) / np.float32(dim))).astype(np.float32)
    im, ir = inv(32), inv(128)
    c = np.zeros((2, 160), np.float32)
    c[0] = np.concatenate([im, im, ir, ir])
    c[1] = np.concatenate([np.zeros(16), np.full(16, np.pi / 2), np.zeros(64), np.full(64, np.pi / 2)]).astype(np.float32)
    return c


WEIGHT_NAMES = ["ln_mix_pre", "ln_mix_post", "ln_ffn_pre", "ln_ffn_post", "w_in", "mla_q_norm", "mla_w_uq", "mla_kv_norm",
                "mla_w_ukv", "mla_w_o", "conv_w_dw", "conv_b_dw", "conv_ln_g", "conv_ln_b", "conv_w_pw", "ret_decay_logits",
                "ret_gn_g", "ret_w_o", "w_out", "ffn_w_gate", "ffn_w_up", "ffn_w_down"]


def kernel(**inputs):
    x = np.ascontiguousarray(np.asarray(inputs["x"], dtype=np.float32))
    pos = np.ascontiguousarray(np.asarray(inputs["positions"], dtype=np.int32))
    B, S, _ = x.shape
    ncores = 8
    NS = B // ncores
    nc = build(S, NS)
    shared = {k: np.ascontiguousarray(np.asarray(inputs[k], dtype=np.float32)) for k in WEIGHT_NAMES}
    shared["rope_consts"] = rope_consts()
    in_maps = []
    for c in range(ncores):
        m = dict(shared)
        m["x"] = x[c * NS:(c + 1) * NS]
        m["positions"] = pos[c * NS:(c + 1) * NS]
        in_maps.append(m)
    res = run_bass_kernel_spmd(nc, in_maps, core_ids=list(range(ncores)))
    return np.concatenate([r["y"] for r in res.results], axis=0).astype(np.float32)
```

```python
import numpy as np
import concourse.bass as bass
import concourse.mybir as mybir
from concourse.bass_utils import run_bass_kernel_spmd
from contextlib import ExitStack

F32 = mybir.dt.float32
BF16 = mybir.dt.bfloat16
I32 = mybir.dt.int32
AF = mybir.ActivationFunctionType
ALU = mybir.AluOpType

D = 1024
L = 2
NH = 8
RH = 4
FH = 2816
INC = 7584
EPS = 1e-6
SAME_ENGINE_SYNC = False
import os
SUB = os.environ.get("SUB", "lcrg")
CUT = float(os.environ.get("CUT", "9"))
TWO_PI = float(2 * np.pi)
PI = float(np.pi)

O_CQ, O_CKV, O_KPE, O_CONV, O_RQ, O_RK, O_RV, O_RG, O_GATE = 0, 256, 384, 416, 1440, 1952, 2464, 3488, 4512


class Tr:
    __slots__ = ("w", "r", "sem", "cnt", "name", "excl")

    def __init__(self, name="", excl=False):
        self.excl = excl
        self.w = {}
        self.r = {}
        self.sem = None
        self.cnt = 0
        self.name = name


class Eng:
    def __init__(self, e, sem, key):
        self.e = e
        self.sem = sem
        self.key = key
        self.cnt = 0
        self.waited = {}


class Prog:
    def __init__(self, nc, es):
        self.nc = nc
        self.es = es
        self.sems = {}
        self.nsem = 0
        self.E = {}
        self.pool = []
        self.live = []
        self.uid = 0
        for name, e in (("pe", nc.tensor), ("act", nc.scalar), ("dve", nc.vector), ("pool", nc.gpsimd), ("sp", nc.sync)):
            s, k = self.newsem("e_" + name)
            self.E[name] = Eng(e, s, k)

    def newsem(self, name):
        s = self.es.enter_context(self.nc.semaphore(name + "_%d" % self.nsem))
        k = self.nsem
        self.nsem += 1
        self.sems[k] = s
        return s, k

    def _waits(self, E, r, w, pw):
        need = {}
        for t in r:
            for k, v in t.w.items():
                if need.get(k, 0) < v:
                    need[k] = v
            if t.excl:
                for k, v in t.r.items():
                    if need.get(k, 0) < v:
                        need[k] = v
        for t in w:
            for d in (t.w, t.r):
                for k, v in d.items():
                    if need.get(k, 0) < v:
                        need[k] = v
        for t in pw:
            for d in (t.w, t.r):
                for k, v in d.items():
                    if need.get(k, 0) < v:
                        need[k] = v
        for k, v in need.items():
            if k == E.key and not SAME_ENGINE_SYNC:
                continue
            if E.waited.get(k, 0) < v:
                E.e.wait_ge(self.sems[k], v)
                E.waited[k] = v

    def op(self, en, fn, r=(), w=(), pw=()):
        E = self.E[en]
        self._waits(E, r, w, pw)
        ins = fn(E.e)
        E.cnt += 1
        ins.then_inc(E.sem, 1)
        for t in r:
            t.r[E.key] = E.cnt
        for t in w:
            t.w = {E.key: E.cnt}
            t.r = {}
        for t in pw:
            t.w[E.key] = E.cnt

    def dma(self, q, out, in_, sb, r=(), w=(), pw=()):
        Q = self.E[q]
        self._waits(Q, r, w, pw)
        if sb.sem is None:
            if self.pool:
                sb.sem, sb.cnt = self.pool.pop()
            else:
                sb.sem = self.newsem("d")
            self.live.append(sb)
        sem, key = sb.sem
        sb.cnt += 16
        Q.e.dma_start(out=out, in_=in_).then_inc(sem, 16)
        for t in r:
            t.r[key] = sb.cnt
        for t in w:
            t.w = {key: sb.cnt}
            t.r = {}
        for t in pw:
            t.w[key] = sb.cnt

    def barrier(self):
        sp = self.E["sp"]
        for en in ("pe", "act", "dve", "pool"):
            E = self.E[en]
            if sp.waited.get(E.key, 0) < E.cnt:
                sp.e.wait_ge(E.sem, E.cnt)
                sp.waited[E.key] = E.cnt
        for sb in self.live:
            sem, key = sb.sem
            if sp.waited.get(key, 0) < sb.cnt:
                sp.e.wait_ge(sem, sb.cnt)
                sp.waited[key] = sb.cnt
        if not hasattr(self, "bar"):
            self.bar = self.newsem("bar")
            self.barcnt = 0
        self.barcnt += 1
        sp.e.sem_inc(self.bar[0], 1)
        for en in ("pe", "act", "dve", "pool"):
            E = self.E[en]
            E.e.wait_ge(self.bar[0], self.barcnt)
            for en2 in ("pe", "act", "dve", "pool"):
                E.waited[self.E[en2].key] = max(E.waited.get(self.E[en2].key, 0), self.E[en2].cnt)
            for sb in self.live:
                E.waited[sb.sem[1]] = max(E.waited.get(sb.sem[1], 0), sb.cnt)
        for sb in self.live:
            self.pool.append((sb.sem, sb.cnt))
            sb.sem = None
        self.live = []

    def wait_all(self, en, trs):
        E = self.E[en]
        self._waits(E, trs, (), ())

    def mm(self, out, lhsT, rhs, start, stop, r, tr, first=None):
        if first is None:
            first = start
        self.op("pe", lambda e: e.matmul(out, lhsT, rhs, start=bool(start), stop=bool(stop)), r=r,
                w=[tr] if first else (), pw=() if first else [tr])

    def tp(self, out, in_, ident, r, tr, first):
        self.op("pe", lambda e: e.transpose(out, in_, ident), r=r, w=[tr] if first else (), pw=() if first else [tr])


_UID = [0]


def sbt(nc, name, shape, dt):
    _UID[0] += 1
    return nc.sbuf_tensor("%s_u%d" % (name, _UID[0]), shape, dt)


class Ring:
    cnt = [0]

    def __init__(self, nc, es, name, shape, dt, n):
        Ring.cnt[0] += 1
        self.t = [es.enter_context(sbt(nc, "%s_%d_%d" % (name, Ring.cnt[0], i), shape, dt)) for i in range(n)]
        self.tr = [Tr("%s%d" % (name, i)) for i in range(n)]
        self.i = 0
        self.n = n

    def next(self):
        i = self.i
        self.i = (i + 1) % self.n
        return self.t[i], self.tr[i]


def build(S, NS, debug=False, nlayers=L, stages="WTNACRMF"):
    T = S // 128
    NB = S // 512
    nc = bass.Bass("TRN2", target_bir_lowering=False)

    def din(name, shape, dt=F32):
        return nc.dram_tensor(name, shape, dt, kind="ExternalInput").ap()

    x_in = din("x", [NS, S, D])
    pos_in = din("positions", [NS, S], I32)
    ln_mix_pre = din("ln_mix_pre", [L, D]); ln_mix_post = din("ln_mix_post", [L, D])
    ln_ffn_pre = din("ln_ffn_pre", [L, D]); ln_ffn_post = din("ln_ffn_post", [L, D])
    w_in = din("w_in", [L, D, INC])
    mla_q_norm = din("mla_q_norm", [L, 256]); mla_w_uq = din("mla_w_uq", [L, 256, 768])
    mla_kv_norm = din("mla_kv_norm", [L, 128]); mla_w_ukv = din("mla_w_ukv", [L, 128, 1024])
    mla_w_o = din("mla_w_o", [L, 512, D])
    conv_w_dw = din("conv_w_dw", [L, 31, 512]); conv_b_dw = din("conv_b_dw", [L, 512])
    conv_ln_g = din("conv_ln_g", [L, 512]); conv_ln_b = din("conv_ln_b", [L, 512])
    conv_w_pw = din("conv_w_pw", [L, 512, D])
    ret_decay_logits = din("ret_decay_logits", [L, 2, RH])
    ret_gn_g = din("ret_gn_g", [L, D]); ret_w_o = din("ret_w_o", [L, D, D])
    w_out = din("w_out", [L, D, D])
    ffn_w_gate = din("ffn_w_gate", [L, D, FH]); ffn_w_up = din("ffn_w_up", [L, D, FH]); ffn_w_down = din("ffn_w_down", [L, FH, D])
    cst_in = din("rope_consts", [2, 160])
    y_out = nc.dram_tensor("y", [NS, S, D], F32, kind="ExternalOutput").ap()

    skind = "ExternalOutput" if debug else "Internal"

    def scr(name, shape, dt):
        return nc.dram_tensor(name, shape, dt, kind=skind).ap()

    wb = {
        "in": scr("wb_in", [L, D, INC], BF16), "uq": scr("wb_uq", [L, 256, 768], BF16), "ukv": scr("wb_ukv", [L, 128, 1024], BF16),
        "mo": scr("wb_mo", [L, 512, D], BF16), "pw": scr("wb_pw", [L, 512, D], BF16), "ro": scr("wb_ro", [L, D, D], BF16),
        "wo": scr("wb_wo", [L, D, D], BF16), "fg": scr("wb_fg", [L, D, FH], BF16), "fu": scr("wb_fu", [L, D, FH], BF16),
        "fd": scr("wb_fd", [L, FH, D], BF16),
    }
    wsrc = {"in": w_in, "uq": mla_w_uq, "ukv": mla_w_ukv, "mo": mla_w_o, "pw": conv_w_pw, "ro": ret_w_o, "wo": w_out,
            "fg": ffn_w_gate, "fu": ffn_w_up, "fd": ffn_w_down}
    wb_tr = {(k, l): Tr("wb_%s%d" % (k, l)) for k in wb for l in range(L)}

    tabd = scr("tabd", [NS, 128, T * 160], F32)
    QTd = scr("QTd", [NS, NH, 96, S], BF16); KTd = scr("KTd", [NS, NH, 96, S], BF16)
    Vd = scr("Vd", [NS, S, NH * 65], BF16)
    oTd = scr("oTd", [NS, 4, 128, S], BF16)
    aTd = scr("aTd", [NS, 4, 128, S], F32); cvTd = scr("cvTd", [NS, 4, 128, S], BF16)
    rqd = scr("rqd", [NS, S, 512], BF16); rkd = scr("rkd", [NS, S, 512], BF16)
    rvd = scr("rvd", [NS, S, 1024], BF16); sgd = scr("sgd", [NS, S, 1024], F32)
    rtTd = scr("rtTd", [NS, 8, 128, S], BF16)
    gtd = scr("gtd", [NS, 24, 128, S], F32)
    x1d = scr("x1d", [NS, S, D], F32)
    xLd = scr("xLd", [NS, S, D], F32)
    if debug:
        dbg_qT = scr("dbg_qT", [128, 3 * RH * 128], BF16); dbg_kT = scr("dbg_kT", [128, RH * 128], BF16)
        dbg_sm = scr("dbg_sm", [128, RH * 128], BF16); dbg_DT = scr("dbg_DT", [128, RH * 128], F32)
        dbg_CF = scr("dbg_CF", [128, RH * 128], F32); dbg_CB = scr("dbg_CB", [128, RH * 128], F32)
        dbg_O = scr("dbg_O", [128, 1024], F32); dbg_Sf = scr("dbg_Sf", [128, 1024], F32); dbg_R = scr("dbg_R", [128, 1024], BF16)
        dbg_pc = scr("dbg_pc", [128, 16], F32); dbg_kf = scr("dbg_kf", [128, 512], BF16)
    dtr = {}
    for nm in ("tabd", "QTd", "KTd", "Vd", "oTd", "aTd", "cvTd", "rqd", "rkd", "rvd", "sgd", "rtTd", "gtd", "x1d", "xLd", "y"):
        for s in range(NS):
            dtr[(nm, s)] = Tr("%s_%d" % (nm, s))

    with ExitStack() as es:
        P = Prog(nc, es)
        ps = [es.enter_context(nc.psum_tensor("psb%d" % i, [128, 512], F32)) for i in range(8)]
        pst = [Tr("ps%d" % i, excl=True) for i in range(8)]
        identf = es.enter_context(sbt(nc, "identf", [128, 128], F32))
        ident = es.enter_context(sbt(nc, "ident", [128, 128], BF16))
        onesf = es.enter_context(sbt(nc, "onesf", [128, 128], F32))
        epsc = es.enter_context(sbt(nc, "epsc", [128, 1], F32))
        ctr = Tr("consts")
        P.op("pool", lambda e: e.memset(identf[:], 0.0), w=[ctr])
        P.op("pool", lambda e: e.affine_select(out=identf[:], in_=identf[:], pattern=[[-1, 128]], compare_op=ALU.not_equal,
                                               fill=1.0, base=0, channel_multiplier=1), w=[ctr])
        P.op("pool", lambda e: e.tensor_copy(out=ident[:], in_=identf[:]), r=[ctr], pw=[ctr])
        P.op("pool", lambda e: e.memset(onesf[:], 1.0), pw=[ctr])
        P.op("pool", lambda e: e.memset(epsc[:], EPS), pw=[ctr])

        def rstd_from_ssq(ssq, rstd, n, trs):
            P.op("act", lambda e: e.activation(out=rstd, in_=ssq, func=AF.Sqrt, bias=epsc[:, 0:1], scale=1.0 / n), r=trs + [ctr], w=trs)
            P.op("dve", lambda e: e.reciprocal(out=rstd, in_=rstd), r=trs, w=trs)

        def ssq4(xt, sqt, sm, xtr_, sqtr, smtr):
            P.op("act", lambda e: e.activation(out=sqt[:], in_=xt[:], func=AF.Square), r=[xtr_], w=[sqtr])
            P.op("dve", lambda e: e.reduce_sum(out=sm[:, 0:1], in_=sqt[:], axis=mybir.AxisListType.X), r=[sqtr], w=[smtr])

        def stage_W(l):
            with ExitStack() as st:
                stf = Ring(nc, st, "wstf", [128, 2048], F32, 3)
                stb = Ring(nc, st, "wstb", [128, 2048], BF16, 3)
                i = 0
                for k in ("in", "uq", "ukv", "mo", "pw", "ro", "wo", "fg", "fu", "fd"):
                    src = wsrc[k][l]
                    dst = wb[k][l]
                    K, N = src.shape
                    for kc in range(K // 128):
                        for c0 in range(0, N, 2048):
                            w = min(2048, N - c0)
                            f, ftr = stf.next()
                            b, btr = stb.next()
                            P.dma("sp", f[:, 0:w], src[kc * 128:(kc + 1) * 128, c0:c0 + w], ftr, w=[ftr])
                            en = ("dve", "pool", "act")[i % 3]
                            if en == "act":
                                P.op(en, lambda e, f=f, b=b, w=w: e.copy(out=b[:, 0:w], in_=f[:, 0:w]), r=[ftr], w=[btr])
                            else:
                                P.op(en, lambda e, f=f, b=b, w=w: e.tensor_copy(out=b[:, 0:w], in_=f[:, 0:w]), r=[ftr], w=[btr])
                            P.dma("pool", dst[kc * 128:(kc + 1) * 128, c0:c0 + w], b[:, 0:w], btr, r=[btr], pw=[wb_tr[(k, l)]])
                            i += 1

                P.barrier()
        def stage_T(s):
            with ExitStack() as st:
                posrow = st.enter_context(sbt(nc, "posrow", [2, S], F32))
                posi = st.enter_context(sbt(nc, "posi", [1, S], I32))
                cst = st.enter_context(sbt(nc, "cst", [2, 160], F32))
                tab = st.enter_context(sbt(nc, "tab", [128, T * 160], F32))
                tmpf = st.enter_context(sbt(nc, "tmpf", [128, T * 160], F32))
                tmpi = st.enter_context(sbt(nc, "tmpi", [128, T * 160], I32))
                t_pr, t_pi, t_c, t_tab, t_f, t_i = Tr(), Tr(), Tr(), Tr(), Tr(), Tr()
                P.op("dve", lambda e: e.memset(posrow[:], 1.0), w=[t_pr])
                P.dma("sp", posi[:], pos_in[s:s + 1, :], t_pi, w=[t_pi])
                P.dma("sp", cst[:], cst_in[:, :], t_c, w=[t_c])
                P.op("dve", lambda e: e.tensor_copy(out=posrow[0:1, :], in_=posi[:]), r=[t_pi], pw=[t_pr])
                for t0 in range(0, T, 3):
                    n = min(3, T - t0)
                    bk = (t0 // 3) % 2
                    for j in range(n):
                        t = t0 + j
                        P.mm(ps[bk][:, j * 160:(j + 1) * 160], posrow[0:2, t * 128:(t + 1) * 128], cst[0:2, :], True, True,
                             [t_pr, t_c], pst[bk], first=(j == 0))
                    P.op("dve", lambda e, bk=bk, n=n, t0=t0: e.tensor_copy(out=tab[:, t0 * 160:(t0 + n) * 160], in_=ps[bk][:, 0:n * 160]),
                         r=[pst[bk]], pw=[t_tab])
                W = T * 160
                for c0 in range(0, W, 2560):
                    c1 = min(W, c0 + 2560)
                    a = tab[:, c0:c1]; f = tmpf[:, c0:c1]; ii = tmpi[:, c0:c1]
                    P.op("dve", lambda e, a=a, f=f: e.tensor_scalar(out=f, in0=a, scalar1=1.0 / TWO_PI, scalar2=None, op0=ALU.mult), r=[t_tab], w=[t_f])
                    P.op("dve", lambda e, f=f, ii=ii: e.tensor_copy(out=ii, in_=f), r=[t_f], w=[t_i])
                    P.op("dve", lambda e, f=f, ii=ii: e.tensor_copy(out=f, in_=ii), r=[t_i], w=[t_f])
                    P.op("dve", lambda e, a=a, f=f: e.scalar_tensor_tensor(out=a, in0=f, scalar=-6.28125, in1=a, op0=ALU.mult, op1=ALU.add), r=[t_f], w=[t_tab])
                    P.op("dve", lambda e, a=a, f=f: e.scalar_tensor_tensor(out=a, in0=f, scalar=-(TWO_PI - 6.28125), in1=a, op0=ALU.mult, op1=ALU.add), r=[t_f], w=[t_tab])
                    P.op("dve", lambda e, a=a, f=f: e.tensor_scalar(out=f, in0=a, scalar1=PI, scalar2=-TWO_PI, op0=ALU.is_gt, op1=ALU.mult), r=[t_tab], w=[t_f])
                    P.op("dve", lambda e, a=a, f=f: e.tensor_tensor(out=a, in0=a, in1=f, op=ALU.add), r=[t_f], w=[t_tab])
                    P.op("dve", lambda e, a=a, f=f: e.tensor_scalar(out=f, in0=a, scalar1=-PI, scalar2=TWO_PI, op0=ALU.is_lt, op1=ALU.mult), r=[t_tab], w=[t_f])
                    P.op("dve", lambda e, a=a, f=f: e.tensor_tensor(out=a, in0=a, in1=f, op=ALU.add), r=[t_f], w=[t_tab])
                    P.op("dve", lambda e, a=a: e.tensor_scalar(out=a, in0=a, scalar1=-3.1415925, scalar2=3.1415925, op0=ALU.max, op1=ALU.min), r=[t_tab], w=[t_tab])
                    P.op("act", lambda e, a=a: e.activation(out=a, in_=a, func=AF.Sin), r=[t_tab], w=[t_tab])
                P.dma("pool", tabd[s], tab[:], t_tab, r=[t_tab], w=[dtr[("tabd", s)]])

                P.barrier()
        def stage_NP(l, s, xsrc, xtr):
            with ExitStack() as st:
                hT = st.enter_context(sbt(nc, "hT", [128, 8, S], BF16)); t_hT = Tr("hT")
                gbc = st.enter_context(sbt(nc, "np_gbc", [128, D], F32)); t_g = Tr()
                gq = st.enter_context(sbt(nc, "np_gq", [128, 384], F32))
                tab = st.enter_context(sbt(nc, "np_tab", [128, T, 160], F32)); t_tab = Tr()
                P.dma("sp", gbc[:], ln_mix_pre[l].partition_broadcast(128), t_g, w=[t_g])
                P.dma("sp", gq[:, 0:256], mla_q_norm[l].partition_broadcast(128), t_g, pw=[t_g])
                P.dma("sp", gq[:, 256:384], mla_kv_norm[l].partition_broadcast(128), t_g, pw=[t_g])
                P.dma("sp", tab[:].rearrange("p t c -> p (t c)"), tabd[s], t_tab, r=[dtr[("tabd", s)]], w=[t_tab])
                xr = Ring(nc, st, "np_x", [128, D], F32, 3)
                hb = Ring(nc, st, "np_hb", [128, D], BF16, 2)
                sq = Ring(nc, st, "np_sq", [128, D], F32, 2)
                stt = Ring(nc, st, "np_st", [128, 8], F32, 4)
                for t in range(T):
                    xt, xtr_ = xr.next()
                    P.dma("sp", xt[:], xsrc[s, t * 128:(t + 1) * 128, :], xtr_, r=[xtr[s]], w=[xtr_])
                    sqt, sqtr = sq.next()
                    sm, smtr = stt.next()
                    ssq4(xt, sqt, sm, xtr_, sqtr, smtr)
                    rstd_from_ssq(sm[:, 0:1], sm[:, 1:2], D, [smtr])
                    h, htr = hb.next()
                    P.op("dve", lambda e, xt=xt, h=h, sm=sm: e.scalar_tensor_tensor(out=h[:], in0=xt[:], scalar=sm[:, 1:2], in1=gbc[:], op0=ALU.mult, op1=ALU.mult),
                         r=[xtr_, smtr, t_g], w=[htr])
                    bk = t % 2
                    pv = ps[bk][:].bitcast(BF16)
                    for kc in range(8):
                        P.tp(pv[:, kc * 128:(kc + 1) * 128], h[:, kc * 128:(kc + 1) * 128], ident[:], [htr, ctr], pst[bk], kc == 0)
                    en = "act" if t % 2 == 0 else "dve"
                    if en == "act":
                        P.op("act", lambda e, pv=pv, t=t: e.copy(out=hT[:, :, t * 128:(t + 1) * 128], in_=pv.rearrange("p (k c) -> p k c", k=8)), r=[pst[bk]], pw=[t_hT])
                    else:
                        P.op("dve", lambda e, pv=pv, t=t: e.tensor_copy(out=hT[:, :, t * 128:(t + 1) * 128], in_=pv.rearrange("p (k c) -> p k c", k=8)), r=[pst[bk]], pw=[t_hT])

                win = wb["in"][l].rearrange("(kc p) n -> p kc n", p=128)
                wtr_in = wb_tr[("in", l)]

                for g in ([ExitStack()] if "l" in SUB else []):
                    wl = g.enter_context(sbt(nc, "wl", [128, 8, 416], BF16)); t_wl = Tr()
                    wuq = g.enter_context(sbt(nc, "wuq", [128, 2, 768], BF16)); t_wuq = Tr()
                    wk = g.enter_context(sbt(nc, "wk", [128, 8, 64], BF16))
                    wv = g.enter_context(sbt(nc, "wv", [128, 8, 64], BF16)); t_wkv = Tr()
                    P.dma("sp", wl[:], win[:, :, 0:416], t_wl, r=[wtr_in], w=[t_wl])
                    P.dma("sp", wuq[:], wb["uq"][l].rearrange("(kc p) n -> p kc n", p=128), t_wuq, r=[wb_tr[("uq", l)]], w=[t_wuq])
                    ukv_v = wb["ukv"][l].rearrange("p (h c) -> p h c", h=8)
                    P.dma("sp", wk[:], ukv_v[:, :, 0:64], t_wkv, r=[wb_tr[("ukv", l)]], w=[t_wkv])
                    P.dma("sp", wv[:], ukv_v[:, :, 64:128], t_wkv, pw=[t_wkv])
                    lat = Ring(nc, g, "lat", [128, 416], BF16, 2)
                    sqj = Ring(nc, g, "lsq", [128, 256], F32, 2)
                    stl = Ring(nc, g, "lst", [128, 8], F32, 3)
                    tmpr = Ring(nc, g, "ltmp", [128, 4, 8, 16], F32, 2)
                    latT = Ring(nc, g, "latT", [128, 4, 128], BF16, 2)
                    qsb = Ring(nc, g, "qsb", [128, 8, 128], BF16, 2)
                    for qt_, qttr_ in zip(qsb.t, qsb.tr):
                        P.op("pool", lambda e, qt_=qt_: e.memset(qt_[:], 0.0), w=[qttr_])
                    qTb = Ring(nc, g, "qTb", [96, 8, 512], BF16, 2)
                    kTb = Ring(nc, g, "kTb", [96, 8, 512], BF16, 2)
                    ckb = Ring(nc, g, "ckb", [128, 512], BF16, 2)
                    kpb = Ring(nc, g, "kpb", [32, 512], BF16, 2)
                    vsb = Ring(nc, g, "vsb", [128, 8, 65], BF16, 2)
                    for vt, vtr in zip(vsb.t, vsb.tr):
                        P.op("pool", lambda e, vt=vt: e.memset(vt[:], 1.0), w=[vtr])
                    for tb in range(NB):
                        qT, qTtr = qTb.next()
                        kT, kTtr = kTb.next()
                        ck, cktr = ckb.next()
                        kp, kptr = kpb.next()
                        for tt in range(4):
                            t = tb * 4 + tt
                            tok = slice(t * 128, (t + 1) * 128)
                            for kc in range(8):
                                P.mm(ps[2][:, 0:416], hT[:, kc, tok], wl[:, kc, :], kc == 0, kc == 7, [t_hT, t_wl], pst[2])
                            la, latr = lat.next()
                            sj, sjtr = sqj.next()
                            sm, smtr = stl.next()
                            P.op("pool", lambda e, sm=sm: e.memset(sm[:], 0.0), w=[smtr])
                            P.op("act", lambda e, sj=sj, sm=sm: e.activation(out=sj[:, 0:256], in_=ps[2][:, 0:256], func=AF.Square, accum_out=sm[:, 0:1]), r=[pst[2], smtr], w=[sjtr], pw=[smtr])
                            P.op("act", lambda e, sj=sj, sm=sm: e.activation(out=sj[:, 0:128], in_=ps[2][:, 256:384], func=AF.Square, accum_out=sm[:, 1:2]), r=[pst[2], smtr], w=[sjtr], pw=[smtr])
                            P.op("act", lambda e, sm=sm: e.activation(out=sm[:, 2:3], in_=sm[:, 0:1], func=AF.Sqrt, bias=epsc[:, 0:1], scale=1.0 / 256), r=[smtr, ctr], pw=[smtr])
                            P.op("act", lambda e, sm=sm: e.activation(out=sm[:, 3:4], in_=sm[:, 1:2], func=AF.Sqrt, bias=epsc[:, 0:1], scale=1.0 / 128), r=[smtr], pw=[smtr])
                            P.op("dve", lambda e, sm=sm: e.reciprocal(out=sm[:, 4:6], in_=sm[:, 2:4]), r=[smtr], pw=[smtr])
                            P.op("dve", lambda e, la=la, sm=sm: e.scalar_tensor_tensor(out=la[:, 0:256], in0=ps[2][:, 0:256], scalar=sm[:, 4:5], in1=gq[:, 0:256], op0=ALU.mult, op1=ALU.mult),
                                 r=[pst[2], smtr, t_g], w=[latr])
                            P.op("dve", lambda e, la=la, sm=sm: e.scalar_tensor_tensor(out=la[:, 256:384], in0=ps[2][:, 256:384], scalar=sm[:, 5:6], in1=gq[:, 256:384], op0=ALU.mult, op1=ALU.mult),
                                 r=[pst[2], smtr, t_g], pw=[latr])
                            tm, tmtr = tmpr.next()
                            sn = tab[:, t, 0:16]; cs = tab[:, t, 16:32]
                            x1 = ps[2][:, 384:400]; x2 = ps[2][:, 400:416]
                            P.op("dve", lambda e, tm=tm, x1=x1, cs=cs: e.tensor_tensor(out=tm[:, 0, 0, :], in0=x1, in1=cs, op=ALU.mult), r=[pst[2], t_tab], w=[tmtr])
                            P.op("dve", lambda e, tm=tm, x2=x2, sn=sn: e.tensor_tensor(out=tm[:, 1, 0, :], in0=x2, in1=sn, op=ALU.mult), r=[pst[2]], pw=[tmtr])
                            P.op("dve", lambda e, tm=tm, x2=x2, cs=cs: e.tensor_tensor(out=tm[:, 2, 0, :], in0=x2, in1=cs, op=ALU.mult), r=[pst[2]], pw=[tmtr])
                            P.op("dve", lambda e, tm=tm, x1=x1, sn=sn: e.tensor_tensor(out=tm[:, 3, 0, :], in0=x1, in1=sn, op=ALU.mult), r=[pst[2]], pw=[tmtr])
                            P.op("dve", lambda e, tm=tm, la=la: e.tensor_tensor(out=la[:, 384:400], in0=tm[:, 0, 0, :], in1=tm[:, 1, 0, :], op=ALU.subtract), r=[tmtr], pw=[latr])
                            P.op("dve", lambda e, tm=tm, la=la: e.tensor_tensor(out=la[:, 400:416], in0=tm[:, 2, 0, :], in1=tm[:, 3, 0, :], op=ALU.add), r=[tmtr], pw=[latr])
                            if CUT < 2:
                                continue
                            pv = ps[3][:].bitcast(BF16)
                            for j in range(3):
                                P.tp(pv[:, j * 128:(j + 1) * 128], la[:, j * 128:(j + 1) * 128], ident[:], [latr, ctr], pst[3], j == 0)
                            P.tp(pv[0:32, 384:512], la[:, 384:416], ident[:], [latr], pst[3], False)
                            lT, lTtr = latT.next()
                            P.op("dve", lambda e, lT=lT, pv=pv: e.tensor_copy(out=lT[:, 0:3, :], in_=pv[:, 0:384].rearrange("p (j c) -> p j c", j=3)), r=[pst[3]], w=[lTtr])
                            P.op("dve", lambda e, ck=ck, pv=pv, tt=tt: e.tensor_copy(out=ck[:, tt * 128:(tt + 1) * 128], in_=pv[:, 256:384]), r=[pst[3]], pw=[cktr] if tt else (), w=[cktr] if tt == 0 else ())
                            P.op("dve", lambda e, kp=kp, pv=pv, tt=tt: e.tensor_copy(out=kp[:, tt * 128:(tt + 1) * 128], in_=pv[0:32, 384:512]), r=[pst[3]], pw=[kptr] if tt else (), w=[kptr] if tt == 0 else ())
                            if CUT < 2.1:
                                continue
                            for kc in range(2):
                                P.mm(ps[4][:, 0:480], lT[:, kc, :], wuq[:, kc, 0:480], kc == 0, kc == 1, [lTtr, t_wuq], pst[4])
                            for kc in range(2):
                                P.mm(ps[5][:, 0:288], lT[:, kc, :], wuq[:, kc, 480:768], kc == 0, kc == 1, [lTtr, t_wuq], pst[5])
                            if CUT < 2.3:
                                continue
                            q, qtr = qsb.next()
                            tm2, tm2tr = tmpr.next()
                            first = True
                            for (bk, h0, nh) in ((4, 0, 5), (5, 5, 3)):
                                pq = ps[bk][:, 0:nh * 96].rearrange("p (h d) -> p h d", h=nh)
                                qo = q[:, h0:h0 + nh, :]
                                P.op("act", lambda e, pq=pq, qo=qo: e.copy(out=qo[:, :, 0:64], in_=pq[:, :, 0:64]), r=[pst[bk]], pw=[qtr])
                                if CUT < 2.5:
                                    continue
                                csb = tab[:, t:t + 1, 16:32].to_broadcast([128, nh, 16])
                                snb = tab[:, t:t + 1, 0:16].to_broadcast([128, nh, 16])
                                x1 = pq[:, :, 64:80]; x2 = pq[:, :, 80:96]
                                tv = tm2[:, :, h0:h0 + nh, :]
                                P.op("dve", lambda e, tv=tv, x1=x1, csb=csb: e.tensor_tensor(out=tv[:, 0], in0=x1, in1=csb, op=ALU.mult), r=[pst[bk], t_tab], w=[tm2tr] if first else (), pw=() if first else [tm2tr])
                                P.op("dve", lambda e, tv=tv, x2=x2, snb=snb: e.tensor_tensor(out=tv[:, 1], in0=x2, in1=snb, op=ALU.mult), r=[pst[bk]], pw=[tm2tr])
                                P.op("dve", lambda e, tv=tv, x2=x2, csb=csb: e.tensor_tensor(out=tv[:, 2], in0=x2, in1=csb, op=ALU.mult), r=[pst[bk]], pw=[tm2tr])
                                P.op("dve", lambda e, tv=tv, x1=x1, snb=snb: e.tensor_tensor(out=tv[:, 3], in0=x1, in1=snb, op=ALU.mult), r=[pst[bk]], pw=[tm2tr])
                                P.op("dve", lambda e, tv=tv, qo=qo: e.tensor_tensor(out=qo[:, :, 64:80], in0=tv[:, 0], in1=tv[:, 1], op=ALU.subtract), r=[tm2tr], pw=[qtr])
                                P.op("dve", lambda e, tv=tv, qo=qo: e.tensor_tensor(out=qo[:, :, 80:96], in0=tv[:, 2], in1=tv[:, 3], op=ALU.add), r=[tm2tr], pw=[qtr])
                                first = False
                            if CUT < 2.7:
                                continue
                            pv6 = ps[6][:].bitcast(BF16)
                            for h in range(8):
                                P.tp(pv6[:, h * 128:(h + 1) * 128], q[:, h, :], ident[:], [qtr, ctr], pst[6], h == 0)
                            if CUT < 2.9:
                                continue
                            P.op("act", lambda e, qT=qT, pv6=pv6, tt=tt: e.copy(out=qT[:, :, tt * 128:(tt + 1) * 128], in_=pv6[0:96, :].rearrange("p (h c) -> p h c", h=8)),
                                 r=[pst[6]], w=[qTtr] if tt == 0 else (), pw=[qTtr] if tt else ())
                            if CUT < 4:
                                continue
                            P.mm(ps[7][:, 0:512], lT[:, 2, :], wv[:].rearrange("p h c -> p (h c)"), True, True, [lTtr, t_wkv], pst[7])
                            v, vtr = vsb.next()
                            P.op("dve", lambda e, v=v: e.tensor_copy(out=v[:, :, 0:64], in_=ps[7][:, 0:512].rearrange("p (h c) -> p h c", h=8)), r=[pst[7]], w=[vtr])
                            P.dma("pool", Vd[s, tok, :], v[:].rearrange("p h c -> p (h c)"), vtr, r=[vtr], pw=[dtr[("Vd", s)]])
                        if CUT < 5:
                            continue
                        blk = slice(tb * 512, (tb + 1) * 512)
                        for h in range(8):
                            bk = 2 + (h % 2) * 5
                            P.mm(ps[bk][0:64, :], wk[:, h, :], ck[:], True, True, [cktr, t_wkv], pst[bk])
                            if h % 2 == 0:
                                P.op("act", lambda e, kT=kT, h=h, bk=bk: e.copy(out=kT[0:64, h, :], in_=ps[bk][0:64, :]), r=[pst[bk]], w=[kTtr] if h == 0 else (), pw=[kTtr] if h else ())
                            else:
                                P.op("dve", lambda e, kT=kT, h=h, bk=bk: e.tensor_copy(out=kT[0:64, h, :], in_=ps[bk][0:64, :]), r=[pst[bk]], pw=[kTtr])
                        for h in range(8):
                            if h % 2 == 0:
                                P.op("act", lambda e, kT=kT, kp=kp, h=h: e.copy(out=kT[64:96, h, :], in_=kp[:, :]), r=[kptr], pw=[kTtr])
                            else:
                                P.op("dve", lambda e, kT=kT, kp=kp, h=h: e.tensor_copy(out=kT[64:96, h, :], in_=kp[:, :]), r=[kptr], pw=[kTtr])
                        P.dma("pool", QTd[s, :, :, blk].rearrange("h p c -> p h c"), qT[:], qTtr, r=[qTtr], pw=[dtr[("QTd", s)]])
                        P.dma("pool", KTd[s, :, :, blk].rearrange("h p c -> p h c"), kT[:], kTtr, r=[kTtr], pw=[dtr[("KTd", s)]])

                    P.barrier()
                    g.close()
                for g in ([ExitStack()] if "c" in SUB else []):
                    wc = g.enter_context(sbt(nc, "wc", [128, 8, 1024], BF16)); t_wc = Tr()
                    P.dma("sp", wc[:], win[:, :, O_CONV:O_CONV + 1024], t_wc, r=[wtr_in], w=[t_wc])
                    sgr = Ring(nc, g, "cv_sg", [128, 512], F32, 2)
                    aor = Ring(nc, g, "cv_a", [128, 512], F32, 3)
                    for tb in range(NB):
                        blk = slice(tb * 512, (tb + 1) * 512)
                        for j in range(4):
                            ba, bg = (2, 3) if j % 2 == 0 else (4, 5)
                            for kc in range(8):
                                P.mm(ps[ba][:], wc[:, kc, j * 128:(j + 1) * 128], hT[:, kc, blk], kc == 0, kc == 7, [t_hT, t_wc], pst[ba])
                            for kc in range(8):
                                P.mm(ps[bg][:], wc[:, kc, 512 + j * 128:512 + (j + 1) * 128], hT[:, kc, blk], kc == 0, kc == 7, [t_hT, t_wc], pst[bg])
                            sg, sgtr = sgr.next()
                            ao, aotr = aor.next()
                            P.op("act", lambda e, sg=sg, bg=bg: e.activation(out=sg[:], in_=ps[bg][:], func=AF.Sigmoid), r=[pst[bg]], w=[sgtr])
                            P.op("dve", lambda e, ao=ao, sg=sg, ba=ba: e.tensor_tensor(out=ao[:], in0=ps[ba][:], in1=sg[:], op=ALU.mult), r=[pst[ba], sgtr], w=[aotr])
                            P.dma("pool", aTd[s, j, :, blk], ao[:], aotr, r=[aotr], pw=[dtr[("aTd", s)]])

                    P.barrier()
                    g.close()
                for g in ([ExitStack()] if "r" in SUB else []):
                    wr = Ring(nc, g, "wr", [128, 8, 512], BF16, 2)
                    tmq = Ring(nc, g, "r_tm", [128, 4, 4, 64], F32, 2)
                    o16 = Ring(nc, g, "r_o16", [128, 512], BF16, 3)
                    o32 = Ring(nc, g, "r_o32", [128, 512], F32, 3)
                    for (kind, c0) in (("q", O_RQ), ("k", O_RK), ("v0", O_RV), ("v1", O_RV + 512), ("g0", O_RG), ("g1", O_RG + 512)):
                        w, wtr = wr.next()
                        P.dma("sp", w[:], win[:, :, c0:c0 + 512], wtr, r=[wtr_in], w=[wtr])
                        for t in range(T):
                            tok = slice(t * 128, (t + 1) * 128)
                            bk = 2 + (t % 4)
                            for kc in range(8):
                                P.mm(ps[bk][:], hT[:, kc, tok], w[:, kc, :], kc == 0, kc == 7, [t_hT, wtr], pst[bk])
                            if kind in ("q", "k"):
                                o, otr = o16.next()
                                tm, tmtr = tmq.next()
                                pq = ps[bk][:].rearrange("p (h d) -> p h d", h=4)
                                ov = o[:].rearrange("p (h d) -> p h d", h=4)
                                csb = tab[:, t:t + 1, 96:160].to_broadcast([128, 4, 64])
                                snb = tab[:, t:t + 1, 32:96].to_broadcast([128, 4, 64])
                                x1 = pq[:, :, 0:64]; x2 = pq[:, :, 64:128]
                                P.op("dve", lambda e, tm=tm, x1=x1, csb=csb: e.tensor_tensor(out=tm[:, 0], in0=x1, in1=csb, op=ALU.mult), r=[pst[bk], t_tab], w=[tmtr])
                                P.op("dve", lambda e, tm=tm, x2=x2, snb=snb: e.tensor_tensor(out=tm[:, 1], in0=x2, in1=snb, op=ALU.mult), r=[pst[bk]], pw=[tmtr])
                                P.op("dve", lambda e, tm=tm, x2=x2, csb=csb: e.tensor_tensor(out=tm[:, 2], in0=x2, in1=csb, op=ALU.mult), r=[pst[bk]], pw=[tmtr])
                                P.op("dve", lambda e, tm=tm, x1=x1, snb=snb: e.tensor_tensor(out=tm[:, 3], in0=x1, in1=snb, op=ALU.mult), r=[pst[bk]], pw=[tmtr])
                                if kind == "q":
                                    P.op("dve", lambda e, tm=tm, ov=ov: e.tensor_tensor(out=ov[:, :, 0:64], in0=tm[:, 0], in1=tm[:, 1], op=ALU.subtract), r=[tmtr], w=[otr])
                                    P.op("dve", lambda e, tm=tm, ov=ov: e.tensor_tensor(out=ov[:, :, 64:128], in0=tm[:, 2], in1=tm[:, 3], op=ALU.add), r=[tmtr], pw=[otr])
                                else:
                                    sc = float(128 ** -0.5)
                                    P.op("dve", lambda e, tm=tm: e.tensor_tensor(out=tm[:, 0], in0=tm[:, 0], in1=tm[:, 1], op=ALU.subtract), r=[tmtr], w=[tmtr])
                                    P.op("dve", lambda e, tm=tm: e.tensor_tensor(out=tm[:, 2], in0=tm[:, 2], in1=tm[:, 3], op=ALU.add), r=[tmtr], w=[tmtr])
                                    P.op("act", lambda e, tm=tm, ov=ov: e.mul(out=ov[:, :, 0:64], in_=tm[:, 0], mul=sc), r=[tmtr], w=[otr])
                                    P.op("act", lambda e, tm=tm, ov=ov: e.mul(out=ov[:, :, 64:128], in_=tm[:, 2], mul=sc), r=[tmtr], pw=[otr])
                                dst = (rqd if kind == "q" else rkd)[s, tok, :]
                                P.dma("pool", dst, o[:], otr, r=[otr], pw=[dtr[("rqd" if kind == "q" else "rkd", s)]])
                            elif kind in ("v0", "v1"):
                                o, otr = o16.next()
                                if t % 2 == 0:
                                    P.op("act", lambda e, o=o, bk=bk: e.copy(out=o[:], in_=ps[bk][:]), r=[pst[bk]], w=[otr])
                                else:
                                    P.op("dve", lambda e, o=o, bk=bk: e.tensor_copy(out=o[:], in_=ps[bk][:]), r=[pst[bk]], w=[otr])
                                half = 0 if kind == "v0" else 512
                                P.dma("pool", rvd[s, tok, half:half + 512], o[:], otr, r=[otr], pw=[dtr[("rvd", s)]])
                            else:
                                o, otr = o32.next()
                                P.op("act", lambda e, o=o, bk=bk: e.activation(out=o[:], in_=ps[bk][:], func=AF.Silu), r=[pst[bk]], w=[otr])
                                half = 0 if kind == "g0" else 512
                                P.dma("pool", sgd[s, tok, half:half + 512], o[:], otr, r=[otr], pw=[dtr[("sgd", s)]])

                    P.barrier()
                    g.close()
                for g in ([ExitStack()] if "g" in SUB else []):
                    wr = Ring(nc, g, "wg_", [128, 8, 512], BF16, 2)
                    gor = Ring(nc, g, "g_o", [128, 512], F32, 3)
                    for gg in range(6):
                        w, wtr = wr.next()
                        P.dma("sp", w[:], win[:, :, O_GATE + gg * 512:O_GATE + (gg + 1) * 512], wtr, r=[wtr_in], w=[wtr])
                        for tb in range(NB):
                            blk = slice(tb * 512, (tb + 1) * 512)
                            for j in range(4):
                                bk = 2 + (j % 4)
                                for kc in range(8):
                                    P.mm(ps[bk][:], w[:, kc, j * 128:(j + 1) * 128], hT[:, kc, blk], kc == 0, kc == 7, [t_hT, wtr], pst[bk])
                                o, otr = gor.next()
                                P.op("act", lambda e, o=o, bk=bk: e.activation(out=o[:], in_=ps[bk][:], func=AF.Sigmoid), r=[pst[bk]], w=[otr])
                                P.dma("pool", gtd[s, gg * 4 + j, :, blk], o[:], otr, r=[otr], pw=[dtr[("gtd", s)]])

                    P.barrier()
                    g.close()
                P.barrier()
        def stage_A(l, s, with_conv=True):
            scale = float(96 ** -0.5)
            with ExitStack() as st:
                cgen = stage_C_gen(l, s, st, 7) if with_conv else iter(())
                next(cgen, None)
                V = st.enter_context(sbt(nc, "at_V", [128, T, NH * 65], BF16)); t_V = Tr()
                sel = st.enter_context(sbt(nc, "at_sel", [65, 64], F32)); t_sel = Tr()
                P.op("pool", lambda e: e.memset(sel[:], 0.0), w=[t_sel])
                P.op("pool", lambda e: e.memset(sel[64:65, :], 1.0), pw=[t_sel])
                Vv = Vd[s].rearrange("(t p) c -> p t c", p=128)
                for t0 in range(0, T, 8):
                    P.dma("sp", V[:, t0:t0 + 8, :], Vv[:, t0:t0 + 8, :], t_V, r=[dtr[("Vd", s)]], pw=[t_V] if t0 else (), w=[t_V] if t0 == 0 else ())
                qr = Ring(nc, st, "at_q", [96, S], BF16, 2)
                kr = Ring(nc, st, "at_k", [96, S], BF16, 2)
                pr = Ring(nc, st, "at_p", [128, 512], BF16, 4)
                osb = Ring(nc, st, "at_o", [65, 512], F32, 2)
                rbc = Ring(nc, st, "at_r", [64, 512], F32, 2)
                oT = Ring(nc, st, "at_oT", [64, 512], BF16, 2)
                its = [(h, qb, kt) for h in range(NH) for qb in range(NB) for kt in range(T)]
                PF = 3
                qk = {}
                crate = min(1.0, 1.06 * (139 * NB + 8) / len(its))

                def get_qk(h):
                    if h not in qk:
                        q, qtr = qr.next()
                        k, ktr = kr.next()
                        P.dma("sp", q[:], QTd[s, h], qtr, r=[dtr[("QTd", s)]], w=[qtr])
                        P.dma("sp", k[:], KTd[s, h], ktr, r=[dtr[("KTd", s)]], w=[ktr])
                        qk[h] = (q, qtr, k, ktr)
                    return qk[h]

                def emit_score(i):
                    h, qb, kt = its[i]
                    q, qtr, k, ktr = get_qk(h)
                    sbk = i % 4
                    P.mm(ps[sbk][:], k[:, kt * 128:(kt + 1) * 128], q[:, qb * 512:(qb + 1) * 512], True, True, [qtr, ktr], pst[sbk])

                for i in range(min(PF, len(its))):
                    emit_score(i)
                for i, (h, qb, kt) in enumerate(its):
                    qs = slice(qb * 512, (qb + 1) * 512)
                    ob = 4 + (qb % 2)
                    sbk = i % 4
                    p, ptr = pr.next()
                    P.op("act", lambda e, p=p, sbk=sbk: e.activation(out=p[:], in_=ps[sbk][:], func=AF.Exp, scale=scale), r=[pst[sbk]], w=[ptr])
                    if i + PF < len(its):
                        emit_score(i + PF)
                    P.mm(ps[ob][0:65, :], V[:, kt, h * 65:(h + 1) * 65], p[:], kt == 0, kt == T - 1, [t_V, ptr], pst[ob])
                    if int((i + 1) * crate) > int(i * crate):
                        next(cgen, None)
                    if kt == T - 1:
                        o, otr = osb.next()
                        P.op("dve", lambda e, o=o, ob=ob: e.tensor_copy(out=o[:], in_=ps[ob][0:65, :]), r=[pst[ob]], w=[otr])
                        P.mm(ps[6][0:64, :], sel[:], o[:], True, True, [t_sel, otr], pst[6])
                        rb, rbtr = rbc.next()
                        P.op("dve", lambda e, rb=rb: e.reciprocal(out=rb[:], in_=ps[6][0:64, :]), r=[pst[6]], w=[rbtr])
                        ot, ottr = oT.next()
                        P.op("dve", lambda e, ot=ot, o=o, rb=rb: e.tensor_tensor(out=ot[:], in0=o[0:64, :], in1=rb[:], op=ALU.mult), r=[otr, rbtr], w=[ottr])
                        P.dma("pool", oTd[s, h // 2, (h % 2) * 64:(h % 2) * 64 + 64, qs], ot[:], ottr, r=[ottr], pw=[dtr[("oTd", s)]])
                for _ in cgen:
                    pass
                P.barrier()

        def stage_C_gen(l, s, st, pb):
            cw = st.enter_context(sbt(nc, "cv_cw", [34, 512], F32)); t_cw = Tr()
            cwT = st.enter_context(sbt(nc, "cv_cwT", [128, 4, 34], F32)); t_cwT = Tr()
            P.dma("sp", cw[0:31, :], conv_w_dw[l], t_cw, w=[t_cw])
            P.dma("sp", cw[31:32, :], conv_b_dw[l:l + 1, :], t_cw, pw=[t_cw])
            P.dma("sp", cw[32:33, :], conv_ln_g[l:l + 1, :], t_cw, pw=[t_cw])
            P.dma("sp", cw[33:34, :], conv_ln_b[l:l + 1, :], t_cw, pw=[t_cw])
            for cc in range(4):
                P.tp(ps[pb][:, cc * 34:(cc + 1) * 34], cw[0:34, cc * 128:(cc + 1) * 128], identf[0:34, 0:34], [t_cw, ctr], pst[pb], cc == 0)
            P.op("dve", lambda e: e.tensor_copy(out=cwT[:], in_=ps[pb][:, 0:136].rearrange("p (c j) -> p c j", c=4)), r=[pst[pb]], w=[t_cwT])
            apr = Ring(nc, st, "cv_ap", [128, 4, 542], F32, 2)
            yr = Ring(nc, st, "cv_y", [128, 4, 512], F32, 2)
            sqr = Ring(nc, st, "cv_sq", [128, 512], F32, 2)
            mr = Ring(nc, st, "cv_m", [128, 3, 512], F32, 2)
            tr_ = Ring(nc, st, "cv_t", [128, 512], F32, 2)
            outr = Ring(nc, st, "cv_o", [128, 512], BF16, 3)
            yield
            for tb in range(NB):
                blk = slice(tb * 512, (tb + 1) * 512)
                ap_, aptr = apr.next()
                lo = tb * 512 - 15
                hi = tb * 512 + 512 + 15
                first = True
                if lo < 0:
                    P.op("pool", lambda e: e.memset(ap_[:, :, 0:15], 0.0), w=[aptr])
                    first = False
                if hi > S:
                    P.op("pool", lambda e: e.memset(ap_[:, :, 527:542], 0.0), w=[aptr] if first else (), pw=() if first else [aptr])
                    first = False
                c_lo = max(lo, 0); c_hi = min(hi, S)
                for cc in range(4):
                    P.dma("sp", ap_[:, cc, c_lo - lo:c_hi - lo], aTd[s, cc, :, c_lo:c_hi], aptr, r=[dtr[("aTd", s)]], w=[aptr] if first else (), pw=() if first else [aptr])
                    first = False
                y_, ytr = yr.next()
                for cc in range(4):
                    P.op("dve", lambda e: e.tensor_scalar(out=y_[:, cc, :], in0=ap_[:, cc, 0:512], scalar1=cwT[:, cc, 0:1], scalar2=cwT[:, cc, 31:32], op0=ALU.mult, op1=ALU.add),
                         r=[aptr, t_cwT], w=[ytr] if cc == 0 else (), pw=[ytr] if cc else ())
                    yield
                    for j in range(1, 31):
                        P.op("dve", lambda e: e.scalar_tensor_tensor(out=y_[:, cc, :], in0=ap_[:, cc, j:j + 512], scalar=cwT[:, cc, j:j + 1], in1=y_[:, cc, :], op0=ALU.mult, op1=ALU.add),
                             r=[aptr], pw=[ytr])
                        yield
                for cc in range(4):
                    P.mm(ps[pb][:], onesf[:], y_[:, cc, :], cc == 0, cc == 3, [ytr, ctr], pst[pb])
                m, mtr = mr.next()
                P.op("dve", lambda e: e.tensor_scalar(out=m[:, 0, :], in0=ps[pb][:], scalar1=1.0 / 512, scalar2=None, op0=ALU.mult), r=[pst[pb]], w=[mtr])
                yield
                for cc in range(4):
                    sq_, sqtr = sqr.next()
                    P.op("dve", lambda e: e.tensor_tensor(out=sq_[:], in0=y_[:, cc, :], in1=y_[:, cc, :], op=ALU.mult), r=[ytr], w=[sqtr])
                    P.mm(ps[pb][:], onesf[:], sq_[:], cc == 0, cc == 3, [sqtr, ctr], pst[pb])
                    yield
                P.op("dve", lambda e: e.tensor_tensor(out=m[:, 1, :], in0=m[:, 0, :], in1=m[:, 0, :], op=ALU.mult), r=[mtr], pw=[mtr])
                P.op("dve", lambda e: e.scalar_tensor_tensor(out=m[:, 1, :], in0=ps[pb][:], scalar=1.0 / 512, in1=m[:, 1, :], op0=ALU.mult, op1=ALU.subtract), r=[pst[pb], mtr], pw=[mtr])
                yield
                P.op("act", lambda e: e.activation(out=m[:, 2, :], in_=m[:, 1, :], func=AF.Sqrt, bias=epsc[:, 0:1], scale=1.0), r=[mtr, ctr], pw=[mtr])
                P.op("dve", lambda e: e.reciprocal(out=m[:, 2, :], in_=m[:, 2, :]), r=[mtr], pw=[mtr])
                yield
                for cc in range(4):
                    t_, ttr = tr_.next()
                    P.op("dve", lambda e: e.tensor_tensor(out=t_[:], in0=y_[:, cc, :], in1=m[:, 0, :], op=ALU.subtract), r=[ytr, mtr], w=[ttr])
                    yield
                    P.op("dve", lambda e: e.tensor_tensor(out=t_[:], in0=t_[:], in1=m[:, 2, :], op=ALU.mult), r=[mtr], w=[ttr])
                    o, otr = outr.next()
                    P.op("act", lambda e: e.activation(out=o[:], in_=t_[:], func=AF.Silu, bias=cwT[:, cc, 33:34], scale=cwT[:, cc, 32:33]), r=[ttr, t_cwT], w=[otr])
                    P.dma("pool", cvTd[s, cc, :, blk], o[:], otr, r=[otr], pw=[dtr[("cvTd", s)]])
                    yield

        def stage_R(l, s):
            with ExitStack() as st:
                lgr = st.enter_context(sbt(nc, "rt_lg", [128, 8], F32)); t_lg = Tr()
                iof = st.enter_context(sbt(nc, "rt_iof", [128, 128], F32))
                ioq = st.enter_context(sbt(nc, "rt_ioq", [128, 128], F32))
                iop = st.enter_context(sbt(nc, "rt_iop", [128, 1], F32))
                t_io = Tr()
                tA = st.enter_context(sbt(nc, "rt_tA", [128, 128], F32))
                tB = st.enter_context(sbt(nc, "rt_tB", [128, 128], F32))
                tC = st.enter_context(sbt(nc, "rt_tC", [128, 128], F32)); t_tmp = Tr()
                DT = st.enter_context(sbt(nc, "rt_DT", [128, RH, 128], F32))
                CF = st.enter_context(sbt(nc, "rt_CF", [128, RH, 128], F32))
                CB = st.enter_context(sbt(nc, "rt_CB", [128, RH, 128], F32))
                pc = st.enter_context(sbt(nc, "rt_pc", [128, 4, RH], F32))
                t_tb = Tr()
                P.dma("sp", lgr[:], ret_decay_logits[l].rearrange("a h -> (a h)").partition_broadcast(128), t_lg, w=[t_lg])
                P.op("act", lambda e: e.activation(out=lgr[:], in_=lgr[:], func=AF.Exp, scale=-1.0), r=[t_lg], w=[t_lg])
                P.op("dve", lambda e: e.tensor_scalar(out=lgr[:], in0=lgr[:], scalar1=1.0, scalar2=None, op0=ALU.add), r=[t_lg], w=[t_lg])
                P.op("act", lambda e: e.activation(out=lgr[:], in_=lgr[:], func=AF.Ln), r=[t_lg], w=[t_lg])
                P.op("dve", lambda e: e.tensor_scalar(out=lgr[:], in0=lgr[:], scalar1=-1.0, scalar2=None, op0=ALU.mult), r=[t_lg], w=[t_lg])
                P.op("pool", lambda e: e.iota(iof[:], [[1, 128]], base=0, channel_multiplier=-1, allow_small_or_imprecise_dtypes=True), w=[t_io])
                P.op("pool", lambda e: e.iota(ioq[:], [[1, 128]], base=0, channel_multiplier=0, allow_small_or_imprecise_dtypes=True), pw=[t_io])
                P.op("dve", lambda e: e.tensor_scalar(out=iop[:], in0=iof[:, 0:1], scalar1=-1.0, scalar2=None, op0=ALU.mult), r=[t_io], pw=[t_io])
                for h in range(RH):
                    lf = lgr[:, h:h + 1]; lb = lgr[:, RH + h:RH + h + 1]
                    P.op("dve", lambda e: e.tensor_scalar(out=tA[:], in0=iof[:], scalar1=0.0, scalar2=None, op0=ALU.max), r=[t_io], w=[t_tmp])
                    P.op("act", lambda e, lf=lf: e.activation(out=tA[:], in_=tA[:], func=AF.Exp, scale=lf), r=[t_tmp, t_lg], w=[t_tmp])
                    P.op("dve", lambda e: e.tensor_scalar(out=tB[:], in0=iof[:], scalar1=0.0, scalar2=None, op0=ALU.is_ge), r=[t_io], pw=[t_tmp])
                    P.op("dve", lambda e: e.tensor_tensor(out=tA[:], in0=tA[:], in1=tB[:], op=ALU.mult), r=[t_tmp], w=[t_tmp])
                    P.op("dve", lambda e: e.tensor_scalar(out=tC[:], in0=iof[:], scalar1=-1.0, scalar2=0.0, op0=ALU.mult, op1=ALU.max), r=[t_io], pw=[t_tmp])
                    P.op("act", lambda e, lb=lb: e.activation(out=tC[:], in_=tC[:], func=AF.Exp, scale=lb), r=[t_tmp], w=[t_tmp])
                    P.op("dve", lambda e: e.tensor_scalar(out=tB[:], in0=iof[:], scalar1=0.0, scalar2=None, op0=ALU.is_lt), r=[t_io], w=[t_tmp])
                    P.op("dve", lambda e: e.tensor_tensor(out=tC[:], in0=tC[:], in1=tB[:], op=ALU.mult), r=[t_tmp], w=[t_tmp])
                    P.op("dve", lambda e, h=h: e.tensor_tensor(out=DT[:, h, :], in0=tA[:], in1=tC[:], op=ALU.add), r=[t_tmp], pw=[t_tb])
                    P.op("dve", lambda e: e.tensor_scalar(out=tA[:], in0=ioq[:], scalar1=1.0, scalar2=None, op0=ALU.add), r=[t_io], w=[t_tmp])
                    P.op("act", lambda e, h=h, lf=lf: e.activation(out=CF[:, h, :], in_=tA[:], func=AF.Exp, scale=lf), r=[t_tmp], pw=[t_tb])
                    P.op("dve", lambda e: e.tensor_scalar(out=tA[:], in0=ioq[:], scalar1=-1.0, scalar2=128.0, op0=ALU.mult, op1=ALU.add), r=[t_io], w=[t_tmp])
                    P.op("act", lambda e, h=h, lb=lb: e.activation(out=CB[:, h, :], in_=tA[:], func=AF.Exp, scale=lb), r=[t_tmp], pw=[t_tb])
                    P.op("dve", lambda e: e.tensor_scalar(out=tA[:, 0:1], in0=iop[:], scalar1=-1.0, scalar2=127.0, op0=ALU.mult, op1=ALU.add), r=[t_io], w=[t_tmp])
                    P.op("act", lambda e, h=h, lf=lf: e.activation(out=pc[:, 0, h:h + 1], in_=tA[:, 0:1], func=AF.Exp, scale=lf), r=[t_tmp], pw=[t_tb])
                    P.op("act", lambda e, h=h, lb=lb: e.activation(out=pc[:, 1, h:h + 1], in_=iop[:], func=AF.Exp, scale=lb), r=[t_io], pw=[t_tb])
                    P.op("act", lambda e, h=h, lf=lf: e.activation(out=pc[:, 2, h:h + 1], in_=lf, func=AF.Exp, scale=128.0), r=[t_lg], pw=[t_tb])
                    P.op("act", lambda e, h=h, lb=lb: e.activation(out=pc[:, 3, h:h + 1], in_=lb, func=AF.Exp, scale=128.0), r=[t_lg], pw=[t_tb])
                gnb = st.enter_context(sbt(nc, "rt_gn", [128, D], F32)); t_gn = Tr()
                P.dma("sp", gnb[:], ret_gn_g[l].partition_broadcast(128), t_gn, w=[t_gn])

                Rall = st.enter_context(sbt(nc, "rt_Rall", [128, T, 1024], BF16)); t_Rall = Tr()
                Rb = st.enter_context(sbt(nc, "rt_Rb", [128, RH, 256], F32)); t_Rb = Tr()
                Sf = st.enter_context(sbt(nc, "rt_Sf", [128, RH, 256], F32))
                Sfb = st.enter_context(sbt(nc, "rt_Sfb", [128, RH, 256], BF16)); t_Sf = Tr()
                P.op("pool", lambda e: e.memset(Rb[:], 0.0), w=[t_Rb])
                P.op("pool", lambda e: e.memset(Rall[:, T - 1, :], 0.0), w=[t_Rall])
                P.op("pool", lambda e: e.memset(Sf[:], 0.0), w=[t_Sf])
                P.op("pool", lambda e: e.memset(Sfb[:], 0.0), pw=[t_Sf])
                kin = Ring(nc, st, "rt_k", [128, 512], BF16, 3)
                vin = Ring(nc, st, "rt_v", [128, 1024], BF16, 3)
                qin = Ring(nc, st, "rt_q", [128, 512], BF16, 2)
                gin = Ring(nc, st, "rt_g", [128, 1024], F32, 2)
                kc_ = Ring(nc, st, "rt_kc", [128, RH, 128], BF16, 2)
                for c in range(T - 1, 0, -1):
                    tok = slice(c * 128, (c + 1) * 128)
                    k, ktr = kin.next(); v, vtr = vin.next()
                    P.dma("sp", k[:], rkd[s, tok, :], ktr, r=[dtr[("rkd", s)]], w=[ktr])
                    P.dma("sp", v[:], rvd[s, tok, :], vtr, r=[dtr[("rvd", s)]], w=[vtr])
                    kb, kbtr = kc_.next()
                    for h in range(RH):
                        if h % 2:
                            P.op("dve", lambda e, kb=kb, k=k, h=h: e.tensor_scalar(out=kb[:, h, :], in0=k[:, h * 128:(h + 1) * 128], scalar1=pc[:, 1, h:h + 1], scalar2=None, op0=ALU.mult),
                                 r=[ktr, t_tb], pw=[kbtr])
                        else:
                            P.op("act", lambda e, kb=kb, k=k, h=h: e.mul(out=kb[:, h, :], in_=k[:, h * 128:(h + 1) * 128], mul=pc[:, 1, h:h + 1]),
                                 r=[ktr, t_tb], w=[kbtr] if h == 0 else (), pw=[kbtr] if h else ())
                    for h in range(RH):
                        bk = h // 2
                        P.mm(ps[bk][:, (h % 2) * 256:(h % 2) * 256 + 256], kb[:, h, :], v[:, h * 256:(h + 1) * 256], True, True, [kbtr, vtr], pst[bk], first=(h % 2 == 0))
                    for h in range(RH):
                        bk = h // 2
                        P.op("dve", lambda e, h=h, bk=bk: e.scalar_tensor_tensor(out=Rb[:, h, :], in0=Rb[:, h, :], scalar=pc[:, 3, h:h + 1], in1=ps[bk][:, (h % 2) * 256:(h % 2) * 256 + 256], op0=ALU.mult, op1=ALU.add),
                             r=[pst[bk], t_tb], w=[t_Rb])
                    P.op("act", lambda e, c=c: e.copy(out=Rall[:, c - 1, :], in_=Rb[:].rearrange("p h e -> p (h e)")), r=[t_Rb], pw=[t_Rall])
                qT3 = Ring(nc, st, "rt_qT", [128, 3, RH, 128], BF16, 2)
                kTr = Ring(nc, st, "rt_kT", [128, RH, 128], BF16, 2)
                stm = Ring(nc, st, "rt_stm", [128, RH, 128], BF16, 2)
                bnr = Ring(nc, st, "rt_bn", [128, RH, 8], F32, 2)
                onr = Ring(nc, st, "rt_on", [128, D], F32, 2)
                gtd_ = Ring(nc, st, "rt_gt", [128, D], BF16, 2)
                gTr = Ring(nc, st, "rt_gT", [128, 8, 128], BF16, 2)
                for c in range(T):
                    tok = slice(c * 128, (c + 1) * 128)
                    k, ktr = kin.next(); v, vtr = vin.next(); q, qtr = qin.next(); g_, gtr = gin.next()
                    P.dma("sp", q[:], rqd[s, tok, :], qtr, r=[dtr[("rqd", s)]], w=[qtr])
                    P.dma("sp", k[:], rkd[s, tok, :], ktr, r=[dtr[("rkd", s)]], w=[ktr])
                    P.dma("sp", v[:], rvd[s, tok, :], vtr, r=[dtr[("rvd", s)]], w=[vtr])
                    P.dma("sp", g_[:], sgd[s, tok, :], gtr, r=[dtr[("sgd", s)]], w=[gtr])
                    pv = ps[0][:].bitcast(BF16)
                    for h in range(RH):
                        P.tp(pv[:, h * 128:(h + 1) * 128], q[:, h * 128:(h + 1) * 128], ident[:], [qtr, ctr], pst[0], h == 0)
                    for h in range(RH):
                        P.tp(pv[:, 512 + h * 128:512 + (h + 1) * 128], k[:, h * 128:(h + 1) * 128], ident[:], [ktr], pst[0], False)
                    qT, qTtr = qT3.next(); kT, kTtr = kTr.next()
                    pq = pv[:, 0:512].rearrange("p (h c) -> p h c", h=RH)
                    P.op("act", lambda e, qT=qT, pq=pq: e.copy(out=qT[:, 0], in_=pq), r=[pst[0]], w=[qTtr])
                    P.op("dve", lambda e, qT=qT, pq=pq: e.tensor_tensor(out=qT[:, 1], in0=pq, in1=CF[:], op=ALU.mult), r=[pst[0], t_tb], pw=[qTtr])
                    P.op("dve", lambda e, qT=qT, pq=pq: e.tensor_tensor(out=qT[:, 2], in0=pq, in1=CB[:], op=ALU.mult), r=[pst[0], t_tb], pw=[qTtr])
                    P.op("act", lambda e, kT=kT, pv=pv: e.copy(out=kT[:], in_=pv[:, 512:1024].rearrange("p (h c) -> p h c", h=RH)), r=[pst[0]], w=[kTtr])
                    kf, kftr = kc_.next()
                    for h in range(RH):
                        P.op("act", lambda e, kf=kf, k=k, h=h: e.mul(out=kf[:, h, :], in_=k[:, h * 128:(h + 1) * 128], mul=pc[:, 0, h:h + 1]),
                             r=[ktr, t_tb], w=[kftr] if h == 0 else (), pw=[kftr] if h else ())
                    for h in range(RH):
                        P.mm(ps[1][:, h * 128:(h + 1) * 128], kT[:, h, :], qT[:, 0, h, :], True, True, [kTtr, qTtr], pst[1], first=(h == 0))
                    sm_, smtr = stm.next()
                    P.op("dve", lambda e, sm_=sm_: e.tensor_tensor(out=sm_[:], in0=ps[1][:].rearrange("p (h c) -> p h c", h=RH), in1=DT[:], op=ALU.mult), r=[pst[1], t_tb], w=[smtr])
                    for h in range(RH):
                        bk = 2 + h // 2
                        oc = slice((h % 2) * 256, (h % 2) * 256 + 256)
                        P.mm(ps[bk][:, oc], sm_[:, h, :], v[:, h * 256:(h + 1) * 256], True, False, [smtr, vtr], pst[bk], first=(h % 2 == 0))
                        P.mm(ps[bk][:, oc], qT[:, 1, h, :], Sfb[:, h, :], False, False, [qTtr, t_Sf], pst[bk])
                        P.mm(ps[bk][:, oc], qT[:, 2, h, :], Rall[:, c, h * 256:(h + 1) * 256], False, True, [qTtr, t_Rall], pst[bk])
                    for h in range(RH):
                        bk = 4 + h // 2
                        oc = slice((h % 2) * 256, (h % 2) * 256 + 256)
                        P.mm(ps[bk][:, oc], kf[:, h, :], v[:, h * 256:(h + 1) * 256], True, True, [kftr, vtr], pst[bk], first=(h % 2 == 0))
                    for h in range(RH):
                        bk = 4 + h // 2
                        oc = slice((h % 2) * 256, (h % 2) * 256 + 256)
                        P.op("dve", lambda e, h=h, bk=bk, oc=oc: e.scalar_tensor_tensor(out=Sf[:, h, :], in0=Sf[:, h, :], scalar=pc[:, 2, h:h + 1], in1=ps[bk][:, oc], op0=ALU.mult, op1=ALU.add),
                             r=[pst[bk], t_tb], w=[t_Sf])
                    P.op("act", lambda e: e.copy(out=Sfb[:], in_=Sf[:]), r=[t_Sf], w=[t_Sf])
                    if debug and c == 1:
                        dO = st.enter_context(sbt(nc, "dbgO", [128, 1024], F32)); t_dO = Tr()
                        P.op("dve", lambda e: e.tensor_copy(out=dO[:, 0:512], in_=ps[2][:]), r=[pst[2]], w=[t_dO])
                        P.op("dve", lambda e: e.tensor_copy(out=dO[:, 512:1024], in_=ps[3][:]), r=[pst[3]], pw=[t_dO])
                        P.dma("pool", dbg_O[:, :], dO[:], t_dO, r=[t_dO])
                        P.dma("pool", dbg_qT[:, :], qT[:].rearrange("p a h c -> p (a h c)"), qTtr, r=[qTtr])
                        P.dma("pool", dbg_kT[:, :], kT[:].rearrange("p h c -> p (h c)"), kTtr, r=[kTtr])
                        P.dma("pool", dbg_sm[:, :], sm_[:].rearrange("p h c -> p (h c)"), smtr, r=[smtr])
                        P.dma("pool", dbg_kf[:, :], kf[:].rearrange("p h c -> p (h c)"), kftr, r=[kftr])
                        P.dma("pool", dbg_DT[:, :], DT[:].rearrange("p h c -> p (h c)"), t_dO, r=[t_tb])
                        P.dma("pool", dbg_CF[:, :], CF[:].rearrange("p h c -> p (h c)"), t_dO, r=[t_tb])
                        P.dma("pool", dbg_CB[:, :], CB[:].rearrange("p h c -> p (h c)"), t_dO, r=[t_tb])
                        P.dma("pool", dbg_pc[:, :], pc[:].rearrange("p a h -> p (a h)"), t_dO, r=[t_tb])
                        P.dma("pool", dbg_Sf[:, :], Sf[:].rearrange("p h e -> p (h e)"), t_dO, r=[t_Sf])
                        P.dma("pool", dbg_R[:, :], Rall[:, c, :], t_dO, r=[t_Rall])
                    bn, bntr = bnr.next()
                    on, ontr = onr.next()
                    for h in range(RH):
                        bk = 2 + h // 2
                        oc = slice((h % 2) * 256, (h % 2) * 256 + 256)
                        P.op("dve", lambda e, bn=bn, h=h, bk=bk, oc=oc: e.bn_stats(out=bn[:, h, 0:6], in_=ps[bk][:, oc]), r=[pst[bk]], w=[bntr] if h == 0 else (), pw=[bntr] if h else ())
                    for h in range(RH):
                        P.op("dve", lambda e, bn=bn, h=h: e.bn_aggr(out=bn[:, h, 6:8], in_=bn[:, h, 0:6]), r=[bntr], pw=[bntr])
                    P.op("act", lambda e, bn=bn: e.activation(out=bn[:, :, 0], in_=bn[:, :, 7], func=AF.Sqrt, bias=epsc[:, 0:1], scale=1.0), r=[bntr, ctr], pw=[bntr])
                    P.op("dve", lambda e, bn=bn: e.reciprocal(out=bn[:, :, 1], in_=bn[:, :, 0]), r=[bntr], pw=[bntr])
                    for h in range(RH):
                        bk = 2 + h // 2
                        oc = slice((h % 2) * 256, (h % 2) * 256 + 256)
                        P.op("dve", lambda e, on=on, bn=bn, h=h, bk=bk, oc=oc: e.tensor_scalar(out=on[:, h * 256:(h + 1) * 256], in0=ps[bk][:, oc], scalar1=bn[:, h, 6:7], scalar2=bn[:, h, 1:2], op0=ALU.subtract, op1=ALU.mult),
                             r=[pst[bk], bntr], w=[ontr] if h == 0 else (), pw=[ontr] if h else ())
                    P.op("pool", lambda e, on=on: e.tensor_tensor(out=on[:], in0=on[:], in1=gnb[:], op=ALU.mult), r=[ontr, t_gn], w=[ontr])
                    gt, gttr = gtd_.next()
                    P.op("dve", lambda e, gt=gt, on=on, g_=g_: e.tensor_tensor(out=gt[:], in0=on[:], in1=g_[:], op=ALU.mult), r=[ontr, gtr], w=[gttr])
                    pv6 = ps[6][:].bitcast(BF16)
                    for kc in range(8):
                        P.tp(pv6[:, kc * 128:(kc + 1) * 128], gt[:, kc * 128:(kc + 1) * 128], ident[:], [gttr, ctr], pst[6], kc == 0)
                    gT, gTtr = gTr.next()
                    P.op("act", lambda e, gT=gT, pv6=pv6: e.copy(out=gT[:], in_=pv6.rearrange("p (k c) -> p k c", k=8)), r=[pst[6]], w=[gTtr])
                    P.dma("pool", rtTd[s, :, :, tok].rearrange("k p c -> p k c"), gT[:], gTtr, r=[gTtr], pw=[dtr[("rtTd", s)]])

                P.barrier()
        def postnorm_residual(bA, bB, xt, xtr_, gb, t_g, o, otr, sqr, stt):
            sq_, sqtr = sqr.next()
            sm, smtr = stt.next()
            P.op("act", lambda e: e.activation(out=sq_[:, 0:512], in_=ps[bA][:], func=AF.Square), r=[pst[bA]], w=[sqtr])
            P.op("act", lambda e: e.activation(out=sq_[:, 512:1024], in_=ps[bB][:], func=AF.Square), r=[pst[bB]], pw=[sqtr])
            P.op("dve", lambda e: e.reduce_sum(out=sm[:, 2:3], in_=sq_[:], axis=mybir.AxisListType.X), r=[sqtr], w=[smtr])
            P.op("act", lambda e: e.activation(out=sm[:, 3:4], in_=sm[:, 2:3], func=AF.Sqrt, bias=epsc[:, 0:1], scale=1.0 / D), r=[smtr, ctr], pw=[smtr])
            P.op("dve", lambda e: e.reciprocal(out=sm[:, 3:4], in_=sm[:, 3:4]), r=[smtr], pw=[smtr])
            P.op("dve", lambda e: e.scalar_tensor_tensor(out=o[:, 0:512], in0=ps[bA][:], scalar=sm[:, 3:4], in1=gb[:, 0:512], op0=ALU.mult, op1=ALU.mult), r=[pst[bA], smtr, t_g], w=[otr])
            P.op("dve", lambda e: e.scalar_tensor_tensor(out=o[:, 512:1024], in0=ps[bB][:], scalar=sm[:, 3:4], in1=gb[:, 512:1024], op0=ALU.mult, op1=ALU.mult), r=[pst[bB], smtr], pw=[otr])
            P.op("pool", lambda e: e.tensor_tensor(out=o[:], in0=o[:], in1=xt[:], op=ALU.add), r=[xtr_], w=[otr])

        def stage_M(l, s, xsrc, xtr):
            with ExitStack() as st:
                wmo = st.enter_context(sbt(nc, "m_wmo", [128, 4, D], BF16))
                wpw = st.enter_context(sbt(nc, "m_wpw", [128, 4, D], BF16))
                wro = st.enter_context(sbt(nc, "m_wro", [128, 8, D], BF16))
                wou = st.enter_context(sbt(nc, "m_wou", [128, 8, D], BF16))
                gb = st.enter_context(sbt(nc, "m_gb", [128, D], F32))
                t_w = Tr(); t_g = Tr()
                P.dma("sp", wmo[:], wb["mo"][l].rearrange("(kc p) n -> p kc n", p=128), t_w, r=[wb_tr[("mo", l)]], w=[t_w])
                P.dma("sp", wpw[:], wb["pw"][l].rearrange("(kc p) n -> p kc n", p=128), t_w, r=[wb_tr[("pw", l)]], pw=[t_w])
                P.dma("sp", wro[:], wb["ro"][l].rearrange("(kc p) n -> p kc n", p=128), t_w, r=[wb_tr[("ro", l)]], pw=[t_w])
                P.dma("sp", wou[:], wb["wo"][l].rearrange("(kc p) n -> p kc n", p=128), t_w, r=[wb_tr[("wo", l)]], pw=[t_w])
                P.dma("sp", gb[:], ln_mix_post[l].partition_broadcast(128), t_g, w=[t_g])
                oTr = Ring(nc, st, "m_oT", [128, 4, 512], BF16, 2)
                cTr = Ring(nc, st, "m_cT", [128, 4, 512], BF16, 2)
                rTr = Ring(nc, st, "m_rT", [128, 8, 512], BF16, 2)
                gtr_ = Ring(nc, st, "m_gt", [128, 3, 512], F32, 3)
                mt = Ring(nc, st, "m_t", [128, 2, 512], F32, 2)
                mgr = Ring(nc, st, "m_mg", [128, 8, 512], BF16, 2)
                xr = Ring(nc, st, "m_x", [128, D], F32, 2)
                outr = Ring(nc, st, "m_o", [128, D], F32, 2)
                sqr = Ring(nc, st, "m_sq", [128, D], F32, 2)
                stt = Ring(nc, st, "m_st", [128, 8], F32, 3)
                for tb in range(NB):
                    blk = slice(tb * 512, (tb + 1) * 512)
                    o_, otr_ = oTr.next(); c_, ctr_ = cTr.next(); r_, rtr_ = rTr.next()
                    P.dma("sp", o_[:], oTd[s, :, :, blk].rearrange("k p c -> p k c"), otr_, r=[dtr[("oTd", s)]], w=[otr_])
                    P.dma("sp", c_[:], cvTd[s, :, :, blk].rearrange("k p c -> p k c"), ctr_, r=[dtr[("cvTd", s)]], w=[ctr_])
                    P.dma("sp", r_[:], rtTd[s, :, :, blk].rearrange("k p c -> p k c"), rtr_, r=[dtr[("rtTd", s)]], w=[rtr_])
                    mg, mgtr = mgr.next()
                    for rc in range(8):
                        cs = slice(rc * 128, (rc + 1) * 128)
                        gt, gttr = gtr_.next()
                        for b in range(3):
                            P.dma("sp", gt[:, b, :], gtd[s, b * 8 + rc, :, blk], gttr, r=[dtr[("gtd", s)]], w=[gttr] if b == 0 else (), pw=[gttr] if b else ())
                        b0 = (rc % 2) * 3
                        for kc in range(4):
                            P.mm(ps[b0][:], wmo[:, kc, cs], o_[:, kc, :], kc == 0, kc == 3, [t_w, otr_], pst[b0])
                        for kc in range(4):
                            P.mm(ps[b0 + 1][:], wpw[:, kc, cs], c_[:, kc, :], kc == 0, kc == 3, [t_w, ctr_], pst[b0 + 1])
                        for kc in range(8):
                            P.mm(ps[b0 + 2][:], wro[:, kc, cs], r_[:, kc, :], kc == 0, kc == 7, [t_w, rtr_], pst[b0 + 2])
                        t_, ttr = mt.next()
                        P.op("dve", lambda e, t_=t_, gt=gt, b0=b0: e.tensor_tensor(out=t_[:, 0, :], in0=ps[b0][:], in1=gt[:, 0, :], op=ALU.mult), r=[pst[b0], gttr], w=[ttr])
                        P.op("dve", lambda e, t_=t_, gt=gt, b0=b0: e.tensor_tensor(out=t_[:, 1, :], in0=ps[b0 + 1][:], in1=gt[:, 1, :], op=ALU.mult), r=[pst[b0 + 1], gttr], pw=[ttr])
                        P.op("pool", lambda e, t_=t_: e.tensor_tensor(out=t_[:, 0, :], in0=t_[:, 0, :], in1=t_[:, 1, :], op=ALU.add), r=[ttr], w=[ttr])
                        P.op("dve", lambda e, t_=t_, gt=gt, b0=b0: e.tensor_tensor(out=t_[:, 1, :], in0=ps[b0 + 2][:], in1=gt[:, 2, :], op=ALU.mult), r=[pst[b0 + 2], gttr], w=[ttr])
                        P.op("pool", lambda e, t_=t_, mg=mg, rc=rc: e.tensor_tensor(out=mg[:, rc, :], in0=t_[:, 0, :], in1=t_[:, 1, :], op=ALU.add), r=[ttr], w=[mgtr] if rc == 0 else (), pw=[mgtr] if rc else ())
                    for tt in range(4):
                        t = tb * 4 + tt
                        tok = slice(t * 128, (t + 1) * 128)
                        xt, xtr_ = xr.next()
                        P.dma("sp", xt[:], xsrc[s, tok, :], xtr_, r=[xtr[s]], w=[xtr_])
                        for nb in range(2):
                            for kc in range(8):
                                P.mm(ps[6 + nb][:], mg[:, kc, tt * 128:(tt + 1) * 128], wou[:, kc, nb * 512:(nb + 1) * 512], kc == 0, kc == 7, [mgtr, t_w], pst[6 + nb])
                        o, otr = outr.next()
                        postnorm_residual(6, 7, xt, xtr_, gb, t_g, o, otr, sqr, stt)
                        P.dma("pool", x1d[s, tok, :], o[:], otr, r=[otr], pw=[dtr[("x1d", s)]])

                P.barrier()
        def stage_F(l, s, ydst, ykey):
            with ExitStack() as st:
                wg = st.enter_context(sbt(nc, "f_wg", [128, 8, FH], BF16))
                wu = st.enter_context(sbt(nc, "f_wu", [128, 8, FH], BF16))
                t_w = Tr(); t_g = Tr()
                gpre = st.enter_context(sbt(nc, "f_gpre", [128, D], F32))
                gpost = st.enter_context(sbt(nc, "f_gpost", [128, D], F32))
                P.dma("sp", wg[:], wb["fg"][l].rearrange("(kc p) n -> p kc n", p=128), t_w, r=[wb_tr[("fg", l)]], w=[t_w])
                P.dma("sp", wu[:], wb["fu"][l].rearrange("(kc p) n -> p kc n", p=128), t_w, r=[wb_tr[("fu", l)]], pw=[t_w])
                P.dma("sp", gpre[:], ln_ffn_pre[l].partition_broadcast(128), t_g, w=[t_g])
                P.dma("sp", gpost[:], ln_ffn_post[l].partition_broadcast(128), t_g, pw=[t_g])
                wdr = Ring(nc, st, "f_wd", [128, D], BF16, 8)
                xr = Ring(nc, st, "f_x", [128, D], F32, 5)
                hb = Ring(nc, st, "f_hb", [128, D], BF16, 2)
                sqr = Ring(nc, st, "f_sq", [128, D], F32, 2)
                stt = Ring(nc, st, "f_st", [128, 8], F32, 4)
                h2T = Ring(nc, st, "f_h2T", [128, 8, 512], BF16, 2)
                hid = Ring(nc, st, "f_hid", [128, 22, 512], BF16, 1)
                sgr = Ring(nc, st, "f_sg", [128, 512], F32, 2)
                outr = Ring(nc, st, "f_o", [128, D], F32, 2)
                wdv = wb["fd"][l]
                for tb in range(NB):
                    hT_, hTtr = h2T.next()
                    xts = []
                    for tt in range(4):
                        t = tb * 4 + tt
                        tok = slice(t * 128, (t + 1) * 128)
                        xt, xtr_ = xr.next()
                        xts.append((xt, xtr_))
                        P.dma("sp", xt[:], x1d[s, tok, :], xtr_, r=[dtr[("x1d", s)]], w=[xtr_])
                        sq_, sqtr = sqr.next()
                        sm, smtr = stt.next()
                        ssq4(xt, sq_, sm, xtr_, sqtr, smtr)
                        rstd_from_ssq(sm[:, 0:1], sm[:, 1:2], D, [smtr])
                        h, htr = hb.next()
                        P.op("dve", lambda e, xt=xt, h=h, sm=sm: e.scalar_tensor_tensor(out=h[:], in0=xt[:], scalar=sm[:, 1:2], in1=gpre[:], op0=ALU.mult, op1=ALU.mult), r=[xtr_, smtr, t_g], w=[htr])
                        pv = ps[4 + (tt % 2)][:].bitcast(BF16)
                        bk = 4 + (tt % 2)
                        for kc in range(8):
                            P.tp(pv[:, kc * 128:(kc + 1) * 128], h[:, kc * 128:(kc + 1) * 128], ident[:], [htr, ctr], pst[bk], kc == 0)
                        P.op("act", lambda e, hT_=hT_, pv=pv, tt=tt: e.copy(out=hT_[:, :, tt * 128:(tt + 1) * 128], in_=pv.rearrange("p (k c) -> p k c", k=8)), r=[pst[bk]],
                             w=[hTtr] if tt == 0 else (), pw=[hTtr] if tt else ())
                    hd, hdtr = hid.next()
                    for hc in range(22):
                        bg, bu = (4, 5) if hc % 2 == 0 else (6, 7)
                        cs = slice(hc * 128, (hc + 1) * 128)
                        for kc in range(8):
                            P.mm(ps[bg][:], wg[:, kc, cs], hT_[:, kc, :], kc == 0, kc == 7, [t_w, hTtr], pst[bg])
                        for kc in range(8):
                            P.mm(ps[bu][:], wu[:, kc, cs], hT_[:, kc, :], kc == 0, kc == 7, [t_w, hTtr], pst[bu])
                        sg, sgtr = sgr.next()
                        P.op("act", lambda e, sg=sg, bg=bg: e.activation(out=sg[:], in_=ps[bg][:], func=AF.Silu), r=[pst[bg]], w=[sgtr])
                        P.op("dve", lambda e, hd=hd, sg=sg, bu=bu, hc=hc: e.tensor_tensor(out=hd[:, hc, :], in0=ps[bu][:], in1=sg[:], op=ALU.mult), r=[pst[bu], sgtr],
                             w=[hdtr] if hc == 0 else (), pw=[hdtr] if hc else ())
                    for half in range(2):
                        for hc in range(22):
                            wd, wdtr = wdr.next()
                            P.dma("sp", wd[:], wdv[hc * 128:(hc + 1) * 128, :], wdtr, r=[wb_tr[("fd", l)]], w=[wdtr])
                            for t2 in range(2):
                                tt = half * 2 + t2
                                for nb in range(2):
                                    bk = t2 * 2 + nb
                                    P.mm(ps[bk][:], hd[:, hc, tt * 128:(tt + 1) * 128], wd[:, nb * 512:(nb + 1) * 512], hc == 0, hc == 21, [hdtr, wdtr], pst[bk])
                        for t2 in range(2):
                            tt = half * 2 + t2
                            t = tb * 4 + tt
                            tok = slice(t * 128, (t + 1) * 128)
                            xt, xtr_ = xts[tt]
                            o, otr = outr.next()
                            postnorm_residual(t2 * 2, t2 * 2 + 1, xt, xtr_, gpost, t_g, o, otr, sqr, stt)
                            P.dma("pool", ydst[s, tok, :], o[:], otr, r=[otr], pw=[dtr[(ykey, s)]])

                P.barrier()
        for l in range(nlayers):
            if "W" in stages:
                stage_W(l)
        for s in range(NS):
            if "T" in stages:
                stage_T(s)
        xin_tr = [Tr() for _ in range(NS)]
        for l in range(nlayers):
            last = (l == nlayers - 1)
            for s in range(NS):
                if l == 0:
                    xsrc, xtr = x_in, xin_tr
                else:
                    xsrc, xtr = xLd, [dtr[("xLd", s_)] for s_ in range(NS)]
                if "N" in stages:
                    stage_NP(l, s, xsrc, xtr)
                if "A" in stages:
                    stage_A(l, s, with_conv=("C" in stages))
                if "R" in stages:
                    stage_R(l, s)
                if "M" in stages:
                    stage_M(l, s, xsrc, xtr)
                if "F" in stages:
                    stage_F(l, s, y_out if last else xLd, "y" if last else "xLd")
        P.wait_all("sp", [dtr[("y", s)] for s in range(NS)])
        for en in ("pe", "act", "dve", "pool"):
            E = P.E[en]
            if E.cnt:
                if P.E["sp"].waited.get(E.key, 0) < E.cnt:
                    P.E["sp"].e.wait_ge(E.sem, E.cnt)
    return nc


def rope_consts():
    def inv(dim):
        return (np.float32(10000.0) ** (-(np.arange(0, dim, 2, dtype=np.float32)) / np.float32(dim))).astype(np.float32)
    im, ir = inv(32), inv(128)
    c = np.zeros((2, 160), np.float32)
    c[0] = np.concatenate([im, im, ir, ir])
    c[1] = np.concatenate([np.zeros(16), np.full(16, np.pi / 2), np.zeros(64), np.full(64, np.pi / 2)]).astype(np.float32)
    return c


WEIGHT_NAMES = ["ln_mix_pre", "ln_mix_post", "ln_ffn_pre", "ln_ffn_post", "w_in", "mla_q_norm", "mla_w_uq", "mla_kv_norm",
                "mla_w_ukv", "mla_w_o", "conv_w_dw", "conv_b_dw", "conv_ln_g", "conv_ln_b", "conv_w_pw", "ret_decay_logits",
                "ret_gn_g", "ret_w_o", "w_out", "ffn_w_gate", "ffn_w_up", "ffn_w_down"]


def kernel(**inputs):
    x = np.ascontiguousarray(np.asarray(inputs["x"], dtype=np.float32))
    pos = np.ascontiguousarray(np.asarray(inputs["positions"], dtype=np.int32))
    B, S, _ = x.shape
    ncores = 8
    NS = B // ncores
    nc = build(S, NS)
    shared = {k: np.ascontiguousarray(np.asarray(inputs[k], dtype=np.float32)) for k in WEIGHT_NAMES}
    shared["rope_consts"] = rope_consts()
    in_maps = []
    for c in range(ncores):
        m = dict(shared)
        m["x"] = x[c * NS:(c + 1) * NS]
        m["positions"] = pos[c * NS:(c + 1) * NS]
        in_maps.append(m)
    res = run_bass_kernel_spmd(nc, in_maps, core_ids=list(range(ncores)))
    return np.concatenate([r["y"] for r in res.results], axis=0).astype(np.float32)
```

```python
import numpy as np
import concourse.bass as bass
import concourse.mybir as mybir
from concourse.bass_utils import run_bass_kernel_spmd
from contextlib import ExitStack

F32 = mybir.dt.float32
BF16 = mybir.dt.bfloat16
I32 = mybir.dt.int32
AF = mybir.ActivationFunctionType
ALU = mybir.AluOpType

D = 1024
L = 2
NH = 8
RH = 4
FH = 2816
INC = 7584
EPS = 1e-6
SAME_ENGINE_SYNC = False
import os
SUB = os.environ.get("SUB", "lcrg")
CUT = float(os.environ.get("CUT", "9"))
TWO_PI = float(2 * np.pi)
PI = float(np.pi)

O_CQ, O_CKV, O_KPE, O_CONV, O_RQ, O_RK, O_RV, O_RG, O_GATE = 0, 256, 384, 416, 1440, 1952, 2464, 3488, 4512


class Tr:
    __slots__ = ("w", "r", "sem", "cnt", "name", "excl")

    def __init__(self, name="", excl=False):
        self.excl = excl
        self.w = {}
        self.r = {}
        self.sem = None
        self.cnt = 0
        self.name = name


class Eng:
    def __init__(self, e, sem, key):
        self.e = e
        self.sem = sem
        self.key = key
        self.cnt = 0
        self.waited = {}


class Prog:
    def __init__(self, nc, es):
        self.nc = nc
        self.es = es
        self.sems = {}
        self.nsem = 0
        self.E = {}
        self.pool = []
        self.live = []
        self.uid = 0
        for name, e in (("pe", nc.tensor), ("act", nc.scalar), ("dve", nc.vector), ("pool", nc.gpsimd), ("sp", nc.sync)):
            s, k = self.newsem("e_" + name)
            self.E[name] = Eng(e, s, k)

    def newsem(self, name):
        s = self.es.enter_context(self.nc.semaphore(name + "_%d" % self.nsem))
        k = self.nsem
        self.nsem += 1
        self.sems[k] = s
        return s, k

    def _waits(self, E, r, w, pw):
        need = {}
        for t in r:
            for k, v in t.w.items():
                if need.get(k, 0) < v:
                    need[k] = v
            if t.excl:
                for k, v in t.r.items():
                    if need.get(k, 0) < v:
                        need[k] = v
        for t in w:
            for d in (t.w, t.r):
                for k, v in d.items():
                    if need.get(k, 0) < v:
                        need[k] = v
        for t in pw:
            for d in (t.w, t.r):
                for k, v in d.items():
                    if need.get(k, 0) < v:
                        need[k] = v
        for k, v in need.items():
            if k == E.key and not SAME_ENGINE_SYNC:
                continue
            if E.waited.get(k, 0) < v:
                E.e.wait_ge(self.sems[k], v)
                E.waited[k] = v

    def op(self, en, fn, r=(), w=(), pw=()):
        E = self.E[en]
        self._waits(E, r, w, pw)
        ins = fn(E.e)
        E.cnt += 1
        ins.then_inc(E.sem, 1)
        for t in r:
            t.r[E.key] = E.cnt
        for t in w:
            t.w = {E.key: E.cnt}
            t.r = {}
        for t in pw:
            t.w[E.key] = E.cnt

    def dma(self, q, out, in_, sb, r=(), w=(), pw=()):
        Q = self.E[q]
        self._waits(Q, r, w, pw)
        if sb.sem is None:
            if self.pool:
                sb.sem, sb.cnt = self.pool.pop()
            else:
                sb.sem = self.newsem("d")
            self.live.append(sb)
        sem, key = sb.sem
        sb.cnt += 16
        Q.e.dma_start(out=out, in_=in_).then_inc(sem, 16)
        for t in r:
            t.r[key] = sb.cnt
        for t in w:
            t.w = {key: sb.cnt}
            t.r = {}
        for t in pw:
            t.w[key] = sb.cnt

    def barrier(self):
        sp = self.E["sp"]
        for en in ("pe", "act", "dve", "pool"):
            E = self.E[en]
            if sp.waited.get(E.key, 0) < E.cnt:
                sp.e.wait_ge(E.sem, E.cnt)
                sp.waited[E.key] = E.cnt
        for sb in self.live:
            sem, key = sb.sem
            if sp.waited.get(key, 0) < sb.cnt:
                sp.e.wait_ge(sem, sb.cnt)
                sp.waited[key] = sb.cnt
        if not hasattr(self, "bar"):
            self.bar = self.newsem("bar")
            self.barcnt = 0
        self.barcnt += 1
        sp.e.sem_inc(self.bar[0], 1)
        for en in ("pe", "act", "dve", "pool"):
            E = self.E[en]
            E.e.wait_ge(self.bar[0], self.barcnt)
            for en2 in ("pe", "act", "dve", "pool"):
                E.waited[self.E[en2].key] = max(E.waited.get(self.E[en2].key, 0), self.E[en2].cnt)
            for sb in self.live:
                E.waited[sb.sem[1]] = max(E.waited.get(sb.sem[1], 0), sb.cnt)
        for sb in self.live:
            self.pool.append((sb.sem, sb.cnt))
            sb.sem = None
        self.live = []

    def wait_all(self, en, trs):
        E = self.E[en]
        self._waits(E, trs, (), ())

    def mm(self, out, lhsT, rhs, start, stop, r, tr, first=None):
        if first is None:
            first = start
        self.op("pe", lambda e: e.matmul(out, lhsT, rhs, start=bool(start), stop=bool(stop)), r=r,
                w=[tr] if first else (), pw=() if first else [tr])

    def tp(self, out, in_, ident, r, tr, first):
        self.op("pe", lambda e: e.transpose(out, in_, ident), r=r, w=[tr] if first else (), pw=() if first else [tr])


_UID = [0]


def sbt(nc, name, shape, dt):
    _UID[0] += 1
    return nc.sbuf_tensor("%s_u%d" % (name, _UID[0]), shape, dt)


class Ring:
    cnt = [0]

    def __init__(self, nc, es, name, shape, dt, n):
        Ring.cnt[0] += 1
        self.t = [es.enter_context(sbt(nc, "%s_%d_%d" % (name, Ring.cnt[0], i), shape, dt)) for i in range(n)]
        self.tr = [Tr("%s%d" % (name, i)) for i in range(n)]
        self.i = 0
        self.n = n

    def next(self):
        i = self.i
        self.i = (i + 1) % self.n
        return self.t[i], self.tr[i]


def build(S, NS, debug=False, nlayers=L, stages="WTNACRMF"):
    T = S // 128
    NB = S // 512
    nc = bass.Bass("TRN2", target_bir_lowering=False)

    def din(name, shape, dt=F32):
        return nc.dram_tensor(name, shape, dt, kind="ExternalInput").ap()

    x_in = din("x", [NS, S, D])
    pos_in = din("positions", [NS, S], I32)
    ln_mix_pre = din("ln_mix_pre", [L, D]); ln_mix_post = din("ln_mix_post", [L, D])
    ln_ffn_pre = din("ln_ffn_pre", [L, D]); ln_ffn_post = din("ln_ffn_post", [L, D])
    w_in = din("w_in", [L, D, INC])
    mla_q_norm = din("mla_q_norm", [L, 256]); mla_w_uq = din("mla_w_uq", [L, 256, 768])
    mla_kv_norm = din("mla_kv_norm", [L, 128]); mla_w_ukv = din("mla_w_ukv", [L, 128, 1024])
    mla_w_o = din("mla_w_o", [L, 512, D])
    conv_w_dw = din("conv_w_dw", [L, 31, 512]); conv_b_dw = din("conv_b_dw", [L, 512])
    conv_ln_g = din("conv_ln_g", [L, 512]); conv_ln_b = din("conv_ln_b", [L, 512])
    conv_w_pw = din("conv_w_pw", [L, 512, D])
    ret_decay_logits = din("ret_decay_logits", [L, 2, RH])
    ret_gn_g = din("ret_gn_g", [L, D]); ret_w_o = din("ret_w_o", [L, D, D])
    w_out = din("w_out", [L, D, D])
    ffn_w_gate = din("ffn_w_gate", [L, D, FH]); ffn_w_up = din("ffn_w_up", [L, D, FH]); ffn_w_down = din("ffn_w_down", [L, FH, D])
    cst_in = din("rope_consts", [2, 160])
    y_out = nc.dram_tensor("y", [NS, S, D], F32, kind="ExternalOutput").ap()

    skind = "ExternalOutput" if debug else "Internal"

    def scr(name, shape, dt):
        return nc.dram_tensor(name, shape, dt, kind=skind).ap()

    wb = {
        "in": scr("wb_in", [L, D, INC], BF16), "uq": scr("wb_uq", [L, 256, 768], BF16), "ukv": scr("wb_ukv", [L, 128, 1024], BF16),
        "mo": scr("wb_mo", [L, 512, D], BF16), "pw": scr("wb_pw", [L, 512, D], BF16), "ro": scr("wb_ro", [L, D, D], BF16),
        "wo": scr("wb_wo", [L, D, D], BF16), "fg": scr("wb_fg", [L, D, FH], BF16), "fu": scr("wb_fu", [L, D, FH], BF16),
        "fd": scr("wb_fd", [L, FH, D], BF16),
    }
    wsrc = {"in": w_in, "uq": mla_w_uq, "ukv": mla_w_ukv, "mo": mla_w_o, "pw": conv_w_pw, "ro": ret_w_o, "wo": w_out,
            "fg": ffn_w_gate, "fu": ffn_w_up, "fd": ffn_w_down}
    wb_tr = {(k, l): Tr("wb_%s%d" % (k, l)) for k in wb for l in range(L)}

    tabd = scr("tabd", [NS, 128, T * 160], F32)
    QTd = scr("QTd", [NS, NH, 96, S], BF16); KTd = scr("KTd", [NS, NH, 96, S], BF16)
    Vd = scr("Vd", [NS, S, NH * 65], BF16)
    oTd = scr("oTd", [NS, 4, 128, S], BF16)
    aTd = scr("aTd", [NS, 4, 128, S], F32); cvTd = scr("cvTd", [NS, 4, 128, S], BF16)
    rqd = scr("rqd", [NS, S, 512], BF16); rkd = scr("rkd", [NS, S, 512], BF16)
    rvd = scr("rvd", [NS, S, 1024], BF16); sgd = scr("sgd", [NS, S, 1024], F32)
    rtTd = scr("rtTd", [NS, 8, 128, S], BF16)
    gtd = scr("gtd", [NS, 24, 128, S], F32)
    x1d = scr("x1d", [NS, S, D], F32)
    xLd = scr("xLd", [NS, S, D], F32)
    if debug:
        dbg_qT = scr("dbg_qT", [128, 3 * RH * 128], BF16); dbg_kT = scr("dbg_kT", [128, RH * 128], BF16)
        dbg_sm = scr("dbg_sm", [128, RH * 128], BF16); dbg_DT = scr("dbg_DT", [128, RH * 128], F32)
        dbg_CF = scr("dbg_CF", [128, RH * 128], F32); dbg_CB = scr("dbg_CB", [128, RH * 128], F32)
        dbg_O = scr("dbg_O", [128, 1024], F32); dbg_Sf = scr("dbg_Sf", [128, 1024], F32); dbg_R = scr("dbg_R", [128, 1024], BF16)
        dbg_pc = scr("dbg_pc", [128, 16], F32); dbg_kf = scr("dbg_kf", [128, 512], BF16)
    dtr = {}
    for nm in ("tabd", "QTd", "KTd", "Vd", "oTd", "aTd", "cvTd", "rqd", "rkd", "rvd", "sgd", "rtTd", "gtd", "x1d", "xLd", "y"):
        for s in range(NS):
            dtr[(nm, s)] = Tr("%s_%d" % (nm, s))

    with ExitStack() as es:
        P = Prog(nc, es)
        ps = [es.enter_context(nc.psum_tensor("psb%d" % i, [128, 512], F32)) for i in range(8)]
        pst = [Tr("ps%d" % i, excl=True) for i in range(8)]
        identf = es.enter_context(sbt(nc, "identf", [128, 128], F32))
        ident = es.enter_context(sbt(nc, "ident", [128, 128], BF16))
        onesf = es.enter_context(sbt(nc, "onesf", [128, 128], F32))
        epsc = es.enter_context(sbt(nc, "epsc", [128, 1], F32))
        ctr = Tr("consts")
        P.op("pool", lambda e: e.memset(identf[:], 0.0), w=[ctr])
        P.op("pool", lambda e: e.affine_select(out=identf[:], in_=identf[:], pattern=[[-1, 128]], compare_op=ALU.not_equal,
                                               fill=1.0, base=0, channel_multiplier=1), w=[ctr])
        P.op("pool", lambda e: e.tensor_copy(out=ident[:], in_=identf[:]), r=[ctr], pw=[ctr])
        P.op("pool", lambda e: e.memset(onesf[:], 1.0), pw=[ctr])
        P.op("pool", lambda e: e.memset(epsc[:], EPS), pw=[ctr])

        def rstd_from_ssq(ssq, rstd, n, trs):
            P.op("act", lambda e: e.activation(out=rstd, in_=ssq, func=AF.Sqrt, bias=epsc[:, 0:1], scale=1.0 / n), r=trs + [ctr], w=trs)
            P.op("dve", lambda e: e.reciprocal(out=rstd, in_=rstd), r=trs, w=trs)

        def ssq4(xt, sqt, sm, xtr_, sqtr, smtr):
            P.op("act", lambda e: e.activation(out=sqt[:], in_=xt[:], func=AF.Square), r=[xtr_], w=[sqtr])
            P.op("dve", lambda e: e.reduce_sum(out=sm[:, 0:1], in_=sqt[:], axis=mybir.AxisListType.X), r=[sqtr], w=[smtr])

        def stage_W(l):
            with ExitStack() as st:
                stf = Ring(nc, st, "wstf", [128, 2048], F32, 3)
                stb = Ring(nc, st, "wstb", [128, 2048], BF16, 3)
                i = 0
                for k in ("in", "uq", "ukv", "mo", "pw", "ro", "wo", "fg", "fu", "fd"):
                    src = wsrc[k][l]
                    dst = wb[k][l]
                    K, N = src.shape
                    for kc in range(K // 128):
                        for c0 in range(0, N, 2048):
                            w = min(2048, N - c0)
                            f, ftr = stf.next()
                            b, btr = stb.next()
                            P.dma("sp", f[:, 0:w], src[kc * 128:(kc + 1) * 128, c0:c0 + w], ftr, w=[ftr])
                            en = ("dve", "pool", "act")[i % 3]
                            if en == "act":
                                P.op(en, lambda e, f=f, b=b, w=w: e.copy(out=b[:, 0:w], in_=f[:, 0:w]), r=[ftr], w=[btr])
                            else:
                                P.op(en, lambda e, f=f, b=b, w=w: e.tensor_copy(out=b[:, 0:w], in_=f[:, 0:w]), r=[ftr], w=[btr])
                            P.dma("pool", dst[kc * 128:(kc + 1) * 128, c0:c0 + w], b[:, 0:w], btr, r=[btr], pw=[wb_tr[(k, l)]])
                            i += 1

                P.barrier()
        def stage_T(s):
            with ExitStack() as st:
                posrow = st.enter_context(sbt(nc, "posrow", [2, S], F32))
                posi = st.enter_context(sbt(nc, "posi", [1, S], I32))
                cst = st.enter_context(sbt(nc, "cst", [2, 160], F32))
                tab = st.enter_context(sbt(nc, "tab", [128, T * 160], F32))
                tmpf = st.enter_context(sbt(nc, "tmpf", [128, T * 160], F32))
                tmpi = st.enter_context(sbt(nc, "tmpi", [128, T * 160], I32))
                t_pr, t_pi, t_c, t_tab, t_f, t_i = Tr(), Tr(), Tr(), Tr(), Tr(), Tr()
                P.op("dve", lambda e: e.memset(posrow[:], 1.0), w=[t_pr])
                P.dma("sp", posi[:], pos_in[s:s + 1, :], t_pi, w=[t_pi])
                P.dma("sp", cst[:], cst_in[:, :], t_c, w=[t_c])
                P.op("dve", lambda e: e.tensor_copy(out=posrow[0:1, :], in_=posi[:]), r=[t_pi], pw=[t_pr])
                for t0 in range(0, T, 3):
                    n = min(3, T - t0)
                    bk = (t0 // 3) % 2
                    for j in range(n):
                        t = t0 + j
                        P.mm(ps[bk][:, j * 160:(j + 1) * 160], posrow[0:2, t * 128:(t + 1) * 128], cst[0:2, :], True, True,
                             [t_pr, t_c], pst[bk], first=(j == 0))
                    P.op("dve", lambda e, bk=bk, n=n, t0=t0: e.tensor_copy(out=tab[:, t0 * 160:(t0 + n) * 160], in_=ps[bk][:, 0:n * 160]),
                         r=[pst[bk]], pw=[t_tab])
                W = T * 160
                for c0 in range(0, W, 2560):
                    c1 = min(W, c0 + 2560)
                    a = tab[:, c0:c1]; f = tmpf[:, c0:c1]; ii = tmpi[:, c0:c1]
                    P.op("dve", lambda e, a=a, f=f: e.tensor_scalar(out=f, in0=a, scalar1=1.0 / TWO_PI, scalar2=None, op0=ALU.mult), r=[t_tab], w=[t_f])
                    P.op("dve", lambda e, f=f, ii=ii: e.tensor_copy(out=ii, in_=f), r=[t_f], w=[t_i])
                    P.op("dve", lambda e, f=f, ii=ii: e.tensor_copy(out=f, in_=ii), r=[t_i], w=[t_f])
                    P.op("dve", lambda e, a=a, f=f: e.scalar_tensor_tensor(out=a, in0=f, scalar=-6.28125, in1=a, op0=ALU.mult, op1=ALU.add), r=[t_f], w=[t_tab])
                    P.op("dve", lambda e, a=a, f=f: e.scalar_tensor_tensor(out=a, in0=f, scalar=-(TWO_PI - 6.28125), in1=a, op0=ALU.mult, op1=ALU.add), r=[t_f], w=[t_tab])
                    P.op("dve", lambda e, a=a, f=f: e.tensor_scalar(out=f, in0=a, scalar1=PI, scalar2=-TWO_PI, op0=ALU.is_gt, op1=ALU.mult), r=[t_tab], w=[t_f])
                    P.op("dve", lambda e, a=a, f=f: e.tensor_tensor(out=a, in0=a, in1=f, op=ALU.add), r=[t_f], w=[t_tab])
                    P.op("dve", lambda e, a=a, f=f: e.tensor_scalar(out=f, in0=a, scalar1=-PI, scalar2=TWO_PI, op0=ALU.is_lt, op1=ALU.mult), r=[t_tab], w=[t_f])
                    P.op("dve", lambda e, a=a, f=f: e.tensor_tensor(out=a, in0=a, in1=f, op=ALU.add), r=[t_f], w=[t_tab])
                    P.op("dve", lambda e, a=a: e.tensor_scalar(out=a, in0=a, scalar1=-3.1415925, scalar2=3.1415925, op0=ALU.max, op1=ALU.min), r=[t_tab], w=[t_tab])
                    P.op("act", lambda e, a=a: e.activation(out=a, in_=a, func=AF.Sin), r=[t_tab], w=[t_tab])
                P.dma("pool", tabd[s], tab[:], t_tab, r=[t_tab], w=[dtr[("tabd", s)]])

                P.barrier()
        def stage_NP(l, s, xsrc, xtr):
            with ExitStack() as st:
                hT = st.enter_context(sbt(nc, "hT", [128, 8, S], BF16)); t_hT = Tr("hT")
                gbc = st.enter_context(sbt(nc, "np_gbc", [128, D], F32)); t_g = Tr()
                gq = st.enter_context(sbt(nc, "np_gq", [128, 384], F32))
                tab = st.enter_context(sbt(nc, "np_tab", [128, T, 160], F32)); t_tab = Tr()
                P.dma("sp", gbc[:], ln_mix_pre[l].partition_broadcast(128), t_g, w=[t_g])
                P.dma("sp", gq[:, 0:256], mla_q_norm[l].partition_broadcast(128), t_g, pw=[t_g])
                P.dma("sp", gq[:, 256:384], mla_kv_norm[l].partition_broadcast(128), t_g, pw=[t_g])
                P.dma("sp", tab[:].rearrange("p t c -> p (t c)"), tabd[s], t_tab, r=[dtr[("tabd", s)]], w=[t_tab])
                xr = Ring(nc, st, "np_x", [128, D], F32, 3)
                hb = Ring(nc, st, "np_hb", [128, D], BF16, 2)
                sq = Ring(nc, st, "np_sq", [128, D], F32, 2)
                stt = Ring(nc, st, "np_st", [128, 8], F32, 4)
                for t in range(T):
                    xt, xtr_ = xr.next()
                    P.dma("sp", xt[:], xsrc[s, t * 128:(t + 1) * 128, :], xtr_, r=[xtr[s]], w=[xtr_])
                    sqt, sqtr = sq.next()
                    sm, smtr = stt.next()
                    ssq4(xt, sqt, sm, xtr_, sqtr, smtr)
                    rstd_from_ssq(sm[:, 0:1], sm[:, 1:2], D, [smtr])
                    h, htr = hb.next()
                    P.op("dve", lambda e, xt=xt, h=h, sm=sm: e.scalar_tensor_tensor(out=h[:], in0=xt[:], scalar=sm[:, 1:2], in1=gbc[:], op0=ALU.mult, op1=ALU.mult),
                         r=[xtr_, smtr, t_g], w=[htr])
                    bk = t % 2
                    pv = ps[bk][:].bitcast(BF16)
                    for kc in range(8):
                        P.tp(pv[:, kc * 128:(kc + 1) * 128], h[:, kc * 128:(kc + 1) * 128], ident[:], [htr, ctr], pst[bk], kc == 0)
                    en = "act" if t % 2 == 0 else "dve"
                    if en == "act":
                        P.op("act", lambda e, pv=pv, t=t: e.copy(out=hT[:, :, t * 128:(t + 1) * 128], in_=pv.rearrange("p (k c) -> p k c", k=8)), r=[pst[bk]], pw=[t_hT])
                    else:
                        P.op("dve", lambda e, pv=pv, t=t: e.tensor_copy(out=hT[:, :, t * 128:(t + 1) * 128], in_=pv.rearrange("p (k c) -> p k c", k=8)), r=[pst[bk]], pw=[t_hT])

                win = wb["in"][l].rearrange("(kc p) n -> p kc n", p=128)
                wtr_in = wb_tr[("in", l)]

                for g in ([ExitStack()] if "l" in SUB else []):
                    wl = g.enter_context(sbt(nc, "wl", [128, 8, 416], BF16)); t_wl = Tr()
                    wuq = g.enter_context(sbt(nc, "wuq", [128, 2, 768], BF16)); t_wuq = Tr()
                    wk = g.enter_context(sbt(nc, "wk", [128, 8, 64], BF16))
                    wv = g.enter_context(sbt(nc, "wv", [128, 8, 64], BF16)); t_wkv = Tr()
                    P.dma("sp", wl[:], win[:, :, 0:416], t_wl, r=[wtr_in], w=[t_wl])
                    P.dma("sp", wuq[:], wb["uq"][l].rearrange("(kc p) n -> p kc n", p=128), t_wuq, r=[wb_tr[("uq", l)]], w=[t_wuq])
                    ukv_v = wb["ukv"][l].rearrange("p (h c) -> p h c", h=8)
                    P.dma("sp", wk[:], ukv_v[:, :, 0:64], t_wkv, r=[wb_tr[("ukv", l)]], w=[t_wkv])
                    P.dma("sp", wv[:], ukv_v[:, :, 64:128], t_wkv, pw=[t_wkv])
                    lat = Ring(nc, g, "lat", [128, 416], BF16, 2)
                    sqj = Ring(nc, g, "lsq", [128, 256], F32, 2)
                    stl = Ring(nc, g, "lst", [128, 8], F32, 3)
                    tmpr = Ring(nc, g, "ltmp", [128, 4, 8, 16], F32, 2)
                    latT = Ring(nc, g, "latT", [128, 4, 128], BF16, 2)
                    qsb = Ring(nc, g, "qsb", [128, 8, 128], BF16, 2)
                    for qt_, qttr_ in zip(qsb.t, qsb.tr):
                        P.op("pool", lambda e, qt_=qt_: e.memset(qt_[:], 0.0), w=[qttr_])
                    qTb = Ring(nc, g, "qTb", [96, 8, 512], BF16, 2)
                    kTb = Ring(nc, g, "kTb", [96, 8, 512], BF16, 2)
                    ckb = Ring(nc, g, "ckb", [128, 512], BF16, 2)
                    kpb = Ring(nc, g, "kpb", [32, 512], BF16, 2)
                    vsb = Ring(nc, g, "vsb", [128, 8, 65], BF16, 2)
                    for vt, vtr in zip(vsb.t, vsb.tr):
                        P.op("pool", lambda e, vt=vt: e.memset(vt[:], 1.0), w=[vtr])
                    for tb in range(NB):
                        qT, qTtr = qTb.next()
                        kT, kTtr = kTb.next()
                        ck, cktr = ckb.next()
                        kp, kptr = kpb.next()
                        for tt in range(4):
                            t = tb * 4 + tt
                            tok = slice(t * 128, (t + 1) * 128)
                            bl = 2 if t % 2 == 0 else 0
                            bt = 3 if t % 2 == 0 else 1
                            for kc in range(8):
                                P.mm(ps[bl][:, 0:416], hT[:, kc, tok], wl[:, kc, :], kc == 0, kc == 7, [t_hT, t_wl], pst[bl])
                            la, latr = lat.next()
                            sj, sjtr = sqj.next()
                            sm, smtr = stl.next()
                            P.op("pool", lambda e, sm=sm: e.memset(sm[:], 0.0), w=[smtr])
                            P.op("act", lambda e, sj=sj, sm=sm: e.activation(out=sj[:, 0:256], in_=ps[bl][:, 0:256], func=AF.Square, accum_out=sm[:, 0:1]), r=[pst[bl], smtr], w=[sjtr], pw=[smtr])
                            P.op("act", lambda e, sj=sj, sm=sm: e.activation(out=sj[:, 0:128], in_=ps[bl][:, 256:384], func=AF.Square, accum_out=sm[:, 1:2]), r=[pst[bl], smtr], w=[sjtr], pw=[smtr])
                            P.op("act", lambda e, sm=sm: e.activation(out=sm[:, 2:3], in_=sm[:, 0:1], func=AF.Sqrt, bias=epsc[:, 0:1], scale=1.0 / 256), r=[smtr, ctr], pw=[smtr])
                            P.op("act", lambda e, sm=sm: e.activation(out=sm[:, 3:4], in_=sm[:, 1:2], func=AF.Sqrt, bias=epsc[:, 0:1], scale=1.0 / 128), r=[smtr], pw=[smtr])
                            P.op("dve", lambda e, sm=sm: e.reciprocal(out=sm[:, 4:6], in_=sm[:, 2:4]), r=[smtr], pw=[smtr])
                            P.op("dve", lambda e, la=la, sm=sm: e.scalar_tensor_tensor(out=la[:, 0:256], in0=ps[bl][:, 0:256], scalar=sm[:, 4:5], in1=gq[:, 0:256], op0=ALU.mult, op1=ALU.mult),
                                 r=[pst[bl], smtr, t_g], w=[latr])
                            P.op("dve", lambda e, la=la, sm=sm: e.scalar_tensor_tensor(out=la[:, 256:384], in0=ps[bl][:, 256:384], scalar=sm[:, 5:6], in1=gq[:, 256:384], op0=ALU.mult, op1=ALU.mult),
                                 r=[pst[bl], smtr, t_g], pw=[latr])
                            tm, tmtr = tmpr.next()
                            sn = tab[:, t, 0:16]; cs = tab[:, t, 16:32]
                            x1 = ps[bl][:, 384:400]; x2 = ps[bl][:, 400:416]
                            P.op("dve", lambda e, tm=tm, x1=x1, cs=cs: e.tensor_tensor(out=tm[:, 0, 0, :], in0=x1, in1=cs, op=ALU.mult), r=[pst[bl], t_tab], w=[tmtr])
                            P.op("dve", lambda e, tm=tm, x2=x2, sn=sn: e.tensor_tensor(out=tm[:, 1, 0, :], in0=x2, in1=sn, op=ALU.mult), r=[pst[bl]], pw=[tmtr])
                            P.op("dve", lambda e, tm=tm, x2=x2, cs=cs: e.tensor_tensor(out=tm[:, 2, 0, :], in0=x2, in1=cs, op=ALU.mult), r=[pst[bl]], pw=[tmtr])
                            P.op("dve", lambda e, tm=tm, x1=x1, sn=sn: e.tensor_tensor(out=tm[:, 3, 0, :], in0=x1, in1=sn, op=ALU.mult), r=[pst[bl]], pw=[tmtr])
                            P.op("dve", lambda e, tm=tm, la=la: e.tensor_tensor(out=la[:, 384:400], in0=tm[:, 0, 0, :], in1=tm[:, 1, 0, :], op=ALU.subtract), r=[tmtr], pw=[latr])
                            P.op("dve", lambda e, tm=tm, la=la: e.tensor_tensor(out=la[:, 400:416], in0=tm[:, 2, 0, :], in1=tm[:, 3, 0, :], op=ALU.add), r=[tmtr], pw=[latr])
                            if CUT < 2:
                                continue
                            pv = ps[bt][:].bitcast(BF16)
                            for j in range(3):
                                P.tp(pv[:, j * 128:(j + 1) * 128], la[:, j * 128:(j + 1) * 128], ident[:], [latr, ctr], pst[bt], j == 0)
                            P.tp(pv[0:32, 384:512], la[:, 384:416], ident[:], [latr], pst[bt], False)
                            lT, lTtr = latT.next()
                            P.op("dve", lambda e, lT=lT, pv=pv: e.tensor_copy(out=lT[:, 0:3, :], in_=pv[:, 0:384].rearrange("p (j c) -> p j c", j=3)), r=[pst[bt]], w=[lTtr])
                            P.op("dve", lambda e, ck=ck, pv=pv, tt=tt: e.tensor_copy(out=ck[:, tt * 128:(tt + 1) * 128], in_=pv[:, 256:384]), r=[pst[bt]], pw=[cktr] if tt else (), w=[cktr] if tt == 0 else ())
                            P.op("dve", lambda e, kp=kp, pv=pv, tt=tt: e.tensor_copy(out=kp[:, tt * 128:(tt + 1) * 128], in_=pv[0:32, 384:512]), r=[pst[bt]], pw=[kptr] if tt else (), w=[kptr] if tt == 0 else ())
                            if CUT < 2.1:
                                continue
                            for kc in range(2):
                                P.mm(ps[4][:, 0:480], lT[:, kc, :], wuq[:, kc, 0:480], kc == 0, kc == 1, [lTtr, t_wuq], pst[4])
                            for kc in range(2):
                                P.mm(ps[5][:, 0:288], lT[:, kc, :], wuq[:, kc, 480:768], kc == 0, kc == 1, [lTtr, t_wuq], pst[5])
                            if CUT < 2.3:
                                continue
                            q, qtr = qsb.next()
                            tm2, tm2tr = tmpr.next()
                            first = True
                            for (bk, h0, nh) in ((4, 0, 5), (5, 5, 3)):
                                pq = ps[bk][:, 0:nh * 96].rearrange("p (h d) -> p h d", h=nh)
                                qo = q[:, h0:h0 + nh, :]
                                P.op("act", lambda e, pq=pq, qo=qo: e.copy(out=qo[:, :, 0:64], in_=pq[:, :, 0:64]), r=[pst[bk]], pw=[qtr])
                                if CUT < 2.5:
                                    continue
                                csb = tab[:, t:t + 1, 16:32].to_broadcast([128, nh, 16])
                                snb = tab[:, t:t + 1, 0:16].to_broadcast([128, nh, 16])
                                x1 = pq[:, :, 64:80]; x2 = pq[:, :, 80:96]
                                tv = tm2[:, :, h0:h0 + nh, :]
                                P.op("dve", lambda e, tv=tv, x1=x1, csb=csb: e.tensor_tensor(out=tv[:, 0], in0=x1, in1=csb, op=ALU.mult), r=[pst[bk], t_tab], w=[tm2tr] if first else (), pw=() if first else [tm2tr])
                                P.op("dve", lambda e, tv=tv, x2=x2, snb=snb: e.tensor_tensor(out=tv[:, 1], in0=x2, in1=snb, op=ALU.mult), r=[pst[bk]], pw=[tm2tr])
                                P.op("dve", lambda e, tv=tv, x2=x2, csb=csb: e.tensor_tensor(out=tv[:, 2], in0=x2, in1=csb, op=ALU.mult), r=[pst[bk]], pw=[tm2tr])
                                P.op("dve", lambda e, tv=tv, x1=x1, snb=snb: e.tensor_tensor(out=tv[:, 3], in0=x1, in1=snb, op=ALU.mult), r=[pst[bk]], pw=[tm2tr])
                                P.op("dve", lambda e, tv=tv, qo=qo: e.tensor_tensor(out=qo[:, :, 64:80], in0=tv[:, 0], in1=tv[:, 1], op=ALU.subtract), r=[tm2tr], pw=[qtr])
                                P.op("dve", lambda e, tv=tv, qo=qo: e.tensor_tensor(out=qo[:, :, 80:96], in0=tv[:, 2], in1=tv[:, 3], op=ALU.add), r=[tm2tr], pw=[qtr])
                                first = False
                            if CUT < 2.7:
                                continue
                            pv6 = ps[6][:].bitcast(BF16)
                            for h in range(8):
                                P.tp(pv6[:, h * 128:(h + 1) * 128], q[:, h, :], ident[:], [qtr, ctr], pst[6], h == 0)
                            if CUT < 2.9:
                                continue
                            P.op("act", lambda e, qT=qT, pv6=pv6, tt=tt: e.copy(out=qT[:, :, tt * 128:(tt + 1) * 128], in_=pv6[0:96, :].rearrange("p (h c) -> p h c", h=8)),
                                 r=[pst[6]], w=[qTtr] if tt == 0 else (), pw=[qTtr] if tt else ())
                            if CUT < 4:
                                continue
                            P.mm(ps[7][:, 0:512], lT[:, 2, :], wv[:].rearrange("p h c -> p (h c)"), True, True, [lTtr, t_wkv], pst[7])
                            v, vtr = vsb.next()
                            P.op("dve", lambda e, v=v: e.tensor_copy(out=v[:, :, 0:64], in_=ps[7][:, 0:512].rearrange("p (h c) -> p h c", h=8)), r=[pst[7]], w=[vtr])
                            P.dma("sp", Vd[s, tok, :], v[:].rearrange("p h c -> p (h c)"), vtr, r=[vtr], pw=[dtr[("Vd", s)]])
                        if CUT < 5:
                            continue
                        blk = slice(tb * 512, (tb + 1) * 512)
                        for h in range(8):
                            bk = 2 + (h % 2) * 5
                            P.mm(ps[bk][0:64, :], wk[:, h, :], ck[:], True, True, [cktr, t_wkv], pst[bk])
                            if h % 2 == 0:
                                P.op("act", lambda e, kT=kT, h=h, bk=bk: e.copy(out=kT[0:64, h, :], in_=ps[bk][0:64, :]), r=[pst[bk]], w=[kTtr] if h == 0 else (), pw=[kTtr] if h else ())
                            else:
                                P.op("dve", lambda e, kT=kT, h=h, bk=bk: e.tensor_copy(out=kT[0:64, h, :], in_=ps[bk][0:64, :]), r=[pst[bk]], pw=[kTtr])
                        for h in range(8):
                            if h % 2 == 0:
                                P.op("act", lambda e, kT=kT, kp=kp, h=h: e.copy(out=kT[64:96, h, :], in_=kp[:, :]), r=[kptr], pw=[kTtr])
                            else:
                                P.op("dve", lambda e, kT=kT, kp=kp, h=h: e.tensor_copy(out=kT[64:96, h, :], in_=kp[:, :]), r=[kptr], pw=[kTtr])
                        P.dma("sp", QTd[s, :, :, blk].rearrange("h p c -> p h c"), qT[:], qTtr, r=[qTtr], pw=[dtr[("QTd", s)]])
                        P.dma("sp", KTd[s, :, :, blk].rearrange("h p c -> p h c"), kT[:], kTtr, r=[kTtr], pw=[dtr[("KTd", s)]])

                    P.barrier()
                    g.close()
                for g in ([ExitStack()] if "c" in SUB else []):
                    wc = g.enter_context(sbt(nc, "wc", [128, 8, 1024], BF16)); t_wc = Tr()
                    P.dma("sp", wc[:], win[:, :, O_CONV:O_CONV + 1024], t_wc, r=[wtr_in], w=[t_wc])
                    sgr = Ring(nc, g, "cv_sg", [128, 512], F32, 2)
                    aor = Ring(nc, g, "cv_a", [128, 512], F32, 3)
                    for tb in range(NB):
                        blk = slice(tb * 512, (tb + 1) * 512)
                        for j in range(4):
                            ba, bg = (2, 3) if j % 2 == 0 else (4, 5)
                            for kc in range(8):
                                P.mm(ps[ba][:], wc[:, kc, j * 128:(j + 1) * 128], hT[:, kc, blk], kc == 0, kc == 7, [t_hT, t_wc], pst[ba])
                            for kc in range(8):
                                P.mm(ps[bg][:], wc[:, kc, 512 + j * 128:512 + (j + 1) * 128], hT[:, kc, blk], kc == 0, kc == 7, [t_hT, t_wc], pst[bg])
                            sg, sgtr = sgr.next()
                            ao, aotr = aor.next()
                            P.op("act", lambda e, sg=sg, bg=bg: e.activation(out=sg[:], in_=ps[bg][:], func=AF.Sigmoid), r=[pst[bg]], w=[sgtr])
                            P.op("dve", lambda e, ao=ao, sg=sg, ba=ba: e.tensor_tensor(out=ao[:], in0=ps[ba][:], in1=sg[:], op=ALU.mult), r=[pst[ba], sgtr], w=[aotr])
                            P.dma("sp", aTd[s, j, :, blk], ao[:], aotr, r=[aotr], pw=[dtr[("aTd", s)]])

                    P.barrier()
                    g.close()
                for g in ([ExitStack()] if "r" in SUB else []):
                    wr = Ring(nc, g, "wr", [128, 8, 512], BF16, 2)
                    tmq = Ring(nc, g, "r_tm", [128, 4, 4, 64], F32, 2)
                    o16 = Ring(nc, g, "r_o16", [128, 512], BF16, 3)
                    o32 = Ring(nc, g, "r_o32", [128, 512], F32, 3)
                    for (kind, c0) in (("q", O_RQ), ("k", O_RK), ("v0", O_RV), ("v1", O_RV + 512), ("g0", O_RG), ("g1", O_RG + 512)):
                        w, wtr = wr.next()
                        P.dma("sp", w[:], win[:, :, c0:c0 + 512], wtr, r=[wtr_in], w=[wtr])
                        for t in range(T):
                            tok = slice(t * 128, (t + 1) * 128)
                            bk = 2 + (t % 4)
                            for kc in range(8):
                                P.mm(ps[bk][:], hT[:, kc, tok], w[:, kc, :], kc == 0, kc == 7, [t_hT, wtr], pst[bk])
                            if kind in ("q", "k"):
                                o, otr = o16.next()
                                tm, tmtr = tmq.next()
                                pq = ps[bk][:].rearrange("p (h d) -> p h d", h=4)
                                ov = o[:].rearrange("p (h d) -> p h d", h=4)
                                csb = tab[:, t:t + 1, 96:160].to_broadcast([128, 4, 64])
                                snb = tab[:, t:t + 1, 32:96].to_broadcast([128, 4, 64])
                                x1 = pq[:, :, 0:64]; x2 = pq[:, :, 64:128]
                                P.op("dve", lambda e, tm=tm, x1=x1, csb=csb: e.tensor_tensor(out=tm[:, 0], in0=x1, in1=csb, op=ALU.mult), r=[pst[bk], t_tab], w=[tmtr])
                                P.op("dve", lambda e, tm=tm, x2=x2, snb=snb: e.tensor_tensor(out=tm[:, 1], in0=x2, in1=snb, op=ALU.mult), r=[pst[bk]], pw=[tmtr])
                                P.op("dve", lambda e, tm=tm, x2=x2, csb=csb: e.tensor_tensor(out=tm[:, 2], in0=x2, in1=csb, op=ALU.mult), r=[pst[bk]], pw=[tmtr])
                                P.op("dve", lambda e, tm=tm, x1=x1, snb=snb: e.tensor_tensor(out=tm[:, 3], in0=x1, in1=snb, op=ALU.mult), r=[pst[bk]], pw=[tmtr])
                                if kind == "q":
                                    P.op("dve", lambda e, tm=tm, ov=ov: e.tensor_tensor(out=ov[:, :, 0:64], in0=tm[:, 0], in1=tm[:, 1], op=ALU.subtract), r=[tmtr], w=[otr])
                                    P.op("dve", lambda e, tm=tm, ov=ov: e.tensor_tensor(out=ov[:, :, 64:128], in0=tm[:, 2], in1=tm[:, 3], op=ALU.add), r=[tmtr], pw=[otr])
                                else:
                                    sc = float(128 ** -0.5)
                                    P.op("dve", lambda e, tm=tm: e.tensor_tensor(out=tm[:, 0], in0=tm[:, 0], in1=tm[:, 1], op=ALU.subtract), r=[tmtr], w=[tmtr])
                                    P.op("dve", lambda e, tm=tm: e.tensor_tensor(out=tm[:, 2], in0=tm[:, 2], in1=tm[:, 3], op=ALU.add), r=[tmtr], w=[tmtr])
                                    P.op("act", lambda e, tm=tm, ov=ov: e.mul(out=ov[:, :, 0:64], in_=tm[:, 0], mul=sc), r=[tmtr], w=[otr])
                                    P.op("act", lambda e, tm=tm, ov=ov: e.mul(out=ov[:, :, 64:128], in_=tm[:, 2], mul=sc), r=[tmtr], pw=[otr])
                                dst = (rqd if kind == "q" else rkd)[s, tok, :]
                                P.dma("sp", dst, o[:], otr, r=[otr], pw=[dtr[("rqd" if kind == "q" else "rkd", s)]])
                            elif kind in ("v0", "v1"):
                                o, otr = o16.next()
                                if t % 2 == 0:
                                    P.op("act", lambda e, o=o, bk=bk: e.copy(out=o[:], in_=ps[bk][:]), r=[pst[bk]], w=[otr])
                                else:
                                    P.op("dve", lambda e, o=o, bk=bk: e.tensor_copy(out=o[:], in_=ps[bk][:]), r=[pst[bk]], w=[otr])
                                half = 0 if kind == "v0" else 512
                                P.dma("sp", rvd[s, tok, half:half + 512], o[:], otr, r=[otr], pw=[dtr[("rvd", s)]])
                            else:
                                o, otr = o32.next()
                                P.op("act", lambda e, o=o, bk=bk: e.activation(out=o[:], in_=ps[bk][:], func=AF.Silu), r=[pst[bk]], w=[otr])
                                half = 0 if kind == "g0" else 512
                                P.dma("sp", sgd[s, tok, half:half + 512], o[:], otr, r=[otr], pw=[dtr[("sgd", s)]])

                    P.barrier()
                    g.close()
                for g in ([ExitStack()] if "g" in SUB else []):
                    wr = Ring(nc, g, "wg_", [128, 8, 512], BF16, 2)
                    gor = Ring(nc, g, "g_o", [128, 512], F32, 3)
                    for gg in range(6):
                        w, wtr = wr.next()
                        P.dma("sp", w[:], win[:, :, O_GATE + gg * 512:O_GATE + (gg + 1) * 512], wtr, r=[wtr_in], w=[wtr])
                        for tb in range(NB):
                            blk = slice(tb * 512, (tb + 1) * 512)
                            for j in range(4):
                                bk = 2 + (j % 4)
                                for kc in range(8):
                                    P.mm(ps[bk][:], w[:, kc, j * 128:(j + 1) * 128], hT[:, kc, blk], kc == 0, kc == 7, [t_hT, wtr], pst[bk])
                                o, otr = gor.next()
                                P.op("act", lambda e, o=o, bk=bk: e.activation(out=o[:], in_=ps[bk][:], func=AF.Sigmoid), r=[pst[bk]], w=[otr])
                                P.dma("sp", gtd[s, gg * 4 + j, :, blk], o[:], otr, r=[otr], pw=[dtr[("gtd", s)]])

                    P.barrier()
                    g.close()
                P.barrier()
        def stage_A(l, s, with_conv=True):
            scale = float(96 ** -0.5)
            with ExitStack() as st:
                cgen = stage_C_gen(l, s, st, 7) if with_conv else iter(())
                next(cgen, None)
                V = st.enter_context(sbt(nc, "at_V", [128, T, NH * 65], BF16)); t_V = Tr()
                sel = st.enter_context(sbt(nc, "at_sel", [65, 64], F32)); t_sel = Tr()
                P.op("pool", lambda e: e.memset(sel[:], 0.0), w=[t_sel])
                P.op("pool", lambda e: e.memset(sel[64:65, :], 1.0), pw=[t_sel])
                Vv = Vd[s].rearrange("(t p) c -> p t c", p=128)
                for t0 in range(0, T, 8):
                    P.dma("sp", V[:, t0:t0 + 8, :], Vv[:, t0:t0 + 8, :], t_V, r=[dtr[("Vd", s)]], pw=[t_V] if t0 else (), w=[t_V] if t0 == 0 else ())
                qr = Ring(nc, st, "at_q", [96, S], BF16, 2)
                kr = Ring(nc, st, "at_k", [96, S], BF16, 2)
                pr = Ring(nc, st, "at_p", [128, 512], BF16, 4)
                osb = Ring(nc, st, "at_o", [65, 512], F32, 2)
                rbc = Ring(nc, st, "at_r", [64, 512], F32, 2)
                oT = Ring(nc, st, "at_oT", [64, 512], BF16, 2)
                its = [(h, qb, kt) for h in range(NH) for qb in range(NB) for kt in range(T)]
                PF = 3
                qk = {}
                crate = min(1.0, 1.06 * (139 * NB + 8) / len(its))

                def get_qk(h):
                    if h not in qk:
                        q, qtr = qr.next()
                        k, ktr = kr.next()
                        P.dma("sp", q[:], QTd[s, h], qtr, r=[dtr[("QTd", s)]], w=[qtr])
                        P.dma("sp", k[:], KTd[s, h], ktr, r=[dtr[("KTd", s)]], w=[ktr])
                        qk[h] = (q, qtr, k, ktr)
                    return qk[h]

                def emit_score(i):
                    h, qb, kt = its[i]
                    q, qtr, k, ktr = get_qk(h)
                    sbk = i % 4
                    P.mm(ps[sbk][:], k[:, kt * 128:(kt + 1) * 128], q[:, qb * 512:(qb + 1) * 512], True, True, [qtr, ktr], pst[sbk])

                for i in range(min(PF, len(its))):
                    emit_score(i)
                for i, (h, qb, kt) in enumerate(its):
                    qs = slice(qb * 512, (qb + 1) * 512)
                    ob = 4 + (qb % 2)
                    sbk = i % 4
                    p, ptr = pr.next()
                    P.op("act", lambda e, p=p, sbk=sbk: e.activation(out=p[:], in_=ps[sbk][:], func=AF.Exp, scale=scale), r=[pst[sbk]], w=[ptr])
                    if i + PF < len(its):
                        emit_score(i + PF)
                    P.mm(ps[ob][0:65, :], V[:, kt, h * 65:(h + 1) * 65], p[:], kt == 0, kt == T - 1, [t_V, ptr], pst[ob])
                    if int((i + 1) * crate) > int(i * crate):
                        next(cgen, None)
                    if kt == T - 1:
                        o, otr = osb.next()
                        P.op("dve", lambda e, o=o, ob=ob: e.tensor_copy(out=o[:], in_=ps[ob][0:65, :]), r=[pst[ob]], w=[otr])
                        P.mm(ps[6][0:64, :], sel[:], o[:], True, True, [t_sel, otr], pst[6])
                        rb, rbtr = rbc.next()
                        P.op("dve", lambda e, rb=rb: e.reciprocal(out=rb[:], in_=ps[6][0:64, :]), r=[pst[6]], w=[rbtr])
                        ot, ottr = oT.next()
                        P.op("dve", lambda e, ot=ot, o=o, rb=rb: e.tensor_tensor(out=ot[:], in0=o[0:64, :], in1=rb[:], op=ALU.mult), r=[otr, rbtr], w=[ottr])
                        P.dma("pool", oTd[s, h // 2, (h % 2) * 64:(h % 2) * 64 + 64, qs], ot[:], ottr, r=[ottr], pw=[dtr[("oTd", s)]])
                for _ in cgen:
                    pass
                P.barrier()

        def stage_C_gen(l, s, st, pb):
            cw = st.enter_context(sbt(nc, "cv_cw", [34, 512], F32)); t_cw = Tr()
            cwT = st.enter_context(sbt(nc, "cv_cwT", [128, 4, 34], F32)); t_cwT = Tr()
            P.dma("sp", cw[0:31, :], conv_w_dw[l], t_cw, w=[t_cw])
            P.dma("sp", cw[31:32, :], conv_b_dw[l:l + 1, :], t_cw, pw=[t_cw])
            P.dma("sp", cw[32:33, :], conv_ln_g[l:l + 1, :], t_cw, pw=[t_cw])
            P.dma("sp", cw[33:34, :], conv_ln_b[l:l + 1, :], t_cw, pw=[t_cw])
            for cc in range(4):
                P.tp(ps[pb][:, cc * 34:(cc + 1) * 34], cw[0:34, cc * 128:(cc + 1) * 128], identf[0:34, 0:34], [t_cw, ctr], pst[pb], cc == 0)
            P.op("dve", lambda e: e.tensor_copy(out=cwT[:], in_=ps[pb][:, 0:136].rearrange("p (c j) -> p c j", c=4)), r=[pst[pb]], w=[t_cwT])
            apr = Ring(nc, st, "cv_ap", [128, 4, 542], F32, 2)
            yr = Ring(nc, st, "cv_y", [128, 4, 512], F32, 2)
            sqr = Ring(nc, st, "cv_sq", [128, 512], F32, 2)
            mr = Ring(nc, st, "cv_m", [128, 3, 512], F32, 2)
            tr_ = Ring(nc, st, "cv_t", [128, 512], F32, 2)
            outr = Ring(nc, st, "cv_o", [128, 512], BF16, 3)
            yield
            for tb in range(NB):
                blk = slice(tb * 512, (tb + 1) * 512)
                ap_, aptr = apr.next()
                lo = tb * 512 - 15
                hi = tb * 512 + 512 + 15
                first = True
                if lo < 0:
                    P.op("pool", lambda e: e.memset(ap_[:, :, 0:15], 0.0), w=[aptr])
                    first = False
                if hi > S:
                    P.op("pool", lambda e: e.memset(ap_[:, :, 527:542], 0.0), w=[aptr] if first else (), pw=() if first else [aptr])
                    first = False
                c_lo = max(lo, 0); c_hi = min(hi, S)
                for cc in range(4):
                    P.dma("sp", ap_[:, cc, c_lo - lo:c_hi - lo], aTd[s, cc, :, c_lo:c_hi], aptr, r=[dtr[("aTd", s)]], w=[aptr] if first else (), pw=() if first else [aptr])
                    first = False
                y_, ytr = yr.next()
                for cc in range(4):
                    P.op("dve", lambda e: e.tensor_scalar(out=y_[:, cc, :], in0=ap_[:, cc, 0:512], scalar1=cwT[:, cc, 0:1], scalar2=cwT[:, cc, 31:32], op0=ALU.mult, op1=ALU.add),
                         r=[aptr, t_cwT], w=[ytr] if cc == 0 else (), pw=[ytr] if cc else ())
                    yield
                    for j in range(1, 31):
                        P.op("dve", lambda e: e.scalar_tensor_tensor(out=y_[:, cc, :], in0=ap_[:, cc, j:j + 512], scalar=cwT[:, cc, j:j + 1], in1=y_[:, cc, :], op0=ALU.mult, op1=ALU.add),
                             r=[aptr], pw=[ytr])
                        yield
                for cc in range(4):
                    P.mm(ps[pb][:], onesf[:], y_[:, cc, :], cc == 0, cc == 3, [ytr, ctr], pst[pb])
                m, mtr = mr.next()
                P.op("dve", lambda e: e.tensor_scalar(out=m[:, 0, :], in0=ps[pb][:], scalar1=1.0 / 512, scalar2=None, op0=ALU.mult), r=[pst[pb]], w=[mtr])
                yield
                for cc in range(4):
                    sq_, sqtr = sqr.next()
                    P.op("dve", lambda e: e.tensor_tensor(out=sq_[:], in0=y_[:, cc, :], in1=y_[:, cc, :], op=ALU.mult), r=[ytr], w=[sqtr])
                    P.mm(ps[pb][:], onesf[:], sq_[:], cc == 0, cc == 3, [sqtr, ctr], pst[pb])
                    yield
                P.op("dve", lambda e: e.tensor_tensor(out=m[:, 1, :], in0=m[:, 0, :], in1=m[:, 0, :], op=ALU.mult), r=[mtr], pw=[mtr])
                P.op("dve", lambda e: e.scalar_tensor_tensor(out=m[:, 1, :], in0=ps[pb][:], scalar=1.0 / 512, in1=m[:, 1, :], op0=ALU.mult, op1=ALU.subtract), r=[pst[pb], mtr], pw=[mtr])
                yield
                P.op("act", lambda e: e.activation(out=m[:, 2, :], in_=m[:, 1, :], func=AF.Sqrt, bias=epsc[:, 0:1], scale=1.0), r=[mtr, ctr], pw=[mtr])
                P.op("dve", lambda e: e.reciprocal(out=m[:, 2, :], in_=m[:, 2, :]), r=[mtr], pw=[mtr])
                yield
                for cc in range(4):
                    t_, ttr = tr_.next()
                    P.op("dve", lambda e: e.tensor_tensor(out=t_[:], in0=y_[:, cc, :], in1=m[:, 0, :], op=ALU.subtract), r=[ytr, mtr], w=[ttr])
                    yield
                    P.op("dve", lambda e: e.tensor_tensor(out=t_[:], in0=t_[:], in1=m[:, 2, :], op=ALU.mult), r=[mtr], w=[ttr])
                    o, otr = outr.next()
                    P.op("act", lambda e: e.activation(out=o[:], in_=t_[:], func=AF.Silu, bias=cwT[:, cc, 33:34], scale=cwT[:, cc, 32:33]), r=[ttr, t_cwT], w=[otr])
                    P.dma("pool", cvTd[s, cc, :, blk], o[:], otr, r=[otr], pw=[dtr[("cvTd", s)]])
                    yield

        def stage_R(l, s):
            with ExitStack() as st:
                lgr = st.enter_context(sbt(nc, "rt_lg", [128, 8], F32)); t_lg = Tr()
                iof = st.enter_context(sbt(nc, "rt_iof", [128, 128], F32))
                ioq = st.enter_context(sbt(nc, "rt_ioq", [128, 128], F32))
                iop = st.enter_context(sbt(nc, "rt_iop", [128, 1], F32))
                t_io = Tr()
                tA = st.enter_context(sbt(nc, "rt_tA", [128, 128], F32))
                tB = st.enter_context(sbt(nc, "rt_tB", [128, 128], F32))
                tC = st.enter_context(sbt(nc, "rt_tC", [128, 128], F32)); t_tmp = Tr()
                DT = st.enter_context(sbt(nc, "rt_DT", [128, RH, 128], F32))
                CF = st.enter_context(sbt(nc, "rt_CF", [128, RH, 128], F32))
                CB = st.enter_context(sbt(nc, "rt_CB", [128, RH, 128], F32))
                pc = st.enter_context(sbt(nc, "rt_pc", [128, 4, RH], F32))
                t_tb = Tr()
                P.dma("sp", lgr[:], ret_decay_logits[l].rearrange("a h -> (a h)").partition_broadcast(128), t_lg, w=[t_lg])
                P.op("act", lambda e: e.activation(out=lgr[:], in_=lgr[:], func=AF.Exp, scale=-1.0), r=[t_lg], w=[t_lg])
                P.op("dve", lambda e: e.tensor_scalar(out=lgr[:], in0=lgr[:], scalar1=1.0, scalar2=None, op0=ALU.add), r=[t_lg], w=[t_lg])
                P.op("act", lambda e: e.activation(out=lgr[:], in_=lgr[:], func=AF.Ln), r=[t_lg], w=[t_lg])
                P.op("dve", lambda e: e.tensor_scalar(out=lgr[:], in0=lgr[:], scalar1=-1.0, scalar2=None, op0=ALU.mult), r=[t_lg], w=[t_lg])
                P.op("pool", lambda e: e.iota(iof[:], [[1, 128]], base=0, channel_multiplier=-1, allow_small_or_imprecise_dtypes=True), w=[t_io])
                P.op("pool", lambda e: e.iota(ioq[:], [[1, 128]], base=0, channel_multiplier=0, allow_small_or_imprecise_dtypes=True), pw=[t_io])
                P.op("dve", lambda e: e.tensor_scalar(out=iop[:], in0=iof[:, 0:1], scalar1=-1.0, scalar2=None, op0=ALU.mult), r=[t_io], pw=[t_io])
                for h in range(RH):
                    lf = lgr[:, h:h + 1]; lb = lgr[:, RH + h:RH + h + 1]
                    P.op("dve", lambda e: e.tensor_scalar(out=tA[:], in0=iof[:], scalar1=0.0, scalar2=None, op0=ALU.max), r=[t_io], w=[t_tmp])
                    P.op("act", lambda e, lf=lf: e.activation(out=tA[:], in_=tA[:], func=AF.Exp, scale=lf), r=[t_tmp, t_lg], w=[t_tmp])
                    P.op("dve", lambda e: e.tensor_scalar(out=tB[:], in0=iof[:], scalar1=0.0, scalar2=None, op0=ALU.is_ge), r=[t_io], pw=[t_tmp])
                    P.op("dve", lambda e: e.tensor_tensor(out=tA[:], in0=tA[:], in1=tB[:], op=ALU.mult), r=[t_tmp], w=[t_tmp])
                    P.op("dve", lambda e: e.tensor_scalar(out=tC[:], in0=iof[:], scalar1=-1.0, scalar2=0.0, op0=ALU.mult, op1=ALU.max), r=[t_io], pw=[t_tmp])
                    P.op("act", lambda e, lb=lb: e.activation(out=tC[:], in_=tC[:], func=AF.Exp, scale=lb), r=[t_tmp], w=[t_tmp])
                    P.op("dve", lambda e: e.tensor_scalar(out=tB[:], in0=iof[:], scalar1=0.0, scalar2=None, op0=ALU.is_lt), r=[t_io], w=[t_tmp])
                    P.op("dve", lambda e: e.tensor_tensor(out=tC[:], in0=tC[:], in1=tB[:], op=ALU.mult), r=[t_tmp], w=[t_tmp])
                    P.op("dve", lambda e, h=h: e.tensor_tensor(out=DT[:, h, :], in0=tA[:], in1=tC[:], op=ALU.add), r=[t_tmp], pw=[t_tb])
                    P.op("dve", lambda e: e.tensor_scalar(out=tA[:], in0=ioq[:], scalar1=1.0, scalar2=None, op0=ALU.add), r=[t_io], w=[t_tmp])
                    P.op("act", lambda e, h=h, lf=lf: e.activation(out=CF[:, h, :], in_=tA[:], func=AF.Exp, scale=lf), r=[t_tmp], pw=[t_tb])
                    P.op("dve", lambda e: e.tensor_scalar(out=tA[:], in0=ioq[:], scalar1=-1.0, scalar2=128.0, op0=ALU.mult, op1=ALU.add), r=[t_io], w=[t_tmp])
                    P.op("act", lambda e, h=h, lb=lb: e.activation(out=CB[:, h, :], in_=tA[:], func=AF.Exp, scale=lb), r=[t_tmp], pw=[t_tb])
                    P.op("dve", lambda e: e.tensor_scalar(out=tA[:, 0:1], in0=iop[:], scalar1=-1.0, scalar2=127.0, op0=ALU.mult, op1=ALU.add), r=[t_io], w=[t_tmp])
                    P.op("act", lambda e, h=h, lf=lf: e.activation(out=pc[:, 0, h:h + 1], in_=tA[:, 0:1], func=AF.Exp, scale=lf), r=[t_tmp], pw=[t_tb])
                    P.op("act", lambda e, h=h, lb=lb: e.activation(out=pc[:, 1, h:h + 1], in_=iop[:], func=AF.Exp, scale=lb), r=[t_io], pw=[t_tb])
                    P.op("act", lambda e, h=h, lf=lf: e.activation(out=pc[:, 2, h:h + 1], in_=lf, func=AF.Exp, scale=128.0), r=[t_lg], pw=[t_tb])
                    P.op("act", lambda e, h=h, lb=lb: e.activation(out=pc[:, 3, h:h + 1], in_=lb, func=AF.Exp, scale=128.0), r=[t_lg], pw=[t_tb])
                gnb = st.enter_context(sbt(nc, "rt_gn", [128, D], F32)); t_gn = Tr()
                P.dma("sp", gnb[:], ret_gn_g[l].partition_broadcast(128), t_gn, w=[t_gn])

                Rall = st.enter_context(sbt(nc, "rt_Rall", [128, T, 1024], BF16)); t_Rall = Tr()
                Rb = st.enter_context(sbt(nc, "rt_Rb", [128, RH, 256], F32)); t_Rb = Tr()
                Sf = st.enter_context(sbt(nc, "rt_Sf", [128, RH, 256], F32))
                Sfb = st.enter_context(sbt(nc, "rt_Sfb", [128, RH, 256], BF16)); t_Sf = Tr()
                P.op("pool", lambda e: e.memset(Rb[:], 0.0), w=[t_Rb])
                P.op("pool", lambda e: e.memset(Rall[:, T - 1, :], 0.0), w=[t_Rall])
                P.op("pool", lambda e: e.memset(Sf[:], 0.0), w=[t_Sf])
                P.op("pool", lambda e: e.memset(Sfb[:], 0.0), pw=[t_Sf])
                kin = Ring(nc, st, "rt_k", [128, 512], BF16, 3)
                vin = Ring(nc, st, "rt_v", [128, 1024], BF16, 3)
                qin = Ring(nc, st, "rt_q", [128, 512], BF16, 2)
                gin = Ring(nc, st, "rt_g", [128, 1024], F32, 2)
                kc_ = Ring(nc, st, "rt_kc", [128, RH, 128], BF16, 2)
                for c in range(T - 1, 0, -1):
                    tok = slice(c * 128, (c + 1) * 128)
                    k, ktr = kin.next(); v, vtr = vin.next()
                    P.dma("sp", k[:], rkd[s, tok, :], ktr, r=[dtr[("rkd", s)]], w=[ktr])
                    P.dma("sp", v[:], rvd[s, tok, :], vtr, r=[dtr[("rvd", s)]], w=[vtr])
                    kb, kbtr = kc_.next()
                    for h in range(RH):
                        if h % 2:
                            P.op("dve", lambda e, kb=kb, k=k, h=h: e.tensor_scalar(out=kb[:, h, :], in0=k[:, h * 128:(h + 1) * 128], scalar1=pc[:, 1, h:h + 1], scalar2=None, op0=ALU.mult),
                                 r=[ktr, t_tb], pw=[kbtr])
                        else:
                            P.op("act", lambda e, kb=kb, k=k, h=h: e.mul(out=kb[:, h, :], in_=k[:, h * 128:(h + 1) * 128], mul=pc[:, 1, h:h + 1]),
                                 r=[ktr, t_tb], w=[kbtr] if h == 0 else (), pw=[kbtr] if h else ())
                    for h in range(RH):
                        bk = h // 2
                        P.mm(ps[bk][:, (h % 2) * 256:(h % 2) * 256 + 256], kb[:, h, :], v[:, h * 256:(h + 1) * 256], True, True, [kbtr, vtr], pst[bk], first=(h % 2 == 0))
                    for h in range(RH):
                        bk = h // 2
                        P.op("dve", lambda e, h=h, bk=bk: e.scalar_tensor_tensor(out=Rb[:, h, :], in0=Rb[:, h, :], scalar=pc[:, 3, h:h + 1], in1=ps[bk][:, (h % 2) * 256:(h % 2) * 256 + 256], op0=ALU.mult, op1=ALU.add),
                             r=[pst[bk], t_tb], w=[t_Rb])
                    P.op("act", lambda e, c=c: e.copy(out=Rall[:, c - 1, :], in_=Rb[:].rearrange("p h e -> p (h e)")), r=[t_Rb], pw=[t_Rall])
                qT3 = Ring(nc, st, "rt_qT", [128, 3, RH, 128], BF16, 2)
                kTr = Ring(nc, st, "rt_kT", [128, RH, 128], BF16, 2)
                stm = Ring(nc, st, "rt_stm", [128, RH, 128], BF16, 2)
                bnr = Ring(nc, st, "rt_bn", [128, RH, 8], F32, 2)
                onr = Ring(nc, st, "rt_on", [128, D], F32, 2)
                gtd_ = Ring(nc, st, "rt_gt", [128, D], BF16, 2)
                gTr = Ring(nc, st, "rt_gT", [128, 8, 128], BF16, 2)
                for c in range(T):
                    tok = slice(c * 128, (c + 1) * 128)
                    k, ktr = kin.next(); v, vtr = vin.next(); q, qtr = qin.next(); g_, gtr = gin.next()
                    P.dma("sp", q[:], rqd[s, tok, :], qtr, r=[dtr[("rqd", s)]], w=[qtr])
                    P.dma("sp", k[:], rkd[s, tok, :], ktr, r=[dtr[("rkd", s)]], w=[ktr])
                    P.dma("sp", v[:], rvd[s, tok, :], vtr, r=[dtr[("rvd", s)]], w=[vtr])
                    P.dma("sp", g_[:], sgd[s, tok, :], gtr, r=[dtr[("sgd", s)]], w=[gtr])
                    pv = ps[0][:].bitcast(BF16)
                    for h in range(RH):
                        P.tp(pv[:, h * 128:(h + 1) * 128], q[:, h * 128:(h + 1) * 128], ident[:], [qtr, ctr], pst[0], h == 0)
                    for h in range(RH):
                        P.tp(pv[:, 512 + h * 128:512 + (h + 1) * 128], k[:, h * 128:(h + 1) * 128], ident[:], [ktr], pst[0], False)
                    qT, qTtr = qT3.next(); kT, kTtr = kTr.next()
                    pq = pv[:, 0:512].rearrange("p (h c) -> p h c", h=RH)
                    P.op("act", lambda e, qT=qT, pq=pq: e.copy(out=qT[:, 0], in_=pq), r=[pst[0]], w=[qTtr])
                    P.op("dve", lambda e, qT=qT, pq=pq: e.tensor_tensor(out=qT[:, 1], in0=pq, in1=CF[:], op=ALU.mult), r=[pst[0], t_tb], pw=[qTtr])
                    P.op("dve", lambda e, qT=qT, pq=pq: e.tensor_tensor(out=qT[:, 2], in0=pq, in1=CB[:], op=ALU.mult), r=[pst[0], t_tb], pw=[qTtr])
                    P.op("act", lambda e, kT=kT, pv=pv: e.copy(out=kT[:], in_=pv[:, 512:1024].rearrange("p (h c) -> p h c", h=RH)), r=[pst[0]], w=[kTtr])
                    kf, kftr = kc_.next()
                    for h in range(RH):
                        P.op("act", lambda e, kf=kf, k=k, h=h: e.mul(out=kf[:, h, :], in_=k[:, h * 128:(h + 1) * 128], mul=pc[:, 0, h:h + 1]),
                             r=[ktr, t_tb], w=[kftr] if h == 0 else (), pw=[kftr] if h else ())
                    for h in range(RH):
                        P.mm(ps[1][:, h * 128:(h + 1) * 128], kT[:, h, :], qT[:, 0, h, :], True, True, [kTtr, qTtr], pst[1], first=(h == 0))
                    sm_, smtr = stm.next()
                    P.op("dve", lambda e, sm_=sm_: e.tensor_tensor(out=sm_[:], in0=ps[1][:].rearrange("p (h c) -> p h c", h=RH), in1=DT[:], op=ALU.mult), r=[pst[1], t_tb], w=[smtr])
                    for h in range(RH):
                        bk = 2 + h // 2
                        oc = slice((h % 2) * 256, (h % 2) * 256 + 256)
                        P.mm(ps[bk][:, oc], sm_[:, h, :], v[:, h * 256:(h + 1) * 256], True, False, [smtr, vtr], pst[bk], first=(h % 2 == 0))
                        P.mm(ps[bk][:, oc], qT[:, 1, h, :], Sfb[:, h, :], False, False, [qTtr, t_Sf], pst[bk])
                        P.mm(ps[bk][:, oc], qT[:, 2, h, :], Rall[:, c, h * 256:(h + 1) * 256], False, True, [qTtr, t_Rall], pst[bk])
                    for h in range(RH):
                        bk = 4 + h // 2
                        oc = slice((h % 2) * 256, (h % 2) * 256 + 256)
                        P.mm(ps[bk][:, oc], kf[:, h, :], v[:, h * 256:(h + 1) * 256], True, True, [kftr, vtr], pst[bk], first=(h % 2 == 0))
                    for h in range(RH):
                        bk = 4 + h // 2
                        oc = slice((h % 2) * 256, (h % 2) * 256 + 256)
                        P.op("dve", lambda e, h=h, bk=bk, oc=oc: e.scalar_tensor_tensor(out=Sf[:, h, :], in0=Sf[:, h, :], scalar=pc[:, 2, h:h + 1], in1=ps[bk][:, oc], op0=ALU.mult, op1=ALU.add),
                             r=[pst[bk], t_tb], w=[t_Sf])
                    P.op("act", lambda e: e.copy(out=Sfb[:], in_=Sf[:]), r=[t_Sf], w=[t_Sf])
                    if debug and c == 1:
                        dO = st.enter_context(sbt(nc, "dbgO", [128, 1024], F32)); t_dO = Tr()
                        P.op("dve", lambda e: e.tensor_copy(out=dO[:, 0:512], in_=ps[2][:]), r=[pst[2]], w=[t_dO])
                        P.op("dve", lambda e: e.tensor_copy(out=dO[:, 512:1024], in_=ps[3][:]), r=[pst[3]], pw=[t_dO])
                        P.dma("pool", dbg_O[:, :], dO[:], t_dO, r=[t_dO])
                        P.dma("pool", dbg_qT[:, :], qT[:].rearrange("p a h c -> p (a h c)"), qTtr, r=[qTtr])
                        P.dma("pool", dbg_kT[:, :], kT[:].rearrange("p h c -> p (h c)"), kTtr, r=[kTtr])
                        P.dma("pool", dbg_sm[:, :], sm_[:].rearrange("p h c -> p (h c)"), smtr, r=[smtr])
                        P.dma("pool", dbg_kf[:, :], kf[:].rearrange("p h c -> p (h c)"), kftr, r=[kftr])
                        P.dma("pool", dbg_DT[:, :], DT[:].rearrange("p h c -> p (h c)"), t_dO, r=[t_tb])
                        P.dma("pool", dbg_CF[:, :], CF[:].rearrange("p h c -> p (h c)"), t_dO, r=[t_tb])
                        P.dma("pool", dbg_CB[:, :], CB[:].rearrange("p h c -> p (h c)"), t_dO, r=[t_tb])
                        P.dma("pool", dbg_pc[:, :], pc[:].rearrange("p a h -> p (a h)"), t_dO, r=[t_tb])
                        P.dma("pool", dbg_Sf[:, :], Sf[:].rearrange("p h e -> p (h e)"), t_dO, r=[t_Sf])
                        P.dma("pool", dbg_R[:, :], Rall[:, c, :], t_dO, r=[t_Rall])
                    bn, bntr = bnr.next()
                    on, ontr = onr.next()
                    for h in range(RH):
                        bk = 2 + h // 2
                        oc = slice((h % 2) * 256, (h % 2) * 256 + 256)
                        P.op("dve", lambda e, bn=bn, h=h, bk=bk, oc=oc: e.bn_stats(out=bn[:, h, 0:6], in_=ps[bk][:, oc]), r=[pst[bk]], w=[bntr] if h == 0 else (), pw=[bntr] if h else ())
                    for h in range(RH):
                        P.op("dve", lambda e, bn=bn, h=h: e.bn_aggr(out=bn[:, h, 6:8], in_=bn[:, h, 0:6]), r=[bntr], pw=[bntr])
                    P.op("act", lambda e, bn=bn: e.activation(out=bn[:, :, 0], in_=bn[:, :, 7], func=AF.Sqrt, bias=epsc[:, 0:1], scale=1.0), r=[bntr, ctr], pw=[bntr])
                    P.op("dve", lambda e, bn=bn: e.reciprocal(out=bn[:, :, 1], in_=bn[:, :, 0]), r=[bntr], pw=[bntr])
                    for h in range(RH):
                        bk = 2 + h // 2
                        oc = slice((h % 2) * 256, (h % 2) * 256 + 256)
                        P.op("dve", lambda e, on=on, bn=bn, h=h, bk=bk, oc=oc: e.tensor_scalar(out=on[:, h * 256:(h + 1) * 256], in0=ps[bk][:, oc], scalar1=bn[:, h, 6:7], scalar2=bn[:, h, 1:2], op0=ALU.subtract, op1=ALU.mult),
                             r=[pst[bk], bntr], w=[ontr] if h == 0 else (), pw=[ontr] if h else ())
                    P.op("pool", lambda e, on=on: e.tensor_tensor(out=on[:], in0=on[:], in1=gnb[:], op=ALU.mult), r=[ontr, t_gn], w=[ontr])
                    gt, gttr = gtd_.next()
                    P.op("dve", lambda e, gt=gt, on=on, g_=g_: e.tensor_tensor(out=gt[:], in0=on[:], in1=g_[:], op=ALU.mult), r=[ontr, gtr], w=[gttr])
                    pv6 = ps[6][:].bitcast(BF16)
                    for kc in range(8):
                        P.tp(pv6[:, kc * 128:(kc + 1) * 128], gt[:, kc * 128:(kc + 1) * 128], ident[:], [gttr, ctr], pst[6], kc == 0)
                    gT, gTtr = gTr.next()
                    P.op("act", lambda e, gT=gT, pv6=pv6: e.copy(out=gT[:], in_=pv6.rearrange("p (k c) -> p k c", k=8)), r=[pst[6]], w=[gTtr])
                    P.dma("pool", rtTd[s, :, :, tok].rearrange("k p c -> p k c"), gT[:], gTtr, r=[gTtr], pw=[dtr[("rtTd", s)]])

                P.barrier()
        def postnorm_residual(bA, bB, xt, xtr_, gb, t_g, o, otr, sqr, stt):
            sq_, sqtr = sqr.next()
            sm, smtr = stt.next()
            P.op("act", lambda e: e.activation(out=sq_[:, 0:512], in_=ps[bA][:], func=AF.Square), r=[pst[bA]], w=[sqtr])
            P.op("act", lambda e: e.activation(out=sq_[:, 512:1024], in_=ps[bB][:], func=AF.Square), r=[pst[bB]], pw=[sqtr])
            P.op("dve", lambda e: e.reduce_sum(out=sm[:, 2:3], in_=sq_[:], axis=mybir.AxisListType.X), r=[sqtr], w=[smtr])
            P.op("act", lambda e: e.activation(out=sm[:, 3:4], in_=sm[:, 2:3], func=AF.Sqrt, bias=epsc[:, 0:1], scale=1.0 / D), r=[smtr, ctr], pw=[smtr])
            P.op("dve", lambda e: e.reciprocal(out=sm[:, 3:4], in_=sm[:, 3:4]), r=[smtr], pw=[smtr])
            P.op("dve", lambda e: e.scalar_tensor_tensor(out=o[:, 0:512], in0=ps[bA][:], scalar=sm[:, 3:4], in1=gb[:, 0:512], op0=ALU.mult, op1=ALU.mult), r=[pst[bA], smtr, t_g], w=[otr])
            P.op("dve", lambda e: e.scalar_tensor_tensor(out=o[:, 512:1024], in0=ps[bB][:], scalar=sm[:, 3:4], in1=gb[:, 512:1024], op0=ALU.mult, op1=ALU.mult), r=[pst[bB], smtr], pw=[otr])
            P.op("pool", lambda e: e.tensor_tensor(out=o[:], in0=o[:], in1=xt[:], op=ALU.add), r=[xtr_], w=[otr])

        def stage_M(l, s, xsrc, xtr):
            with ExitStack() as st:
                wmo = st.enter_context(sbt(nc, "m_wmo", [128, 4, D], BF16))
                wpw = st.enter_context(sbt(nc, "m_wpw", [128, 4, D], BF16))
                wro = st.enter_context(sbt(nc, "m_wro", [128, 8, D], BF16))
                wou = st.enter_context(sbt(nc, "m_wou", [128, 8, D], BF16))
                gb = st.enter_context(sbt(nc, "m_gb", [128, D], F32))
                t_w = Tr(); t_g = Tr()
                P.dma("sp", wmo[:], wb["mo"][l].rearrange("(kc p) n -> p kc n", p=128), t_w, r=[wb_tr[("mo", l)]], w=[t_w])
                P.dma("sp", wpw[:], wb["pw"][l].rearrange("(kc p) n -> p kc n", p=128), t_w, r=[wb_tr[("pw", l)]], pw=[t_w])
                P.dma("sp", wro[:], wb["ro"][l].rearrange("(kc p) n -> p kc n", p=128), t_w, r=[wb_tr[("ro", l)]], pw=[t_w])
                P.dma("sp", wou[:], wb["wo"][l].rearrange("(kc p) n -> p kc n", p=128), t_w, r=[wb_tr[("wo", l)]], pw=[t_w])
                P.dma("sp", gb[:], ln_mix_post[l].partition_broadcast(128), t_g, w=[t_g])
                oTr = Ring(nc, st, "m_oT", [128, 4, 512], BF16, 2)
                cTr = Ring(nc, st, "m_cT", [128, 4, 512], BF16, 2)
                rTr = Ring(nc, st, "m_rT", [128, 8, 512], BF16, 2)
                gtr_ = Ring(nc, st, "m_gt", [128, 3, 512], F32, 3)
                mt = Ring(nc, st, "m_t", [128, 2, 512], F32, 2)
                mgr = Ring(nc, st, "m_mg", [128, 8, 512], BF16, 2)
                xr = Ring(nc, st, "m_x", [128, D], F32, 2)
                outr = Ring(nc, st, "m_o", [128, D], F32, 2)
                sqr = Ring(nc, st, "m_sq", [128, D], F32, 2)
                stt = Ring(nc, st, "m_st", [128, 8], F32, 3)
                for tb in range(NB):
                    blk = slice(tb * 512, (tb + 1) * 512)
                    o_, otr_ = oTr.next(); c_, ctr_ = cTr.next(); r_, rtr_ = rTr.next()
                    P.dma("sp", o_[:], oTd[s, :, :, blk].rearrange("k p c -> p k c"), otr_, r=[dtr[("oTd", s)]], w=[otr_])
                    P.dma("sp", c_[:], cvTd[s, :, :, blk].rearrange("k p c -> p k c"), ctr_, r=[dtr[("cvTd", s)]], w=[ctr_])
                    P.dma("sp", r_[:], rtTd[s, :, :, blk].rearrange("k p c -> p k c"), rtr_, r=[dtr[("rtTd", s)]], w=[rtr_])
                    mg, mgtr = mgr.next()
                    for rc in range(8):
                        cs = slice(rc * 128, (rc + 1) * 128)
                        gt, gttr = gtr_.next()
                        for b in range(3):
                            P.dma("sp", gt[:, b, :], gtd[s, b * 8 + rc, :, blk], gttr, r=[dtr[("gtd", s)]], w=[gttr] if b == 0 else (), pw=[gttr] if b else ())
                        b0 = (rc % 2) * 3
                        for kc in range(4):
                            P.mm(ps[b0][:], wmo[:, kc, cs], o_[:, kc, :], kc == 0, kc == 3, [t_w, otr_], pst[b0])
                        for kc in range(4):
                            P.mm(ps[b0 + 1][:], wpw[:, kc, cs], c_[:, kc, :], kc == 0, kc == 3, [t_w, ctr_], pst[b0 + 1])
                        for kc in range(8):
                            P.mm(ps[b0 + 2][:], wro[:, kc, cs], r_[:, kc, :], kc == 0, kc == 7, [t_w, rtr_], pst[b0 + 2])
                        t_, ttr = mt.next()
                        P.op("dve", lambda e, t_=t_, gt=gt, b0=b0: e.tensor_tensor(out=t_[:, 0, :], in0=ps[b0][:], in1=gt[:, 0, :], op=ALU.mult), r=[pst[b0], gttr], w=[ttr])
                        P.op("dve", lambda e, t_=t_, gt=gt, b0=b0: e.tensor_tensor(out=t_[:, 1, :], in0=ps[b0 + 1][:], in1=gt[:, 1, :], op=ALU.mult), r=[pst[b0 + 1], gttr], pw=[ttr])
                        P.op("pool", lambda e, t_=t_: e.tensor_tensor(out=t_[:, 0, :], in0=t_[:, 0, :], in1=t_[:, 1, :], op=ALU.add), r=[ttr], w=[ttr])
                        P.op("dve", lambda e, t_=t_, gt=gt, b0=b0: e.tensor_tensor(out=t_[:, 1, :], in0=ps[b0 + 2][:], in1=gt[:, 2, :], op=ALU.mult), r=[pst[b0 + 2], gttr], w=[ttr])
                        P.op("pool", lambda e, t_=t_, mg=mg, rc=rc: e.tensor_tensor(out=mg[:, rc, :], in0=t_[:, 0, :], in1=t_[:, 1, :], op=ALU.add), r=[ttr], w=[mgtr] if rc == 0 else (), pw=[mgtr] if rc else ())
                    for tt in range(4):
                        t = tb * 4 + tt
                        tok = slice(t * 128, (t + 1) * 128)
                        xt, xtr_ = xr.next()
                        P.dma("sp", xt[:], xsrc[s, tok, :], xtr_, r=[xtr[s]], w=[xtr_])
                        for nb in range(2):
                            for kc in range(8):
                                P.mm(ps[6 + nb][:], mg[:, kc, tt * 128:(tt + 1) * 128], wou[:, kc, nb * 512:(nb + 1) * 512], kc == 0, kc == 7, [mgtr, t_w], pst[6 + nb])
                        o, otr = outr.next()
                        postnorm_residual(6, 7, xt, xtr_, gb, t_g, o, otr, sqr, stt)
                        P.dma("pool", x1d[s, tok, :], o[:], otr, r=[otr], pw=[dtr[("x1d", s)]])

                P.barrier()
        def stage_F(l, s, ydst, ykey):
            with ExitStack() as st:
                wg = st.enter_context(sbt(nc, "f_wg", [128, 8, FH], BF16))
                wu = st.enter_context(sbt(nc, "f_wu", [128, 8, FH], BF16))
                t_w = Tr(); t_g = Tr()
                gpre = st.enter_context(sbt(nc, "f_gpre", [128, D], F32))
                gpost = st.enter_context(sbt(nc, "f_gpost", [128, D], F32))
                P.dma("sp", wg[:], wb["fg"][l].rearrange("(kc p) n -> p kc n", p=128), t_w, r=[wb_tr[("fg", l)]], w=[t_w])
                P.dma("sp", wu[:], wb["fu"][l].rearrange("(kc p) n -> p kc n", p=128), t_w, r=[wb_tr[("fu", l)]], pw=[t_w])
                P.dma("sp", gpre[:], ln_ffn_pre[l].partition_broadcast(128), t_g, w=[t_g])
                P.dma("sp", gpost[:], ln_ffn_post[l].partition_broadcast(128), t_g, pw=[t_g])
                wdr = Ring(nc, st, "f_wd", [128, D], BF16, 8)
                xr = Ring(nc, st, "f_x", [128, D], F32, 8)
                hb = Ring(nc, st, "f_hb", [128, D], BF16, 2)
                sqr = Ring(nc, st, "f_sq", [128, D], F32, 2)
                stt = Ring(nc, st, "f_st", [128, 8], F32, 8)
                h2T = Ring(nc, st, "f_h2T", [128, 8, 512], BF16, 2)
                hid = Ring(nc, st, "f_hid", [128, 22, 512], BF16, 1)
                sgr = Ring(nc, st, "f_sg", [128, 512], F32, 2)
                outr = Ring(nc, st, "f_o", [128, D], F32, 2)
                wdv = wb["fd"][l]
                def prep(tb):
                    hT_, hTtr = h2T.next()
                    xts = []
                    for tt in range(4):
                        t = tb * 4 + tt
                        tok = slice(t * 128, (t + 1) * 128)
                        xt, xtr_ = xr.next()
                        xts.append((xt, xtr_))
                        P.dma("sp", xt[:], x1d[s, tok, :], xtr_, r=[dtr[("x1d", s)]], w=[xtr_])
                        sq_, sqtr = sqr.next()
                        sm, smtr = stt.next()
                        ssq4(xt, sq_, sm, xtr_, sqtr, smtr)
                        rstd_from_ssq(sm[:, 0:1], sm[:, 1:2], D, [smtr])
                        h, htr = hb.next()
                        P.op("dve", lambda e, xt=xt, h=h, sm=sm: e.scalar_tensor_tensor(out=h[:], in0=xt[:], scalar=sm[:, 1:2], in1=gpre[:], op0=ALU.mult, op1=ALU.mult), r=[xtr_, smtr, t_g], w=[htr])
                        bk = 4 + (tt % 2)
                        pv = ps[bk][:].bitcast(BF16)
                        for kc in range(8):
                            P.tp(pv[:, kc * 128:(kc + 1) * 128], h[:, kc * 128:(kc + 1) * 128], ident[:], [htr, ctr], pst[bk], kc == 0)
                        P.op("act", lambda e, hT_=hT_, pv=pv, tt=tt: e.copy(out=hT_[:, :, tt * 128:(tt + 1) * 128], in_=pv.rearrange("p (k c) -> p k c", k=8)), r=[pst[bk]],
                             w=[hTtr] if tt == 0 else (), pw=[hTtr] if tt else ())
                    return hT_, hTtr, xts

                nxt = prep(0)
                for tb in range(NB):
                    hT_, hTtr, xts = nxt
                    hd, hdtr = hid.next()
                    for hc in range(22):
                        bg, bu = (4, 5) if hc % 2 == 0 else (6, 7)
                        cs = slice(hc * 128, (hc + 1) * 128)
                        for kc in range(8):
                            P.mm(ps[bg][:], wg[:, kc, cs], hT_[:, kc, :], kc == 0, kc == 7, [t_w, hTtr], pst[bg])
                        for kc in range(8):
                            P.mm(ps[bu][:], wu[:, kc, cs], hT_[:, kc, :], kc == 0, kc == 7, [t_w, hTtr], pst[bu])
                        sg, sgtr = sgr.next()
                        P.op("act", lambda e, sg=sg, bg=bg: e.activation(out=sg[:], in_=ps[bg][:], func=AF.Silu), r=[pst[bg]], w=[sgtr])
                        P.op("dve", lambda e, hd=hd, sg=sg, bu=bu, hc=hc: e.tensor_tensor(out=hd[:, hc, :], in0=ps[bu][:], in1=sg[:], op=ALU.mult), r=[pst[bu], sgtr],
                             w=[hdtr] if hc == 0 else (), pw=[hdtr] if hc else ())
                    if tb + 1 < NB:
                        nxt = prep(tb + 1)
                    for hc in range(22):
                        wd, wdtr = wdr.next()
                        P.dma("sp", wd[:], wdv[hc * 128:(hc + 1) * 128, :], wdtr, r=[wb_tr[("fd", l)]], w=[wdtr])
                        for tt in range(4):
                            for nb in range(2):
                                bk = tt * 2 + nb
                                P.mm(ps[bk][:], hd[:, hc, tt * 128:(tt + 1) * 128], wd[:, nb * 512:(nb + 1) * 512], hc == 0, hc == 21, [hdtr, wdtr], pst[bk])
                    for tt in (2, 3, 0, 1):
                        t = tb * 4 + tt
                        tok = slice(t * 128, (t + 1) * 128)
                        xt, xtr_ = xts[tt]
                        o, otr = outr.next()
                        postnorm_residual(tt * 2, tt * 2 + 1, xt, xtr_, gpost, t_g, o, otr, sqr, stt)
                        P.dma("pool", ydst[s, tok, :], o[:], otr, r=[otr], pw=[dtr[(ykey, s)]])
                P.barrier()

        for l in range(nlayers):
            if "W" in stages:
                stage_W(l)
        for s in range(NS):
            if "T" in stages:
                stage_T(s)
        xin_tr = [Tr() for _ in range(NS)]
        for l in range(nlayers):
            last = (l == nlayers - 1)
            for s in range(NS):
                if l == 0:
                    xsrc, xtr = x_in, xin_tr
                else:
                    xsrc, xtr = xLd, [dtr[("xLd", s_)] for s_ in range(NS)]
                if "N" in stages:
                    stage_NP(l, s, xsrc, xtr)
                if "A" in stages:
                    stage_A(l, s, with_conv=("C" in stages))
                if "R" in stages:
                    stage_R(l, s)
                if "M" in stages:
                    stage_M(l, s, xsrc, xtr)
                if "F" in stages:
                    stage_F(l, s, y_out if last else xLd, "y" if last else "xLd")
        P.wait_all("sp", [dtr[("y", s)] for s in range(NS)])
        for en in ("pe", "act", "dve", "pool"):
            E = P.E[en]
            if E.cnt:
                if P.E["sp"].waited.get(E.key, 0) < E.cnt:
                    P.E["sp"].e.wait_ge(E.sem, E.cnt)
    return nc


def rope_consts():
    def inv(dim):
        return (np.float32(10000.0) ** (-(np.arange(0, dim, 2, dtype=np.float32)) / np.float32(dim))).astype(np.float32)
    im, ir = inv(32), inv(128)
    c = np.zeros((2, 160), np.float32)
    c[0] = np.concatenate([im, im, ir, ir])
    c[1] = np.concatenate([np.zeros(16), np.full(16, np.pi / 2), np.zeros(64), np.full(64, np.pi / 2)]).astype(np.float32)
    return c


WEIGHT_NAMES = ["ln_mix_pre", "ln_mix_post", "ln_ffn_pre", "ln_ffn_post", "w_in", "mla_q_norm", "mla_w_uq", "mla_kv_norm",
                "mla_w_ukv", "mla_w_o", "conv_w_dw", "conv_b_dw", "conv_ln_g", "conv_ln_b", "conv_w_pw", "ret_decay_logits",
                "ret_gn_g", "ret_w_o", "w_out", "ffn_w_gate", "ffn_w_up", "ffn_w_down"]


def kernel(**inputs):
    x = np.ascontiguousarray(np.asarray(inputs["x"], dtype=np.float32))
    pos = np.ascontiguousarray(np.asarray(inputs["positions"], dtype=np.int32))
    B, S, _ = x.shape
    ncores = 8
    NS = B // ncores
    nc = build(S, NS)
    shared = {k: np.ascontiguousarray(np.asarray(inputs[k], dtype=np.float32)) for k in WEIGHT_NAMES}
    shared["rope_consts"] = rope_consts()
    in_maps = []
    for c in range(ncores):
        m = dict(shared)
        m["x"] = x[c * NS:(c + 1) * NS]
        m["positions"] = pos[c * NS:(c + 1) * NS]
        in_maps.append(m)
    res = run_bass_kernel_spmd(nc, in_maps, core_ids=list(range(ncores)))
    return np.concatenate([r["y"] for r in res.results], axis=0).astype(np.float32)
```

```python
import numpy as np
import concourse.bass as bass
import concourse.mybir as mybir
from concourse.bass_utils import run_bass_kernel_spmd
from contextlib import ExitStack

F32 = mybir.dt.float32
BF16 = mybir.dt.bfloat16
I32 = mybir.dt.int32
AF = mybir.ActivationFunctionType
ALU = mybir.AluOpType

D = 1024
L = 2
NH = 8
RH = 4
FH = 2816
INC = 7584
EPS = 1e-6
SAME_ENGINE_SYNC = False
import os
SUB = os.environ.get("SUB", "lcrg")
CUT = float(os.environ.get("CUT", "9"))
TWO_PI = float(2 * np.pi)
PI = float(np.pi)

O_CQ, O_CKV, O_KPE, O_CONV, O_RQ, O_RK, O_RV, O_RG, O_GATE = 0, 256, 384, 416, 1440, 1952, 2464, 3488, 4512


class Tr:
    __slots__ = ("w", "r", "sem", "cnt", "name", "excl")

    def __init__(self, name="", excl=False):
        self.excl = excl
        self.w = {}
        self.r = {}
        self.sem = None
        self.cnt = 0
        self.name = name


class Eng:
    def __init__(self, e, sem, key):
        self.e = e
        self.sem = sem
        self.key = key
        self.cnt = 0
        self.waited = {}


class Prog:
    def __init__(self, nc, es):
        self.nc = nc
        self.es = es
        self.sems = {}
        self.nsem = 0
        self.E = {}
        self.pool = []
        self.live = []
        self.uid = 0
        for name, e in (("pe", nc.tensor), ("act", nc.scalar), ("dve", nc.vector), ("pool", nc.gpsimd), ("sp", nc.sync)):
            s, k = self.newsem("e_" + name)
            self.E[name] = Eng(e, s, k)

    def newsem(self, name):
        s = self.es.enter_context(self.nc.semaphore(name + "_%d" % self.nsem))
        k = self.nsem
        self.nsem += 1
        self.sems[k] = s
        return s, k

    def _waits(self, E, r, w, pw):
        need = {}
        for t in r:
            for k, v in t.w.items():
                if need.get(k, 0) < v:
                    need[k] = v
            if t.excl:
                for k, v in t.r.items():
                    if need.get(k, 0) < v:
                        need[k] = v
        for t in w:
            for d in (t.w, t.r):
                for k, v in d.items():
                    if need.get(k, 0) < v:
                        need[k] = v
        for t in pw:
            for d in (t.w, t.r):
                for k, v in d.items():
                    if need.get(k, 0) < v:
                        need[k] = v
        for k, v in need.items():
            if k == E.key and not SAME_ENGINE_SYNC:
                continue
            if E.waited.get(k, 0) < v:
                E.e.wait_ge(self.sems[k], v)
                E.waited[k] = v

    def op(self, en, fn, r=(), w=(), pw=()):
        E = self.E[en]
        self._waits(E, r, w, pw)
        ins = fn(E.e)
        E.cnt += 1
        ins.then_inc(E.sem, 1)
        for t in r:
            t.r[E.key] = E.cnt
        for t in w:
            t.w = {E.key: E.cnt}
            t.r = {}
        for t in pw:
            t.w[E.key] = E.cnt

    def dma(self, q, out, in_, sb, r=(), w=(), pw=()):
        Q = self.E[q]
        self._waits(Q, r, w, pw)
        if sb.sem is None:
            if self.pool:
                sb.sem, sb.cnt = self.pool.pop()
            else:
                sb.sem = self.newsem("d")
            self.live.append(sb)
        sem, key = sb.sem
        sb.cnt += 16
        Q.e.dma_start(out=out, in_=in_).then_inc(sem, 16)
        for t in r:
            t.r[key] = sb.cnt
        for t in w:
            t.w = {key: sb.cnt}
            t.r = {}
        for t in pw:
            t.w[key] = sb.cnt

    def barrier(self):
        sp = self.E["sp"]
        for en in ("pe", "act", "dve", "pool"):
            E = self.E[en]
            if sp.waited.get(E.key, 0) < E.cnt:
                sp.e.wait_ge(E.sem, E.cnt)
                sp.waited[E.key] = E.cnt
        for sb in self.live:
            sem, key = sb.sem
            if sp.waited.get(key, 0) < sb.cnt:
                sp.e.wait_ge(sem, sb.cnt)
                sp.waited[key] = sb.cnt
        if not hasattr(self, "bar"):
            self.bar = self.newsem("bar")
            self.barcnt = 0
        self.barcnt += 1
        sp.e.sem_inc(self.bar[0], 1)
        for en in ("pe", "act", "dve", "pool"):
            E = self.E[en]
            E.e.wait_ge(self.bar[0], self.barcnt)
            for en2 in ("pe", "act", "dve", "pool"):
                E.waited[self.E[en2].key] = max(E.waited.get(self.E[en2].key, 0), self.E[en2].cnt)
            for sb in self.live:
                E.waited[sb.sem[1]] = max(E.waited.get(sb.sem[1], 0), sb.cnt)
        for sb in self.live:
            self.pool.append((sb.sem, sb.cnt))
            sb.sem = None
        self.live = []

    def wait_all(self, en, trs):
        E = self.E[en]
        self._waits(E, trs, (), ())

    def mm(self, out, lhsT, rhs, start, stop, r, tr, first=None):
        if first is None:
            first = start
        self.op("pe", lambda e: e.matmul(out, lhsT, rhs, start=bool(start), stop=bool(stop)), r=r,
                w=[tr] if first else (), pw=() if first else [tr])

    def tp(self, out, in_, ident, r, tr, first):
        self.op("pe", lambda e: e.transpose(out, in_, ident), r=r, w=[tr] if first else (), pw=() if first else [tr])


_UID = [0]


def sbt(nc, name, shape, dt):
    _UID[0] += 1
    return nc.sbuf_tensor("%s_u%d" % (name, _UID[0]), shape, dt)


class Ring:
    cnt = [0]

    def __init__(self, nc, es, name, shape, dt, n):
        Ring.cnt[0] += 1
        self.t = [es.enter_context(sbt(nc, "%s_%d_%d" % (name, Ring.cnt[0], i), shape, dt)) for i in range(n)]
        self.tr = [Tr("%s%d" % (name, i)) for i in range(n)]
        self.i = 0
        self.n = n

    def next(self):
        i = self.i
        self.i = (i + 1) % self.n
        return self.t[i], self.tr[i]


def build(S, NS, debug=False, nlayers=L, stages="WTNACRMF"):
    T = S // 128
    NB = S // 512
    nc = bass.Bass("TRN2", target_bir_lowering=False)

    def din(name, shape, dt=F32):
        return nc.dram_tensor(name, shape, dt, kind="ExternalInput").ap()

    x_in = din("x", [NS, S, D])
    pos_in = din("positions", [NS, S], I32)
    ln_mix_pre = din("ln_mix_pre", [L, D]); ln_mix_post = din("ln_mix_post", [L, D])
    ln_ffn_pre = din("ln_ffn_pre", [L, D]); ln_ffn_post = din("ln_ffn_post", [L, D])
    w_in = din("w_in", [L, D, INC])
    mla_q_norm = din("mla_q_norm", [L, 256]); mla_w_uq = din("mla_w_uq", [L, 256, 768])
    mla_kv_norm = din("mla_kv_norm", [L, 128]); mla_w_ukv = din("mla_w_ukv", [L, 128, 1024])
    mla_w_o = din("mla_w_o", [L, 512, D])
    conv_w_dw = din("conv_w_dw", [L, 31, 512]); conv_b_dw = din("conv_b_dw", [L, 512])
    conv_ln_g = din("conv_ln_g", [L, 512]); conv_ln_b = din("conv_ln_b", [L, 512])
    conv_w_pw = din("conv_w_pw", [L, 512, D])
    ret_decay_logits = din("ret_decay_logits", [L, 2, RH])
    ret_gn_g = din("ret_gn_g", [L, D]); ret_w_o = din("ret_w_o", [L, D, D])
    w_out = din("w_out", [L, D, D])
    ffn_w_gate = din("ffn_w_gate", [L, D, FH]); ffn_w_up = din("ffn_w_up", [L, D, FH]); ffn_w_down = din("ffn_w_down", [L, FH, D])
    cst_in = din("rope_consts", [2, 160])
    y_out = nc.dram_tensor("y", [NS, S, D], F32, kind="ExternalOutput").ap()

    skind = "ExternalOutput" if debug else "Internal"

    def scr(name, shape, dt):
        return nc.dram_tensor(name, shape, dt, kind=skind).ap()

    wb = {
        "in": scr("wb_in", [L, D, INC], BF16), "uq": scr("wb_uq", [L, 256, 768], BF16), "ukv": scr("wb_ukv", [L, 128, 1024], BF16),
        "mo": scr("wb_mo", [L, 512, D], BF16), "pw": scr("wb_pw", [L, 512, D], BF16), "ro": scr("wb_ro", [L, D, D], BF16),
        "wo": scr("wb_wo", [L, D, D], BF16), "fg": scr("wb_fg", [L, D, FH], BF16), "fu": scr("wb_fu", [L, D, FH], BF16),
        "fd": scr("wb_fd", [L, FH, D], BF16),
    }
    wsrc = {"in": w_in, "uq": mla_w_uq, "ukv": mla_w_ukv, "mo": mla_w_o, "pw": conv_w_pw, "ro": ret_w_o, "wo": w_out,
            "fg": ffn_w_gate, "fu": ffn_w_up, "fd": ffn_w_down}
    wb_tr = {(k, l): Tr("wb_%s%d" % (k, l)) for k in wb for l in range(L)}

    tabd = scr("tabd", [NS, 128, T * 160], F32)
    QTd = scr("QTd", [NS, NH, 96, S], BF16); KTd = scr("KTd", [NS, NH, 96, S], BF16)
    Vd = scr("Vd", [NS, S, NH * 65], BF16)
    oTd = scr("oTd", [NS, 4, 128, S], BF16)
    aTd = scr("aTd", [NS, 4, 128, S], F32); cvTd = scr("cvTd", [NS, 4, 128, S], BF16)
    rqd = scr("rqd", [NS, S, 512], BF16); rkd = scr("rkd", [NS, S, 512], BF16)
    rvd = scr("rvd", [NS, S, 1024], BF16); sgd = scr("sgd", [NS, S, 1024], F32)
    rtTd = scr("rtTd", [NS, 8, 128, S], BF16)
    gtd = scr("gtd", [NS, 24, 128, S], F32)
    hTd = scr("hTd", [NS, 8, 128, S], BF16)
    x1d = scr("x1d", [NS, S, D], F32)
    xLd = scr("xLd", [NS, S, D], F32)
    if debug:
        dbg_qT = scr("dbg_qT", [128, 3 * RH * 128], BF16); dbg_kT = scr("dbg_kT", [128, RH * 128], BF16)
        dbg_sm = scr("dbg_sm", [128, RH * 128], BF16); dbg_DT = scr("dbg_DT", [128, RH * 128], F32)
        dbg_CF = scr("dbg_CF", [128, RH * 128], F32); dbg_CB = scr("dbg_CB", [128, RH * 128], F32)
        dbg_O = scr("dbg_O", [128, 1024], F32); dbg_Sf = scr("dbg_Sf", [128, 1024], F32); dbg_R = scr("dbg_R", [128, 1024], BF16)
        dbg_pc = scr("dbg_pc", [128, 16], F32); dbg_kf = scr("dbg_kf", [128, 512], BF16)
    dtr = {}
    for nm in ("hTd", "tabd", "QTd", "KTd", "Vd", "oTd", "aTd", "cvTd", "rqd", "rkd", "rvd", "sgd", "rtTd", "gtd", "x1d", "xLd", "y"):
        for s in range(NS):
            dtr[(nm, s)] = Tr("%s_%d" % (nm, s))

    with ExitStack() as es:
        P = Prog(nc, es)
        ps = [es.enter_context(nc.psum_tensor("psb%d" % i, [128, 512], F32)) for i in range(8)]
        pst = [Tr("ps%d" % i, excl=True) for i in range(8)]
        identf = es.enter_context(sbt(nc, "identf", [128, 128], F32))
        ident = es.enter_context(sbt(nc, "ident", [128, 128], BF16))
        onesf = es.enter_context(sbt(nc, "onesf", [128, 128], F32))
        epsc = es.enter_context(sbt(nc, "epsc", [128, 1], F32))
        ctr = Tr("consts")
        P.op("pool", lambda e: e.memset(identf[:], 0.0), w=[ctr])
        P.op("pool", lambda e: e.affine_select(out=identf[:], in_=identf[:], pattern=[[-1, 128]], compare_op=ALU.not_equal,
                                               fill=1.0, base=0, channel_multiplier=1), w=[ctr])
        P.op("pool", lambda e: e.tensor_copy(out=ident[:], in_=identf[:]), r=[ctr], pw=[ctr])
        P.op("pool", lambda e: e.memset(onesf[:], 1.0), pw=[ctr])
        P.op("pool", lambda e: e.memset(epsc[:], EPS), pw=[ctr])

        def rstd_from_ssq(ssq, rstd, n, trs):
            P.op("act", lambda e: e.activation(out=rstd, in_=ssq, func=AF.Sqrt, bias=epsc[:, 0:1], scale=1.0 / n), r=trs + [ctr], w=trs)
            P.op("dve", lambda e: e.reciprocal(out=rstd, in_=rstd), r=trs, w=trs)

        def ssq4(xt, sqt, sm, xtr_, sqtr, smtr):
            P.op("act", lambda e: e.activation(out=sqt[:], in_=xt[:], func=AF.Square), r=[xtr_], w=[sqtr])
            P.op("dve", lambda e: e.reduce_sum(out=sm[:, 0:1], in_=sqt[:], axis=mybir.AxisListType.X), r=[sqtr], w=[smtr])

        def stage_W(l):
            with ExitStack() as st:
                stf = Ring(nc, st, "wstf", [128, 2048], F32, 3)
                stb = Ring(nc, st, "wstb", [128, 2048], BF16, 3)
                i = 0
                for k in ("in", "uq", "ukv", "mo", "pw", "ro", "wo", "fg", "fu", "fd"):
                    src = wsrc[k][l]
                    dst = wb[k][l]
                    K, N = src.shape
                    for kc in range(K // 128):
                        for c0 in range(0, N, 2048):
                            w = min(2048, N - c0)
                            f, ftr = stf.next()
                            b, btr = stb.next()
                            P.dma("sp", f[:, 0:w], src[kc * 128:(kc + 1) * 128, c0:c0 + w], ftr, w=[ftr])
                            en = ("dve", "pool", "act")[i % 3]
                            if en == "act":
                                P.op(en, lambda e, f=f, b=b, w=w: e.copy(out=b[:, 0:w], in_=f[:, 0:w]), r=[ftr], w=[btr])
                            else:
                                P.op(en, lambda e, f=f, b=b, w=w: e.tensor_copy(out=b[:, 0:w], in_=f[:, 0:w]), r=[ftr], w=[btr])
                            P.dma("pool", dst[kc * 128:(kc + 1) * 128, c0:c0 + w], b[:, 0:w], btr, r=[btr], pw=[wb_tr[(k, l)]])
                            i += 1

                P.barrier()
        def stage_T(s):
            with ExitStack() as st:
                posrow = st.enter_context(sbt(nc, "posrow", [2, S], F32))
                posi = st.enter_context(sbt(nc, "posi", [1, S], I32))
                cst = st.enter_context(sbt(nc, "cst", [2, 160], F32))
                tab = st.enter_context(sbt(nc, "tab", [128, T * 160], F32))
                tmpf = st.enter_context(sbt(nc, "tmpf", [128, T * 160], F32))
                tmpi = st.enter_context(sbt(nc, "tmpi", [128, T * 160], I32))
                t_pr, t_pi, t_c, t_tab, t_f, t_i = Tr(), Tr(), Tr(), Tr(), Tr(), Tr()
                P.op("dve", lambda e: e.memset(posrow[:], 1.0), w=[t_pr])
                P.dma("sp", posi[:], pos_in[s:s + 1, :], t_pi, w=[t_pi])
                P.dma("sp", cst[:], cst_in[:, :], t_c, w=[t_c])
                P.op("dve", lambda e: e.tensor_copy(out=posrow[0:1, :], in_=posi[:]), r=[t_pi], pw=[t_pr])
                for t0 in range(0, T, 3):
                    n = min(3, T - t0)
                    bk = (t0 // 3) % 2
                    for j in range(n):
                        t = t0 + j
                        P.mm(ps[bk][:, j * 160:(j + 1) * 160], posrow[0:2, t * 128:(t + 1) * 128], cst[0:2, :], True, True,
                             [t_pr, t_c], pst[bk], first=(j == 0))
                    P.op("dve", lambda e, bk=bk, n=n, t0=t0: e.tensor_copy(out=tab[:, t0 * 160:(t0 + n) * 160], in_=ps[bk][:, 0:n * 160]),
                         r=[pst[bk]], pw=[t_tab])
                W = T * 160
                for c0 in range(0, W, 2560):
                    c1 = min(W, c0 + 2560)
                    a = tab[:, c0:c1]; f = tmpf[:, c0:c1]; ii = tmpi[:, c0:c1]
                    P.op("dve", lambda e, a=a, f=f: e.tensor_scalar(out=f, in0=a, scalar1=1.0 / TWO_PI, scalar2=None, op0=ALU.mult), r=[t_tab], w=[t_f])
                    P.op("dve", lambda e, f=f, ii=ii: e.tensor_copy(out=ii, in_=f), r=[t_f], w=[t_i])
                    P.op("dve", lambda e, f=f, ii=ii: e.tensor_copy(out=f, in_=ii), r=[t_i], w=[t_f])
                    P.op("dve", lambda e, a=a, f=f: e.scalar_tensor_tensor(out=a, in0=f, scalar=-6.28125, in1=a, op0=ALU.mult, op1=ALU.add), r=[t_f], w=[t_tab])
                    P.op("dve", lambda e, a=a, f=f: e.scalar_tensor_tensor(out=a, in0=f, scalar=-(TWO_PI - 6.28125), in1=a, op0=ALU.mult, op1=ALU.add), r=[t_f], w=[t_tab])
                    P.op("dve", lambda e, a=a, f=f: e.tensor_scalar(out=f, in0=a, scalar1=PI, scalar2=-TWO_PI, op0=ALU.is_gt, op1=ALU.mult), r=[t_tab], w=[t_f])
                    P.op("dve", lambda e, a=a, f=f: e.tensor_tensor(out=a, in0=a, in1=f, op=ALU.add), r=[t_f], w=[t_tab])
                    P.op("dve", lambda e, a=a, f=f: e.tensor_scalar(out=f, in0=a, scalar1=-PI, scalar2=TWO_PI, op0=ALU.is_lt, op1=ALU.mult), r=[t_tab], w=[t_f])
                    P.op("dve", lambda e, a=a, f=f: e.tensor_tensor(out=a, in0=a, in1=f, op=ALU.add), r=[t_f], w=[t_tab])
                    P.op("dve", lambda e, a=a: e.tensor_scalar(out=a, in0=a, scalar1=-3.1415925, scalar2=3.1415925, op0=ALU.max, op1=ALU.min), r=[t_tab], w=[t_tab])
                    P.op("act", lambda e, a=a: e.activation(out=a, in_=a, func=AF.Sin), r=[t_tab], w=[t_tab])
                P.dma("pool", tabd[s], tab[:], t_tab, r=[t_tab], w=[dtr[("tabd", s)]])

                P.barrier()
        def stage_NP(l, s, xsrc, xtr):
            with ExitStack() as st:
                hT = st.enter_context(sbt(nc, "hT", [128, 8, S], BF16)); t_hT = Tr("hT")
                gbc = st.enter_context(sbt(nc, "np_gbc", [128, D], F32)); t_g = Tr()
                gq = st.enter_context(sbt(nc, "np_gq", [128, 384], F32))
                tab = st.enter_context(sbt(nc, "np_tab", [128, T, 160], F32)); t_tab = Tr()
                P.dma("sp", gbc[:], ln_mix_pre[l].partition_broadcast(128), t_g, w=[t_g])
                P.dma("sp", gq[:, 0:256], mla_q_norm[l].partition_broadcast(128), t_g, pw=[t_g])
                P.dma("sp", gq[:, 256:384], mla_kv_norm[l].partition_broadcast(128), t_g, pw=[t_g])
                P.dma("sp", tab[:].rearrange("p t c -> p (t c)"), tabd[s], t_tab, r=[dtr[("tabd", s)]], w=[t_tab])
                xr = Ring(nc, st, "np_x", [128, D], F32, 3)
                hb = Ring(nc, st, "np_hb", [128, D], BF16, 2)
                sq = Ring(nc, st, "np_sq", [128, D], F32, 2)
                stt = Ring(nc, st, "np_st", [128, 8], F32, 4)
                for t in range(T):
                    xt, xtr_ = xr.next()
                    P.dma("sp", xt[:], xsrc[s, t * 128:(t + 1) * 128, :], xtr_, r=[xtr[s]], w=[xtr_])
                    sqt, sqtr = sq.next()
                    sm, smtr = stt.next()
                    ssq4(xt, sqt, sm, xtr_, sqtr, smtr)
                    rstd_from_ssq(sm[:, 0:1], sm[:, 1:2], D, [smtr])
                    h, htr = hb.next()
                    P.op("dve", lambda e, xt=xt, h=h, sm=sm: e.scalar_tensor_tensor(out=h[:], in0=xt[:], scalar=sm[:, 1:2], in1=gbc[:], op0=ALU.mult, op1=ALU.mult),
                         r=[xtr_, smtr, t_g], w=[htr])
                    bk = t % 2
                    pv = ps[bk][:].bitcast(BF16)
                    for kc in range(8):
                        P.tp(pv[:, kc * 128:(kc + 1) * 128], h[:, kc * 128:(kc + 1) * 128], ident[:], [htr, ctr], pst[bk], kc == 0)
                    en = "act" if t % 2 == 0 else "dve"
                    if en == "act":
                        P.op("act", lambda e, pv=pv, t=t: e.copy(out=hT[:, :, t * 128:(t + 1) * 128], in_=pv.rearrange("p (k c) -> p k c", k=8)), r=[pst[bk]], pw=[t_hT])
                    else:
                        P.op("dve", lambda e, pv=pv, t=t: e.tensor_copy(out=hT[:, :, t * 128:(t + 1) * 128], in_=pv.rearrange("p (k c) -> p k c", k=8)), r=[pst[bk]], pw=[t_hT])

                win = wb["in"][l].rearrange("(kc p) n -> p kc n", p=128)
                wtr_in = wb_tr[("in", l)]
                for kc in range(8):
                    P.dma("sp", hTd[s, kc], hT[:, kc, :], t_hT, r=[t_hT], pw=[dtr[("hTd", s)]])

                for g in ([ExitStack()] if "l" in SUB else []):
                    wl = g.enter_context(sbt(nc, "wl", [128, 8, 416], BF16)); t_wl = Tr()
                    wuq = g.enter_context(sbt(nc, "wuq", [128, 2, 768], BF16)); t_wuq = Tr()
                    wk = g.enter_context(sbt(nc, "wk", [128, 8, 64], BF16))
                    wv = g.enter_context(sbt(nc, "wv", [128, 8, 64], BF16)); t_wkv = Tr()
                    P.dma("sp", wl[:], win[:, :, 0:416], t_wl, r=[wtr_in], w=[t_wl])
                    P.dma("sp", wuq[:], wb["uq"][l].rearrange("(kc p) n -> p kc n", p=128), t_wuq, r=[wb_tr[("uq", l)]], w=[t_wuq])
                    ukv_v = wb["ukv"][l].rearrange("p (h c) -> p h c", h=8)
                    P.dma("sp", wk[:], ukv_v[:, :, 0:64], t_wkv, r=[wb_tr[("ukv", l)]], w=[t_wkv])
                    P.dma("sp", wv[:], ukv_v[:, :, 64:128], t_wkv, pw=[t_wkv])
                    lat = Ring(nc, g, "lat", [128, 416], BF16, 2)
                    sqj = Ring(nc, g, "lsq", [128, 256], F32, 2)
                    stl = Ring(nc, g, "lst", [128, 8], F32, 3)
                    tmpr = Ring(nc, g, "ltmp", [128, 4, 8, 16], F32, 2)
                    latT = Ring(nc, g, "latT", [128, 4, 128], BF16, 2)
                    qsb = Ring(nc, g, "qsb", [128, 8, 128], BF16, 2)
                    for qt_, qttr_ in zip(qsb.t, qsb.tr):
                        P.op("pool", lambda e, qt_=qt_: e.memset(qt_[:], 0.0), w=[qttr_])
                    qTb = Ring(nc, g, "qTb", [96, 8, 512], BF16, 2)
                    kTb = Ring(nc, g, "kTb", [96, 8, 512], BF16, 2)
                    ckb = Ring(nc, g, "ckb", [128, 512], BF16, 2)
                    kpb = Ring(nc, g, "kpb", [32, 512], BF16, 2)
                    vsb = Ring(nc, g, "vsb", [128, 8, 65], BF16, 2)
                    for vt, vtr in zip(vsb.t, vsb.tr):
                        P.op("pool", lambda e, vt=vt: e.memset(vt[:], 1.0), w=[vtr])
                    for tb in range(NB):
                        qT, qTtr = qTb.next()
                        kT, kTtr = kTb.next()
                        ck, cktr = ckb.next()
                        kp, kptr = kpb.next()
                        for tt in range(4):
                            t = tb * 4 + tt
                            tok = slice(t * 128, (t + 1) * 128)
                            bl = 2 if t % 2 == 0 else 0
                            bt = 3 if t % 2 == 0 else 1
                            for kc in range(8):
                                P.mm(ps[bl][:, 0:416], hT[:, kc, tok], wl[:, kc, :], kc == 0, kc == 7, [t_hT, t_wl], pst[bl])
                            la, latr = lat.next()
                            sj, sjtr = sqj.next()
                            sm, smtr = stl.next()
                            P.op("pool", lambda e, sm=sm: e.memset(sm[:], 0.0), w=[smtr])
                            P.op("act", lambda e, sj=sj, sm=sm: e.activation(out=sj[:, 0:256], in_=ps[bl][:, 0:256], func=AF.Square, accum_out=sm[:, 0:1]), r=[pst[bl], smtr], w=[sjtr], pw=[smtr])
                            P.op("act", lambda e, sj=sj, sm=sm: e.activation(out=sj[:, 0:128], in_=ps[bl][:, 256:384], func=AF.Square, accum_out=sm[:, 1:2]), r=[pst[bl], smtr], w=[sjtr], pw=[smtr])
                            P.op("act", lambda e, sm=sm: e.activation(out=sm[:, 2:3], in_=sm[:, 0:1], func=AF.Sqrt, bias=epsc[:, 0:1], scale=1.0 / 256), r=[smtr, ctr], pw=[smtr])
                            P.op("act", lambda e, sm=sm: e.activation(out=sm[:, 3:4], in_=sm[:, 1:2], func=AF.Sqrt, bias=epsc[:, 0:1], scale=1.0 / 128), r=[smtr], pw=[smtr])
                            P.op("dve", lambda e, sm=sm: e.reciprocal(out=sm[:, 4:6], in_=sm[:, 2:4]), r=[smtr], pw=[smtr])
                            P.op("dve", lambda e, la=la, sm=sm: e.scalar_tensor_tensor(out=la[:, 0:256], in0=ps[bl][:, 0:256], scalar=sm[:, 4:5], in1=gq[:, 0:256], op0=ALU.mult, op1=ALU.mult),
                                 r=[pst[bl], smtr, t_g], w=[latr])
                            P.op("dve", lambda e, la=la, sm=sm: e.scalar_tensor_tensor(out=la[:, 256:384], in0=ps[bl][:, 256:384], scalar=sm[:, 5:6], in1=gq[:, 256:384], op0=ALU.mult, op1=ALU.mult),
                                 r=[pst[bl], smtr, t_g], pw=[latr])
                            tm, tmtr = tmpr.next()
                            sn = tab[:, t, 0:16]; cs = tab[:, t, 16:32]
                            x1 = ps[bl][:, 384:400]; x2 = ps[bl][:, 400:416]
                            P.op("dve", lambda e, tm=tm, x1=x1, cs=cs: e.tensor_tensor(out=tm[:, 0, 0, :], in0=x1, in1=cs, op=ALU.mult), r=[pst[bl], t_tab], w=[tmtr])
                            P.op("dve", lambda e, tm=tm, x2=x2, sn=sn: e.tensor_tensor(out=tm[:, 1, 0, :], in0=x2, in1=sn, op=ALU.mult), r=[pst[bl]], pw=[tmtr])
                            P.op("dve", lambda e, tm=tm, x2=x2, cs=cs: e.tensor_tensor(out=tm[:, 2, 0, :], in0=x2, in1=cs, op=ALU.mult), r=[pst[bl]], pw=[tmtr])
                            P.op("dve", lambda e, tm=tm, x1=x1, sn=sn: e.tensor_tensor(out=tm[:, 3, 0, :], in0=x1, in1=sn, op=ALU.mult), r=[pst[bl]], pw=[tmtr])
                            P.op("dve", lambda e, tm=tm, la=la: e.tensor_tensor(out=la[:, 384:400], in0=tm[:, 0, 0, :], in1=tm[:, 1, 0, :], op=ALU.subtract), r=[tmtr], pw=[latr])
                            P.op("dve", lambda e, tm=tm, la=la: e.tensor_tensor(out=la[:, 400:416], in0=tm[:, 2, 0, :], in1=tm[:, 3, 0, :], op=ALU.add), r=[tmtr], pw=[latr])
                            if CUT < 2:
                                continue
                            pv = ps[bt][:].bitcast(BF16)
                            for j in range(3):
                                P.tp(pv[:, j * 128:(j + 1) * 128], la[:, j * 128:(j + 1) * 128], ident[:], [latr, ctr], pst[bt], j == 0)
                            P.tp(pv[0:32, 384:512], la[:, 384:416], ident[:], [latr], pst[bt], False)
                            lT, lTtr = latT.next()
                            P.op("dve", lambda e, lT=lT, pv=pv: e.tensor_copy(out=lT[:, 0:3, :], in_=pv[:, 0:384].rearrange("p (j c) -> p j c", j=3)), r=[pst[bt]], w=[lTtr])
                            P.op("dve", lambda e, ck=ck, pv=pv, tt=tt: e.tensor_copy(out=ck[:, tt * 128:(tt + 1) * 128], in_=pv[:, 256:384]), r=[pst[bt]], pw=[cktr] if tt else (), w=[cktr] if tt == 0 else ())
                            P.op("dve", lambda e, kp=kp, pv=pv, tt=tt: e.tensor_copy(out=kp[:, tt * 128:(tt + 1) * 128], in_=pv[0:32, 384:512]), r=[pst[bt]], pw=[kptr] if tt else (), w=[kptr] if tt == 0 else ())
                            if CUT < 2.1:
                                continue
                            for kc in range(2):
                                P.mm(ps[4][:, 0:480], lT[:, kc, :], wuq[:, kc, 0:480], kc == 0, kc == 1, [lTtr, t_wuq], pst[4])
                            for kc in range(2):
                                P.mm(ps[5][:, 0:288], lT[:, kc, :], wuq[:, kc, 480:768], kc == 0, kc == 1, [lTtr, t_wuq], pst[5])
                            if CUT < 2.3:
                                continue
                            q, qtr = qsb.next()
                            tm2, tm2tr = tmpr.next()
                            first = True
                            for (bk, h0, nh) in ((4, 0, 5), (5, 5, 3)):
                                pq = ps[bk][:, 0:nh * 96].rearrange("p (h d) -> p h d", h=nh)
                                qo = q[:, h0:h0 + nh, :]
                                P.op("act", lambda e, pq=pq, qo=qo: e.copy(out=qo[:, :, 0:64], in_=pq[:, :, 0:64]), r=[pst[bk]], pw=[qtr])
                                if CUT < 2.5:
                                    continue
                                csb = tab[:, t:t + 1, 16:32].to_broadcast([128, nh, 16])
                                snb = tab[:, t:t + 1, 0:16].to_broadcast([128, nh, 16])
                                x1 = pq[:, :, 64:80]; x2 = pq[:, :, 80:96]
                                tv = tm2[:, :, h0:h0 + nh, :]
                                P.op("dve", lambda e, tv=tv, x1=x1, csb=csb: e.tensor_tensor(out=tv[:, 0], in0=x1, in1=csb, op=ALU.mult), r=[pst[bk], t_tab], w=[tm2tr] if first else (), pw=() if first else [tm2tr])
                                P.op("dve", lambda e, tv=tv, x2=x2, snb=snb: e.tensor_tensor(out=tv[:, 1], in0=x2, in1=snb, op=ALU.mult), r=[pst[bk]], pw=[tm2tr])
                                P.op("dve", lambda e, tv=tv, x2=x2, csb=csb: e.tensor_tensor(out=tv[:, 2], in0=x2, in1=csb, op=ALU.mult), r=[pst[bk]], pw=[tm2tr])
                                P.op("dve", lambda e, tv=tv, x1=x1, snb=snb: e.tensor_tensor(out=tv[:, 3], in0=x1, in1=snb, op=ALU.mult), r=[pst[bk]], pw=[tm2tr])
                                P.op("dve", lambda e, tv=tv, qo=qo: e.tensor_tensor(out=qo[:, :, 64:80], in0=tv[:, 0], in1=tv[:, 1], op=ALU.subtract), r=[tm2tr], pw=[qtr])
                                P.op("dve", lambda e, tv=tv, qo=qo: e.tensor_tensor(out=qo[:, :, 80:96], in0=tv[:, 2], in1=tv[:, 3], op=ALU.add), r=[tm2tr], pw=[qtr])
                                first = False
                            if CUT < 2.7:
                                continue
                            pv6 = ps[6][:].bitcast(BF16)
                            for h in range(8):
                                P.tp(pv6[:, h * 128:(h + 1) * 128], q[:, h, :], ident[:], [qtr, ctr], pst[6], h == 0)
                            if CUT < 2.9:
                                continue
                            P.op("act", lambda e, qT=qT, pv6=pv6, tt=tt: e.copy(out=qT[:, :, tt * 128:(tt + 1) * 128], in_=pv6[0:96, :].rearrange("p (h c) -> p h c", h=8)),
                                 r=[pst[6]], w=[qTtr] if tt == 0 else (), pw=[qTtr] if tt else ())
                            if CUT < 4:
                                continue
                            P.mm(ps[7][:, 0:512], lT[:, 2, :], wv[:].rearrange("p h c -> p (h c)"), True, True, [lTtr, t_wkv], pst[7])
                            v, vtr = vsb.next()
                            P.op("dve", lambda e, v=v: e.tensor_copy(out=v[:, :, 0:64], in_=ps[7][:, 0:512].rearrange("p (h c) -> p h c", h=8)), r=[pst[7]], w=[vtr])
                            P.dma("sp", Vd[s, tok, :], v[:].rearrange("p h c -> p (h c)"), vtr, r=[vtr], pw=[dtr[("Vd", s)]])
                        if CUT < 5:
                            continue
                        blk = slice(tb * 512, (tb + 1) * 512)
                        for h in range(8):
                            bk = 2 + (h % 2) * 5
                            P.mm(ps[bk][0:64, :], wk[:, h, :], ck[:], True, True, [cktr, t_wkv], pst[bk])
                            if h % 2 == 0:
                                P.op("act", lambda e, kT=kT, h=h, bk=bk: e.copy(out=kT[0:64, h, :], in_=ps[bk][0:64, :]), r=[pst[bk]], w=[kTtr] if h == 0 else (), pw=[kTtr] if h else ())
                            else:
                                P.op("dve", lambda e, kT=kT, h=h, bk=bk: e.tensor_copy(out=kT[0:64, h, :], in_=ps[bk][0:64, :]), r=[pst[bk]], pw=[kTtr])
                        for h in range(8):
                            if h % 2 == 0:
                                P.op("act", lambda e, kT=kT, kp=kp, h=h: e.copy(out=kT[64:96, h, :], in_=kp[:, :]), r=[kptr], pw=[kTtr])
                            else:
                                P.op("dve", lambda e, kT=kT, kp=kp, h=h: e.tensor_copy(out=kT[64:96, h, :], in_=kp[:, :]), r=[kptr], pw=[kTtr])
                        P.dma("sp", QTd[s, :, :, blk].rearrange("h p c -> p h c"), qT[:], qTtr, r=[qTtr], pw=[dtr[("QTd", s)]])
                        P.dma("sp", KTd[s, :, :, blk].rearrange("h p c -> p h c"), kT[:], kTtr, r=[kTtr], pw=[dtr[("KTd", s)]])

                    P.barrier()
                    g.close()
                for g in ([ExitStack()] if "c" in SUB else []):
                    wc = g.enter_context(sbt(nc, "wc", [128, 8, 1024], BF16)); t_wc = Tr()
                    P.dma("sp", wc[:], win[:, :, O_CONV:O_CONV + 1024], t_wc, r=[wtr_in], w=[t_wc])
                    sgr = Ring(nc, g, "cv_sg", [128, 512], F32, 2)
                    aor = Ring(nc, g, "cv_a", [128, 512], F32, 3)
                    for tb in range(NB):
                        blk = slice(tb * 512, (tb + 1) * 512)
                        for j in range(4):
                            ba, bg = (2, 3) if j % 2 == 0 else (4, 5)
                            for kc in range(8):
                                P.mm(ps[ba][:], wc[:, kc, j * 128:(j + 1) * 128], hT[:, kc, blk], kc == 0, kc == 7, [t_hT, t_wc], pst[ba])
                            for kc in range(8):
                                P.mm(ps[bg][:], wc[:, kc, 512 + j * 128:512 + (j + 1) * 128], hT[:, kc, blk], kc == 0, kc == 7, [t_hT, t_wc], pst[bg])
                            sg, sgtr = sgr.next()
                            ao, aotr = aor.next()
                            P.op("act", lambda e, sg=sg, bg=bg: e.activation(out=sg[:], in_=ps[bg][:], func=AF.Sigmoid), r=[pst[bg]], w=[sgtr])
                            P.op("dve", lambda e, ao=ao, sg=sg, ba=ba: e.tensor_tensor(out=ao[:], in0=ps[ba][:], in1=sg[:], op=ALU.mult), r=[pst[ba], sgtr], w=[aotr])
                            P.dma("sp", aTd[s, j, :, blk], ao[:], aotr, r=[aotr], pw=[dtr[("aTd", s)]])

                    P.barrier()
                    g.close()
                for g in ([ExitStack()] if "r" in SUB else []):
                    wr = Ring(nc, g, "wr", [128, 8, 512], BF16, 2)
                    tmq = Ring(nc, g, "r_tm", [128, 4, 4, 64], F32, 2)
                    o16 = Ring(nc, g, "r_o16", [128, 512], BF16, 3)
                    o32 = Ring(nc, g, "r_o32", [128, 512], F32, 3)
                    for (kind, c0) in (("q", O_RQ), ("k", O_RK), ("v0", O_RV), ("v1", O_RV + 512), ("g0", O_RG), ("g1", O_RG + 512)):
                        w, wtr = wr.next()
                        P.dma("sp", w[:], win[:, :, c0:c0 + 512], wtr, r=[wtr_in], w=[wtr])
                        for t in range(T):
                            tok = slice(t * 128, (t + 1) * 128)
                            bk = 2 + (t % 4)
                            for kc in range(8):
                                P.mm(ps[bk][:], hT[:, kc, tok], w[:, kc, :], kc == 0, kc == 7, [t_hT, wtr], pst[bk])
                            if kind in ("q", "k"):
                                o, otr = o16.next()
                                tm, tmtr = tmq.next()
                                pq = ps[bk][:].rearrange("p (h d) -> p h d", h=4)
                                ov = o[:].rearrange("p (h d) -> p h d", h=4)
                                csb = tab[:, t:t + 1, 96:160].to_broadcast([128, 4, 64])
                                snb = tab[:, t:t + 1, 32:96].to_broadcast([128, 4, 64])
                                x1 = pq[:, :, 0:64]; x2 = pq[:, :, 64:128]
                                P.op("dve", lambda e, tm=tm, x1=x1, csb=csb: e.tensor_tensor(out=tm[:, 0], in0=x1, in1=csb, op=ALU.mult), r=[pst[bk], t_tab], w=[tmtr])
                                P.op("dve", lambda e, tm=tm, x2=x2, snb=snb: e.tensor_tensor(out=tm[:, 1], in0=x2, in1=snb, op=ALU.mult), r=[pst[bk]], pw=[tmtr])
                                P.op("dve", lambda e, tm=tm, x2=x2, csb=csb: e.tensor_tensor(out=tm[:, 2], in0=x2, in1=csb, op=ALU.mult), r=[pst[bk]], pw=[tmtr])
                                P.op("dve", lambda e, tm=tm, x1=x1, snb=snb: e.tensor_tensor(out=tm[:, 3], in0=x1, in1=snb, op=ALU.mult), r=[pst[bk]], pw=[tmtr])
                                if kind == "q":
                                    P.op("dve", lambda e, tm=tm, ov=ov: e.tensor_tensor(out=ov[:, :, 0:64], in0=tm[:, 0], in1=tm[:, 1], op=ALU.subtract), r=[tmtr], w=[otr])
                                    P.op("dve", lambda e, tm=tm, ov=ov: e.tensor_tensor(out=ov[:, :, 64:128], in0=tm[:, 2], in1=tm[:, 3], op=ALU.add), r=[tmtr], pw=[otr])
                                else:
                                    sc = float(128 ** -0.5)
                                    P.op("dve", lambda e, tm=tm: e.tensor_tensor(out=tm[:, 0], in0=tm[:, 0], in1=tm[:, 1], op=ALU.subtract), r=[tmtr], w=[tmtr])
                                    P.op("dve", lambda e, tm=tm: e.tensor_tensor(out=tm[:, 2], in0=tm[:, 2], in1=tm[:, 3], op=ALU.add), r=[tmtr], w=[tmtr])
                                    P.op("act", lambda e, tm=tm, ov=ov: e.mul(out=ov[:, :, 0:64], in_=tm[:, 0], mul=sc), r=[tmtr], w=[otr])
                                    P.op("act", lambda e, tm=tm, ov=ov: e.mul(out=ov[:, :, 64:128], in_=tm[:, 2], mul=sc), r=[tmtr], pw=[otr])
                                dst = (rqd if kind == "q" else rkd)[s, tok, :]
                                P.dma("sp", dst, o[:], otr, r=[otr], pw=[dtr[("rqd" if kind == "q" else "rkd", s)]])
                            elif kind in ("v0", "v1"):
                                o, otr = o16.next()
                                if t % 2 == 0:
                                    P.op("act", lambda e, o=o, bk=bk: e.copy(out=o[:], in_=ps[bk][:]), r=[pst[bk]], w=[otr])
                                else:
                                    P.op("dve", lambda e, o=o, bk=bk: e.tensor_copy(out=o[:], in_=ps[bk][:]), r=[pst[bk]], w=[otr])
                                half = 0 if kind == "v0" else 512
                                P.dma("sp", rvd[s, tok, half:half + 512], o[:], otr, r=[otr], pw=[dtr[("rvd", s)]])
                            else:
                                o, otr = o32.next()
                                P.op("act", lambda e, o=o, bk=bk: e.activation(out=o[:], in_=ps[bk][:], func=AF.Silu), r=[pst[bk]], w=[otr])
                                half = 0 if kind == "g0" else 512
                                P.dma("sp", sgd[s, tok, half:half + 512], o[:], otr, r=[otr], pw=[dtr[("sgd", s)]])

                    P.barrier()
                    g.close()
                for g in ([ExitStack()] if ("g" in SUB and debug) else []):
                    wr = Ring(nc, g, "wg_", [128, 8, 512], BF16, 2)
                    gor = Ring(nc, g, "g_o", [128, 512], F32, 3)
                    for gg in range(6):
                        w, wtr = wr.next()
                        P.dma("sp", w[:], win[:, :, O_GATE + gg * 512:O_GATE + (gg + 1) * 512], wtr, r=[wtr_in], w=[wtr])
                        for tb in range(NB):
                            blk = slice(tb * 512, (tb + 1) * 512)
                            for j in range(4):
                                bk = 2 + (j % 4)
                                for kc in range(8):
                                    P.mm(ps[bk][:], w[:, kc, j * 128:(j + 1) * 128], hT[:, kc, blk], kc == 0, kc == 7, [t_hT, wtr], pst[bk])
                                o, otr = gor.next()
                                P.op("act", lambda e, o=o, bk=bk: e.activation(out=o[:], in_=ps[bk][:], func=AF.Sigmoid), r=[pst[bk]], w=[otr])
                                P.dma("sp", gtd[s, gg * 4 + j, :, blk], o[:], otr, r=[otr], pw=[dtr[("gtd", s)]])

                    P.barrier()
                    g.close()
                P.barrier()
        def stage_A(l, s, with_conv=True):
            scale = float(96 ** -0.5)
            with ExitStack() as st:
                cgen = stage_C_gen(l, s, st, 7) if with_conv else iter(())
                next(cgen, None)
                V = st.enter_context(sbt(nc, "at_V", [128, T, NH * 65], BF16)); t_V = Tr()
                sel = st.enter_context(sbt(nc, "at_sel", [65, 64], F32)); t_sel = Tr()
                P.op("pool", lambda e: e.memset(sel[:], 0.0), w=[t_sel])
                P.op("pool", lambda e: e.memset(sel[64:65, :], 1.0), pw=[t_sel])
                Vv = Vd[s].rearrange("(t p) c -> p t c", p=128)
                for t0 in range(0, T, 8):
                    P.dma("sp", V[:, t0:t0 + 8, :], Vv[:, t0:t0 + 8, :], t_V, r=[dtr[("Vd", s)]], pw=[t_V] if t0 else (), w=[t_V] if t0 == 0 else ())
                qr = Ring(nc, st, "at_q", [96, S], BF16, 2)
                kr = Ring(nc, st, "at_k", [96, S], BF16, 2)
                pr = Ring(nc, st, "at_p", [128, 512], BF16, 4)
                osb = Ring(nc, st, "at_o", [65, 512], F32, 2)
                rbc = Ring(nc, st, "at_r", [64, 512], F32, 2)
                oT = Ring(nc, st, "at_oT", [64, 512], BF16, 2)
                its = [(h, qb, kt) for h in range(NH) for qb in range(NB) for kt in range(T)]
                PF = 3
                qk = {}
                crate = min(1.0, 1.06 * (139 * NB + 8) / len(its))

                def get_qk(h):
                    if h not in qk:
                        q, qtr = qr.next()
                        k, ktr = kr.next()
                        P.dma("sp", q[:], QTd[s, h], qtr, r=[dtr[("QTd", s)]], w=[qtr])
                        P.dma("sp", k[:], KTd[s, h], ktr, r=[dtr[("KTd", s)]], w=[ktr])
                        qk[h] = (q, qtr, k, ktr)
                    return qk[h]

                def emit_score(i):
                    h, qb, kt = its[i]
                    q, qtr, k, ktr = get_qk(h)
                    sbk = i % 4
                    P.mm(ps[sbk][:], k[:, kt * 128:(kt + 1) * 128], q[:, qb * 512:(qb + 1) * 512], True, True, [qtr, ktr], pst[sbk])

                for i in range(min(PF, len(its))):
                    emit_score(i)
                for i, (h, qb, kt) in enumerate(its):
                    qs = slice(qb * 512, (qb + 1) * 512)
                    ob = 4 + (qb % 2)
                    sbk = i % 4
                    p, ptr = pr.next()
                    P.op("act", lambda e, p=p, sbk=sbk: e.activation(out=p[:], in_=ps[sbk][:], func=AF.Exp, scale=scale), r=[pst[sbk]], w=[ptr])
                    if i + PF < len(its):
                        emit_score(i + PF)
                    P.mm(ps[ob][0:65, :], V[:, kt, h * 65:(h + 1) * 65], p[:], kt == 0, kt == T - 1, [t_V, ptr], pst[ob])
                    if int((i + 1) * crate) > int(i * crate):
                        next(cgen, None)
                    if kt == T - 1:
                        o, otr = osb.next()
                        P.op("dve", lambda e, o=o, ob=ob: e.tensor_copy(out=o[:], in_=ps[ob][0:65, :]), r=[pst[ob]], w=[otr])
                        P.mm(ps[6][0:64, :], sel[:], o[:], True, True, [t_sel, otr], pst[6])
                        rb, rbtr = rbc.next()
                        P.op("dve", lambda e, rb=rb: e.reciprocal(out=rb[:], in_=ps[6][0:64, :]), r=[pst[6]], w=[rbtr])
                        ot, ottr = oT.next()
                        P.op("dve", lambda e, ot=ot, o=o, rb=rb: e.tensor_tensor(out=ot[:], in0=o[0:64, :], in1=rb[:], op=ALU.mult), r=[otr, rbtr], w=[ottr])
                        P.dma("pool", oTd[s, h // 2, (h % 2) * 64:(h % 2) * 64 + 64, qs], ot[:], ottr, r=[ottr], pw=[dtr[("oTd", s)]])
                for _ in cgen:
                    pass
                P.barrier()

        def stage_C_gen(l, s, st, pb):
            cw = st.enter_context(sbt(nc, "cv_cw", [34, 512], F32)); t_cw = Tr()
            cwT = st.enter_context(sbt(nc, "cv_cwT", [128, 4, 34], F32)); t_cwT = Tr()
            P.dma("sp", cw[0:31, :], conv_w_dw[l], t_cw, w=[t_cw])
            P.dma("sp", cw[31:32, :], conv_b_dw[l:l + 1, :], t_cw, pw=[t_cw])
            P.dma("sp", cw[32:33, :], conv_ln_g[l:l + 1, :], t_cw, pw=[t_cw])
            P.dma("sp", cw[33:34, :], conv_ln_b[l:l + 1, :], t_cw, pw=[t_cw])
            for cc in range(4):
                P.tp(ps[pb][:, cc * 34:(cc + 1) * 34], cw[0:34, cc * 128:(cc + 1) * 128], identf[0:34, 0:34], [t_cw, ctr], pst[pb], cc == 0)
            P.op("dve", lambda e: e.tensor_copy(out=cwT[:], in_=ps[pb][:, 0:136].rearrange("p (c j) -> p c j", c=4)), r=[pst[pb]], w=[t_cwT])
            apr = Ring(nc, st, "cv_ap", [128, 4, 542], F32, 2)
            yr = Ring(nc, st, "cv_y", [128, 4, 512], F32, 2)
            sqr = Ring(nc, st, "cv_sq", [128, 512], F32, 2)
            mr = Ring(nc, st, "cv_m", [128, 3, 512], F32, 2)
            tr_ = Ring(nc, st, "cv_t", [128, 512], F32, 2)
            outr = Ring(nc, st, "cv_o", [128, 512], BF16, 3)
            yield
            for tb in range(NB):
                blk = slice(tb * 512, (tb + 1) * 512)
                ap_, aptr = apr.next()
                lo = tb * 512 - 15
                hi = tb * 512 + 512 + 15
                first = True
                if lo < 0:
                    P.op("pool", lambda e: e.memset(ap_[:, :, 0:15], 0.0), w=[aptr])
                    first = False
                if hi > S:
                    P.op("pool", lambda e: e.memset(ap_[:, :, 527:542], 0.0), w=[aptr] if first else (), pw=() if first else [aptr])
                    first = False
                c_lo = max(lo, 0); c_hi = min(hi, S)
                for cc in range(4):
                    P.dma("sp", ap_[:, cc, c_lo - lo:c_hi - lo], aTd[s, cc, :, c_lo:c_hi], aptr, r=[dtr[("aTd", s)]], w=[aptr] if first else (), pw=() if first else [aptr])
                    first = False
                y_, ytr = yr.next()
                for cc in range(4):
                    P.op("dve", lambda e: e.tensor_scalar(out=y_[:, cc, :], in0=ap_[:, cc, 0:512], scalar1=cwT[:, cc, 0:1], scalar2=cwT[:, cc, 31:32], op0=ALU.mult, op1=ALU.add),
                         r=[aptr, t_cwT], w=[ytr] if cc == 0 else (), pw=[ytr] if cc else ())
                    yield
                    for j in range(1, 31):
                        P.op("dve", lambda e: e.scalar_tensor_tensor(out=y_[:, cc, :], in0=ap_[:, cc, j:j + 512], scalar=cwT[:, cc, j:j + 1], in1=y_[:, cc, :], op0=ALU.mult, op1=ALU.add),
                             r=[aptr], pw=[ytr])
                        yield
                for cc in range(4):
                    P.mm(ps[pb][:], onesf[:], y_[:, cc, :], cc == 0, cc == 3, [ytr, ctr], pst[pb])
                m, mtr = mr.next()
                P.op("dve", lambda e: e.tensor_scalar(out=m[:, 0, :], in0=ps[pb][:], scalar1=1.0 / 512, scalar2=None, op0=ALU.mult), r=[pst[pb]], w=[mtr])
                yield
                for cc in range(4):
                    sq_, sqtr = sqr.next()
                    P.op("dve", lambda e: e.tensor_tensor(out=sq_[:], in0=y_[:, cc, :], in1=y_[:, cc, :], op=ALU.mult), r=[ytr], w=[sqtr])
                    P.mm(ps[pb][:], onesf[:], sq_[:], cc == 0, cc == 3, [sqtr, ctr], pst[pb])
                    yield
                P.op("dve", lambda e: e.tensor_tensor(out=m[:, 1, :], in0=m[:, 0, :], in1=m[:, 0, :], op=ALU.mult), r=[mtr], pw=[mtr])
                P.op("dve", lambda e: e.scalar_tensor_tensor(out=m[:, 1, :], in0=ps[pb][:], scalar=1.0 / 512, in1=m[:, 1, :], op0=ALU.mult, op1=ALU.subtract), r=[pst[pb], mtr], pw=[mtr])
                yield
                P.op("act", lambda e: e.activation(out=m[:, 2, :], in_=m[:, 1, :], func=AF.Sqrt, bias=epsc[:, 0:1], scale=1.0), r=[mtr, ctr], pw=[mtr])
                P.op("dve", lambda e: e.reciprocal(out=m[:, 2, :], in_=m[:, 2, :]), r=[mtr], pw=[mtr])
                yield
                for cc in range(4):
                    t_, ttr = tr_.next()
                    P.op("dve", lambda e: e.tensor_tensor(out=t_[:], in0=y_[:, cc, :], in1=m[:, 0, :], op=ALU.subtract), r=[ytr, mtr], w=[ttr])
                    yield
                    P.op("dve", lambda e: e.tensor_tensor(out=t_[:], in0=t_[:], in1=m[:, 2, :], op=ALU.mult), r=[mtr], w=[ttr])
                    o, otr = outr.next()
                    P.op("act", lambda e: e.activation(out=o[:], in_=t_[:], func=AF.Silu, bias=cwT[:, cc, 33:34], scale=cwT[:, cc, 32:33]), r=[ttr, t_cwT], w=[otr])
                    P.dma("pool", cvTd[s, cc, :, blk], o[:], otr, r=[otr], pw=[dtr[("cvTd", s)]])
                    yield

        def stage_R(l, s):
            with ExitStack() as st:
                lgr = st.enter_context(sbt(nc, "rt_lg", [128, 8], F32)); t_lg = Tr()
                iof = st.enter_context(sbt(nc, "rt_iof", [128, 128], F32))
                ioq = st.enter_context(sbt(nc, "rt_ioq", [128, 128], F32))
                iop = st.enter_context(sbt(nc, "rt_iop", [128, 1], F32))
                t_io = Tr()
                tA = st.enter_context(sbt(nc, "rt_tA", [128, 128], F32))
                tB = st.enter_context(sbt(nc, "rt_tB", [128, 128], F32))
                tC = st.enter_context(sbt(nc, "rt_tC", [128, 128], F32)); t_tmp = Tr()
                DT = st.enter_context(sbt(nc, "rt_DT", [128, RH, 128], F32))
                CF = st.enter_context(sbt(nc, "rt_CF", [128, RH, 128], F32))
                CB = st.enter_context(sbt(nc, "rt_CB", [128, RH, 128], F32))
                pc = st.enter_context(sbt(nc, "rt_pc", [128, 4, RH], F32))
                t_tb = Tr()
                P.dma("sp", lgr[:], ret_decay_logits[l].rearrange("a h -> (a h)").partition_broadcast(128), t_lg, w=[t_lg])
                P.op("act", lambda e: e.activation(out=lgr[:], in_=lgr[:], func=AF.Exp, scale=-1.0), r=[t_lg], w=[t_lg])
                P.op("dve", lambda e: e.tensor_scalar(out=lgr[:], in0=lgr[:], scalar1=1.0, scalar2=None, op0=ALU.add), r=[t_lg], w=[t_lg])
                P.op("act", lambda e: e.activation(out=lgr[:], in_=lgr[:], func=AF.Ln), r=[t_lg], w=[t_lg])
                P.op("dve", lambda e: e.tensor_scalar(out=lgr[:], in0=lgr[:], scalar1=-1.0, scalar2=None, op0=ALU.mult), r=[t_lg], w=[t_lg])
                P.op("pool", lambda e: e.iota(iof[:], [[1, 128]], base=0, channel_multiplier=-1, allow_small_or_imprecise_dtypes=True), w=[t_io])
                P.op("pool", lambda e: e.iota(ioq[:], [[1, 128]], base=0, channel_multiplier=0, allow_small_or_imprecise_dtypes=True), pw=[t_io])
                P.op("dve", lambda e: e.tensor_scalar(out=iop[:], in0=iof[:, 0:1], scalar1=-1.0, scalar2=None, op0=ALU.mult), r=[t_io], pw=[t_io])
                for h in range(RH):
                    lf = lgr[:, h:h + 1]; lb = lgr[:, RH + h:RH + h + 1]
                    P.op("dve", lambda e: e.tensor_scalar(out=tA[:], in0=iof[:], scalar1=0.0, scalar2=None, op0=ALU.max), r=[t_io], w=[t_tmp])
                    P.op("act", lambda e, lf=lf: e.activation(out=tA[:], in_=tA[:], func=AF.Exp, scale=lf), r=[t_tmp, t_lg], w=[t_tmp])
                    P.op("dve", lambda e: e.tensor_scalar(out=tB[:], in0=iof[:], scalar1=0.0, scalar2=None, op0=ALU.is_ge), r=[t_io], pw=[t_tmp])
                    P.op("dve", lambda e: e.tensor_tensor(out=tA[:], in0=tA[:], in1=tB[:], op=ALU.mult), r=[t_tmp], w=[t_tmp])
                    P.op("dve", lambda e: e.tensor_scalar(out=tC[:], in0=iof[:], scalar1=-1.0, scalar2=0.0, op0=ALU.mult, op1=ALU.max), r=[t_io], pw=[t_tmp])
                    P.op("act", lambda e, lb=lb: e.activation(out=tC[:], in_=tC[:], func=AF.Exp, scale=lb), r=[t_tmp], w=[t_tmp])
                    P.op("dve", lambda e: e.tensor_scalar(out=tB[:], in0=iof[:], scalar1=0.0, scalar2=None, op0=ALU.is_lt), r=[t_io], w=[t_tmp])
                    P.op("dve", lambda e: e.tensor_tensor(out=tC[:], in0=tC[:], in1=tB[:], op=ALU.mult), r=[t_tmp], w=[t_tmp])
                    P.op("dve", lambda e, h=h: e.tensor_tensor(out=DT[:, h, :], in0=tA[:], in1=tC[:], op=ALU.add), r=[t_tmp], pw=[t_tb])
                    P.op("dve", lambda e: e.tensor_scalar(out=tA[:], in0=ioq[:], scalar1=1.0, scalar2=None, op0=ALU.add), r=[t_io], w=[t_tmp])
                    P.op("act", lambda e, h=h, lf=lf: e.activation(out=CF[:, h, :], in_=tA[:], func=AF.Exp, scale=lf), r=[t_tmp], pw=[t_tb])
                    P.op("dve", lambda e: e.tensor_scalar(out=tA[:], in0=ioq[:], scalar1=-1.0, scalar2=128.0, op0=ALU.mult, op1=ALU.add), r=[t_io], w=[t_tmp])
                    P.op("act", lambda e, h=h, lb=lb: e.activation(out=CB[:, h, :], in_=tA[:], func=AF.Exp, scale=lb), r=[t_tmp], pw=[t_tb])
                    P.op("dve", lambda e: e.tensor_scalar(out=tA[:, 0:1], in0=iop[:], scalar1=-1.0, scalar2=127.0, op0=ALU.mult, op1=ALU.add), r=[t_io], w=[t_tmp])
                    P.op("act", lambda e, h=h, lf=lf: e.activation(out=pc[:, 0, h:h + 1], in_=tA[:, 0:1], func=AF.Exp, scale=lf), r=[t_tmp], pw=[t_tb])
                    P.op("act", lambda e, h=h, lb=lb: e.activation(out=pc[:, 1, h:h + 1], in_=iop[:], func=AF.Exp, scale=lb), r=[t_io], pw=[t_tb])
                    P.op("act", lambda e, h=h, lf=lf: e.activation(out=pc[:, 2, h:h + 1], in_=lf, func=AF.Exp, scale=128.0), r=[t_lg], pw=[t_tb])
                    P.op("act", lambda e, h=h, lb=lb: e.activation(out=pc[:, 3, h:h + 1], in_=lb, func=AF.Exp, scale=128.0), r=[t_lg], pw=[t_tb])
                gnb = st.enter_context(sbt(nc, "rt_gn", [128, D], F32)); t_gn = Tr()
                P.dma("sp", gnb[:], ret_gn_g[l].partition_broadcast(128), t_gn, w=[t_gn])

                Rall = st.enter_context(sbt(nc, "rt_Rall", [128, T, 1024], BF16)); t_Rall = Tr()
                Rb = st.enter_context(sbt(nc, "rt_Rb", [128, RH, 256], F32)); t_Rb = Tr()
                Sf = st.enter_context(sbt(nc, "rt_Sf", [128, RH, 256], F32))
                Sfb = st.enter_context(sbt(nc, "rt_Sfb", [128, RH, 256], BF16)); t_Sf = Tr()
                P.op("pool", lambda e: e.memset(Rb[:], 0.0), w=[t_Rb])
                P.op("pool", lambda e: e.memset(Rall[:, T - 1, :], 0.0), w=[t_Rall])
                P.op("pool", lambda e: e.memset(Sf[:], 0.0), w=[t_Sf])
                P.op("pool", lambda e: e.memset(Sfb[:], 0.0), pw=[t_Sf])
                kin = Ring(nc, st, "rt_k", [128, 512], BF16, 3)
                vin = Ring(nc, st, "rt_v", [128, 1024], BF16, 3)
                qin = Ring(nc, st, "rt_q", [128, 512], BF16, 2)
                gin = Ring(nc, st, "rt_g", [128, 1024], F32, 2)
                kc_ = Ring(nc, st, "rt_kc", [128, RH, 128], BF16, 2)
                for c in range(T - 1, 0, -1):
                    tok = slice(c * 128, (c + 1) * 128)
                    k, ktr = kin.next(); v, vtr = vin.next()
                    P.dma("sp", k[:], rkd[s, tok, :], ktr, r=[dtr[("rkd", s)]], w=[ktr])
                    P.dma("sp", v[:], rvd[s, tok, :], vtr, r=[dtr[("rvd", s)]], w=[vtr])
                    kb, kbtr = kc_.next()
                    for h in range(RH):
                        if h % 2:
                            P.op("dve", lambda e, kb=kb, k=k, h=h: e.tensor_scalar(out=kb[:, h, :], in0=k[:, h * 128:(h + 1) * 128], scalar1=pc[:, 1, h:h + 1], scalar2=None, op0=ALU.mult),
                                 r=[ktr, t_tb], pw=[kbtr])
                        else:
                            P.op("act", lambda e, kb=kb, k=k, h=h: e.mul(out=kb[:, h, :], in_=k[:, h * 128:(h + 1) * 128], mul=pc[:, 1, h:h + 1]),
                                 r=[ktr, t_tb], w=[kbtr] if h == 0 else (), pw=[kbtr] if h else ())
                    for h in range(RH):
                        bk = h // 2
                        P.mm(ps[bk][:, (h % 2) * 256:(h % 2) * 256 + 256], kb[:, h, :], v[:, h * 256:(h + 1) * 256], True, True, [kbtr, vtr], pst[bk], first=(h % 2 == 0))
                    for h in range(RH):
                        bk = h // 2
                        P.op("dve", lambda e, h=h, bk=bk: e.scalar_tensor_tensor(out=Rb[:, h, :], in0=Rb[:, h, :], scalar=pc[:, 3, h:h + 1], in1=ps[bk][:, (h % 2) * 256:(h % 2) * 256 + 256], op0=ALU.mult, op1=ALU.add),
                             r=[pst[bk], t_tb], w=[t_Rb])
                    P.op("act", lambda e, c=c: e.copy(out=Rall[:, c - 1, :], in_=Rb[:].rearrange("p h e -> p (h e)")), r=[t_Rb], pw=[t_Rall])
                qT3 = Ring(nc, st, "rt_qT", [128, 3, RH, 128], BF16, 2)
                kTr = Ring(nc, st, "rt_kT", [128, RH, 128], BF16, 2)
                stm = Ring(nc, st, "rt_stm", [128, RH, 128], BF16, 2)
                bnr = Ring(nc, st, "rt_bn", [128, RH, 8], F32, 2)
                onr = Ring(nc, st, "rt_on", [128, D], F32, 2)
                gtd_ = Ring(nc, st, "rt_gt", [128, D], BF16, 2)
                gTr = Ring(nc, st, "rt_gT", [128, 8, 128], BF16, 2)
                for c in range(T):
                    tok = slice(c * 128, (c + 1) * 128)
                    k, ktr = kin.next(); v, vtr = vin.next(); q, qtr = qin.next(); g_, gtr = gin.next()
                    P.dma("sp", q[:], rqd[s, tok, :], qtr, r=[dtr[("rqd", s)]], w=[qtr])
                    P.dma("sp", k[:], rkd[s, tok, :], ktr, r=[dtr[("rkd", s)]], w=[ktr])
                    P.dma("sp", v[:], rvd[s, tok, :], vtr, r=[dtr[("rvd", s)]], w=[vtr])
                    P.dma("sp", g_[:], sgd[s, tok, :], gtr, r=[dtr[("sgd", s)]], w=[gtr])
                    pv = ps[0][:].bitcast(BF16)
                    for h in range(RH):
                        P.tp(pv[:, h * 128:(h + 1) * 128], q[:, h * 128:(h + 1) * 128], ident[:], [qtr, ctr], pst[0], h == 0)
                    for h in range(RH):
                        P.tp(pv[:, 512 + h * 128:512 + (h + 1) * 128], k[:, h * 128:(h + 1) * 128], ident[:], [ktr], pst[0], False)
                    qT, qTtr = qT3.next(); kT, kTtr = kTr.next()
                    pq = pv[:, 0:512].rearrange("p (h c) -> p h c", h=RH)
                    P.op("act", lambda e, qT=qT, pq=pq: e.copy(out=qT[:, 0], in_=pq), r=[pst[0]], w=[qTtr])
                    P.op("dve", lambda e, qT=qT, pq=pq: e.tensor_tensor(out=qT[:, 1], in0=pq, in1=CF[:], op=ALU.mult), r=[pst[0], t_tb], pw=[qTtr])
                    P.op("dve", lambda e, qT=qT, pq=pq: e.tensor_tensor(out=qT[:, 2], in0=pq, in1=CB[:], op=ALU.mult), r=[pst[0], t_tb], pw=[qTtr])
                    P.op("act", lambda e, kT=kT, pv=pv: e.copy(out=kT[:], in_=pv[:, 512:1024].rearrange("p (h c) -> p h c", h=RH)), r=[pst[0]], w=[kTtr])
                    kf, kftr = kc_.next()
                    for h in range(RH):
                        P.op("act", lambda e, kf=kf, k=k, h=h: e.mul(out=kf[:, h, :], in_=k[:, h * 128:(h + 1) * 128], mul=pc[:, 0, h:h + 1]),
                             r=[ktr, t_tb], w=[kftr] if h == 0 else (), pw=[kftr] if h else ())
                    for h in range(RH):
                        P.mm(ps[1][:, h * 128:(h + 1) * 128], kT[:, h, :], qT[:, 0, h, :], True, True, [kTtr, qTtr], pst[1], first=(h == 0))
                    sm_, smtr = stm.next()
                    P.op("dve", lambda e, sm_=sm_: e.tensor_tensor(out=sm_[:], in0=ps[1][:].rearrange("p (h c) -> p h c", h=RH), in1=DT[:], op=ALU.mult), r=[pst[1], t_tb], w=[smtr])
                    for h in range(RH):
                        bk = 2 + h // 2
                        oc = slice((h % 2) * 256, (h % 2) * 256 + 256)
                        P.mm(ps[bk][:, oc], sm_[:, h, :], v[:, h * 256:(h + 1) * 256], True, False, [smtr, vtr], pst[bk], first=(h % 2 == 0))
                        P.mm(ps[bk][:, oc], qT[:, 1, h, :], Sfb[:, h, :], False, False, [qTtr, t_Sf], pst[bk])
                        P.mm(ps[bk][:, oc], qT[:, 2, h, :], Rall[:, c, h * 256:(h + 1) * 256], False, True, [qTtr, t_Rall], pst[bk])
                    for h in range(RH):
                        bk = 4 + h // 2
                        oc = slice((h % 2) * 256, (h % 2) * 256 + 256)
                        P.mm(ps[bk][:, oc], kf[:, h, :], v[:, h * 256:(h + 1) * 256], True, True, [kftr, vtr], pst[bk], first=(h % 2 == 0))
                    for h in range(RH):
                        bk = 4 + h // 2
                        oc = slice((h % 2) * 256, (h % 2) * 256 + 256)
                        P.op("dve", lambda e, h=h, bk=bk, oc=oc: e.scalar_tensor_tensor(out=Sf[:, h, :], in0=Sf[:, h, :], scalar=pc[:, 2, h:h + 1], in1=ps[bk][:, oc], op0=ALU.mult, op1=ALU.add),
                             r=[pst[bk], t_tb], w=[t_Sf])
                    P.op("act", lambda e: e.copy(out=Sfb[:], in_=Sf[:]), r=[t_Sf], w=[t_Sf])
                    if debug and c == 1:
                        dO = st.enter_context(sbt(nc, "dbgO", [128, 1024], F32)); t_dO = Tr()
                        P.op("dve", lambda e: e.tensor_copy(out=dO[:, 0:512], in_=ps[2][:]), r=[pst[2]], w=[t_dO])
                        P.op("dve", lambda e: e.tensor_copy(out=dO[:, 512:1024], in_=ps[3][:]), r=[pst[3]], pw=[t_dO])
                        P.dma("pool", dbg_O[:, :], dO[:], t_dO, r=[t_dO])
                        P.dma("pool", dbg_qT[:, :], qT[:].rearrange("p a h c -> p (a h c)"), qTtr, r=[qTtr])
                        P.dma("pool", dbg_kT[:, :], kT[:].rearrange("p h c -> p (h c)"), kTtr, r=[kTtr])
                        P.dma("pool", dbg_sm[:, :], sm_[:].rearrange("p h c -> p (h c)"), smtr, r=[smtr])
                        P.dma("pool", dbg_kf[:, :], kf[:].rearrange("p h c -> p (h c)"), kftr, r=[kftr])
                        P.dma("pool", dbg_DT[:, :], DT[:].rearrange("p h c -> p (h c)"), t_dO, r=[t_tb])
                        P.dma("pool", dbg_CF[:, :], CF[:].rearrange("p h c -> p (h c)"), t_dO, r=[t_tb])
                        P.dma("pool", dbg_CB[:, :], CB[:].rearrange("p h c -> p (h c)"), t_dO, r=[t_tb])
                        P.dma("pool", dbg_pc[:, :], pc[:].rearrange("p a h -> p (a h)"), t_dO, r=[t_tb])
                        P.dma("pool", dbg_Sf[:, :], Sf[:].rearrange("p h e -> p (h e)"), t_dO, r=[t_Sf])
                        P.dma("pool", dbg_R[:, :], Rall[:, c, :], t_dO, r=[t_Rall])
                    bn, bntr = bnr.next()
                    on, ontr = onr.next()
                    for h in range(RH):
                        bk = 2 + h // 2
                        oc = slice((h % 2) * 256, (h % 2) * 256 + 256)
                        P.op("dve", lambda e, bn=bn, h=h, bk=bk, oc=oc: e.bn_stats(out=bn[:, h, 0:6], in_=ps[bk][:, oc]), r=[pst[bk]], w=[bntr] if h == 0 else (), pw=[bntr] if h else ())
                    for h in range(RH):
                        P.op("dve", lambda e, bn=bn, h=h: e.bn_aggr(out=bn[:, h, 6:8], in_=bn[:, h, 0:6]), r=[bntr], pw=[bntr])
                    P.op("act", lambda e, bn=bn: e.activation(out=bn[:, :, 0], in_=bn[:, :, 7], func=AF.Sqrt, bias=epsc[:, 0:1], scale=1.0), r=[bntr, ctr], pw=[bntr])
                    P.op("dve", lambda e, bn=bn: e.reciprocal(out=bn[:, :, 1], in_=bn[:, :, 0]), r=[bntr], pw=[bntr])
                    for h in range(RH):
                        bk = 2 + h // 2
                        oc = slice((h % 2) * 256, (h % 2) * 256 + 256)
                        P.op("dve", lambda e, on=on, bn=bn, h=h, bk=bk, oc=oc: e.tensor_scalar(out=on[:, h * 256:(h + 1) * 256], in0=ps[bk][:, oc], scalar1=bn[:, h, 6:7], scalar2=bn[:, h, 1:2], op0=ALU.subtract, op1=ALU.mult),
                             r=[pst[bk], bntr], w=[ontr] if h == 0 else (), pw=[ontr] if h else ())
                    P.op("pool", lambda e, on=on: e.tensor_tensor(out=on[:], in0=on[:], in1=gnb[:], op=ALU.mult), r=[ontr, t_gn], w=[ontr])
                    gt, gttr = gtd_.next()
                    P.op("dve", lambda e, gt=gt, on=on, g_=g_: e.tensor_tensor(out=gt[:], in0=on[:], in1=g_[:], op=ALU.mult), r=[ontr, gtr], w=[gttr])
                    pv6 = ps[6][:].bitcast(BF16)
                    for kc in range(8):
                        P.tp(pv6[:, kc * 128:(kc + 1) * 128], gt[:, kc * 128:(kc + 1) * 128], ident[:], [gttr, ctr], pst[6], kc == 0)
                    gT, gTtr = gTr.next()
                    P.op("act", lambda e, gT=gT, pv6=pv6: e.copy(out=gT[:], in_=pv6.rearrange("p (k c) -> p k c", k=8)), r=[pst[6]], w=[gTtr])
                    P.dma("pool", rtTd[s, :, :, tok].rearrange("k p c -> p k c"), gT[:], gTtr, r=[gTtr], pw=[dtr[("rtTd", s)]])

                P.barrier()
        def postnorm_residual(bA, bB, xt, xtr_, gb, t_g, o, otr, sqr, stt):
            sq_, sqtr = sqr.next()
            sm, smtr = stt.next()
            P.op("act", lambda e: e.activation(out=sq_[:, 0:512], in_=ps[bA][:], func=AF.Square), r=[pst[bA]], w=[sqtr])
            P.op("act", lambda e: e.activation(out=sq_[:, 512:1024], in_=ps[bB][:], func=AF.Square), r=[pst[bB]], pw=[sqtr])
            P.op("dve", lambda e: e.reduce_sum(out=sm[:, 2:3], in_=sq_[:], axis=mybir.AxisListType.X), r=[sqtr], w=[smtr])
            P.op("act", lambda e: e.activation(out=sm[:, 3:4], in_=sm[:, 2:3], func=AF.Sqrt, bias=epsc[:, 0:1], scale=1.0 / D), r=[smtr, ctr], pw=[smtr])
            P.op("dve", lambda e: e.reciprocal(out=sm[:, 3:4], in_=sm[:, 3:4]), r=[smtr], pw=[smtr])
            P.op("dve", lambda e: e.scalar_tensor_tensor(out=o[:, 0:512], in0=ps[bA][:], scalar=sm[:, 3:4], in1=gb[:, 0:512], op0=ALU.mult, op1=ALU.mult), r=[pst[bA], smtr, t_g], w=[otr])
            P.op("dve", lambda e: e.scalar_tensor_tensor(out=o[:, 512:1024], in0=ps[bB][:], scalar=sm[:, 3:4], in1=gb[:, 512:1024], op0=ALU.mult, op1=ALU.mult), r=[pst[bB], smtr], pw=[otr])
            P.op("pool", lambda e: e.tensor_tensor(out=o[:], in0=o[:], in1=xt[:], op=ALU.add), r=[xtr_], w=[otr])

        def stage_M(l, s, xsrc, xtr):
            with ExitStack() as st:
                wmo = st.enter_context(sbt(nc, "m_wmo", [128, 4, D], BF16))
                wpw = st.enter_context(sbt(nc, "m_wpw", [128, 4, D], BF16))
                wro = st.enter_context(sbt(nc, "m_wro", [128, 8, D], BF16))
                wou = st.enter_context(sbt(nc, "m_wou", [128, 8, D], BF16))
                gb = st.enter_context(sbt(nc, "m_gb", [128, D], F32))
                t_w = Tr(); t_g = Tr()
                P.dma("sp", wmo[:], wb["mo"][l].rearrange("(kc p) n -> p kc n", p=128), t_w, r=[wb_tr[("mo", l)]], w=[t_w])
                P.dma("sp", wpw[:], wb["pw"][l].rearrange("(kc p) n -> p kc n", p=128), t_w, r=[wb_tr[("pw", l)]], pw=[t_w])
                P.dma("sp", wro[:], wb["ro"][l].rearrange("(kc p) n -> p kc n", p=128), t_w, r=[wb_tr[("ro", l)]], pw=[t_w])
                P.dma("sp", wou[:], wb["wo"][l].rearrange("(kc p) n -> p kc n", p=128), t_w, r=[wb_tr[("wo", l)]], pw=[t_w])
                P.dma("sp", gb[:], ln_mix_post[l].partition_broadcast(128), t_g, w=[t_g])
                wgt = st.enter_context(sbt(nc, "m_wgt", [128, 8, 3072], BF16))
                P.dma("sp", wgt[:, :, 0:1536], wb["in"][l].rearrange("(kc p) n -> p kc n", p=128)[:, :, O_GATE:O_GATE + 1536], t_w, r=[wb_tr[("in", l)]], pw=[t_w])
                P.dma("sp", wgt[:, :, 1536:3072], wb["in"][l].rearrange("(kc p) n -> p kc n", p=128)[:, :, O_GATE + 1536:O_GATE + 3072], t_w, pw=[t_w])
                hTr = Ring(nc, st, "m_hT", [128, 8, 512], BF16, 2)
                oTr = Ring(nc, st, "m_oT", [128, 4, 512], BF16, 2)
                cTr = Ring(nc, st, "m_cT", [128, 4, 512], BF16, 2)
                rTr = Ring(nc, st, "m_rT", [128, 8, 512], BF16, 2)
                gtr_ = Ring(nc, st, "m_gt", [128, 3, 512], F32, 2)
                mt = Ring(nc, st, "m_t", [128, 2, 512], F32, 1)
                mgr = Ring(nc, st, "m_mg", [128, 8, 512], BF16, 2)
                xr = Ring(nc, st, "m_x", [128, D], F32, 2)
                outr = Ring(nc, st, "m_o", [128, D], F32, 2)
                sqr = Ring(nc, st, "m_sq", [128, D], F32, 2)
                stt = Ring(nc, st, "m_st", [128, 8], F32, 3)
                for tb in range(NB):
                    blk = slice(tb * 512, (tb + 1) * 512)
                    o_, otr_ = oTr.next(); c_, ctr_ = cTr.next(); r_, rtr_ = rTr.next()
                    P.dma("sp", o_[:], oTd[s, :, :, blk].rearrange("k p c -> p k c"), otr_, r=[dtr[("oTd", s)]], w=[otr_])
                    P.dma("sp", c_[:], cvTd[s, :, :, blk].rearrange("k p c -> p k c"), ctr_, r=[dtr[("cvTd", s)]], w=[ctr_])
                    P.dma("sp", r_[:], rtTd[s, :, :, blk].rearrange("k p c -> p k c"), rtr_, r=[dtr[("rtTd", s)]], w=[rtr_])
                    hTb, hTbtr = hTr.next()
                    P.dma("sp", hTb[:], hTd[s, :, :, blk].rearrange("k p c -> p k c"), hTbtr, r=[dtr[("hTd", s)]], w=[hTbtr])
                    mg, mgtr = mgr.next()
                    for rc in range(8):
                        cs = slice(rc * 128, (rc + 1) * 128)
                        gt, gttr = gtr_.next()
                        for b in range(3):
                            gc = slice(b * 1024 + rc * 128, b * 1024 + (rc + 1) * 128)
                            for kc in range(8):
                                P.mm(ps[3 + b][:], wgt[:, kc, gc], hTb[:, kc, :], kc == 0, kc == 7, [t_w, hTbtr], pst[3 + b])
                            P.op("act", lambda e, gt=gt, b=b: e.activation(out=gt[:, b, :], in_=ps[3 + b][:], func=AF.Sigmoid), r=[pst[3 + b]],
                                 w=[gttr] if b == 0 else (), pw=[gttr] if b else ())
                        b0 = 0
                        for kc in range(4):
                            P.mm(ps[b0][:], wmo[:, kc, cs], o_[:, kc, :], kc == 0, kc == 3, [t_w, otr_], pst[b0])
                        for kc in range(4):
                            P.mm(ps[b0 + 1][:], wpw[:, kc, cs], c_[:, kc, :], kc == 0, kc == 3, [t_w, ctr_], pst[b0 + 1])
                        for kc in range(8):
                            P.mm(ps[b0 + 2][:], wro[:, kc, cs], r_[:, kc, :], kc == 0, kc == 7, [t_w, rtr_], pst[b0 + 2])
                        t_, ttr = mt.next()
                        P.op("dve", lambda e, t_=t_, gt=gt, b0=b0: e.tensor_tensor(out=t_[:, 0, :], in0=ps[b0][:], in1=gt[:, 0, :], op=ALU.mult), r=[pst[b0], gttr], w=[ttr])
                        P.op("dve", lambda e, t_=t_, gt=gt, b0=b0: e.tensor_tensor(out=t_[:, 1, :], in0=ps[b0 + 1][:], in1=gt[:, 1, :], op=ALU.mult), r=[pst[b0 + 1], gttr], pw=[ttr])
                        P.op("pool", lambda e, t_=t_: e.tensor_tensor(out=t_[:, 0, :], in0=t_[:, 0, :], in1=t_[:, 1, :], op=ALU.add), r=[ttr], w=[ttr])
                        P.op("dve", lambda e, t_=t_, gt=gt, b0=b0: e.tensor_tensor(out=t_[:, 1, :], in0=ps[b0 + 2][:], in1=gt[:, 2, :], op=ALU.mult), r=[pst[b0 + 2], gttr], w=[ttr])
                        P.op("pool", lambda e, t_=t_, mg=mg, rc=rc: e.tensor_tensor(out=mg[:, rc, :], in0=t_[:, 0, :], in1=t_[:, 1, :], op=ALU.add), r=[ttr], w=[mgtr] if rc == 0 else (), pw=[mgtr] if rc else ())
                    for tt in range(4):
                        t = tb * 4 + tt
                        tok = slice(t * 128, (t + 1) * 128)
                        xt, xtr_ = xr.next()
                        P.dma("sp", xt[:], xsrc[s, tok, :], xtr_, r=[xtr[s]], w=[xtr_])
                        ob_ = 6 if tt % 2 == 0 else 4
                        for nb in range(2):
                            for kc in range(8):
                                P.mm(ps[ob_ + nb][:], mg[:, kc, tt * 128:(tt + 1) * 128], wou[:, kc, nb * 512:(nb + 1) * 512], kc == 0, kc == 7, [mgtr, t_w], pst[ob_ + nb])
                        o, otr = outr.next()
                        postnorm_residual(ob_, ob_ + 1, xt, xtr_, gb, t_g, o, otr, sqr, stt)
                        P.dma("pool", x1d[s, tok, :], o[:], otr, r=[otr], pw=[dtr[("x1d", s)]])

                P.barrier()
        def stage_F(l, s, ydst, ykey):
            with ExitStack() as st:
                wg = st.enter_context(sbt(nc, "f_wg", [128, 8, FH], BF16))
                wu = st.enter_context(sbt(nc, "f_wu", [128, 8, FH], BF16))
                t_w = Tr(); t_g = Tr()
                gpre = st.enter_context(sbt(nc, "f_gpre", [128, D], F32))
                gpost = st.enter_context(sbt(nc, "f_gpost", [128, D], F32))
                P.dma("sp", wg[:], wb["fg"][l].rearrange("(kc p) n -> p kc n", p=128), t_w, r=[wb_tr[("fg", l)]], w=[t_w])
                P.dma("sp", wu[:], wb["fu"][l].rearrange("(kc p) n -> p kc n", p=128), t_w, r=[wb_tr[("fu", l)]], pw=[t_w])
                P.dma("sp", gpre[:], ln_ffn_pre[l].partition_broadcast(128), t_g, w=[t_g])
                P.dma("sp", gpost[:], ln_ffn_post[l].partition_broadcast(128), t_g, pw=[t_g])
                wdr = Ring(nc, st, "f_wd", [128, D], BF16, 8)
                xr = Ring(nc, st, "f_x", [128, D], F32, 8)
                hb = Ring(nc, st, "f_hb", [128, D], BF16, 2)
                sqr = Ring(nc, st, "f_sq", [128, D], F32, 2)
                stt = Ring(nc, st, "f_st", [128, 8], F32, 8)
                h2T = Ring(nc, st, "f_h2T", [128, 8, 512], BF16, 2)
                hid = Ring(nc, st, "f_hid", [128, 22, 512], BF16, 1)
                sgr = Ring(nc, st, "f_sg", [128, 512], F32, 2)
                outr = Ring(nc, st, "f_o", [128, D], F32, 2)
                wdv = wb["fd"][l]
                def prep(tb):
                    hT_, hTtr = h2T.next()
                    xts = []
                    for tt in range(4):
                        t = tb * 4 + tt
                        tok = slice(t * 128, (t + 1) * 128)
                        xt, xtr_ = xr.next()
                        xts.append((xt, xtr_))
                        P.dma("sp", xt[:], x1d[s, tok, :], xtr_, r=[dtr[("x1d", s)]], w=[xtr_])
                        sq_, sqtr = sqr.next()
                        sm, smtr = stt.next()
                        ssq4(xt, sq_, sm, xtr_, sqtr, smtr)
                        rstd_from_ssq(sm[:, 0:1], sm[:, 1:2], D, [smtr])
                        h, htr = hb.next()
                        P.op("dve", lambda e, xt=xt, h=h, sm=sm: e.scalar_tensor_tensor(out=h[:], in0=xt[:], scalar=sm[:, 1:2], in1=gpre[:], op0=ALU.mult, op1=ALU.mult), r=[xtr_, smtr, t_g], w=[htr])
                        bk = 4 + (tt % 2)
                        pv = ps[bk][:].bitcast(BF16)
                        for kc in range(8):
                            P.tp(pv[:, kc * 128:(kc + 1) * 128], h[:, kc * 128:(kc + 1) * 128], ident[:], [htr, ctr], pst[bk], kc == 0)
                        P.op("act", lambda e, hT_=hT_, pv=pv, tt=tt: e.copy(out=hT_[:, :, tt * 128:(tt + 1) * 128], in_=pv.rearrange("p (k c) -> p k c", k=8)), r=[pst[bk]],
                             w=[hTtr] if tt == 0 else (), pw=[hTtr] if tt else ())
                    return hT_, hTtr, xts

                nxt = prep(0)
                for tb in range(NB):
                    hT_, hTtr, xts = nxt
                    hd, hdtr = hid.next()
                    for hc in range(22):
                        bg, bu = (4, 5) if hc % 2 == 0 else (6, 7)
                        cs = slice(hc * 128, (hc + 1) * 128)
                        for kc in range(8):
                            P.mm(ps[bg][:], wg[:, kc, cs], hT_[:, kc, :], kc == 0, kc == 7, [t_w, hTtr], pst[bg])
                        for kc in range(8):
                            P.mm(ps[bu][:], wu[:, kc, cs], hT_[:, kc, :], kc == 0, kc == 7, [t_w, hTtr], pst[bu])
                        sg, sgtr = sgr.next()
                        P.op("act", lambda e, sg=sg, bg=bg: e.activation(out=sg[:], in_=ps[bg][:], func=AF.Silu), r=[pst[bg]], w=[sgtr])
                        P.op("dve", lambda e, hd=hd, sg=sg, bu=bu, hc=hc: e.tensor_tensor(out=hd[:, hc, :], in0=ps[bu][:], in1=sg[:], op=ALU.mult), r=[pst[bu], sgtr],
                             w=[hdtr] if hc == 0 else (), pw=[hdtr] if hc else ())
                    if tb + 1 < NB:
                        nxt = prep(tb + 1)
                    for hc in range(22):
                        wd, wdtr = wdr.next()
                        P.dma("sp", wd[:], wdv[hc * 128:(hc + 1) * 128, :], wdtr, r=[wb_tr[("fd", l)]], w=[wdtr])
                        for tt in range(4):
                            for nb in range(2):
                                bk = tt * 2 + nb
                                P.mm(ps[bk][:], hd[:, hc, tt * 128:(tt + 1) * 128], wd[:, nb * 512:(nb + 1) * 512], hc == 0, hc == 21, [hdtr, wdtr], pst[bk])
                    for tt in (2, 3, 0, 1):
                        t = tb * 4 + tt
                        tok = slice(t * 128, (t + 1) * 128)
                        xt, xtr_ = xts[tt]
                        o, otr = outr.next()
                        postnorm_residual(tt * 2, tt * 2 + 1, xt, xtr_, gpost, t_g, o, otr, sqr, stt)
                        P.dma("pool", ydst[s, tok, :], o[:], otr, r=[otr], pw=[dtr[(ykey, s)]])
                P.barrier()

        for l in range(nlayers):
            if "W" in stages:
                stage_W(l)
        for s in range(NS):
            if "T" in stages:
                stage_T(s)
        xin_tr = [Tr() for _ in range(NS)]
        for l in range(nlayers):
            last = (l == nlayers - 1)
            for s in range(NS):
                if l == 0:
                    xsrc, xtr = x_in, xin_tr
                else:
                    xsrc, xtr = xLd, [dtr[("xLd", s_)] for s_ in range(NS)]
                if "N" in stages:
                    stage_NP(l, s, xsrc, xtr)
                if "A" in stages:
                    stage_A(l, s, with_conv=("C" in stages))
                if "R" in stages:
                    stage_R(l, s)
                if "M" in stages:
                    stage_M(l, s, xsrc, xtr)
                if "F" in stages:
                    stage_F(l, s, y_out if last else xLd, "y" if last else "xLd")
        P.wait_all("sp", [dtr[("y", s)] for s in range(NS)])
        for en in ("pe", "act", "dve", "pool"):
            E = P.E[en]
            if E.cnt:
                if P.E["sp"].waited.get(E.key, 0) < E.cnt:
                    P.E["sp"].e.wait_ge(E.sem, E.cnt)
    return nc


def rope_consts():
    def inv(dim):
        return (np.float32(10000.0) ** (-(np.arange(0, dim, 2, dtype=np.float32)) / np.float32(dim))).astype(np.float32)
    im, ir = inv(32), inv(128)
    c = np.zeros((2, 160), np.float32)
    c[0] = np.concatenate([im, im, ir, ir])
    c[1] = np.concatenate([np.zeros(16), np.full(16, np.pi / 2), np.zeros(64), np.full(64, np.pi / 2)]).astype(np.float32)
    return c


WEIGHT_NAMES = ["ln_mix_pre", "ln_mix_post", "ln_ffn_pre", "ln_ffn_post", "w_in", "mla_q_norm", "mla_w_uq", "mla_kv_norm",
                "mla_w_ukv", "mla_w_o", "conv_w_dw", "conv_b_dw", "conv_ln_g", "conv_ln_b", "conv_w_pw", "ret_decay_logits",
                "ret_gn_g", "ret_w_o", "w_out", "ffn_w_gate", "ffn_w_up", "ffn_w_down"]


def kernel(**inputs):
    x = np.ascontiguousarray(np.asarray(inputs["x"], dtype=np.float32))
    pos = np.ascontiguousarray(np.asarray(inputs["positions"], dtype=np.int32))
    B, S, _ = x.shape
    ncores = 8
    NS = B // ncores
    nc = build(S, NS)
    shared = {k: np.ascontiguousarray(np.asarray(inputs[k], dtype=np.float32)) for k in WEIGHT_NAMES}
    shared["rope_consts"] = rope_consts()
    in_maps = []
    for c in range(ncores):
        m = dict(shared)
        m["x"] = x[c * NS:(c + 1) * NS]
        m["positions"] = pos[c * NS:(c + 1) * NS]
        in_maps.append(m)
    res = run_bass_kernel_spmd(nc, in_maps, core_ids=list(range(ncores)))
    return np.concatenate([r["y"] for r in res.results], axis=0).astype(np.float32)
```

```python
import numpy as np
import concourse.bass as bass
import concourse.mybir as mybir
from concourse.bass_utils import run_bass_kernel_spmd
from contextlib import ExitStack

F32 = mybir.dt.float32
BF16 = mybir.dt.bfloat16
I32 = mybir.dt.int32
AF = mybir.ActivationFunctionType
ALU = mybir.AluOpType

D = 1024
L = 2
NH = 8
RH = 4
FH = 2816
INC = 7584
EPS = 1e-6
SAME_ENGINE_SYNC = False
import os
SUB = os.environ.get("SUB", "lcrg")
CUT = float(os.environ.get("CUT", "9"))
TWO_PI = float(2 * np.pi)
PI = float(np.pi)

O_CQ, O_CKV, O_KPE, O_CONV, O_RQ, O_RK, O_RV, O_RG, O_GATE = 0, 256, 384, 416, 1440, 1952, 2464, 3488, 4512


class Tr:
    __slots__ = ("w", "r", "sem", "cnt", "name", "excl")

    def __init__(self, name="", excl=False):
        self.excl = excl
        self.w = {}
        self.r = {}
        self.sem = None
        self.cnt = 0
        self.name = name


class Eng:
    def __init__(self, e, sem, key):
        self.e = e
        self.sem = sem
        self.key = key
        self.cnt = 0
        self.waited = {}


class Prog:
    def __init__(self, nc, es):
        self.nc = nc
        self.es = es
        self.sems = {}
        self.nsem = 0
        self.E = {}
        self.pool = []
        self.live = []
        self.uid = 0
        for name, e in (("pe", nc.tensor), ("act", nc.scalar), ("dve", nc.vector), ("pool", nc.gpsimd), ("sp", nc.sync)):
            s, k = self.newsem("e_" + name)
            self.E[name] = Eng(e, s, k)

    def newsem(self, name):
        s = self.es.enter_context(self.nc.semaphore(name + "_%d" % self.nsem))
        k = self.nsem
        self.nsem += 1
        self.sems[k] = s
        return s, k

    def _waits(self, E, r, w, pw):
        need = {}
        for t in r:
            for k, v in t.w.items():
                if need.get(k, 0) < v:
                    need[k] = v
            if t.excl:
                for k, v in t.r.items():
                    if need.get(k, 0) < v:
                        need[k] = v
        for t in w:
            for d in (t.w, t.r):
                for k, v in d.items():
                    if need.get(k, 0) < v:
                        need[k] = v
        for t in pw:
            for d in (t.w, t.r):
                for k, v in d.items():
                    if need.get(k, 0) < v:
                        need[k] = v
        for k, v in need.items():
            if k == E.key and not SAME_ENGINE_SYNC:
                continue
            if E.waited.get(k, 0) < v:
                E.e.wait_ge(self.sems[k], v)
                E.waited[k] = v

    def op(self, en, fn, r=(), w=(), pw=()):
        E = self.E[en]
        self._waits(E, r, w, pw)
        ins = fn(E.e)
        E.cnt += 1
        ins.then_inc(E.sem, 1)
        for t in r:
            t.r[E.key] = E.cnt
        for t in w:
            t.w = {E.key: E.cnt}
            t.r = {}
        for t in pw:
            t.w[E.key] = E.cnt

    def dma(self, q, out, in_, sb, r=(), w=(), pw=()):
        Q = self.E[q]
        self._waits(Q, r, w, pw)
        if sb.sem is None:
            if self.pool:
                sb.sem, sb.cnt = self.pool.pop()
            else:
                sb.sem = self.newsem("d")
            self.live.append(sb)
        sem, key = sb.sem
        sb.cnt += 16
        Q.e.dma_start(out=out, in_=in_).then_inc(sem, 16)
        for t in r:
            t.r[key] = sb.cnt
        for t in w:
            t.w = {key: sb.cnt}
            t.r = {}
        for t in pw:
            t.w[key] = sb.cnt

    def barrier(self):
        sp = self.E["sp"]
        for en in ("pe", "act", "dve", "pool"):
            E = self.E[en]
            if sp.waited.get(E.key, 0) < E.cnt:
                sp.e.wait_ge(E.sem, E.cnt)
                sp.waited[E.key] = E.cnt
        for sb in self.live:
            sem, key = sb.sem
            if sp.waited.get(key, 0) < sb.cnt:
                sp.e.wait_ge(sem, sb.cnt)
                sp.waited[key] = sb.cnt
        if not hasattr(self, "bar"):
            self.bar = self.newsem("bar")
            self.barcnt = 0
        self.barcnt += 1
        sp.e.sem_inc(self.bar[0], 1)
        for en in ("pe", "act", "dve", "pool"):
            E = self.E[en]
            E.e.wait_ge(self.bar[0], self.barcnt)
            for en2 in ("pe", "act", "dve", "pool"):
                E.waited[self.E[en2].key] = max(E.waited.get(self.E[en2].key, 0), self.E[en2].cnt)
            for sb in self.live:
                E.waited[sb.sem[1]] = max(E.waited.get(sb.sem[1], 0), sb.cnt)
        for sb in self.live:
            self.pool.append((sb.sem, sb.cnt))
            sb.sem = None
        self.live = []

    def wait_all(self, en, trs):
        E = self.E[en]
        self._waits(E, trs, (), ())

    def mm(self, out, lhsT, rhs, start, stop, r, tr, first=None):
        if first is None:
            first = start
        self.op("pe", lambda e: e.matmul(out, lhsT, rhs, start=bool(start), stop=bool(stop)), r=r,
                w=[tr] if first else (), pw=() if first else [tr])

    def tp(self, out, in_, ident, r, tr, first):
        self.op("pe", lambda e: e.transpose(out, in_, ident), r=r, w=[tr] if first else (), pw=() if first else [tr])


_UID = [0]


def sbt(nc, name, shape, dt):
    _UID[0] += 1
    return nc.sbuf_tensor("%s_u%d" % (name, _UID[0]), shape, dt)


class Ring:
    cnt = [0]

    def __init__(self, nc, es, name, shape, dt, n):
        Ring.cnt[0] += 1
        self.t = [es.enter_context(sbt(nc, "%s_%d_%d" % (name, Ring.cnt[0], i), shape, dt)) for i in range(n)]
        self.tr = [Tr("%s%d" % (name, i)) for i in range(n)]
        self.i = 0
        self.n = n

    def next(self):
        i = self.i
        self.i = (i + 1) % self.n
        return self.t[i], self.tr[i]


def build(S, NS, debug=False, nlayers=L, stages="WTNACRMF"):
    T = S // 128
    NB = S // 512
    nc = bass.Bass("TRN2", target_bir_lowering=False)

    def din(name, shape, dt=F32):
        return nc.dram_tensor(name, shape, dt, kind="ExternalInput").ap()

    x_in = din("x", [NS, S, D])
    pos_in = din("positions", [NS, S], I32)
    ln_mix_pre = din("ln_mix_pre", [L, D]); ln_mix_post = din("ln_mix_post", [L, D])
    ln_ffn_pre = din("ln_ffn_pre", [L, D]); ln_ffn_post = din("ln_ffn_post", [L, D])
    w_in = din("w_in", [L, D, INC])
    mla_q_norm = din("mla_q_norm", [L, 256]); mla_w_uq = din("mla_w_uq", [L, 256, 768])
    mla_kv_norm = din("mla_kv_norm", [L, 128]); mla_w_ukv = din("mla_w_ukv", [L, 128, 1024])
    mla_w_o = din("mla_w_o", [L, 512, D])
    conv_w_dw = din("conv_w_dw", [L, 31, 512]); conv_b_dw = din("conv_b_dw", [L, 512])
    conv_ln_g = din("conv_ln_g", [L, 512]); conv_ln_b = din("conv_ln_b", [L, 512])
    conv_w_pw = din("conv_w_pw", [L, 512, D])
    ret_decay_logits = din("ret_decay_logits", [L, 2, RH])
    ret_gn_g = din("ret_gn_g", [L, D]); ret_w_o = din("ret_w_o", [L, D, D])
    w_out = din("w_out", [L, D, D])
    ffn_w_gate = din("ffn_w_gate", [L, D, FH]); ffn_w_up = din("ffn_w_up", [L, D, FH]); ffn_w_down = din("ffn_w_down", [L, FH, D])
    cst_in = din("rope_consts", [2, 160])
    y_out = nc.dram_tensor("y", [NS, S, D], F32, kind="ExternalOutput").ap()

    skind = "ExternalOutput" if debug else "Internal"

    def scr(name, shape, dt):
        return nc.dram_tensor(name, shape, dt, kind=skind).ap()

    wb = {
        "in": scr("wb_in", [L, D, INC], BF16), "uq": scr("wb_uq", [L, 256, 768], BF16), "ukv": scr("wb_ukv", [L, 128, 1024], BF16),
        "mo": scr("wb_mo", [L, 512, D], BF16), "pw": scr("wb_pw", [L, 512, D], BF16), "ro": scr("wb_ro", [L, D, D], BF16),
        "wo": scr("wb_wo", [L, D, D], BF16), "fg": scr("wb_fg", [L, D, FH], BF16), "fu": scr("wb_fu", [L, D, FH], BF16),
        "fd": scr("wb_fd", [L, FH, D], BF16),
    }
    wsrc = {"in": w_in, "uq": mla_w_uq, "ukv": mla_w_ukv, "mo": mla_w_o, "pw": conv_w_pw, "ro": ret_w_o, "wo": w_out,
            "fg": ffn_w_gate, "fu": ffn_w_up, "fd": ffn_w_down}
    wb_tr = {(k, l): Tr("wb_%s%d" % (k, l)) for k in wb for l in range(L)}

    tabd = scr("tabd", [NS, 128, T * 160], F32)
    QTd = scr("QTd", [NS, NH, 96, S], BF16); KTd = scr("KTd", [NS, NH, 96, S], BF16)
    Vd = scr("Vd", [NS, S, NH * 65], BF16)
    oTd = scr("oTd", [NS, 4, 128, S], BF16)
    aTd = scr("aTd", [NS, 4, 128, S], F32); cvTd = scr("cvTd", [NS, 4, 128, S], BF16)
    rqd = scr("rqd", [NS, S, 512], BF16); rkd = scr("rkd", [NS, S, 512], BF16)
    rvd = scr("rvd", [NS, S, 1024], BF16); sgd = scr("sgd", [NS, S, 1024], F32)
    rtTd = scr("rtTd", [NS, 8, 128, S], BF16)
    gtd = scr("gtd", [NS, 24, 128, S], F32)
    hTd = scr("hTd", [NS, 8, 128, S], BF16)
    x1d = scr("x1d", [NS, S, D], F32)
    xLd = scr("xLd", [NS, S, D], F32)
    if debug:
        dbg_qT = scr("dbg_qT", [128, 3 * RH * 128], BF16); dbg_kT = scr("dbg_kT", [128, RH * 128], BF16)
        dbg_sm = scr("dbg_sm", [128, RH * 128], BF16); dbg_DT = scr("dbg_DT", [128, RH * 128], F32)
        dbg_CF = scr("dbg_CF", [128, RH * 128], F32); dbg_CB = scr("dbg_CB", [128, RH * 128], F32)
        dbg_O = scr("dbg_O", [128, 1024], F32); dbg_Sf = scr("dbg_Sf", [128, 1024], F32); dbg_R = scr("dbg_R", [128, 1024], BF16)
        dbg_pc = scr("dbg_pc", [128, 16], F32); dbg_kf = scr("dbg_kf", [128, 512], BF16)
    dtr = {}
    for nm in ("hTd", "tabd", "QTd", "KTd", "Vd", "oTd", "aTd", "cvTd", "rqd", "rkd", "rvd", "sgd", "rtTd", "gtd", "x1d", "xLd", "y"):
        for s in range(NS):
            dtr[(nm, s)] = Tr("%s_%d" % (nm, s))

    with ExitStack() as es:
        P = Prog(nc, es)
        ps = [es.enter_context(nc.psum_tensor("psb%d" % i, [128, 512], F32)) for i in range(8)]
        pst = [Tr("ps%d" % i, excl=True) for i in range(8)]
        identf = es.enter_context(sbt(nc, "identf", [128, 128], F32))
        ident = es.enter_context(sbt(nc, "ident", [128, 128], BF16))
        onesf = es.enter_context(sbt(nc, "onesf", [128, 128], F32))
        epsc = es.enter_context(sbt(nc, "epsc", [128, 1], F32))
        ctr = Tr("consts")
        P.op("pool", lambda e: e.memset(identf[:], 0.0), w=[ctr])
        P.op("pool", lambda e: e.affine_select(out=identf[:], in_=identf[:], pattern=[[-1, 128]], compare_op=ALU.not_equal,
                                               fill=1.0, base=0, channel_multiplier=1), w=[ctr])
        P.op("pool", lambda e: e.tensor_copy(out=ident[:], in_=identf[:]), r=[ctr], pw=[ctr])
        P.op("pool", lambda e: e.memset(onesf[:], 1.0), pw=[ctr])
        P.op("pool", lambda e: e.memset(epsc[:], EPS), pw=[ctr])

        def rstd_from_ssq(ssq, rstd, n, trs):
            P.op("act", lambda e: e.activation(out=rstd, in_=ssq, func=AF.Sqrt, bias=epsc[:, 0:1], scale=1.0 / n), r=trs + [ctr], w=trs)
            P.op("dve", lambda e: e.reciprocal(out=rstd, in_=rstd), r=trs, w=trs)

        def ssq4(xt, sqt, sm, xtr_, sqtr, smtr):
            P.op("act", lambda e: e.activation(out=sqt[:], in_=xt[:], func=AF.Square), r=[xtr_], w=[sqtr])
            P.op("dve", lambda e: e.reduce_sum(out=sm[:, 0:1], in_=sqt[:], axis=mybir.AxisListType.X), r=[sqtr], w=[smtr])

        def stage_W(l):
            with ExitStack() as st:
                stf = Ring(nc, st, "wstf", [128, 2048], F32, 3)
                stb = Ring(nc, st, "wstb", [128, 2048], BF16, 3)
                i = 0
                for k in ("in", "uq", "ukv", "mo", "pw", "ro", "wo", "fg", "fu", "fd"):
                    src = wsrc[k][l]
                    dst = wb[k][l]
                    K, N = src.shape
                    for kc in range(K // 128):
                        for c0 in range(0, N, 2048):
                            w = min(2048, N - c0)
                            f, ftr = stf.next()
                            b, btr = stb.next()
                            P.dma("sp", f[:, 0:w], src[kc * 128:(kc + 1) * 128, c0:c0 + w], ftr, w=[ftr])
                            en = ("dve", "pool", "act")[i % 3]
                            if en == "act":
                                P.op(en, lambda e, f=f, b=b, w=w: e.copy(out=b[:, 0:w], in_=f[:, 0:w]), r=[ftr], w=[btr])
                            else:
                                P.op(en, lambda e, f=f, b=b, w=w: e.tensor_copy(out=b[:, 0:w], in_=f[:, 0:w]), r=[ftr], w=[btr])
                            P.dma("pool", dst[kc * 128:(kc + 1) * 128, c0:c0 + w], b[:, 0:w], btr, r=[btr], pw=[wb_tr[(k, l)]])
                            i += 1

                P.barrier()
        def stage_W_gen(l, st):
            stf = Ring(nc, st, "wgf", [128, 2048], F32, 3)
            stb = Ring(nc, st, "wgb", [128, 2048], BF16, 3)
            for k in ("in", "uq", "ukv", "mo", "pw", "ro", "wo", "fg", "fu", "fd"):
                src = wsrc[k][l]
                dst = wb[k][l]
                K, N = src.shape
                for kc in range(K // 128):
                    for c0 in range(0, N, 2048):
                        w = min(2048, N - c0)
                        f, ftr = stf.next()
                        b, btr = stb.next()
                        P.dma("sp", f[:, 0:w], src[kc * 128:(kc + 1) * 128, c0:c0 + w], ftr, w=[ftr])
                        P.op("pool", lambda e: e.tensor_copy(out=b[:, 0:w], in_=f[:, 0:w]), r=[ftr], w=[btr])
                        P.dma("pool", dst[kc * 128:(kc + 1) * 128, c0:c0 + w], b[:, 0:w], btr, r=[btr], pw=[wb_tr[(k, l)]])
                        yield

        def stage_T(s):
            with ExitStack() as st:
                posrow = st.enter_context(sbt(nc, "posrow", [2, S], F32))
                posi = st.enter_context(sbt(nc, "posi", [1, S], I32))
                cst = st.enter_context(sbt(nc, "cst", [2, 160], F32))
                tab = st.enter_context(sbt(nc, "tab", [128, T * 160], F32))
                tmpf = st.enter_context(sbt(nc, "tmpf", [128, T * 160], F32))
                tmpi = st.enter_context(sbt(nc, "tmpi", [128, T * 160], I32))
                t_pr, t_pi, t_c, t_tab, t_f, t_i = Tr(), Tr(), Tr(), Tr(), Tr(), Tr()
                P.op("dve", lambda e: e.memset(posrow[:], 1.0), w=[t_pr])
                P.dma("sp", posi[:], pos_in[s:s + 1, :], t_pi, w=[t_pi])
                P.dma("sp", cst[:], cst_in[:, :], t_c, w=[t_c])
                P.op("dve", lambda e: e.tensor_copy(out=posrow[0:1, :], in_=posi[:]), r=[t_pi], pw=[t_pr])
                for t0 in range(0, T, 3):
                    n = min(3, T - t0)
                    bk = (t0 // 3) % 2
                    for j in range(n):
                        t = t0 + j
                        P.mm(ps[bk][:, j * 160:(j + 1) * 160], posrow[0:2, t * 128:(t + 1) * 128], cst[0:2, :], True, True,
                             [t_pr, t_c], pst[bk], first=(j == 0))
                    P.op("dve", lambda e, bk=bk, n=n, t0=t0: e.tensor_copy(out=tab[:, t0 * 160:(t0 + n) * 160], in_=ps[bk][:, 0:n * 160]),
                         r=[pst[bk]], pw=[t_tab])
                W = T * 160
                for c0 in range(0, W, 2560):
                    c1 = min(W, c0 + 2560)
                    a = tab[:, c0:c1]; f = tmpf[:, c0:c1]; ii = tmpi[:, c0:c1]
                    P.op("dve", lambda e, a=a, f=f: e.tensor_scalar(out=f, in0=a, scalar1=1.0 / TWO_PI, scalar2=None, op0=ALU.mult), r=[t_tab], w=[t_f])
                    P.op("dve", lambda e, f=f, ii=ii: e.tensor_copy(out=ii, in_=f), r=[t_f], w=[t_i])
                    P.op("dve", lambda e, f=f, ii=ii: e.tensor_copy(out=f, in_=ii), r=[t_i], w=[t_f])
                    P.op("dve", lambda e, a=a, f=f: e.scalar_tensor_tensor(out=a, in0=f, scalar=-6.28125, in1=a, op0=ALU.mult, op1=ALU.add), r=[t_f], w=[t_tab])
                    P.op("dve", lambda e, a=a, f=f: e.scalar_tensor_tensor(out=a, in0=f, scalar=-(TWO_PI - 6.28125), in1=a, op0=ALU.mult, op1=ALU.add), r=[t_f], w=[t_tab])
                    P.op("dve", lambda e, a=a, f=f: e.tensor_scalar(out=f, in0=a, scalar1=PI, scalar2=-TWO_PI, op0=ALU.is_gt, op1=ALU.mult), r=[t_tab], w=[t_f])
                    P.op("dve", lambda e, a=a, f=f: e.tensor_tensor(out=a, in0=a, in1=f, op=ALU.add), r=[t_f], w=[t_tab])
                    P.op("dve", lambda e, a=a, f=f: e.tensor_scalar(out=f, in0=a, scalar1=-PI, scalar2=TWO_PI, op0=ALU.is_lt, op1=ALU.mult), r=[t_tab], w=[t_f])
                    P.op("dve", lambda e, a=a, f=f: e.tensor_tensor(out=a, in0=a, in1=f, op=ALU.add), r=[t_f], w=[t_tab])
                    P.op("dve", lambda e, a=a: e.tensor_scalar(out=a, in0=a, scalar1=-3.1415925, scalar2=3.1415925, op0=ALU.max, op1=ALU.min), r=[t_tab], w=[t_tab])
                    P.op("act", lambda e, a=a: e.activation(out=a, in_=a, func=AF.Sin), r=[t_tab], w=[t_tab])
                P.dma("pool", tabd[s], tab[:], t_tab, r=[t_tab], w=[dtr[("tabd", s)]])

                P.barrier()
        def stage_NP(l, s, xsrc, xtr):
            with ExitStack() as st:
                hT = st.enter_context(sbt(nc, "hT", [128, 8, S], BF16)); t_hT = Tr("hT")
                gbc = st.enter_context(sbt(nc, "np_gbc", [128, D], F32)); t_g = Tr()
                gq = st.enter_context(sbt(nc, "np_gq", [128, 384], F32))
                tab = st.enter_context(sbt(nc, "np_tab", [128, T, 160], F32)); t_tab = Tr()
                P.dma("sp", gbc[:], ln_mix_pre[l].partition_broadcast(128), t_g, w=[t_g])
                P.dma("sp", gq[:, 0:256], mla_q_norm[l].partition_broadcast(128), t_g, pw=[t_g])
                P.dma("sp", gq[:, 256:384], mla_kv_norm[l].partition_broadcast(128), t_g, pw=[t_g])
                P.dma("sp", tab[:].rearrange("p t c -> p (t c)"), tabd[s], t_tab, r=[dtr[("tabd", s)]], w=[t_tab])
                xr = Ring(nc, st, "np_x", [128, D], F32, 3)
                hb = Ring(nc, st, "np_hb", [128, D], BF16, 2)
                sq = Ring(nc, st, "np_sq", [128, D], F32, 2)
                stt = Ring(nc, st, "np_st", [128, 8], F32, 4)
                for t in range(T):
                    xt, xtr_ = xr.next()
                    P.dma("sp", xt[:], xsrc[s, t * 128:(t + 1) * 128, :], xtr_, r=[xtr[s]], w=[xtr_])
                    sqt, sqtr = sq.next()
                    sm, smtr = stt.next()
                    ssq4(xt, sqt, sm, xtr_, sqtr, smtr)
                    rstd_from_ssq(sm[:, 0:1], sm[:, 1:2], D, [smtr])
                    h, htr = hb.next()
                    P.op("dve", lambda e, xt=xt, h=h, sm=sm: e.scalar_tensor_tensor(out=h[:], in0=xt[:], scalar=sm[:, 1:2], in1=gbc[:], op0=ALU.mult, op1=ALU.mult),
                         r=[xtr_, smtr, t_g], w=[htr])
                    bk = t % 2
                    pv = ps[bk][:].bitcast(BF16)
                    for kc in range(8):
                        P.tp(pv[:, kc * 128:(kc + 1) * 128], h[:, kc * 128:(kc + 1) * 128], ident[:], [htr, ctr], pst[bk], kc == 0)
                    en = "act" if t % 2 == 0 else "dve"
                    if en == "act":
                        P.op("act", lambda e, pv=pv, t=t: e.copy(out=hT[:, :, t * 128:(t + 1) * 128], in_=pv.rearrange("p (k c) -> p k c", k=8)), r=[pst[bk]], pw=[t_hT])
                    else:
                        P.op("dve", lambda e, pv=pv, t=t: e.tensor_copy(out=hT[:, :, t * 128:(t + 1) * 128], in_=pv.rearrange("p (k c) -> p k c", k=8)), r=[pst[bk]], pw=[t_hT])

                win = wb["in"][l].rearrange("(kc p) n -> p kc n", p=128)
                wtr_in = wb_tr[("in", l)]
                for kc in range(8):
                    P.dma("sp", hTd[s, kc], hT[:, kc, :], t_hT, r=[t_hT], pw=[dtr[("hTd", s)]])

                for g in ([ExitStack()] if "l" in SUB else []):
                    wl = g.enter_context(sbt(nc, "wl", [128, 8, 416], BF16)); t_wl = Tr()
                    wuq = g.enter_context(sbt(nc, "wuq", [128, 2, 768], BF16)); t_wuq = Tr()
                    wk = g.enter_context(sbt(nc, "wk", [128, 8, 64], BF16))
                    wv = g.enter_context(sbt(nc, "wv", [128, 8, 64], BF16)); t_wkv = Tr()
                    P.dma("sp", wl[:], win[:, :, 0:416], t_wl, r=[wtr_in], w=[t_wl])
                    P.dma("sp", wuq[:], wb["uq"][l].rearrange("(kc p) n -> p kc n", p=128), t_wuq, r=[wb_tr[("uq", l)]], w=[t_wuq])
                    ukv_v = wb["ukv"][l].rearrange("p (h c) -> p h c", h=8)
                    P.dma("sp", wk[:], ukv_v[:, :, 0:64], t_wkv, r=[wb_tr[("ukv", l)]], w=[t_wkv])
                    P.dma("sp", wv[:], ukv_v[:, :, 64:128], t_wkv, pw=[t_wkv])
                    lat = Ring(nc, g, "lat", [128, 416], BF16, 2)
                    sqj = Ring(nc, g, "lsq", [128, 256], F32, 2)
                    stl = Ring(nc, g, "lst", [128, 8], F32, 3)
                    tmpr = Ring(nc, g, "ltmp", [128, 4, 8, 16], F32, 2)
                    latT = Ring(nc, g, "latT", [128, 4, 128], BF16, 2)
                    qsb = Ring(nc, g, "qsb", [128, 8, 128], BF16, 2)
                    for qt_, qttr_ in zip(qsb.t, qsb.tr):
                        P.op("pool", lambda e, qt_=qt_: e.memset(qt_[:], 0.0), w=[qttr_])
                    qTb = Ring(nc, g, "qTb", [96, 8, 512], BF16, 2)
                    kTb = Ring(nc, g, "kTb", [96, 8, 512], BF16, 2)
                    ckb = Ring(nc, g, "ckb", [128, 512], BF16, 2)
                    kpb = Ring(nc, g, "kpb", [32, 512], BF16, 2)
                    vsb = Ring(nc, g, "vsb", [128, 8, 65], BF16, 2)
                    for vt, vtr in zip(vsb.t, vsb.tr):
                        P.op("pool", lambda e, vt=vt: e.memset(vt[:], 1.0), w=[vtr])
                    for tb in range(NB):
                        qT, qTtr = qTb.next()
                        kT, kTtr = kTb.next()
                        ck, cktr = ckb.next()
                        kp, kptr = kpb.next()
                        for tt in range(4):
                            t = tb * 4 + tt
                            tok = slice(t * 128, (t + 1) * 128)
                            bl = 2 if t % 2 == 0 else 0
                            bt = 3 if t % 2 == 0 else 1
                            for kc in range(8):
                                P.mm(ps[bl][:, 0:416], hT[:, kc, tok], wl[:, kc, :], kc == 0, kc == 7, [t_hT, t_wl], pst[bl])
                            la, latr = lat.next()
                            sj, sjtr = sqj.next()
                            sm, smtr = stl.next()
                            P.op("pool", lambda e, sm=sm: e.memset(sm[:], 0.0), w=[smtr])
                            P.op("act", lambda e, sj=sj, sm=sm: e.activation(out=sj[:, 0:256], in_=ps[bl][:, 0:256], func=AF.Square, accum_out=sm[:, 0:1]), r=[pst[bl], smtr], w=[sjtr], pw=[smtr])
                            P.op("act", lambda e, sj=sj, sm=sm: e.activation(out=sj[:, 0:128], in_=ps[bl][:, 256:384], func=AF.Square, accum_out=sm[:, 1:2]), r=[pst[bl], smtr], w=[sjtr], pw=[smtr])
                            P.op("act", lambda e, sm=sm: e.activation(out=sm[:, 2:3], in_=sm[:, 0:1], func=AF.Sqrt, bias=epsc[:, 0:1], scale=1.0 / 256), r=[smtr, ctr], pw=[smtr])
                            P.op("act", lambda e, sm=sm: e.activation(out=sm[:, 3:4], in_=sm[:, 1:2], func=AF.Sqrt, bias=epsc[:, 0:1], scale=1.0 / 128), r=[smtr], pw=[smtr])
                            P.op("dve", lambda e, sm=sm: e.reciprocal(out=sm[:, 4:6], in_=sm[:, 2:4]), r=[smtr], pw=[smtr])
                            P.op("dve", lambda e, la=la, sm=sm: e.scalar_tensor_tensor(out=la[:, 0:256], in0=ps[bl][:, 0:256], scalar=sm[:, 4:5], in1=gq[:, 0:256], op0=ALU.mult, op1=ALU.mult),
                                 r=[pst[bl], smtr, t_g], w=[latr])
                            P.op("dve", lambda e, la=la, sm=sm: e.scalar_tensor_tensor(out=la[:, 256:384], in0=ps[bl][:, 256:384], scalar=sm[:, 5:6], in1=gq[:, 256:384], op0=ALU.mult, op1=ALU.mult),
                                 r=[pst[bl], smtr, t_g], pw=[latr])
                            tm, tmtr = tmpr.next()
                            sn = tab[:, t, 0:16]; cs = tab[:, t, 16:32]
                            x1 = ps[bl][:, 384:400]; x2 = ps[bl][:, 400:416]
                            P.op("dve", lambda e, tm=tm, x1=x1, cs=cs: e.tensor_tensor(out=tm[:, 0, 0, :], in0=x1, in1=cs, op=ALU.mult), r=[pst[bl], t_tab], w=[tmtr])
                            P.op("dve", lambda e, tm=tm, x2=x2, sn=sn: e.tensor_tensor(out=tm[:, 1, 0, :], in0=x2, in1=sn, op=ALU.mult), r=[pst[bl]], pw=[tmtr])
                            P.op("dve", lambda e, tm=tm, x2=x2, cs=cs: e.tensor_tensor(out=tm[:, 2, 0, :], in0=x2, in1=cs, op=ALU.mult), r=[pst[bl]], pw=[tmtr])
                            P.op("dve", lambda e, tm=tm, x1=x1, sn=sn: e.tensor_tensor(out=tm[:, 3, 0, :], in0=x1, in1=sn, op=ALU.mult), r=[pst[bl]], pw=[tmtr])
                            P.op("dve", lambda e, tm=tm, la=la: e.tensor_tensor(out=la[:, 384:400], in0=tm[:, 0, 0, :], in1=tm[:, 1, 0, :], op=ALU.subtract), r=[tmtr], pw=[latr])
                            P.op("dve", lambda e, tm=tm, la=la: e.tensor_tensor(out=la[:, 400:416], in0=tm[:, 2, 0, :], in1=tm[:, 3, 0, :], op=ALU.add), r=[tmtr], pw=[latr])
                            if CUT < 2:
                                continue
                            pv = ps[bt][:].bitcast(BF16)
                            for j in range(3):
                                P.tp(pv[:, j * 128:(j + 1) * 128], la[:, j * 128:(j + 1) * 128], ident[:], [latr, ctr], pst[bt], j == 0)
                            P.tp(pv[0:32, 384:512], la[:, 384:416], ident[:], [latr], pst[bt], False)
                            lT, lTtr = latT.next()
                            P.op("dve", lambda e, lT=lT, pv=pv: e.tensor_copy(out=lT[:, 0:3, :], in_=pv[:, 0:384].rearrange("p (j c) -> p j c", j=3)), r=[pst[bt]], w=[lTtr])
                            P.op("dve", lambda e, ck=ck, pv=pv, tt=tt: e.tensor_copy(out=ck[:, tt * 128:(tt + 1) * 128], in_=pv[:, 256:384]), r=[pst[bt]], pw=[cktr] if tt else (), w=[cktr] if tt == 0 else ())
                            P.op("dve", lambda e, kp=kp, pv=pv, tt=tt: e.tensor_copy(out=kp[:, tt * 128:(tt + 1) * 128], in_=pv[0:32, 384:512]), r=[pst[bt]], pw=[kptr] if tt else (), w=[kptr] if tt == 0 else ())
                            if CUT < 2.1:
                                continue
                            for kc in range(2):
                                P.mm(ps[4][:, 0:480], lT[:, kc, :], wuq[:, kc, 0:480], kc == 0, kc == 1, [lTtr, t_wuq], pst[4])
                            for kc in range(2):
                                P.mm(ps[5][:, 0:288], lT[:, kc, :], wuq[:, kc, 480:768], kc == 0, kc == 1, [lTtr, t_wuq], pst[5])
                            if CUT < 2.3:
                                continue
                            q, qtr = qsb.next()
                            tm2, tm2tr = tmpr.next()
                            first = True
                            for (bk, h0, nh) in ((4, 0, 5), (5, 5, 3)):
                                pq = ps[bk][:, 0:nh * 96].rearrange("p (h d) -> p h d", h=nh)
                                qo = q[:, h0:h0 + nh, :]
                                P.op("act", lambda e, pq=pq, qo=qo: e.copy(out=qo[:, :, 0:64], in_=pq[:, :, 0:64]), r=[pst[bk]], pw=[qtr])
                                if CUT < 2.5:
                                    continue
                                csb = tab[:, t:t + 1, 16:32].to_broadcast([128, nh, 16])
                                snb = tab[:, t:t + 1, 0:16].to_broadcast([128, nh, 16])
                                x1 = pq[:, :, 64:80]; x2 = pq[:, :, 80:96]
                                tv = tm2[:, :, h0:h0 + nh, :]
                                P.op("dve", lambda e, tv=tv, x1=x1, csb=csb: e.tensor_tensor(out=tv[:, 0], in0=x1, in1=csb, op=ALU.mult), r=[pst[bk], t_tab], w=[tm2tr] if first else (), pw=() if first else [tm2tr])
                                P.op("dve", lambda e, tv=tv, x2=x2, snb=snb: e.tensor_tensor(out=tv[:, 1], in0=x2, in1=snb, op=ALU.mult), r=[pst[bk]], pw=[tm2tr])
                                P.op("dve", lambda e, tv=tv, x2=x2, csb=csb: e.tensor_tensor(out=tv[:, 2], in0=x2, in1=csb, op=ALU.mult), r=[pst[bk]], pw=[tm2tr])
                                P.op("dve", lambda e, tv=tv, x1=x1, snb=snb: e.tensor_tensor(out=tv[:, 3], in0=x1, in1=snb, op=ALU.mult), r=[pst[bk]], pw=[tm2tr])
                                P.op("dve", lambda e, tv=tv, qo=qo: e.tensor_tensor(out=qo[:, :, 64:80], in0=tv[:, 0], in1=tv[:, 1], op=ALU.subtract), r=[tm2tr], pw=[qtr])
                                P.op("dve", lambda e, tv=tv, qo=qo: e.tensor_tensor(out=qo[:, :, 80:96], in0=tv[:, 2], in1=tv[:, 3], op=ALU.add), r=[tm2tr], pw=[qtr])
                                first = False
                            if CUT < 2.7:
                                continue
                            pv6 = ps[6][:].bitcast(BF16)
                            for h in range(8):
                                P.tp(pv6[:, h * 128:(h + 1) * 128], q[:, h, :], ident[:], [qtr, ctr], pst[6], h == 0)
                            if CUT < 2.9:
                                continue
                            P.op("act", lambda e, qT=qT, pv6=pv6, tt=tt: e.copy(out=qT[:, :, tt * 128:(tt + 1) * 128], in_=pv6[0:96, :].rearrange("p (h c) -> p h c", h=8)),
                                 r=[pst[6]], w=[qTtr] if tt == 0 else (), pw=[qTtr] if tt else ())
                            if CUT < 4:
                                continue
                            P.mm(ps[7][:, 0:512], lT[:, 2, :], wv[:].rearrange("p h c -> p (h c)"), True, True, [lTtr, t_wkv], pst[7])
                            v, vtr = vsb.next()
                            P.op("dve", lambda e, v=v: e.tensor_copy(out=v[:, :, 0:64], in_=ps[7][:, 0:512].rearrange("p (h c) -> p h c", h=8)), r=[pst[7]], w=[vtr])
                            P.dma("sp", Vd[s, tok, :], v[:].rearrange("p h c -> p (h c)"), vtr, r=[vtr], pw=[dtr[("Vd", s)]])
                        if CUT < 5:
                            continue
                        blk = slice(tb * 512, (tb + 1) * 512)
                        for h in range(8):
                            bk = 2 + (h % 2) * 5
                            P.mm(ps[bk][0:64, :], wk[:, h, :], ck[:], True, True, [cktr, t_wkv], pst[bk])
                            if h % 2 == 0:
                                P.op("act", lambda e, kT=kT, h=h, bk=bk: e.copy(out=kT[0:64, h, :], in_=ps[bk][0:64, :]), r=[pst[bk]], w=[kTtr] if h == 0 else (), pw=[kTtr] if h else ())
                            else:
                                P.op("dve", lambda e, kT=kT, h=h, bk=bk: e.tensor_copy(out=kT[0:64, h, :], in_=ps[bk][0:64, :]), r=[pst[bk]], pw=[kTtr])
                        for h in range(8):
                            if h % 2 == 0:
                                P.op("act", lambda e, kT=kT, kp=kp, h=h: e.copy(out=kT[64:96, h, :], in_=kp[:, :]), r=[kptr], pw=[kTtr])
                            else:
                                P.op("dve", lambda e, kT=kT, kp=kp, h=h: e.tensor_copy(out=kT[64:96, h, :], in_=kp[:, :]), r=[kptr], pw=[kTtr])
                        P.dma("sp", QTd[s, :, :, blk].rearrange("h p c -> p h c"), qT[:], qTtr, r=[qTtr], pw=[dtr[("QTd", s)]])
                        P.dma("sp", KTd[s, :, :, blk].rearrange("h p c -> p h c"), kT[:], kTtr, r=[kTtr], pw=[dtr[("KTd", s)]])

                    P.barrier()
                    g.close()
                for g in ([ExitStack()] if "c" in SUB else []):
                    wc = g.enter_context(sbt(nc, "wc", [128, 8, 1024], BF16)); t_wc = Tr()
                    P.dma("sp", wc[:], win[:, :, O_CONV:O_CONV + 1024], t_wc, r=[wtr_in], w=[t_wc])
                    sgr = Ring(nc, g, "cv_sg", [128, 512], F32, 2)
                    aor = Ring(nc, g, "cv_a", [128, 512], F32, 3)
                    for tb in range(NB):
                        blk = slice(tb * 512, (tb + 1) * 512)
                        for j in range(4):
                            ba, bg = (2, 3) if j % 2 == 0 else (4, 5)
                            for kc in range(8):
                                P.mm(ps[ba][:], wc[:, kc, j * 128:(j + 1) * 128], hT[:, kc, blk], kc == 0, kc == 7, [t_hT, t_wc], pst[ba])
                            for kc in range(8):
                                P.mm(ps[bg][:], wc[:, kc, 512 + j * 128:512 + (j + 1) * 128], hT[:, kc, blk], kc == 0, kc == 7, [t_hT, t_wc], pst[bg])
                            sg, sgtr = sgr.next()
                            ao, aotr = aor.next()
                            P.op("act", lambda e, sg=sg, bg=bg: e.activation(out=sg[:], in_=ps[bg][:], func=AF.Sigmoid), r=[pst[bg]], w=[sgtr])
                            P.op("dve", lambda e, ao=ao, sg=sg, ba=ba: e.tensor_tensor(out=ao[:], in0=ps[ba][:], in1=sg[:], op=ALU.mult), r=[pst[ba], sgtr], w=[aotr])
                            P.dma("sp", aTd[s, j, :, blk], ao[:], aotr, r=[aotr], pw=[dtr[("aTd", s)]])

                    P.barrier()
                    g.close()
                for g in ([ExitStack()] if "r" in SUB else []):
                    wr = Ring(nc, g, "wr", [128, 8, 512], BF16, 2)
                    tmq = Ring(nc, g, "r_tm", [128, 4, 4, 64], F32, 2)
                    o16 = Ring(nc, g, "r_o16", [128, 512], BF16, 3)
                    o32 = Ring(nc, g, "r_o32", [128, 512], F32, 3)
                    for (kind, c0) in (("q", O_RQ), ("k", O_RK), ("v0", O_RV), ("v1", O_RV + 512), ("g0", O_RG), ("g1", O_RG + 512)):
                        w, wtr = wr.next()
                        P.dma("sp", w[:], win[:, :, c0:c0 + 512], wtr, r=[wtr_in], w=[wtr])
                        for t in range(T):
                            tok = slice(t * 128, (t + 1) * 128)
                            bk = 2 + (t % 4)
                            for kc in range(8):
                                P.mm(ps[bk][:], hT[:, kc, tok], w[:, kc, :], kc == 0, kc == 7, [t_hT, wtr], pst[bk])
                            if kind in ("q", "k"):
                                o, otr = o16.next()
                                tm, tmtr = tmq.next()
                                pq = ps[bk][:].rearrange("p (h d) -> p h d", h=4)
                                ov = o[:].rearrange("p (h d) -> p h d", h=4)
                                csb = tab[:, t:t + 1, 96:160].to_broadcast([128, 4, 64])
                                snb = tab[:, t:t + 1, 32:96].to_broadcast([128, 4, 64])
                                x1 = pq[:, :, 0:64]; x2 = pq[:, :, 64:128]
                                P.op("dve", lambda e, tm=tm, x1=x1, csb=csb: e.tensor_tensor(out=tm[:, 0], in0=x1, in1=csb, op=ALU.mult), r=[pst[bk], t_tab], w=[tmtr])
                                P.op("dve", lambda e, tm=tm, x2=x2, snb=snb: e.tensor_tensor(out=tm[:, 1], in0=x2, in1=snb, op=ALU.mult), r=[pst[bk]], pw=[tmtr])
                                P.op("dve", lambda e, tm=tm, x2=x2, csb=csb: e.tensor_tensor(out=tm[:, 2], in0=x2, in1=csb, op=ALU.mult), r=[pst[bk]], pw=[tmtr])
                                P.op("dve", lambda e, tm=tm, x1=x1, snb=snb: e.tensor_tensor(out=tm[:, 3], in0=x1, in1=snb, op=ALU.mult), r=[pst[bk]], pw=[tmtr])
                                if kind == "q":
                                    P.op("dve", lambda e, tm=tm, ov=ov: e.tensor_tensor(out=ov[:, :, 0:64], in0=tm[:, 0], in1=tm[:, 1], op=ALU.subtract), r=[tmtr], w=[otr])
                                    P.op("dve", lambda e, tm=tm, ov=ov: e.tensor_tensor(out=ov[:, :, 64:128], in0=tm[:, 2], in1=tm[:, 3], op=ALU.add), r=[tmtr], pw=[otr])
                                else:
                                    sc = float(128 ** -0.5)
                                    P.op("dve", lambda e, tm=tm: e.tensor_tensor(out=tm[:, 0], in0=tm[:, 0], in1=tm[:, 1], op=ALU.subtract), r=[tmtr], w=[tmtr])
                                    P.op("dve", lambda e, tm=tm: e.tensor_tensor(out=tm[:, 2], in0=tm[:, 2], in1=tm[:, 3], op=ALU.add), r=[tmtr], w=[tmtr])
                                    P.op("act", lambda e, tm=tm, ov=ov: e.mul(out=ov[:, :, 0:64], in_=tm[:, 0], mul=sc), r=[tmtr], w=[otr])
                                    P.op("act", lambda e, tm=tm, ov=ov: e.mul(out=ov[:, :, 64:128], in_=tm[:, 2], mul=sc), r=[tmtr], pw=[otr])
                                dst = (rqd if kind == "q" else rkd)[s, tok, :]
                                P.dma("sp", dst, o[:], otr, r=[otr], pw=[dtr[("rqd" if kind == "q" else "rkd", s)]])
                            elif kind in ("v0", "v1"):
                                o, otr = o16.next()
                                if t % 2 == 0:
                                    P.op("act", lambda e, o=o, bk=bk: e.copy(out=o[:], in_=ps[bk][:]), r=[pst[bk]], w=[otr])
                                else:
                                    P.op("dve", lambda e, o=o, bk=bk: e.tensor_copy(out=o[:], in_=ps[bk][:]), r=[pst[bk]], w=[otr])
                                half = 0 if kind == "v0" else 512
                                P.dma("sp", rvd[s, tok, half:half + 512], o[:], otr, r=[otr], pw=[dtr[("rvd", s)]])
                            else:
                                o, otr = o32.next()
                                P.op("act", lambda e, o=o, bk=bk: e.activation(out=o[:], in_=ps[bk][:], func=AF.Silu), r=[pst[bk]], w=[otr])
                                half = 0 if kind == "g0" else 512
                                P.dma("sp", sgd[s, tok, half:half + 512], o[:], otr, r=[otr], pw=[dtr[("sgd", s)]])

                    P.barrier()
                    g.close()
                for g in ([ExitStack()] if ("g" in SUB and debug) else []):
                    wr = Ring(nc, g, "wg_", [128, 8, 512], BF16, 2)
                    gor = Ring(nc, g, "g_o", [128, 512], F32, 3)
                    for gg in range(6):
                        w, wtr = wr.next()
                        P.dma("sp", w[:], win[:, :, O_GATE + gg * 512:O_GATE + (gg + 1) * 512], wtr, r=[wtr_in], w=[wtr])
                        for tb in range(NB):
                            blk = slice(tb * 512, (tb + 1) * 512)
                            for j in range(4):
                                bk = 2 + (j % 4)
                                for kc in range(8):
                                    P.mm(ps[bk][:], w[:, kc, j * 128:(j + 1) * 128], hT[:, kc, blk], kc == 0, kc == 7, [t_hT, wtr], pst[bk])
                                o, otr = gor.next()
                                P.op("act", lambda e, o=o, bk=bk: e.activation(out=o[:], in_=ps[bk][:], func=AF.Sigmoid), r=[pst[bk]], w=[otr])
                                P.dma("sp", gtd[s, gg * 4 + j, :, blk], o[:], otr, r=[otr], pw=[dtr[("gtd", s)]])

                    P.barrier()
                    g.close()
                P.barrier()
        def stage_A(l, s, with_conv=True, wl=None):
            scale = float(96 ** -0.5)
            with ExitStack() as st:
                wgen = stage_W_gen(wl, st) if wl is not None else iter(())
                cgen = stage_C_gen(l, s, st, 7) if with_conv else iter(())
                next(cgen, None)
                V = st.enter_context(sbt(nc, "at_V", [128, T, NH * 65], BF16)); t_V = Tr()
                sel = st.enter_context(sbt(nc, "at_sel", [65, 64], F32)); t_sel = Tr()
                P.op("pool", lambda e: e.memset(sel[:], 0.0), w=[t_sel])
                P.op("pool", lambda e: e.memset(sel[64:65, :], 1.0), pw=[t_sel])
                Vv = Vd[s].rearrange("(t p) c -> p t c", p=128)
                for t0 in range(0, T, 8):
                    P.dma("sp", V[:, t0:t0 + 8, :], Vv[:, t0:t0 + 8, :], t_V, r=[dtr[("Vd", s)]], pw=[t_V] if t0 else (), w=[t_V] if t0 == 0 else ())
                qr = Ring(nc, st, "at_q", [96, S], BF16, 2)
                kr = Ring(nc, st, "at_k", [96, S], BF16, 2)
                pr = Ring(nc, st, "at_p", [128, 512], BF16, 4)
                osb = Ring(nc, st, "at_o", [65, 512], F32, 2)
                rbc = Ring(nc, st, "at_r", [64, 512], F32, 2)
                oT = Ring(nc, st, "at_oT", [64, 512], BF16, 2)
                its = [(h, qb, kt) for h in range(NH) for qb in range(NB) for kt in range(T)]
                PF = 3
                qk = {}
                crate = min(1.0, 1.06 * (139 * NB + 8) / len(its))

                def get_qk(h):
                    if h not in qk:
                        q, qtr = qr.next()
                        k, ktr = kr.next()
                        P.dma("sp", q[:], QTd[s, h], qtr, r=[dtr[("QTd", s)]], w=[qtr])
                        P.dma("sp", k[:], KTd[s, h], ktr, r=[dtr[("KTd", s)]], w=[ktr])
                        qk[h] = (q, qtr, k, ktr)
                    return qk[h]

                def emit_score(i):
                    h, qb, kt = its[i]
                    q, qtr, k, ktr = get_qk(h)
                    sbk = i % 4
                    P.mm(ps[sbk][:], k[:, kt * 128:(kt + 1) * 128], q[:, qb * 512:(qb + 1) * 512], True, True, [qtr, ktr], pst[sbk])

                for i in range(min(PF, len(its))):
                    emit_score(i)
                for i, (h, qb, kt) in enumerate(its):
                    qs = slice(qb * 512, (qb + 1) * 512)
                    ob = 4 + (qb % 2)
                    sbk = i % 4
                    p, ptr = pr.next()
                    P.op("act", lambda e, p=p, sbk=sbk: e.activation(out=p[:], in_=ps[sbk][:], func=AF.Exp, scale=scale), r=[pst[sbk]], w=[ptr])
                    if i + PF < len(its):
                        emit_score(i + PF)
                    P.mm(ps[ob][0:65, :], V[:, kt, h * 65:(h + 1) * 65], p[:], kt == 0, kt == T - 1, [t_V, ptr], pst[ob])
                    if int((i + 1) * crate) > int(i * crate):
                        next(cgen, None)
                    if i % 16 == 8:
                        next(wgen, None)
                    if kt == T - 1:
                        o, otr = osb.next()
                        P.op("dve", lambda e, o=o, ob=ob: e.tensor_copy(out=o[:], in_=ps[ob][0:65, :]), r=[pst[ob]], w=[otr])
                        P.mm(ps[6][0:64, :], sel[:], o[:], True, True, [t_sel, otr], pst[6])
                        rb, rbtr = rbc.next()
                        P.op("dve", lambda e, rb=rb: e.reciprocal(out=rb[:], in_=ps[6][0:64, :]), r=[pst[6]], w=[rbtr])
                        ot, ottr = oT.next()
                        P.op("dve", lambda e, ot=ot, o=o, rb=rb: e.tensor_tensor(out=ot[:], in0=o[0:64, :], in1=rb[:], op=ALU.mult), r=[otr, rbtr], w=[ottr])
                        P.dma("pool", oTd[s, h // 2, (h % 2) * 64:(h % 2) * 64 + 64, qs], ot[:], ottr, r=[ottr], pw=[dtr[("oTd", s)]])
                for _ in cgen:
                    pass
                for _ in wgen:
                    pass
                P.barrier()

        def stage_C_gen(l, s, st, pb):
            cw = st.enter_context(sbt(nc, "cv_cw", [34, 512], F32)); t_cw = Tr()
            cwT = st.enter_context(sbt(nc, "cv_cwT", [128, 4, 34], F32)); t_cwT = Tr()
            P.dma("sp", cw[0:31, :], conv_w_dw[l], t_cw, w=[t_cw])
            P.dma("sp", cw[31:32, :], conv_b_dw[l:l + 1, :], t_cw, pw=[t_cw])
            P.dma("sp", cw[32:33, :], conv_ln_g[l:l + 1, :], t_cw, pw=[t_cw])
            P.dma("sp", cw[33:34, :], conv_ln_b[l:l + 1, :], t_cw, pw=[t_cw])
            for cc in range(4):
                P.tp(ps[pb][:, cc * 34:(cc + 1) * 34], cw[0:34, cc * 128:(cc + 1) * 128], identf[0:34, 0:34], [t_cw, ctr], pst[pb], cc == 0)
            P.op("dve", lambda e: e.tensor_copy(out=cwT[:], in_=ps[pb][:, 0:136].rearrange("p (c j) -> p c j", c=4)), r=[pst[pb]], w=[t_cwT])
            apr = Ring(nc, st, "cv_ap", [128, 4, 542], F32, 2)
            yr = Ring(nc, st, "cv_y", [128, 4, 512], F32, 2)
            sqr = Ring(nc, st, "cv_sq", [128, 512], F32, 2)
            mr = Ring(nc, st, "cv_m", [128, 3, 512], F32, 2)
            tr_ = Ring(nc, st, "cv_t", [128, 512], F32, 2)
            outr = Ring(nc, st, "cv_o", [128, 512], BF16, 3)
            yield
            for tb in range(NB):
                blk = slice(tb * 512, (tb + 1) * 512)
                ap_, aptr = apr.next()
                lo = tb * 512 - 15
                hi = tb * 512 + 512 + 15
                first = True
                if lo < 0:
                    P.op("pool", lambda e: e.memset(ap_[:, :, 0:15], 0.0), w=[aptr])
                    first = False
                if hi > S:
                    P.op("pool", lambda e: e.memset(ap_[:, :, 527:542], 0.0), w=[aptr] if first else (), pw=() if first else [aptr])
                    first = False
                c_lo = max(lo, 0); c_hi = min(hi, S)
                for cc in range(4):
                    P.dma("sp", ap_[:, cc, c_lo - lo:c_hi - lo], aTd[s, cc, :, c_lo:c_hi], aptr, r=[dtr[("aTd", s)]], w=[aptr] if first else (), pw=() if first else [aptr])
                    first = False
                y_, ytr = yr.next()
                for cc in range(4):
                    P.op("dve", lambda e: e.tensor_scalar(out=y_[:, cc, :], in0=ap_[:, cc, 0:512], scalar1=cwT[:, cc, 0:1], scalar2=cwT[:, cc, 31:32], op0=ALU.mult, op1=ALU.add),
                         r=[aptr, t_cwT], w=[ytr] if cc == 0 else (), pw=[ytr] if cc else ())
                    yield
                    for j in range(1, 31):
                        P.op("dve", lambda e: e.scalar_tensor_tensor(out=y_[:, cc, :], in0=ap_[:, cc, j:j + 512], scalar=cwT[:, cc, j:j + 1], in1=y_[:, cc, :], op0=ALU.mult, op1=ALU.add),
                             r=[aptr], pw=[ytr])
                        yield
                for cc in range(4):
                    P.mm(ps[pb][:], onesf[:], y_[:, cc, :], cc == 0, cc == 3, [ytr, ctr], pst[pb])
                m, mtr = mr.next()
                P.op("dve", lambda e: e.tensor_scalar(out=m[:, 0, :], in0=ps[pb][:], scalar1=1.0 / 512, scalar2=None, op0=ALU.mult), r=[pst[pb]], w=[mtr])
                yield
                for cc in range(4):
                    sq_, sqtr = sqr.next()
                    P.op("dve", lambda e: e.tensor_tensor(out=sq_[:], in0=y_[:, cc, :], in1=y_[:, cc, :], op=ALU.mult), r=[ytr], w=[sqtr])
                    P.mm(ps[pb][:], onesf[:], sq_[:], cc == 0, cc == 3, [sqtr, ctr], pst[pb])
                    yield
                P.op("dve", lambda e: e.tensor_tensor(out=m[:, 1, :], in0=m[:, 0, :], in1=m[:, 0, :], op=ALU.mult), r=[mtr], pw=[mtr])
                P.op("dve", lambda e: e.scalar_tensor_tensor(out=m[:, 1, :], in0=ps[pb][:], scalar=1.0 / 512, in1=m[:, 1, :], op0=ALU.mult, op1=ALU.subtract), r=[pst[pb], mtr], pw=[mtr])
                yield
                P.op("act", lambda e: e.activation(out=m[:, 2, :], in_=m[:, 1, :], func=AF.Sqrt, bias=epsc[:, 0:1], scale=1.0), r=[mtr, ctr], pw=[mtr])
                P.op("dve", lambda e: e.reciprocal(out=m[:, 2, :], in_=m[:, 2, :]), r=[mtr], pw=[mtr])
                yield
                for cc in range(4):
                    t_, ttr = tr_.next()
                    P.op("dve", lambda e: e.tensor_tensor(out=t_[:], in0=y_[:, cc, :], in1=m[:, 0, :], op=ALU.subtract), r=[ytr, mtr], w=[ttr])
                    yield
                    P.op("dve", lambda e: e.tensor_tensor(out=t_[:], in0=t_[:], in1=m[:, 2, :], op=ALU.mult), r=[mtr], w=[ttr])
                    o, otr = outr.next()
                    P.op("act", lambda e: e.activation(out=o[:], in_=t_[:], func=AF.Silu, bias=cwT[:, cc, 33:34], scale=cwT[:, cc, 32:33]), r=[ttr, t_cwT], w=[otr])
                    P.dma("pool", cvTd[s, cc, :, blk], o[:], otr, r=[otr], pw=[dtr[("cvTd", s)]])
                    yield

        def stage_R(l, s):
            with ExitStack() as st:
                lgr = st.enter_context(sbt(nc, "rt_lg", [128, 8], F32)); t_lg = Tr()
                iof = st.enter_context(sbt(nc, "rt_iof", [128, 128], F32))
                ioq = st.enter_context(sbt(nc, "rt_ioq", [128, 128], F32))
                iop = st.enter_context(sbt(nc, "rt_iop", [128, 1], F32))
                t_io = Tr()
                tA = st.enter_context(sbt(nc, "rt_tA", [128, 128], F32))
                tB = st.enter_context(sbt(nc, "rt_tB", [128, 128], F32))
                tC = st.enter_context(sbt(nc, "rt_tC", [128, 128], F32)); t_tmp = Tr()
                DT = st.enter_context(sbt(nc, "rt_DT", [128, RH, 128], F32))
                CF = st.enter_context(sbt(nc, "rt_CF", [128, RH, 128], F32))
                CB = st.enter_context(sbt(nc, "rt_CB", [128, RH, 128], F32))
                pc = st.enter_context(sbt(nc, "rt_pc", [128, 4, RH], F32))
                t_tb = Tr()
                P.dma("sp", lgr[:], ret_decay_logits[l].rearrange("a h -> (a h)").partition_broadcast(128), t_lg, w=[t_lg])
                P.op("act", lambda e: e.activation(out=lgr[:], in_=lgr[:], func=AF.Exp, scale=-1.0), r=[t_lg], w=[t_lg])
                P.op("dve", lambda e: e.tensor_scalar(out=lgr[:], in0=lgr[:], scalar1=1.0, scalar2=None, op0=ALU.add), r=[t_lg], w=[t_lg])
                P.op("act", lambda e: e.activation(out=lgr[:], in_=lgr[:], func=AF.Ln), r=[t_lg], w=[t_lg])
                P.op("dve", lambda e: e.tensor_scalar(out=lgr[:], in0=lgr[:], scalar1=-1.0, scalar2=None, op0=ALU.mult), r=[t_lg], w=[t_lg])
                P.op("pool", lambda e: e.iota(iof[:], [[1, 128]], base=0, channel_multiplier=-1, allow_small_or_imprecise_dtypes=True), w=[t_io])
                P.op("pool", lambda e: e.iota(ioq[:], [[1, 128]], base=0, channel_multiplier=0, allow_small_or_imprecise_dtypes=True), pw=[t_io])
                P.op("dve", lambda e: e.tensor_scalar(out=iop[:], in0=iof[:, 0:1], scalar1=-1.0, scalar2=None, op0=ALU.mult), r=[t_io], pw=[t_io])
                for h in range(RH):
                    lf = lgr[:, h:h + 1]; lb = lgr[:, RH + h:RH + h + 1]
                    P.op("dve", lambda e: e.tensor_scalar(out=tA[:], in0=iof[:], scalar1=0.0, scalar2=None, op0=ALU.max), r=[t_io], w=[t_tmp])
                    P.op("act", lambda e, lf=lf: e.activation(out=tA[:], in_=tA[:], func=AF.Exp, scale=lf), r=[t_tmp, t_lg], w=[t_tmp])
                    P.op("dve", lambda e: e.tensor_scalar(out=tB[:], in0=iof[:], scalar1=0.0, scalar2=None, op0=ALU.is_ge), r=[t_io], pw=[t_tmp])
                    P.op("dve", lambda e: e.tensor_tensor(out=tA[:], in0=tA[:], in1=tB[:], op=ALU.mult), r=[t_tmp], w=[t_tmp])
                    P.op("dve", lambda e: e.tensor_scalar(out=tC[:], in0=iof[:], scalar1=-1.0, scalar2=0.0, op0=ALU.mult, op1=ALU.max), r=[t_io], pw=[t_tmp])
                    P.op("act", lambda e, lb=lb: e.activation(out=tC[:], in_=tC[:], func=AF.Exp, scale=lb), r=[t_tmp], w=[t_tmp])
                    P.op("dve", lambda e: e.tensor_scalar(out=tB[:], in0=iof[:], scalar1=0.0, scalar2=None, op0=ALU.is_lt), r=[t_io], w=[t_tmp])
                    P.op("dve", lambda e: e.tensor_tensor(out=tC[:], in0=tC[:], in1=tB[:], op=ALU.mult), r=[t_tmp], w=[t_tmp])
                    P.op("dve", lambda e, h=h: e.tensor_tensor(out=DT[:, h, :], in0=tA[:], in1=tC[:], op=ALU.add), r=[t_tmp], pw=[t_tb])
                    P.op("dve", lambda e: e.tensor_scalar(out=tA[:], in0=ioq[:], scalar1=1.0, scalar2=None, op0=ALU.add), r=[t_io], w=[t_tmp])
                    P.op("act", lambda e, h=h, lf=lf: e.activation(out=CF[:, h, :], in_=tA[:], func=AF.Exp, scale=lf), r=[t_tmp], pw=[t_tb])
                    P.op("dve", lambda e: e.tensor_scalar(out=tA[:], in0=ioq[:], scalar1=-1.0, scalar2=128.0, op0=ALU.mult, op1=ALU.add), r=[t_io], w=[t_tmp])
                    P.op("act", lambda e, h=h, lb=lb: e.activation(out=CB[:, h, :], in_=tA[:], func=AF.Exp, scale=lb), r=[t_tmp], pw=[t_tb])
                    P.op("dve", lambda e: e.tensor_scalar(out=tA[:, 0:1], in0=iop[:], scalar1=-1.0, scalar2=127.0, op0=ALU.mult, op1=ALU.add), r=[t_io], w=[t_tmp])
                    P.op("act", lambda e, h=h, lf=lf: e.activation(out=pc[:, 0, h:h + 1], in_=tA[:, 0:1], func=AF.Exp, scale=lf), r=[t_tmp], pw=[t_tb])
                    P.op("act", lambda e, h=h, lb=lb: e.activation(out=pc[:, 1, h:h + 1], in_=iop[:], func=AF.Exp, scale=lb), r=[t_io], pw=[t_tb])
                    P.op("act", lambda e, h=h, lf=lf: e.activation(out=pc[:, 2, h:h + 1], in_=lf, func=AF.Exp, scale=128.0), r=[t_lg], pw=[t_tb])
                    P.op("act", lambda e, h=h, lb=lb: e.activation(out=pc[:, 3, h:h + 1], in_=lb, func=AF.Exp, scale=128.0), r=[t_lg], pw=[t_tb])
                gnb = st.enter_context(sbt(nc, "rt_gn", [128, D], F32)); t_gn = Tr()
                P.dma("sp", gnb[:], ret_gn_g[l].partition_broadcast(128), t_gn, w=[t_gn])

                Rall = st.enter_context(sbt(nc, "rt_Rall", [128, T, 1024], BF16)); t_Rall = Tr()
                Rb = st.enter_context(sbt(nc, "rt_Rb", [128, RH, 256], F32)); t_Rb = Tr()
                Sf = st.enter_context(sbt(nc, "rt_Sf", [128, RH, 256], F32))
                Sfb = st.enter_context(sbt(nc, "rt_Sfb", [128, RH, 256], BF16)); t_Sf = Tr()
                P.op("pool", lambda e: e.memset(Rb[:], 0.0), w=[t_Rb])
                P.op("pool", lambda e: e.memset(Rall[:, T - 1, :], 0.0), w=[t_Rall])
                P.op("pool", lambda e: e.memset(Sf[:], 0.0), w=[t_Sf])
                P.op("pool", lambda e: e.memset(Sfb[:], 0.0), pw=[t_Sf])
                kin = Ring(nc, st, "rt_k", [128, 512], BF16, 3)
                vin = Ring(nc, st, "rt_v", [128, 1024], BF16, 3)
                qin = Ring(nc, st, "rt_q", [128, 512], BF16, 2)
                gin = Ring(nc, st, "rt_g", [128, 1024], F32, 2)
                kc_ = Ring(nc, st, "rt_kc", [128, RH, 128], BF16, 2)
                for c in range(T - 1, 0, -1):
                    tok = slice(c * 128, (c + 1) * 128)
                    k, ktr = kin.next(); v, vtr = vin.next()
                    P.dma("sp", k[:], rkd[s, tok, :], ktr, r=[dtr[("rkd", s)]], w=[ktr])
                    P.dma("sp", v[:], rvd[s, tok, :], vtr, r=[dtr[("rvd", s)]], w=[vtr])
                    kb, kbtr = kc_.next()
                    for h in range(RH):
                        if h % 2:
                            P.op("dve", lambda e, kb=kb, k=k, h=h: e.tensor_scalar(out=kb[:, h, :], in0=k[:, h * 128:(h + 1) * 128], scalar1=pc[:, 1, h:h + 1], scalar2=None, op0=ALU.mult),
                                 r=[ktr, t_tb], pw=[kbtr])
                        else:
                            P.op("act", lambda e, kb=kb, k=k, h=h: e.mul(out=kb[:, h, :], in_=k[:, h * 128:(h + 1) * 128], mul=pc[:, 1, h:h + 1]),
                                 r=[ktr, t_tb], w=[kbtr] if h == 0 else (), pw=[kbtr] if h else ())
                    for h in range(RH):
                        bk = h // 2
                        P.mm(ps[bk][:, (h % 2) * 256:(h % 2) * 256 + 256], kb[:, h, :], v[:, h * 256:(h + 1) * 256], True, True, [kbtr, vtr], pst[bk], first=(h % 2 == 0))
                    for h in range(RH):
                        bk = h // 2
                        P.op("dve", lambda e, h=h, bk=bk: e.scalar_tensor_tensor(out=Rb[:, h, :], in0=Rb[:, h, :], scalar=pc[:, 3, h:h + 1], in1=ps[bk][:, (h % 2) * 256:(h % 2) * 256 + 256], op0=ALU.mult, op1=ALU.add),
                             r=[pst[bk], t_tb], w=[t_Rb])
                    P.op("act", lambda e, c=c: e.copy(out=Rall[:, c - 1, :], in_=Rb[:].rearrange("p h e -> p (h e)")), r=[t_Rb], pw=[t_Rall])
                qT3 = Ring(nc, st, "rt_qT", [128, 3, RH, 128], BF16, 2)
                kTr = Ring(nc, st, "rt_kT", [128, RH, 128], BF16, 2)
                stm = Ring(nc, st, "rt_stm", [128, RH, 128], BF16, 2)
                bnr = Ring(nc, st, "rt_bn", [128, RH, 8], F32, 2)
                onr = Ring(nc, st, "rt_on", [128, D], F32, 2)
                gtd_ = Ring(nc, st, "rt_gt", [128, D], BF16, 2)
                gTr = Ring(nc, st, "rt_gT", [128, 8, 128], BF16, 2)
                for c in range(T):
                    tok = slice(c * 128, (c + 1) * 128)
                    k, ktr = kin.next(); v, vtr = vin.next(); q, qtr = qin.next(); g_, gtr = gin.next()
                    P.dma("sp", q[:], rqd[s, tok, :], qtr, r=[dtr[("rqd", s)]], w=[qtr])
                    P.dma("sp", k[:], rkd[s, tok, :], ktr, r=[dtr[("rkd", s)]], w=[ktr])
                    P.dma("sp", v[:], rvd[s, tok, :], vtr, r=[dtr[("rvd", s)]], w=[vtr])
                    P.dma("sp", g_[:], sgd[s, tok, :], gtr, r=[dtr[("sgd", s)]], w=[gtr])
                    pv = ps[0][:].bitcast(BF16)
                    for h in range(RH):
                        P.tp(pv[:, h * 128:(h + 1) * 128], q[:, h * 128:(h + 1) * 128], ident[:], [qtr, ctr], pst[0], h == 0)
                    for h in range(RH):
                        P.tp(pv[:, 512 + h * 128:512 + (h + 1) * 128], k[:, h * 128:(h + 1) * 128], ident[:], [ktr], pst[0], False)
                    qT, qTtr = qT3.next(); kT, kTtr = kTr.next()
                    pq = pv[:, 0:512].rearrange("p (h c) -> p h c", h=RH)
                    P.op("act", lambda e, qT=qT, pq=pq: e.copy(out=qT[:, 0], in_=pq), r=[pst[0]], w=[qTtr])
                    P.op("dve", lambda e, qT=qT, pq=pq: e.tensor_tensor(out=qT[:, 1], in0=pq, in1=CF[:], op=ALU.mult), r=[pst[0], t_tb], pw=[qTtr])
                    P.op("dve", lambda e, qT=qT, pq=pq: e.tensor_tensor(out=qT[:, 2], in0=pq, in1=CB[:], op=ALU.mult), r=[pst[0], t_tb], pw=[qTtr])
                    P.op("act", lambda e, kT=kT, pv=pv: e.copy(out=kT[:], in_=pv[:, 512:1024].rearrange("p (h c) -> p h c", h=RH)), r=[pst[0]], w=[kTtr])
                    kf, kftr = kc_.next()
                    for h in range(RH):
                        P.op("act", lambda e, kf=kf, k=k, h=h: e.mul(out=kf[:, h, :], in_=k[:, h * 128:(h + 1) * 128], mul=pc[:, 0, h:h + 1]),
                             r=[ktr, t_tb], w=[kftr] if h == 0 else (), pw=[kftr] if h else ())
                    for h in range(RH):
                        P.mm(ps[1][:, h * 128:(h + 1) * 128], kT[:, h, :], qT[:, 0, h, :], True, True, [kTtr, qTtr], pst[1], first=(h == 0))
                    sm_, smtr = stm.next()
                    P.op("dve", lambda e, sm_=sm_: e.tensor_tensor(out=sm_[:], in0=ps[1][:].rearrange("p (h c) -> p h c", h=RH), in1=DT[:], op=ALU.mult), r=[pst[1], t_tb], w=[smtr])
                    for h in range(RH):
                        bk = 2 + h // 2
                        oc = slice((h % 2) * 256, (h % 2) * 256 + 256)
                        P.mm(ps[bk][:, oc], sm_[:, h, :], v[:, h * 256:(h + 1) * 256], True, False, [smtr, vtr], pst[bk], first=(h % 2 == 0))
                        P.mm(ps[bk][:, oc], qT[:, 1, h, :], Sfb[:, h, :], False, False, [qTtr, t_Sf], pst[bk])
                        P.mm(ps[bk][:, oc], qT[:, 2, h, :], Rall[:, c, h * 256:(h + 1) * 256], False, True, [qTtr, t_Rall], pst[bk])
                    for h in range(RH):
                        bk = 4 + h // 2
                        oc = slice((h % 2) * 256, (h % 2) * 256 + 256)
                        P.mm(ps[bk][:, oc], kf[:, h, :], v[:, h * 256:(h + 1) * 256], True, True, [kftr, vtr], pst[bk], first=(h % 2 == 0))
                    for h in range(RH):
                        bk = 4 + h // 2
                        oc = slice((h % 2) * 256, (h % 2) * 256 + 256)
                        P.op("dve", lambda e, h=h, bk=bk, oc=oc: e.scalar_tensor_tensor(out=Sf[:, h, :], in0=Sf[:, h, :], scalar=pc[:, 2, h:h + 1], in1=ps[bk][:, oc], op0=ALU.mult, op1=ALU.add),
                             r=[pst[bk], t_tb], w=[t_Sf])
                    P.op("act", lambda e: e.copy(out=Sfb[:], in_=Sf[:]), r=[t_Sf], w=[t_Sf])
                    if debug and c == 1:
                        dO = st.enter_context(sbt(nc, "dbgO", [128, 1024], F32)); t_dO = Tr()
                        P.op("dve", lambda e: e.tensor_copy(out=dO[:, 0:512], in_=ps[2][:]), r=[pst[2]], w=[t_dO])
                        P.op("dve", lambda e: e.tensor_copy(out=dO[:, 512:1024], in_=ps[3][:]), r=[pst[3]], pw=[t_dO])
                        P.dma("pool", dbg_O[:, :], dO[:], t_dO, r=[t_dO])
                        P.dma("pool", dbg_qT[:, :], qT[:].rearrange("p a h c -> p (a h c)"), qTtr, r=[qTtr])
                        P.dma("pool", dbg_kT[:, :], kT[:].rearrange("p h c -> p (h c)"), kTtr, r=[kTtr])
                        P.dma("pool", dbg_sm[:, :], sm_[:].rearrange("p h c -> p (h c)"), smtr, r=[smtr])
                        P.dma("pool", dbg_kf[:, :], kf[:].rearrange("p h c -> p (h c)"), kftr, r=[kftr])
                        P.dma("pool", dbg_DT[:, :], DT[:].rearrange("p h c -> p (h c)"), t_dO, r=[t_tb])
                        P.dma("pool", dbg_CF[:, :], CF[:].rearrange("p h c -> p (h c)"), t_dO, r=[t_tb])
                        P.dma("pool", dbg_CB[:, :], CB[:].rearrange("p h c -> p (h c)"), t_dO, r=[t_tb])
                        P.dma("pool", dbg_pc[:, :], pc[:].rearrange("p a h -> p (a h)"), t_dO, r=[t_tb])
                        P.dma("pool", dbg_Sf[:, :], Sf[:].rearrange("p h e -> p (h e)"), t_dO, r=[t_Sf])
                        P.dma("pool", dbg_R[:, :], Rall[:, c, :], t_dO, r=[t_Rall])
                    bn, bntr = bnr.next()
                    on, ontr = onr.next()
                    for h in range(RH):
                        bk = 2 + h // 2
                        oc = slice((h % 2) * 256, (h % 2) * 256 + 256)
                        P.op("dve", lambda e, bn=bn, h=h, bk=bk, oc=oc: e.bn_stats(out=bn[:, h, 0:6], in_=ps[bk][:, oc]), r=[pst[bk]], w=[bntr] if h == 0 else (), pw=[bntr] if h else ())
                    for h in range(RH):
                        P.op("dve", lambda e, bn=bn, h=h: e.bn_aggr(out=bn[:, h, 6:8], in_=bn[:, h, 0:6]), r=[bntr], pw=[bntr])
                    P.op("act", lambda e, bn=bn: e.activation(out=bn[:, :, 0], in_=bn[:, :, 7], func=AF.Sqrt, bias=epsc[:, 0:1], scale=1.0), r=[bntr, ctr], pw=[bntr])
                    P.op("dve", lambda e, bn=bn: e.reciprocal(out=bn[:, :, 1], in_=bn[:, :, 0]), r=[bntr], pw=[bntr])
                    for h in range(RH):
                        bk = 2 + h // 2
                        oc = slice((h % 2) * 256, (h % 2) * 256 + 256)
                        P.op("dve", lambda e, on=on, bn=bn, h=h, bk=bk, oc=oc: e.tensor_scalar(out=on[:, h * 256:(h + 1) * 256], in0=ps[bk][:, oc], scalar1=bn[:, h, 6:7], scalar2=bn[:, h, 1:2], op0=ALU.subtract, op1=ALU.mult),
                             r=[pst[bk], bntr], w=[ontr] if h == 0 else (), pw=[ontr] if h else ())
                    P.op("pool", lambda e, on=on: e.tensor_tensor(out=on[:], in0=on[:], in1=gnb[:], op=ALU.mult), r=[ontr, t_gn], w=[ontr])
                    gt, gttr = gtd_.next()
                    P.op("dve", lambda e, gt=gt, on=on, g_=g_: e.tensor_tensor(out=gt[:], in0=on[:], in1=g_[:], op=ALU.mult), r=[ontr, gtr], w=[gttr])
                    pv6 = ps[6][:].bitcast(BF16)
                    for kc in range(8):
                        P.tp(pv6[:, kc * 128:(kc + 1) * 128], gt[:, kc * 128:(kc + 1) * 128], ident[:], [gttr, ctr], pst[6], kc == 0)
                    gT, gTtr = gTr.next()
                    P.op("act", lambda e, gT=gT, pv6=pv6: e.copy(out=gT[:], in_=pv6.rearrange("p (k c) -> p k c", k=8)), r=[pst[6]], w=[gTtr])
                    P.dma("pool", rtTd[s, :, :, tok].rearrange("k p c -> p k c"), gT[:], gTtr, r=[gTtr], pw=[dtr[("rtTd", s)]])

                P.barrier()
        def postnorm_residual(bA, bB, xt, xtr_, gb, t_g, o, otr, sqr, stt):
            sq_, sqtr = sqr.next()
            sm, smtr = stt.next()
            P.op("act", lambda e: e.activation(out=sq_[:, 0:512], in_=ps[bA][:], func=AF.Square), r=[pst[bA]], w=[sqtr])
            P.op("act", lambda e: e.activation(out=sq_[:, 512:1024], in_=ps[bB][:], func=AF.Square), r=[pst[bB]], pw=[sqtr])
            P.op("dve", lambda e: e.reduce_sum(out=sm[:, 2:3], in_=sq_[:], axis=mybir.AxisListType.X), r=[sqtr], w=[smtr])
            P.op("act", lambda e: e.activation(out=sm[:, 3:4], in_=sm[:, 2:3], func=AF.Sqrt, bias=epsc[:, 0:1], scale=1.0 / D), r=[smtr, ctr], pw=[smtr])
            P.op("dve", lambda e: e.reciprocal(out=sm[:, 3:4], in_=sm[:, 3:4]), r=[smtr], pw=[smtr])
            P.op("dve", lambda e: e.scalar_tensor_tensor(out=o[:, 0:512], in0=ps[bA][:], scalar=sm[:, 3:4], in1=gb[:, 0:512], op0=ALU.mult, op1=ALU.mult), r=[pst[bA], smtr, t_g], w=[otr])
            P.op("dve", lambda e: e.scalar_tensor_tensor(out=o[:, 512:1024], in0=ps[bB][:], scalar=sm[:, 3:4], in1=gb[:, 512:1024], op0=ALU.mult, op1=ALU.mult), r=[pst[bB], smtr], pw=[otr])
            P.op("pool", lambda e: e.tensor_tensor(out=o[:], in0=o[:], in1=xt[:], op=ALU.add), r=[xtr_], w=[otr])

        def stage_M(l, s, xsrc, xtr):
            with ExitStack() as st:
                wmo = st.enter_context(sbt(nc, "m_wmo", [128, 4, D], BF16))
                wpw = st.enter_context(sbt(nc, "m_wpw", [128, 4, D], BF16))
                wro = st.enter_context(sbt(nc, "m_wro", [128, 8, D], BF16))
                wou = st.enter_context(sbt(nc, "m_wou", [128, 8, D], BF16))
                gb = st.enter_context(sbt(nc, "m_gb", [128, D], F32))
                t_w = Tr(); t_g = Tr()
                P.dma("sp", wmo[:], wb["mo"][l].rearrange("(kc p) n -> p kc n", p=128), t_w, r=[wb_tr[("mo", l)]], w=[t_w])
                P.dma("sp", wpw[:], wb["pw"][l].rearrange("(kc p) n -> p kc n", p=128), t_w, r=[wb_tr[("pw", l)]], pw=[t_w])
                P.dma("sp", wro[:], wb["ro"][l].rearrange("(kc p) n -> p kc n", p=128), t_w, r=[wb_tr[("ro", l)]], pw=[t_w])
                P.dma("sp", wou[:], wb["wo"][l].rearrange("(kc p) n -> p kc n", p=128), t_w, r=[wb_tr[("wo", l)]], pw=[t_w])
                P.dma("sp", gb[:], ln_mix_post[l].partition_broadcast(128), t_g, w=[t_g])
                wgt = st.enter_context(sbt(nc, "m_wgt", [128, 8, 3072], BF16))
                P.dma("sp", wgt[:, :, 0:1536], wb["in"][l].rearrange("(kc p) n -> p kc n", p=128)[:, :, O_GATE:O_GATE + 1536], t_w, r=[wb_tr[("in", l)]], pw=[t_w])
                P.dma("sp", wgt[:, :, 1536:3072], wb["in"][l].rearrange("(kc p) n -> p kc n", p=128)[:, :, O_GATE + 1536:O_GATE + 3072], t_w, pw=[t_w])
                hTr = Ring(nc, st, "m_hT", [128, 8, 512], BF16, 2)
                oTr = Ring(nc, st, "m_oT", [128, 4, 512], BF16, 2)
                cTr = Ring(nc, st, "m_cT", [128, 4, 512], BF16, 2)
                rTr = Ring(nc, st, "m_rT", [128, 8, 512], BF16, 2)
                gtr_ = Ring(nc, st, "m_gt", [128, 3, 512], F32, 2)
                mt = Ring(nc, st, "m_t", [128, 2, 512], F32, 1)
                mgr = Ring(nc, st, "m_mg", [128, 8, 512], BF16, 2)
                xr = Ring(nc, st, "m_x", [128, D], F32, 2)
                outr = Ring(nc, st, "m_o", [128, D], F32, 2)
                sqr = Ring(nc, st, "m_sq", [128, D], F32, 2)
                stt = Ring(nc, st, "m_st", [128, 8], F32, 3)
                for tb in range(NB):
                    blk = slice(tb * 512, (tb + 1) * 512)
                    o_, otr_ = oTr.next(); c_, ctr_ = cTr.next(); r_, rtr_ = rTr.next()
                    P.dma("sp", o_[:], oTd[s, :, :, blk].rearrange("k p c -> p k c"), otr_, r=[dtr[("oTd", s)]], w=[otr_])
                    P.dma("sp", c_[:], cvTd[s, :, :, blk].rearrange("k p c -> p k c"), ctr_, r=[dtr[("cvTd", s)]], w=[ctr_])
                    P.dma("sp", r_[:], rtTd[s, :, :, blk].rearrange("k p c -> p k c"), rtr_, r=[dtr[("rtTd", s)]], w=[rtr_])
                    hTb, hTbtr = hTr.next()
                    P.dma("sp", hTb[:], hTd[s, :, :, blk].rearrange("k p c -> p k c"), hTbtr, r=[dtr[("hTd", s)]], w=[hTbtr])
                    mg, mgtr = mgr.next()
                    for rc in range(8):
                        cs = slice(rc * 128, (rc + 1) * 128)
                        gt, gttr = gtr_.next()
                        for b in range(3):
                            gc = slice(b * 1024 + rc * 128, b * 1024 + (rc + 1) * 128)
                            for kc in range(8):
                                P.mm(ps[3 + b][:], wgt[:, kc, gc], hTb[:, kc, :], kc == 0, kc == 7, [t_w, hTbtr], pst[3 + b])
                            P.op("act", lambda e, gt=gt, b=b: e.activation(out=gt[:, b, :], in_=ps[3 + b][:], func=AF.Sigmoid), r=[pst[3 + b]],
                                 w=[gttr] if b == 0 else (), pw=[gttr] if b else ())
                        b0 = 0
                        for kc in range(4):
                            P.mm(ps[b0][:], wmo[:, kc, cs], o_[:, kc, :], kc == 0, kc == 3, [t_w, otr_], pst[b0])
                        for kc in range(4):
                            P.mm(ps[b0 + 1][:], wpw[:, kc, cs], c_[:, kc, :], kc == 0, kc == 3, [t_w, ctr_], pst[b0 + 1])
                        for kc in range(8):
                            P.mm(ps[b0 + 2][:], wro[:, kc, cs], r_[:, kc, :], kc == 0, kc == 7, [t_w, rtr_], pst[b0 + 2])
                        t_, ttr = mt.next()
                        P.op("dve", lambda e, t_=t_, gt=gt, b0=b0: e.tensor_tensor(out=t_[:, 0, :], in0=ps[b0][:], in1=gt[:, 0, :], op=ALU.mult), r=[pst[b0], gttr], w=[ttr])
                        P.op("dve", lambda e, t_=t_, gt=gt, b0=b0: e.tensor_tensor(out=t_[:, 1, :], in0=ps[b0 + 1][:], in1=gt[:, 1, :], op=ALU.mult), r=[pst[b0 + 1], gttr], pw=[ttr])
                        P.op("pool", lambda e, t_=t_: e.tensor_tensor(out=t_[:, 0, :], in0=t_[:, 0, :], in1=t_[:, 1, :], op=ALU.add), r=[ttr], w=[ttr])
                        P.op("dve", lambda e, t_=t_, gt=gt, b0=b0: e.tensor_tensor(out=t_[:, 1, :], in0=ps[b0 + 2][:], in1=gt[:, 2, :], op=ALU.mult), r=[pst[b0 + 2], gttr], w=[ttr])
                        P.op("pool", lambda e, t_=t_, mg=mg, rc=rc: e.tensor_tensor(out=mg[:, rc, :], in0=t_[:, 0, :], in1=t_[:, 1, :], op=ALU.add), r=[ttr], w=[mgtr] if rc == 0 else (), pw=[mgtr] if rc else ())
                    for tt in range(4):
                        t = tb * 4 + tt
                        tok = slice(t * 128, (t + 1) * 128)
                        xt, xtr_ = xr.next()
                        P.dma("sp", xt[:], xsrc[s, tok, :], xtr_, r=[xtr[s]], w=[xtr_])
                        ob_ = 6 if tt % 2 == 0 else 4
                        for nb in range(2):
                            for kc in range(8):
                                P.mm(ps[ob_ + nb][:], mg[:, kc, tt * 128:(tt + 1) * 128], wou[:, kc, nb * 512:(nb + 1) * 512], kc == 0, kc == 7, [mgtr, t_w], pst[ob_ + nb])
                        o, otr = outr.next()
                        postnorm_residual(ob_, ob_ + 1, xt, xtr_, gb, t_g, o, otr, sqr, stt)
                        P.dma("pool", x1d[s, tok, :], o[:], otr, r=[otr], pw=[dtr[("x1d", s)]])

                P.barrier()
        def stage_F(l, s, ydst, ykey):
            with ExitStack() as st:
                wg = st.enter_context(sbt(nc, "f_wg", [128, 8, FH], BF16))
                wu = st.enter_context(sbt(nc, "f_wu", [128, 8, FH], BF16))
                t_w = Tr(); t_g = Tr()
                gpre = st.enter_context(sbt(nc, "f_gpre", [128, D], F32))
                gpost = st.enter_context(sbt(nc, "f_gpost", [128, D], F32))
                P.dma("sp", wg[:], wb["fg"][l].rearrange("(kc p) n -> p kc n", p=128), t_w, r=[wb_tr[("fg", l)]], w=[t_w])
                P.dma("sp", wu[:], wb["fu"][l].rearrange("(kc p) n -> p kc n", p=128), t_w, r=[wb_tr[("fu", l)]], pw=[t_w])
                P.dma("sp", gpre[:], ln_ffn_pre[l].partition_broadcast(128), t_g, w=[t_g])
                P.dma("sp", gpost[:], ln_ffn_post[l].partition_broadcast(128), t_g, pw=[t_g])
                wdr = Ring(nc, st, "f_wd", [128, D], BF16, 8)
                xr = Ring(nc, st, "f_x", [128, D], F32, 8)
                hb = Ring(nc, st, "f_hb", [128, D], BF16, 2)
                sqr = Ring(nc, st, "f_sq", [128, D], F32, 2)
                stt = Ring(nc, st, "f_st", [128, 8], F32, 8)
                h2T = Ring(nc, st, "f_h2T", [128, 8, 512], BF16, 2)
                hid = Ring(nc, st, "f_hid", [128, 22, 512], BF16, 1)
                sgr = Ring(nc, st, "f_sg", [128, 512], F32, 2)
                outr = Ring(nc, st, "f_o", [128, D], F32, 2)
                wdv = wb["fd"][l]
                def prep(tb):
                    hT_, hTtr = h2T.next()
                    xts = []
                    for tt in range(4):
                        t = tb * 4 + tt
                        tok = slice(t * 128, (t + 1) * 128)
                        xt, xtr_ = xr.next()
                        xts.append((xt, xtr_))
                        P.dma("sp", xt[:], x1d[s, tok, :], xtr_, r=[dtr[("x1d", s)]], w=[xtr_])
                        sq_, sqtr = sqr.next()
                        sm, smtr = stt.next()
                        ssq4(xt, sq_, sm, xtr_, sqtr, smtr)
                        rstd_from_ssq(sm[:, 0:1], sm[:, 1:2], D, [smtr])
                        h, htr = hb.next()
                        P.op("dve", lambda e, xt=xt, h=h, sm=sm: e.scalar_tensor_tensor(out=h[:], in0=xt[:], scalar=sm[:, 1:2], in1=gpre[:], op0=ALU.mult, op1=ALU.mult), r=[xtr_, smtr, t_g], w=[htr])
                        bk = 4 + (tt % 2)
                        pv = ps[bk][:].bitcast(BF16)
                        for kc in range(8):
                            P.tp(pv[:, kc * 128:(kc + 1) * 128], h[:, kc * 128:(kc + 1) * 128], ident[:], [htr, ctr], pst[bk], kc == 0)
                        P.op("act", lambda e, hT_=hT_, pv=pv, tt=tt: e.copy(out=hT_[:, :, tt * 128:(tt + 1) * 128], in_=pv.rearrange("p (k c) -> p k c", k=8)), r=[pst[bk]],
                             w=[hTtr] if tt == 0 else (), pw=[hTtr] if tt else ())
                    return hT_, hTtr, xts

                nxt = prep(0)
                for tb in range(NB):
                    hT_, hTtr, xts = nxt
                    hd, hdtr = hid.next()
                    for hc in range(22):
                        bg, bu = (4, 5) if hc % 2 == 0 else (6, 7)
                        cs = slice(hc * 128, (hc + 1) * 128)
                        for kc in range(8):
                            P.mm(ps[bg][:], wg[:, kc, cs], hT_[:, kc, :], kc == 0, kc == 7, [t_w, hTtr], pst[bg])
                        for kc in range(8):
                            P.mm(ps[bu][:], wu[:, kc, cs], hT_[:, kc, :], kc == 0, kc == 7, [t_w, hTtr], pst[bu])
                        sg, sgtr = sgr.next()
                        P.op("act", lambda e, sg=sg, bg=bg: e.activation(out=sg[:], in_=ps[bg][:], func=AF.Silu), r=[pst[bg]], w=[sgtr])
                        P.op("dve", lambda e, hd=hd, sg=sg, bu=bu, hc=hc: e.tensor_tensor(out=hd[:, hc, :], in0=ps[bu][:], in1=sg[:], op=ALU.mult), r=[pst[bu], sgtr],
                             w=[hdtr] if hc == 0 else (), pw=[hdtr] if hc else ())
                    if tb + 1 < NB:
                        nxt = prep(tb + 1)
                    for hc in range(22):
                        wd, wdtr = wdr.next()
                        P.dma("sp", wd[:], wdv[hc * 128:(hc + 1) * 128, :], wdtr, r=[wb_tr[("fd", l)]], w=[wdtr])
                        for tt in range(4):
                            for nb in range(2):
                                bk = tt * 2 + nb
                                P.mm(ps[bk][:], hd[:, hc, tt * 128:(tt + 1) * 128], wd[:, nb * 512:(nb + 1) * 512], hc == 0, hc == 21, [hdtr, wdtr], pst[bk])
                    for tt in (2, 3, 0, 1):
                        t = tb * 4 + tt
                        tok = slice(t * 128, (t + 1) * 128)
                        xt, xtr_ = xts[tt]
                        o, otr = outr.next()
                        postnorm_residual(tt * 2, tt * 2 + 1, xt, xtr_, gpost, t_g, o, otr, sqr, stt)
                        P.dma("pool", ydst[s, tok, :], o[:], otr, r=[otr], pw=[dtr[(ykey, s)]])
                P.barrier()

        hide_w1 = ("W" in stages and "A" in stages and nlayers > 1)
        for l in range(nlayers):
            if "W" in stages and not (hide_w1 and l == 1):
                stage_W(l)
        for s in range(NS):
            if "T" in stages:
                stage_T(s)
        xin_tr = [Tr() for _ in range(NS)]
        for l in range(nlayers):
            last = (l == nlayers - 1)
            for s in range(NS):
                if l == 0:
                    xsrc, xtr = x_in, xin_tr
                else:
                    xsrc, xtr = xLd, [dtr[("xLd", s_)] for s_ in range(NS)]
                if "N" in stages:
                    stage_NP(l, s, xsrc, xtr)
                if "A" in stages:
                    stage_A(l, s, with_conv=("C" in stages), wl=(1 if (hide_w1 and l == 0 and s == 0) else None))
                if "R" in stages:
                    stage_R(l, s)
                if "M" in stages:
                    stage_M(l, s, xsrc, xtr)
                if "F" in stages:
                    stage_F(l, s, y_out if last else xLd, "y" if last else "xLd")
        P.wait_all("sp", [dtr[("y", s)] for s in range(NS)])
        for en in ("pe", "act", "dve", "pool"):
            E = P.E[en]
            if E.cnt:
                if P.E["sp"].waited.get(E.key, 0) < E.cnt:
                    P.E["sp"].e.wait_ge(E.sem, E.cnt)
    return nc


def rope_consts():
    def inv(dim):
        return (np.float32(10000.0) ** (-(np.arange(0, dim, 2, dtype=np.float32)) / np.float32(dim))).astype(np.float32)
    im, ir = inv(32), inv(128)
    c = np.zeros((2, 160), np.float32)
    c[0] = np.concatenate([im, im, ir, ir])
    c[1] = np.concatenate([np.zeros(16), np.full(16, np.pi / 2), np.zeros(64), np.full(64, np.pi / 2)]).astype(np.float32)
    return c


WEIGHT_NAMES = ["ln_mix_pre", "ln_mix_post", "ln_ffn_pre", "ln_ffn_post", "w_in", "mla_q_norm", "mla_w_uq", "mla_kv_norm",
                "mla_w_ukv", "mla_w_o", "conv_w_dw", "conv_b_dw", "conv_ln_g", "conv_ln_b", "conv_w_pw", "ret_decay_logits",
                "ret_gn_g", "ret_w_o", "w_out", "ffn_w_gate", "ffn_w_up", "ffn_w_down"]


def kernel(**inputs):
    x = np.ascontiguousarray(np.asarray(inputs["x"], dtype=np.float32))
    pos = np.ascontiguousarray(np.asarray(inputs["positions"], dtype=np.int32))
    B, S, _ = x.shape
    ncores = 8
    NS = B // ncores
    nc = build(S, NS)
    shared = {k: np.ascontiguousarray(np.asarray(inputs[k], dtype=np.float32)) for k in WEIGHT_NAMES}
    shared["rope_consts"] = rope_consts()
    in_maps = []
    for c in range(ncores):
        m = dict(shared)
        m["x"] = x[c * NS:(c + 1) * NS]
        m["positions"] = pos[c * NS:(c + 1) * NS]
        in_maps.append(m)
    res = run_bass_kernel_spmd(nc, in_maps, core_ids=list(range(ncores)))
    return np.concatenate([r["y"] for r in res.results], axis=0).astype(np.float32)
```
